# Optimizing a Trainium2 kernel written in Bass

```python
import jax, jax.numpy as jnp
from jax import lax
import numpy as np

D_MODEL = 2048
BATCH = 2
SEQ = 8192
DEPTH = 2

GRID_W = 64
CTX_LEN = 256
N_MOD = 9
D_FF = 5632
CONV_DIM = D_MODEL // 2
CONV_WIDTH = 3
FOURIER_DIM = D_MODEL // 2
FOURIER_GROUPS = 8
FOURIER_GROUP_DIM = FOURIER_DIM // FOURIER_GROUPS
HEAD_DIM = 64
N_Q_HEADS = D_MODEL // HEAD_DIM
N_KV_HEADS = 4
KV_REP = N_Q_HEADS // N_KV_HEADS
WINDOW = 128
BLOCK = 128
ROPE_BASE = 10000.0
LN_EPS = 1e-5
NEG_INF = -1e30
ALPHA = (2 * DEPTH) ** 0.25
BETA = (8 * DEPTH) ** -0.25
N_EVEN = (DEPTH + 1) // 2
N_ODD = DEPTH // 2

kernel_name = "hybrid_conv_fourier_swa_dit_prefix"


def layer_norm(x, g, b):
    xf = x.astype(jnp.float32)
    mu = xf.mean(-1, keepdims=True)
    var = jnp.square(xf - mu).mean(-1, keepdims=True)
    y = (xf - mu) * lax.rsqrt(var + LN_EPS)
    return (y * g.astype(jnp.float32) + b.astype(jnp.float32)).astype(x.dtype)


def modulate(x, shift, scale):
    return x * (1 + scale) + shift


def post_norm_residual(x, y, gate, g, b):
    return layer_norm(ALPHA * x + gate * y, g, b)


def swiglu(h, wg, wu, wd):
    return (jax.nn.silu(h @ wg) * (h @ wu)) @ wd


def ffn_sublayer(x, shift, scale, gate, wg, wu, wd, g, b):
    y = swiglu(modulate(x, shift, scale), wg, wu, wd)
    return post_norm_residual(x, 0.5 * y, gate, g, b)


def conv_fourier_mixer(h, w_in, w_conv, w_out):
    bsz, seq_len, _ = h.shape
    u = h @ w_in
    g_b, g_c, x_in, u_f = jnp.split(u, [CONV_DIM, 2 * CONV_DIM, 3 * CONV_DIM], axis=-1)
    v = g_c * x_in
    pad = CONV_WIDTH // 2
    vp = jnp.pad(v, ((0, 0), (pad, pad), (0, 0)))
    conv = sum(w_conv[k] * vp[:, k:k + seq_len] for k in range(CONV_WIDTH))
    y_a = g_b * conv
    ug = u_f.astype(jnp.float32).reshape(bsz, seq_len, FOURIER_GROUPS, FOURIER_GROUP_DIM)
    y_b = jnp.fft.fft2(ug, axes=(1, 3), norm="ortho").real
    y_b = y_b.reshape(bsz, seq_len, FOURIER_DIM).astype(h.dtype)
    return jnp.concatenate([y_a, y_b], axis=-1) @ w_out


def axial_rope_tables(seq_len):
    rows = seq_len // GRID_W
    row = jnp.repeat(jnp.arange(rows, dtype=jnp.float32), GRID_W)
    col = jnp.tile(jnp.arange(GRID_W, dtype=jnp.float32), rows)
    n_freq = HEAD_DIM // 4
    inv_freq = jnp.power(ROPE_BASE, -jnp.arange(n_freq, dtype=jnp.float32) / n_freq)
    ang = jnp.concatenate([row[:, None] * inv_freq, col[:, None] * inv_freq], axis=-1)
    return jnp.cos(ang), jnp.sin(ang)


def apply_axial_rope(x, cos, sin):
    b, l, h, d = x.shape
    xa = x.reshape(b, l, h, 2, 2, d // 4)
    x1, x2 = xa[..., 0, :], xa[..., 1, :]
    c = cos.reshape(l, 1, 2, d // 4).astype(x.dtype)
    s = sin.reshape(l, 1, 2, d // 4).astype(x.dtype)
    out = jnp.stack([x1 * c - x2 * s, x2 * c + x1 * s], axis=-2)
    return out.reshape(b, l, h, d)


def attend(sink_g, scores, values):
    sink_col = jnp.broadcast_to(sink_g[None, :, :, None, None], scores[0].shape[:-1] + (1,))
    p = jax.nn.softmax(jnp.concatenate([sink_col] + scores, axis=-1), axis=-1)
    out = 0.0
    start = 1
    for s, v in zip(scores, values):
        n = s.shape[-1]
        out = out + jnp.einsum('bgrqk,bkgd->bqgrd', p[..., start:start + n].astype(v.dtype), v)
        start += n
    return out


def window_attention(h_lat, h_ctx, w_in, sink, w_out, cos, sin, need_ctx_out):
    bsz, seq_len, _ = h_lat.shape
    n_blk = seq_len // BLOCK
    scale = HEAD_DIM ** -0.5
    sink_g = sink.astype(jnp.float32).reshape(N_KV_HEADS, KV_REP)

    def project(h):
        b, n, _ = h.shape
        u = h @ w_in
        q, k, v = jnp.split(u, [N_Q_HEADS * HEAD_DIM, (N_Q_HEADS + N_KV_HEADS) * HEAD_DIM], axis=-1)
        return (q.reshape(b, n, N_Q_HEADS, HEAD_DIM), k.reshape(b, n, N_KV_HEADS, HEAD_DIM),
                v.reshape(b, n, N_KV_HEADS, HEAD_DIM))

    q_c, k_c, v_c = project(h_ctx)
    q_l, k_l, v_l = project(h_lat)
    q_l = apply_axial_rope(q_l, cos, sin)
    k_l = apply_axial_rope(k_l, cos, sin)

    qb = q_l.reshape(bsz, n_blk, BLOCK, N_KV_HEADS, KV_REP, HEAD_DIM).transpose(1, 0, 2, 3, 4, 5)

    def band(t):
        tp = jnp.pad(t, ((0, 0), (BLOCK, BLOCK), (0, 0), (0, 0)))
        tb = tp.reshape(bsz, n_blk + 2, BLOCK, N_KV_HEADS, HEAD_DIM)
        tband = jnp.concatenate([tb[:, :-2], tb[:, 1:-1], tb[:, 2:]], axis=2)
        return tband.transpose(1, 0, 2, 3, 4)

    k_band, v_band = band(k_l), band(v_l)
    blk = jnp.arange(n_blk)[:, None, None]
    a = jnp.arange(BLOCK)[None, :, None]
    j = jnp.arange(3 * BLOCK)[None, None, :]
    k_pos = blk * BLOCK + j - BLOCK
    mask = (jnp.abs(j - BLOCK - a) <= WINDOW) & (k_pos >= 0) & (k_pos < seq_len)

    def block_step(args):
        qi, ki, vi, mi = args
        s_ctx = jnp.einsum('bqgrd,bcgd->bgrqc', qi, k_c).astype(jnp.float32) * scale
        s_loc = jnp.einsum('bqgrd,bkgd->bgrqk', qi, ki).astype(jnp.float32) * scale
        s_loc = jnp.where(mi, s_loc, NEG_INF)
        return attend(sink_g, [s_ctx, s_loc], [v_c, vi])

    o = lax.map(block_step, (qb, k_band, v_band, mask))
    o = o.transpose(1, 0, 2, 3, 4, 5).reshape(bsz, seq_len, N_Q_HEADS * HEAD_DIM)
    y_lat = o @ w_out

    y_ctx = None
    if need_ctx_out:
        n_ctx = h_ctx.shape[1]
        qc = q_c.reshape(bsz, n_ctx, N_KV_HEADS, KV_REP, HEAD_DIM)
        s_cc = jnp.einsum('bqgrd,bcgd->bgrqc', qc, k_c).astype(jnp.float32) * scale
        oc = attend(sink_g, [s_cc], [v_c]).reshape(bsz, n_ctx, N_Q_HEADS * HEAD_DIM)
        y_ctx = oc @ w_out
    return y_lat, y_ctx


def setup_inputs(seed: int = 0) -> dict:
    key = jax.random.key(seed)
    ks = jax.random.split(key, 20)
    f32 = jnp.float32

    def nrm(k, shape, fan_in, gain=1.0):
        return jax.random.normal(k, shape, f32) * (gain * fan_in ** -0.5)

    qkv_w = (N_Q_HEADS + 2 * N_KV_HEADS) * HEAD_DIM
    return {
        "x": jax.random.normal(ks[0], (BATCH, SEQ, D_MODEL), f32),
        "c": jax.random.normal(ks[1], (BATCH, D_MODEL), f32),
        "ctx": jax.random.normal(ks[2], (BATCH, CTX_LEN, D_MODEL), f32),
        "c_ctx": jax.random.normal(ks[3], (D_MODEL,), f32),
        "w_mod": nrm(ks[4], (DEPTH, D_MODEL, N_MOD * D_MODEL), D_MODEL),
        "b_mod": 0.01 * jax.random.normal(ks[5], (DEPTH, N_MOD * D_MODEL), f32),
        "ln_g": 1.0 + 0.02 * jax.random.normal(ks[6], (DEPTH, 3, D_MODEL), f32),
        "ln_b": 0.02 * jax.random.normal(ks[7], (DEPTH, 3, D_MODEL), f32),
        "ffn_w_gate": nrm(ks[8], (DEPTH, 2, D_MODEL, D_FF), D_MODEL),
        "ffn_w_up": nrm(ks[9], (DEPTH, 2, D_MODEL, D_FF), D_MODEL),
        "ffn_w_down": nrm(ks[10], (DEPTH, 2, D_FF, D_MODEL), D_FF, BETA),
        "ab_w_in": nrm(ks[11], (N_EVEN, D_MODEL, 3 * CONV_DIM + FOURIER_DIM), D_MODEL),
        "ab_conv": nrm(ks[12], (N_EVEN, CONV_WIDTH, CONV_DIM), CONV_WIDTH),
        "ab_w_out": nrm(ks[13], (N_EVEN, CONV_DIM + FOURIER_DIM, D_MODEL), CONV_DIM + FOURIER_DIM, BETA),
        "attn_w_in": nrm(ks[14], (N_ODD, D_MODEL, qkv_w), D_MODEL),
        "attn_sink": jax.random.normal(ks[15], (N_ODD, N_Q_HEADS), f32),
        "attn_w_out": nrm(ks[16], (N_ODD, N_Q_HEADS * HEAD_DIM, D_MODEL), N_Q_HEADS * HEAD_DIM, BETA),
    }


def reference(x, c, ctx, c_ctx, w_mod, b_mod, ln_g, ln_b, ffn_w_gate, ffn_w_up, ffn_w_down,
              ab_w_in, ab_conv, ab_w_out, attn_w_in, attn_sink, attn_w_out):
    xl, xc = x, ctx
    cos, sin = axial_rope_tables(x.shape[1])
    for layer in range(DEPTH):
        last = layer == DEPTH - 1
        even = layer % 2 == 0
        idx = layer // 2
        ml = jnp.split((jax.nn.silu(c) @ w_mod[layer] + b_mod[layer])[:, None, :], N_MOD, axis=-1)
        mc = jnp.split((jax.nn.silu(c_ctx) @ w_mod[layer] + b_mod[layer])[None, None, :], N_MOD, axis=-1)
        ctx_ffn1 = not (last and even)
        ctx_rest = not last

        xl = ffn_sublayer(xl, ml[0], ml[1], ml[2], ffn_w_gate[layer, 0], ffn_w_up[layer, 0],
                          ffn_w_down[layer, 0], ln_g[layer, 0], ln_b[layer, 0])
        if ctx_ffn1:
            xc = ffn_sublayer(xc, mc[0], mc[1], mc[2], ffn_w_gate[layer, 0], ffn_w_up[layer, 0],
                              ffn_w_down[layer, 0], ln_g[layer, 0], ln_b[layer, 0])

        if even:
            yl = conv_fourier_mixer(modulate(xl, ml[3], ml[4]), ab_w_in[idx], ab_conv[idx], ab_w_out[idx])
            xl = post_norm_residual(xl, yl, ml[5], ln_g[layer, 1], ln_b[layer, 1])
            if ctx_rest:
                yc = conv_fourier_mixer(modulate(xc, mc[3], mc[4]), ab_w_in[idx], ab_conv[idx], ab_w_out[idx])
                xc = post_norm_residual(xc, yc, mc[5], ln_g[layer, 1], ln_b[layer, 1])
        else:
            yl, yc = window_attention(modulate(xl, ml[3], ml[4]), modulate(xc, mc[3], mc[4]),
                                      attn_w_in[idx], attn_sink[idx], attn_w_out[idx], cos, sin, ctx_rest)
            xl = post_norm_residual(xl, yl, ml[5], ln_g[layer, 1], ln_b[layer, 1])
            if ctx_rest:
                xc = post_norm_residual(xc, yc, mc[5], ln_g[layer, 1], ln_b[layer, 1])

        xl = ffn_sublayer(xl, ml[6], ml[7], ml[8], ffn_w_gate[layer, 1], ffn_w_up[layer, 1],
                          ffn_w_down[layer, 1], ln_g[layer, 2], ln_b[layer, 2])
        if ctx_rest:
            xc = ffn_sublayer(xc, mc[6], mc[7], mc[8], ffn_w_gate[layer, 1], ffn_w_up[layer, 1],
                              ffn_w_down[layer, 1], ln_g[layer, 2], ln_b[layer, 2])
    return xl
```

```python
import numpy as np
from contextlib import ExitStack
import concourse.bass as bass
import concourse.mybir as mybir
from concourse.bass_utils import run_bass_kernel_spmd

F32 = mybir.dt.float32
BF16 = mybir.dt.bfloat16
AF = mybir.ActivationFunctionType
ALU = mybir.AluOpType
AX = mybir.AxisListType

D = 2048
DFF = 5632
NCH = 16
FCH = 44
SEQ = 8192
CTX = 256
NCORE = 8
ALPHA = 4.0 ** 0.25
LN_EPS = 1e-5
TB = 512


class Buf:
    __slots__ = ("w", "r", "name")

    def __init__(self, name=""):
        self.w = []
        self.r = []
        self.name = name


class Ctr:
    LIMIT = 30000

    def __init__(self, P, name, step):
        self.P, self.name, self.step = P, name, step
        self.k = 0
        self.done = []
        self._new()

    def _new(self):
        self.sem = self.P.stack.enter_context(self.P.nc.semaphore(f"{self.name}_{self.k}"))
        self.k += 1
        self.val = 0

    def next(self):
        if self.val + self.step > self.LIMIT:
            self.done.append((self.sem, self.val))
            self._new()
        self.val += self.step
        return (self.sem, self.val)


class Eng:
    def __init__(self, P, name):
        self.name = name
        self.ops = []
        self.waited = {}
        self.ctr = Ctr(P, "e" + name, 1)


class Prog:
    def __init__(self, nc):
        self.nc = nc
        self.stack = ExitStack()
        self.engs = {n: Eng(self, n) for n in ("pe", "act", "dve", "pool", "sp")}
        self.dctr = {}
        self.fuzzy = {}
        self.n = 0

    def sb(self, name, shape, dt):
        return self.stack.enter_context(self.nc.sbuf_tensor("s_" + name, list(shape), dt))

    def ps(self, name, shape, dt=F32):
        return self.stack.enter_context(self.nc.psum_tensor("p_" + name, list(shape), dt))

    def op(self, eng, fn, reads=(), writes=(), dma=None, cowrites=()):
        E = self.engs[eng]
        deps = {}

        def add(tok):
            s, v = tok
            k = id(s)
            if k in self.fuzzy:
                v = max(v, self.fuzzy[k].val if self.fuzzy[k].sem is s else v)
            if k not in deps or deps[k][1] < v:
                deps[k] = (s, v)

        for b in reads:
            for t in b.w:
                add(t)
        for b in writes:
            for t in b.w:
                add(t)
            for t in b.r:
                add(t)
        for b in cowrites:
            for t in b.r:
                add(t)
        if dma is not None:
            if dma not in self.dctr:
                self.dctr[dma] = Ctr(self, "d" + dma, 16)
            ctr = self.dctr[dma]
            if dma == "st":
                self.fuzzy[id(ctr.sem)] = ctr
        else:
            ctr = E.ctr
        waits = []
        for k, (s, v) in deps.items():
            if eng == "pe" and dma is None and s is E.ctr.sem:
                continue
            if E.waited.get(k, 0) >= v:
                continue
            E.waited[k] = v
            waits.append((s, v))
        tok = ctr.next()
        E.ops.append((waits, fn, tok[0], ctr.step))
        for b in reads:
            b.r.append(tok)
        for b in writes:
            b.w = [tok]
            b.r = []
        for b in cowrites:
            b.w.append(tok)
        self.n += 1
        return tok

    def finish(self):
        E = self.engs["sp"]
        waits = []
        for ctr in self.dctr.values():
            for s, v in ctr.done + [(ctr.sem, ctr.val)]:
                if v > 0 and E.waited.get(id(s), 0) < v:
                    waits.append((s, v))
        E.ops.append((waits, None, None, 0))

    def emit(self):
        nc = self.nc

        def mk(E):
            def run(e):
                for waits, fn, sem, step in E.ops:
                    for ws, wv in waits:
                        e.wait_ge(ws, wv)
                    if fn is not None:
                        fn(e).then_inc(sem, step)
            return run

        with nc.Block() as block:
            block.tensor(mk(self.engs["pe"]))
            block.scalar(mk(self.engs["act"]))
            block.vector(mk(self.engs["dve"]))
            block.gpsimd(mk(self.engs["pool"]))
            block.sync(mk(self.engs["sp"]))
        self.stack.close()


def blocks_of(T, tb=TB):
    return [(c, min(c + tb, T)) for c in range(0, T, tb)]


def chunked(ap):
    return ap.rearrange("(c p) t -> p c t", p=128)


class RowLocal:
    def __init__(self, P, T, segs, vecs_ap, nvec):
        self.P, self.T, self.segs = P, T, segs
        self.blocks = blocks_of(T)
        nc = P.nc
        self.xn = P.sb("xn", [128, NCH, TB], F32)
        self.hb = P.sb("hb", [128, NCH, TB], BF16)
        self.ab = P.sb("ab", [128, FCH, TB], BF16)
        self.wa = [P.sb(f"wa{i}", [128, NCH, 256], BF16) for i in range(4)]
        self.wb = [P.sb(f"wb{i}", [128, FCH, 128], BF16) for i in range(2)]
        self.mu = P.sb("mu", [128, TB], F32)
        self.msq = P.sb("msq", [128, TB], F32)
        self.rstd = P.sb("rstd", [128, TB], F32)
        self.sg = [P.sb(f"sg{i}", [128, TB], F32) for i in range(2)]
        self.zt = [P.sb(f"zt{i}", [128, TB], F32) for i in range(2)]
        self.tmp = [P.sb(f"tmp{i}", [128, TB], F32) for i in range(2)]
        self.ones = P.sb("ones", [128, 128], BF16)
        self.vecs = P.sb("vecs", [128, nvec, NCH], F32)
        self.s1 = P.ps("s1", [128, TB])
        self.s2 = P.ps("s2", [128, TB])
        self.pg = [P.ps(f"pg{i}", [128, TB]) for i in range(2)]
        self.pu = [P.ps(f"pu{i}", [128, TB]) for i in range(2)]
        self.py = [P.ps(f"py{i}", [128, TB]) for i in range(2)]
        B = Buf
        self.b_xn, self.b_hb, self.b_ab = B("xn"), B("hb"), B("ab")
        self.b_abc = [B(f"ab{c}") for c in range(FCH)]
        self.b_wa = [B() for _ in range(4)]
        self.b_wb = [B() for _ in range(2)]
        self.b_mu, self.b_msq, self.b_rstd = B(), B(), B()
        self.b_sg, self.b_zt, self.b_tmp = [B(), B()], [B(), B()], [B(), B()]
        self.b_s1, self.b_s2 = B(), B()
        self.b_pg, self.b_pu, self.b_py = [B(), B()], [B(), B()], [B(), B()]
        self.b_ones, self.b_vecs = B(), B()
        self.wa_i = 0
        self.wb_i = 0
        self.zt_i = 0
        self.dq = 0
        ones, vecs = self.ones, self.vecs
        P.op("dve", lambda e: e.memset(ones[:], 1.0), writes=[self.b_ones])
        P.op("sp", lambda e: e.dma_start(out=vecs[:], in_=vecs_ap), writes=[self.b_vecs], dma="vecs")

    def vc(self, v, c):
        return self.vecs[:, v, c:c + 1]

    def derive(self, v, mul=None, add=None):
        vecs = self.vecs
        if add is not None:
            self.P.op("dve", lambda e: e.tensor_scalar_add(out=vecs[:, v, :], in0=vecs[:, v, :], scalar1=float(add)),
                      reads=[self.b_vecs], writes=[self.b_vecs])
        if mul is not None:
            self.P.op("dve", lambda e: e.tensor_scalar_mul(out=vecs[:, v, :], in0=vecs[:, v, :], scalar1=float(mul)),
                      reads=[self.b_vecs], writes=[self.b_vecs])

    def segparts(self, c0, c1):
        out = []
        for si, (s0, s1) in enumerate(self.segs):
            a, b = max(c0, s0), min(c1, s1)
            if a < b:
                out.append((si, a - c0, b - c0))
        return out

    def wq(self):
        return "pool"

    def load_xn(self, src_ap, src_buf, c0, c1):
        w = c1 - c0
        xn = self.xn
        self.P.op("sp", lambda e: e.dma_start(out=xn[:, :, :w], in_=chunked(src_ap)[:, :, c0:c1]),
                  reads=[src_buf], writes=[self.b_xn], dma="xin")

    def layernorm(self, w, vg, vb):
        P = self.P
        xn, hb, ab, ones = self.xn, self.hb, self.ab, self.ones
        mu, msq, rstd, s1, s2 = self.mu, self.msq, self.rstd, self.s1, self.s2
        P.op("act", lambda e: e.activation(out=hb[:, :, :w], in_=xn[:, :, :w], func=AF.Copy),
             reads=[self.b_xn], writes=[self.b_hb])
        P.op("act", lambda e: e.activation(out=ab[:, 0:NCH, :w], in_=xn[:, :, :w], func=AF.Square),
             reads=[self.b_xn], writes=[self.b_ab] + self.b_abc[:NCH])

        def st(src, dst):
            def f(e):
                for c in range(NCH):
                    ins = e.matmul(dst[:, :w], lhsT=ones[:], rhs=src[:, c, :w], start=(c == 0), stop=(c == NCH - 1))
                return ins
            return f
        P.op("pe", st(hb, s1), reads=[self.b_hb, self.b_ones], writes=[self.b_s1])
        P.op("pe", st(ab, s2), reads=[self.b_ab, self.b_ones] + self.b_abc[:NCH], writes=[self.b_s2])
        P.op("dve", lambda e: e.tensor_scalar_mul(out=mu[:, :w], in0=s1[:, :w], scalar1=1.0 / D),
             reads=[self.b_s1], writes=[self.b_mu])
        P.op("dve", lambda e: e.tensor_tensor(out=msq[:, :w], in0=mu[:, :w], in1=mu[:, :w], op=ALU.mult),
             reads=[self.b_mu], writes=[self.b_msq])
        P.op("dve", lambda e: e.scalar_tensor_tensor(out=rstd[:, :w], in0=s2[:, :w], scalar=1.0 / D, in1=msq[:, :w],
                                                     op0=ALU.mult, op1=ALU.subtract),
             reads=[self.b_s2, self.b_msq], writes=[self.b_rstd])
        P.op("dve", lambda e: e.tensor_scalar_add(out=rstd[:, :w], in0=rstd[:, :w], scalar1=LN_EPS),
             reads=[self.b_rstd], writes=[self.b_rstd])
        P.op("act", lambda e: e.activation(out=rstd[:, :w], in_=rstd[:, :w], func=AF.Sqrt),
             reads=[self.b_rstd], writes=[self.b_rstd])
        P.op("dve", lambda e: e.reciprocal(out=rstd[:, :w], in_=rstd[:, :w]),
             reads=[self.b_rstd], writes=[self.b_rstd])
        for c in range(NCH):
            P.op("dve", (lambda c: lambda e: e.tensor_tensor(out=xn[:, c, :w], in0=xn[:, c, :w], in1=mu[:, :w],
                                                             op=ALU.subtract))(c),
                 reads=[self.b_mu, self.b_xn], writes=[self.b_xn])
            P.op("dve", (lambda c: lambda e: e.tensor_tensor(out=xn[:, c, :w], in0=xn[:, c, :w], in1=rstd[:, :w],
                                                             op=ALU.mult))(c),
                 reads=[self.b_rstd, self.b_xn], writes=[self.b_xn])
            P.op("act", (lambda c: lambda e: e.activation(out=xn[:, c, :w], in_=xn[:, c, :w], func=AF.Identity,
                                                          scale=self.vc(vg, c), bias=self.vc(vb, c)))(c),
                 reads=[self.b_vecs, self.b_xn], writes=[self.b_xn])

    def modulate(self, c0, c1, vshift, vscale1):
        P = self.P
        xn, hb = self.xn, self.hb
        for si, a, b in self.segparts(c0, c1):
            for c in range(NCH):
                P.op("act", (lambda c, si, a, b: lambda e: e.activation(
                    out=hb[:, c, a:b], in_=xn[:, c, a:b], func=AF.Identity,
                    scale=self.vc(vscale1[si], c), bias=self.vc(vshift[si], c)))(c, si, a, b),
                    reads=[self.b_vecs, self.b_xn], writes=[self.b_hb])

    def load_wa(self, W_ap, col0, ncols):
        i = self.wa_i
        self.wa_i = (i + 1) % 4
        t = self.wa[i]
        src = W_ap.rearrange("(k p) f -> p k f", p=128)[:, :, col0:col0 + ncols]
        self.P.op(self.wq(), lambda e: e.dma_start(out=t[:, :, :ncols], in_=src), writes=[self.b_wa[i]], dma=f"wa{i}")
        return i

    def load_wb(self, W_ap, col0):
        i = self.wb_i
        self.wb_i = (i + 1) % 2
        t = self.wb[i]
        src = W_ap.rearrange("(k p) f -> p k f", p=128)[:, :, col0:col0 + 128]
        self.P.op(self.wq(), lambda e: e.dma_start(out=t[:], in_=src), writes=[self.b_wb[i]], dma=f"wb{i}")
        return i

    def mm(self, dst, dst_buf, wt, wbuf, j0, kc, X, xbufs, w):
        def f(e):
            for k in range(kc):
                ins = e.matmul(dst[:, :w], lhsT=wt[:, k, j0:j0 + 128], rhs=X[:, k, :w], start=(k == 0), stop=(k == kc - 1))
            return ins
        self.P.op("pe", f, reads=[wbuf] + list(xbufs), writes=[dst_buf])

    def store(self, dst_ap, dst_buf, row0, c0, c1, tile, tbuf):
        w = c1 - c0
        self.P.op("sp", lambda e: e.dma_start(out=dst_ap[row0:row0 + 128, c0:c1], in_=tile[:, :w]),
                  reads=[tbuf], cowrites=[dst_buf], dma="st")

    def residual_out(self, dst_ap, dst_bufs, bi, dc, c0, c1, psum, pbuf, vgate):
        P = self.P
        w = c1 - c0
        i = self.zt_i
        self.zt_i ^= 1
        tmp, zt, xn = self.tmp[i], self.zt[i], self.xn
        for si, a, b in self.segparts(c0, c1):
            P.op("act", (lambda si, a, b: lambda e: e.activation(out=tmp[:, a:b], in_=psum[:, a:b], func=AF.Copy,
                                                                 scale=self.vc(vgate[si], dc)))(si, a, b),
                 reads=[pbuf, self.b_vecs], writes=[self.b_tmp[i]])
        P.op("dve", lambda e: e.scalar_tensor_tensor(out=zt[:, :w], in0=xn[:, dc, :w], scalar=ALPHA, in1=tmp[:, :w],
                                                     op0=ALU.mult, op1=ALU.add),
             reads=[self.b_xn, self.b_tmp[i]], writes=[self.b_zt[i]])
        self.store(dst_ap, dst_bufs[bi], dc * 128, c0, c1, zt, self.b_zt[i])

    def ffn(self, bi, c0, c1, wg, wu, wd, dst_ap, dst_bufs, vgate):
        P = self.P
        w = c1 - c0
        hb, ab = self.hb, self.ab
        nfg = FCH // 2
        ld = lambda fg: (self.load_wa(wg, fg * 256, 256), self.load_wa(wu, fg * 256, 256))
        nxt = ld(0)
        for fg in range(nfg):
            ig, iu = nxt
            if fg + 1 < nfg:
                nxt = ld(fg + 1)
            for j in range(2):
                fc = 2 * fg + j
                pb = fc % 2
                self.mm(self.pg[pb], self.b_pg[pb], self.wa[ig], self.b_wa[ig], j * 128, NCH, hb, [self.b_hb], w)
                self.mm(self.pu[pb], self.b_pu[pb], self.wa[iu], self.b_wa[iu], j * 128, NCH, hb, [self.b_hb], w)
                sg, pg, pu = self.sg[pb], self.pg[pb], self.pu[pb]
                P.op("act", lambda e, sg=sg, pg=pg: e.activation(out=sg[:, :w], in_=pg[:, :w], func=AF.Silu),
                     reads=[self.b_pg[pb]], writes=[self.b_sg[pb]])
                P.op("dve", lambda e, sg=sg, pu=pu, fc=fc: e.tensor_tensor(out=ab[:, fc, :w], in0=sg[:, :w], in1=pu[:, :w],
                                                                          op=ALU.mult),
                     reads=[self.b_sg[pb], self.b_pu[pb]], writes=[self.b_abc[fc]])
        nxt = self.load_wb(wd, 0)
        for dc in range(NCH):
            iw = nxt
            if dc + 1 < NCH:
                nxt = self.load_wb(wd, (dc + 1) * 128)
            pb = dc % 2
            self.mm(self.py[pb], self.b_py[pb], self.wb[iw], self.b_wb[iw], 0, FCH, ab, self.b_abc, w)
            self.residual_out(dst_ap, dst_bufs, bi, dc, c0, c1, self.py[pb], self.b_py[pb], vgate)


def dram_in(nc, name, shape, dt=F32):
    return nc.dram_tensor(name, list(shape), dt, kind="ExternalInput").ap()


def dram_out(nc, name, shape, dt=F32):
    return nc.dram_tensor(name, list(shape), dt, kind="ExternalOutput").ap()


MODC = 2 * 9 * D // NCORE


def build_L0():
    nc = bass.Bass("TRN2", target_bir_lowering=False)
    cv_ap = dram_in(nc, "cv", [128, NCH, 3])
    w_ap = dram_in(nc, "w", [D, MODC])
    b_ap = dram_in(nc, "b", [3, MODC])
    o_ap = dram_out(nc, "mod", [3, MODC])
    P = Prog(nc)
    cv = P.sb("cv", [128, NCH, 3], F32)
    cb = P.sb("cb", [128, NCH, 3], BF16)
    bt = P.sb("bt", [3, MODC], F32)
    ot = P.sb("ot", [3, MODC], F32)
    wt = [P.sb(f"w{i}", [128, NCH, 512], BF16) for i in range(2)]
    ps = [P.ps(f"ps{i}", [128, 512]) for i in range(2)]
    b_cv, b_cb, b_bt, b_ot = Buf(), Buf(), Buf(), Buf()
    b_w, b_ps = [Buf(), Buf()], [Buf(), Buf()]
    P.op("sp", lambda e: e.dma_start(out=cv[:], in_=cv_ap), writes=[b_cv], dma="cv")
    P.op("sp", lambda e: e.dma_start(out=bt[:], in_=b_ap), writes=[b_bt], dma="bt")
    P.op("act", lambda e: e.activation(out=cb[:], in_=cv[:], func=AF.Silu), reads=[b_cv], writes=[b_cb])
    wv = w_ap.rearrange("(k p) f -> p k f", p=128)
    for t in range(MODC // 512):
        i = t % 2
        P.op("pool", lambda e, t=t, i=i: e.dma_start(out=wt[i][:], in_=wv[:, :, t * 512:(t + 1) * 512]),
             writes=[b_w[i]], dma=f"w{i}")

        def f(e, i=i):
            for k in range(NCH):
                ins = e.matmul(ps[i][0:3, :], lhsT=cb[:, k, :], rhs=wt[i][:, k, :], start=(k == 0), stop=(k == NCH - 1))
            return ins
        P.op("pe", f, reads=[b_cb, b_w[i]], writes=[b_ps[i]])
        P.op("dve", lambda e, t=t, i=i: e.tensor_tensor(out=ot[:, t * 512:(t + 1) * 512], in0=ps[i][0:3, :],
                                                        in1=bt[:, t * 512:(t + 1) * 512], op=ALU.add),
             reads=[b_ps[i], b_bt], writes=[b_ot])
    P.op("sp", lambda e: e.dma_start(out=o_ap, in_=ot[:]), reads=[b_ot], writes=[Buf()], dma="st")
    P.finish()
    P.emit()
    return nc


def build_LA(T, segs):
    nc = bass.Bass("TRN2", target_bir_lowering=False)
    nseg = len(segs)
    nvec = 5 * nseg + 2
    x_ap = dram_in(nc, "xT", [D, T])
    vecs_ap = dram_in(nc, "vecs", [128, nvec, NCH])
    wg = dram_in(nc, "wg", [D, DFF])
    wu = dram_in(nc, "wu", [D, DFF])
    wd = dram_in(nc, "wd", [DFF, D])
    win = dram_in(nc, "win", [D, 4096])
    z1 = dram_out(nc, "z1", [D, T])
    gb = dram_out(nc, "gb", [1024, T])
    vv = dram_out(nc, "vv", [1024, T])
    uf = dram_out(nc, "uf", [1024, T])
    P = Prog(nc)
    R = RowLocal(P, T, segs, vecs_ap, nvec)
    gcs = P.sb("gcs", [128, 8, TB], F32)
    b_gcs = [Buf() for _ in range(8)]
    V = lambda s, k: 5 * s + k
    VG, VB = 5 * nseg, 5 * nseg + 1
    for s in range(nseg):
        R.derive(V(s, 1), add=1.0)
        R.derive(V(s, 2), mul=0.5)
        R.derive(V(s, 4), add=1.0)
    b_x = Buf()
    b_z1 = [Buf() for _ in R.blocks]
    b_o = Buf()
    for bi, (c0, c1) in enumerate(R.blocks):
        R.load_xn(x_ap, b_x, c0, c1)
        R.modulate(c0, c1, [V(s, 0) for s in range(nseg)], [V(s, 1) for s in range(nseg)])
        R.ffn(bi, c0, c1, wg, wu, wd, z1, b_z1, [V(s, 2) for s in range(nseg)])
    for bi, (c0, c1) in enumerate(R.blocks):
        w = c1 - c0
        R.load_xn(z1, b_z1[bi], c0, c1)
        R.layernorm(w, VG, VB)
        R.modulate(c0, c1, [V(s, 3) for s in range(nseg)], [V(s, 4) for s in range(nseg)])
        nxt = R.load_wa(win, 0, 256)
        for g in range(16):
            iw = nxt
            if g + 1 < 16:
                nxt = R.load_wa(win, (g + 1) * 256, 256)
            for j in range(2):
                oc = 2 * g + j
                pb = oc % 2
                R.mm(R.pg[pb], R.b_pg[pb], R.wa[iw], R.b_wa[iw], j * 128, NCH, R.hb, [R.b_hb], w)
                pg = R.pg[pb]
                kind, jj = oc // 8, oc % 8
                if kind == 1:
                    P.op("act", lambda e, jj=jj, pg=pg, w=w: e.activation(out=gcs[:, jj, :w], in_=pg[:, :w], func=AF.Copy),
                         reads=[R.b_pg[pb]], writes=[b_gcs[jj]])
                else:
                    i = R.zt_i
                    R.zt_i ^= 1
                    zt = R.zt[i]
                    if kind == 2:
                        P.op("dve", lambda e, jj=jj, pg=pg, zt=zt, w=w: e.tensor_tensor(out=zt[:, :w], in0=pg[:, :w],
                                                                                  in1=gcs[:, jj, :w], op=ALU.mult),
                             reads=[R.b_pg[pb], b_gcs[jj]], writes=[R.b_zt[i]])
                    else:
                        P.op("act", lambda e, pg=pg, zt=zt, w=w: e.activation(out=zt[:, :w], in_=pg[:, :w], func=AF.Copy),
                             reads=[R.b_pg[pb]], writes=[R.b_zt[i]])
                    dst = {0: gb, 2: vv, 3: uf}[kind]
                    R.store(dst, b_o, jj * 128, c0, c1, zt, R.b_zt[i])
    P.finish()
    P.emit()
    return nc


def build_LF():
    nc = bass.Bass("TRN2", target_bir_lowering=False)
    xl = dram_in(nc, "xl", [2, SEQ, 128])
    xc = dram_in(nc, "xc", [2, 128, CTX])
    ftw_ap = dram_in(nc, "ftw", [128, 64, 256])
    fc_ap = dram_in(nc, "fc", [128, 2, 256])
    c64_ap = dram_in(nc, "c64", [64, 2, 64])
    c256_ap = dram_in(nc, "c256", [128, 4, 256])
    yl = dram_out(nc, "yl", [2, 128, SEQ])
    yc = dram_out(nc, "yc", [2, 128, CTX])
    P = Prog(nc)
    ftw = P.sb("ftw", [128, 64, 256], BF16)
    fc = P.sb("fc", [128, 2, 256], BF16)
    c64 = P.sb("c64", [64, 2, 64], BF16)
    c256 = P.sb("c256", [128, 4, 256], BF16)
    XA = P.sb("XA", [128, 64, 128], BF16)
    D1 = P.sb("D1", [128, 64, 256], BF16)
    D2 = P.sb("D2", [64, 128, 256], BF16)
    Y = P.sb("Y", [128, SEQ], F32)
    XC = P.sb("XC", [128, CTX], BF16)
    DA = P.sb("DA", [128, 2, 256], BF16)
    YC = P.sb("YC", [128, CTX], F32)
    pp = [P.ps(f"pp{i}", [128, 512]) for i in range(4)]
    b_pp = [Buf() for _ in range(4)]
    b_t, b_XA, b_D1, b_D2, b_Y, b_XC, b_DA, b_YC = (Buf() for _ in range(8))
    for t, ap, k in ((ftw, ftw_ap, "t0"), (fc, fc_ap, "t1"), (c64, c64_ap, "t2"), (c256, c256_ap, "t3")):
        P.op("pool", lambda e, t=t, ap=ap: e.dma_start(out=t[:], in_=ap), cowrites=[b_t], dma=k)
    pi = [0]

    def nextp():
        pi[0] = (pi[0] + 1) % 4
        return pi[0]
    ev = [0]

    def evac(out_ap, in_ap, rd, wr, co=False):
        ev[0] ^= 1
        kw = dict(cowrites=[wr]) if co else dict(writes=[wr])
        if ev[0]:
            P.op("act", lambda e: e.activation(out=out_ap, in_=in_ap, func=AF.Copy), reads=[rd], **kw)
        else:
            P.op("dve", lambda e: e.tensor_copy(out=out_ap, in_=in_ap), reads=[rd], **kw)

    for u in range(2):
        src = xl[u].rearrange("(a r) c -> a r c", r=64)
        P.op("pool", lambda e, src=src: e.dma_start(out=XA[:], in_=src), writes=[b_XA], dma="xa")
        P.op("pool", lambda e, u=u: e.dma_start(out=XC[:], in_=xc[u]), writes=[b_XC], dma="xc")
        for b0 in range(0, 64, 2):
            i = nextp()

            def f(e, b0=b0, i=i):
                for q in range(2):
                    ins = e.matmul(pp[i][:, q * 256:(q + 1) * 256], lhsT=XA[:, b0 + q, :], rhs=ftw[:, b0 + q, :],
                                   start=True, stop=True)
                return ins
            P.op("pe", f, reads=[b_XA, b_t], writes=[b_pp[i]])
            evac(D1[:, b0:b0 + 2, :], pp[i][:].rearrange("p (q n) -> p q n", q=2), b_pp[i], b_D1, co=True)
        for a0 in range(0, 128, 2):
            i = nextp()

            def f(e, a0=a0, i=i):
                for q in range(2):
                    e.matmul(pp[i][0:64, q * 256:(q + 1) * 256], lhsT=D1[:, :, a0 + q], rhs=fc[:, 0, :], start=True, stop=False)
                    ins = e.matmul(pp[i][0:64, q * 256:(q + 1) * 256], lhsT=D1[:, :, 128 + a0 + q], rhs=fc[:, 1, :],
                                   start=False, stop=True)
                return ins
            P.op("pe", f, reads=[b_D1, b_t], writes=[b_pp[i]])
            evac(D2[:, a0:a0 + 2, :], pp[i][0:64, :].rearrange("p (q n) -> p q n", q=2), b_pp[i], b_D2, co=True)
        Yv = Y[:].rearrange("p (b a) -> p a b", a=128)
        for a0 in range(0, 128, 8):
            i = nextp()

            def f(e, a0=a0, i=i):
                for q in range(8):
                    e.matmul(pp[i][:, q * 64:(q + 1) * 64], lhsT=D2[:, a0 + q, 0:128], rhs=c64[:, 0, :], start=True, stop=False)
                    ins = e.matmul(pp[i][:, q * 64:(q + 1) * 64], lhsT=D2[:, a0 + q, 128:256], rhs=c64[:, 1, :],
                                   start=False, stop=True)
                return ins
            P.op("pe", f, reads=[b_D2, b_t], writes=[b_pp[i]])
            evac(Yv[:, a0:a0 + 8, :], pp[i][:].rearrange("p (a b) -> p a b", a=8), b_pp[i], b_Y, co=True)
        P.op("sp", lambda e, u=u: e.dma_start(out=yl[u], in_=Y[:]), reads=[b_Y], writes=[Buf()], dma="sty")
        i = nextp()

        def f(e, i=i):
            for j in range(2):
                ins = e.matmul(pp[i][:, j * 256:(j + 1) * 256], lhsT=XC[:, j * 128:(j + 1) * 128], rhs=fc[:, 0, :],
                               start=True, stop=True)
            return ins
        P.op("pe", f, reads=[b_XC, b_t], writes=[b_pp[i]])
        evac(DA[:], pp[i][:].rearrange("p (j n) -> p j n", j=2), b_pp[i], b_DA)
        i = nextp()

        def f(e, i=i):
            n = 0
            for j in range(2):
                for ri in range(2):
                    ins = e.matmul(pp[i][:, 0:256], lhsT=DA[:, j, ri * 128:(ri + 1) * 128], rhs=c256[:, 2 * ri + j, :],
                                   start=(n == 0), stop=(n == 3))
                    n += 1
            return ins
        P.op("pe", f, reads=[b_DA, b_t], writes=[b_pp[i]])
        evac(YC[:], pp[i][:, 0:256], b_pp[i], b_YC)
        P.op("sp", lambda e, u=u: e.dma_start(out=yc[u], in_=YC[:]), reads=[b_YC], writes=[Buf()], dma="styc")
    P.finish()
    P.emit()
    return nc


def fft_tables():
    a = np.arange(128)[:, None, None]
    b = np.arange(64)[None, :, None]
    ap = np.arange(128)[None, None, :]
    th = 2 * np.pi * (a * ap / 128.0 + b * ap / 8192.0)
    ftw = np.concatenate([np.cos(th), -np.sin(th)], axis=-1) / np.sqrt(128.0)
    c = np.arange(128)[:, None]
    cp = np.arange(128)[None, :]
    th = 2 * np.pi * c * cp / 128.0
    cr, ci = np.cos(th) / np.sqrt(128.0), -np.sin(th) / np.sqrt(128.0)
    fc = np.stack([np.concatenate([cr, ci], 1), np.concatenate([-ci, cr], 1)], axis=1)
    bb = np.arange(64)[:, None]
    bp = np.arange(64)[None, :]
    th = 2 * np.pi * bb * bp / 64.0
    c64 = np.stack([np.cos(th), np.sin(th)], axis=1) / 8.0
    l = np.arange(256)[:, None]
    lp = np.arange(256)[None, :]
    th = 2 * np.pi * l * lp / 256.0
    C, S = np.cos(th) / 16.0, np.sin(th) / 16.0
    c256 = np.stack([C[0:128], C[128:256], S[0:128], S[128:256]], axis=1)
    f = lambda x: np.ascontiguousarray(x, dtype=np.float32)
    return f(ftw), f(fc), f(c64), f(c256)


NBLK = SEQ // 128


def build_LC(nblk=NBLK):
    nc = bass.Bass("TRN2", target_bir_lowering=False)
    S = nblk * 128
    qt_ap = dram_in(nc, "qt", [128, 4, S])
    kt_ap = dram_in(nc, "kt", [128, S + CTX])
    v_ap = dram_in(nc, "v", [128, nblk + 2, 64])
    mask_ap = dram_in(nc, "mask", [128, 384])
    sink_ap = dram_in(nc, "sink", [128, 8])
    id_ap = dram_in(nc, "ident", [128, 128])
    o_ap = dram_out(nc, "o", [S, 512])
    P = Prog(nc)
    QT = P.sb("QT", [128, 4, S], BF16)
    KT = P.sb("KT", [128, S + CTX], BF16)
    V = P.sb("V", [128, nblk + 2, 64], BF16)
    mask = P.sb("mask", [128, 384], F32)
    sink = P.sb("sink", [128, 8], F32)
    ident = P.sb("ident", [128, 128], BF16)
    sc = [P.sb(f"sc{i}", [128, 640], F32) for i in range(2)]
    pb = [P.sb(f"pb{i}", [128, 640], BF16) for i in range(2)]
    pTs = [P.sb(f"pTs{i}", [128, 5, 128], BF16) for i in range(2)]
    sm = [P.sb(f"sm{i}", [128, 8], F32) for i in range(2)]
    ot = [P.sb(f"ot{i}", [128, 512], F32) for i in range(2)]
    scA = [P.ps(f"scA{i}", [128, 512]) for i in range(2)]
    scB = [P.ps(f"scB{i}", [128, 512]) for i in range(2)]
    pTt = [P.ps(f"pTt{i}", [128, 8, 128], BF16) for i in range(2)]
    ops = [P.ps(f"ops{i}", [128, 512]) for i in range(2)]
    B2 = lambda: [Buf(), Buf()]
    b_c = Buf()
    b_sc, b_pb, b_pTs, b_sm, b_ot, b_scA, b_scB, b_pTt, b_ops = (B2() for _ in range(9))
    for t, ap, k in ((QT, qt_ap, "c0"), (KT, kt_ap, "c1"), (V, v_ap, "c2"), (ident, id_ap, "c3")):
        P.op("pool", lambda e, t=t, ap=ap: e.dma_start(out=t[:], in_=ap), cowrites=[b_c], dma=k)
    for t, ap, k in ((mask, mask_ap, "c4"), (sink, sink_ap, "c5")):
        P.op("sp", lambda e, t=t, ap=ap: e.dma_start(out=t[:], in_=ap), cowrites=[b_c], dma=k)
    n = 0
    for i in range(nblk):
        kb0, kb1 = max(i - 1, 0), min(i + 1, nblk - 1)
        nloc = kb1 - kb0 + 1
        nk = nloc * 128
        mo = (kb0 - (i - 1)) * 128
        nt = nloc + 2
        L = nk + 256
        oi = i % 2
        for r in range(8):
            s = n % 2
            n += 1
            hh, j = r // 4, r % 4
            p0, p1 = hh * 64, hh * 64 + 64
            q = QT[p0:p1, j, i * 128:(i + 1) * 128]

            def f(e, q=q, s=s, kb0=kb0, nk=nk, p0=p0, p1=p1):
                e.matmul(scA[s][:, 0:nk], lhsT=q, rhs=KT[p0:p1, kb0 * 128:kb0 * 128 + nk], start=True, stop=True)
                return e.matmul(scB[s][:, 0:256], lhsT=q, rhs=KT[p0:p1, S:S + CTX], start=True, stop=True)
            P.op("pe", f, reads=[b_c], writes=[b_scA[s], b_scB[s]])
            P.op("dve", lambda e, s=s, nk=nk, mo=mo: e.tensor_tensor(out=sc[s][:, 0:nk], in0=scA[s][:, 0:nk],
                                                                    in1=mask[:, mo:mo + nk], op=ALU.add),
                 reads=[b_scA[s], b_c], writes=[b_sc[s]])
            P.op("act", lambda e, s=s, nk=nk: e.activation(out=sc[s][:, nk:nk + 256], in_=scB[s][:, 0:256], func=AF.Copy),
                 reads=[b_scB[s]], cowrites=[b_sc[s]])
            m = sm[s]
            P.op("dve", lambda e, s=s, L=L, m=m: e.reduce_max(out=m[:, 0:1], in_=sc[s][:, 0:L], axis=AX.X),
                 reads=[b_sc[s]], writes=[b_sm[s]])
            P.op("dve", lambda e, m=m: e.tensor_scalar_mul(out=m[:, 1:2], in0=m[:, 0:1], scalar1=0.125),
                 reads=[b_sm[s]], writes=[b_sm[s]])
            P.op("dve", lambda e, m=m, r=r: e.tensor_tensor(out=m[:, 2:3], in0=m[:, 1:2], in1=sink[:, r:r + 1], op=ALU.max),
                 reads=[b_sm[s], b_c], writes=[b_sm[s]])
            P.op("dve", lambda e, m=m: e.tensor_scalar_mul(out=m[:, 3:4], in0=m[:, 2:3], scalar1=-1.0),
                 reads=[b_sm[s]], writes=[b_sm[s]])
            P.op("act", lambda e, s=s, L=L, m=m: e.activation(out=pb[s][:, 0:L], in_=sc[s][:, 0:L], func=AF.Exp,
                                                             scale=0.125, bias=m[:, 3:4], accum_out=m[:, 4:5]),
                 reads=[b_sc[s], b_sm[s]], writes=[b_pb[s], b_sm[s]])
            P.op("act", lambda e, m=m, r=r: e.activation(out=m[:, 5:6], in_=sink[:, r:r + 1], func=AF.Exp, bias=m[:, 3:4]),
                 reads=[b_sm[s], b_c], writes=[b_sm[s]])
            P.op("dve", lambda e, m=m: e.tensor_tensor(out=m[:, 6:7], in0=m[:, 4:5], in1=m[:, 5:6], op=ALU.add),
                 reads=[b_sm[s]], writes=[b_sm[s]])
            P.op("dve", lambda e, m=m: e.reciprocal(out=m[:, 7:8], in_=m[:, 6:7]), reads=[b_sm[s]], writes=[b_sm[s]])

            def f(e, s=s, nt=nt):
                for t in range(nt):
                    ins = e.transpose(out=pTt[s][:, t, :], in_=pb[s][:, t * 128:(t + 1) * 128], identity=ident[:])
                return ins
            P.op("pe", f, reads=[b_pb[s], b_c], writes=[b_pTt[s]])
            P.op("dve", lambda e, s=s, nt=nt: e.tensor_copy(out=pTs[s][:, 0:nt, :], in_=pTt[s][:, 0:nt, :]),
                 reads=[b_pTt[s]], writes=[b_pTs[s]])

            def f(e, s=s, nt=nt, nloc=nloc, kb0=kb0):
                for t in range(nt):
                    vb = kb0 + t if t < nloc else nblk + (t - nloc)
                    ins = e.matmul(ops[s][:, 0:64], lhsT=pTs[s][:, t, :], rhs=V[:, vb, :], start=(t == 0), stop=(t == nt - 1))
                return ins
            P.op("pe", f, reads=[b_pTs[s], b_c], writes=[b_ops[s]])
            P.op("act", lambda e, s=s, m=m, r=r, oi=oi: e.activation(out=ot[oi][:, r * 64:(r + 1) * 64], in_=ops[s][:, 0:64],
                                                                    func=AF.Copy, scale=m[:, 7:8]),
                 reads=[b_ops[s], b_sm[s]], cowrites=[b_ot[oi]])
        P.op("sp", lambda e, i=i, oi=oi: e.dma_start(out=o_ap[i * 128:(i + 1) * 128, :], in_=ot[oi][:]),
             reads=[b_ot[oi]], writes=[Buf()], dma=f"so{oi}")
        b_ot[oi].w = []
    P.finish()
    P.emit()
    return nc


def attn_mask():
    a = np.arange(128)[:, None]
    j = np.arange(384)[None, :]
    return np.where(np.abs(j - 128 - a) <= 128, 0.0, -1e30).astype(np.float32)


def proj_residual(R, W, dst, dst_bufs, bi, c0, c1, vgate):
    w = c1 - c0
    nxt = R.load_wa(W, 0, 256)
    for g in range(8):
        iw = nxt
        if g + 1 < 8:
            nxt = R.load_wa(W, (g + 1) * 256, 256)
        for j in range(2):
            dc = 2 * g + j
            pb = dc % 2
            R.mm(R.py[pb], R.b_py[pb], R.wa[iw], R.b_wa[iw], j * 128, NCH, R.hb, [R.b_hb], w)
            R.residual_out(dst, dst_bufs, bi, dc, c0, c1, R.py[pb], R.b_py[pb], vgate)


def build_LB(T, segs):
    nc = bass.Bass("TRN2", target_bir_lowering=False)
    nseg = len(segs)
    nvec = 9 * nseg + 11
    z1 = dram_in(nc, "z1", [D, T])
    gb = dram_in(nc, "gb", [1024, T])
    vp = dram_in(nc, "vp", [1024, T])
    vv = dram_in(nc, "vv", [1024, T])
    vn = dram_in(nc, "vn", [1024, T])
    yb = dram_in(nc, "yb", [1024, T])
    vecs_ap = dram_in(nc, "vecs", [128, nvec, NCH])
    wout = dram_in(nc, "wout", [D, D])
    wg2 = dram_in(nc, "wg2", [D, DFF])
    wu2 = dram_in(nc, "wu2", [D, DFF])
    wd2 = dram_in(nc, "wd2", [DFF, D])
    wg3 = dram_in(nc, "wg3", [D, DFF])
    wu3 = dram_in(nc, "wu3", [D, DFF])
    wd3 = dram_in(nc, "wd3", [DFF, D])
    wqkv = dram_in(nc, "wqkv", [D, 2560])
    wperm = dram_in(nc, "wperm", [D, 2304])
    cos_ap = dram_in(nc, "cosT", [128, T])
    sin_ap = dram_in(nc, "sinT", [128, T])
    z4 = dram_out(nc, "z4", [D, T])
    qr = dram_out(nc, "qr", [D, T])
    kr = dram_out(nc, "kr", [256, T])
    vo = dram_out(nc, "vo", [256, T])
    z2 = nc.dram_tensor("z2", [D, T], F32).ap()
    z3 = nc.dram_tensor("z3", [D, T], F32).ap()
    P = Prog(nc)
    R = RowLocal(P, T, segs, vecs_ap, nvec)
    cosT = P.sb("cosT", [128, T], F32)
    sinT = P.sb("sinT", [128, T], F32)
    b_rope = Buf()
    P.op("sp", lambda e: e.dma_start(out=cosT[:], in_=cos_ap), cowrites=[b_rope], dma="r0")
    P.op("sp", lambda e: e.dma_start(out=sinT[:], in_=sin_ap), cowrites=[b_rope], dma="r1")
    V = lambda s, k: 9 * s + k
    L0 = 9 * nseg
    for s in range(nseg):
        R.derive(V(s, 2), add=1.0)
        R.derive(V(s, 3), mul=0.5)
        R.derive(V(s, 5), add=1.0)
        R.derive(V(s, 6), mul=0.5)
        R.derive(V(s, 8), add=1.0)
    SL = lambda k: [V(s, k) for s in range(nseg)]
    b_in = Buf()
    nb = len(R.blocks)
    b_z2 = [Buf() for _ in range(nb)]
    b_z3 = [Buf() for _ in range(nb)]
    b_z4 = [Buf() for _ in range(nb)]
    b_o = Buf()
    for bi, (c0, c1) in enumerate(R.blocks):
        w = c1 - c0
        R.load_xn(z1, b_in, c0, c1)
        R.layernorm(w, L0 + 0, L0 + 1)
        for j in range(8):
            tl = [R.sg[0], R.sg[1], R.zt[0], R.zt[1], R.tmp[0]]
            tb = [R.b_sg[0], R.b_sg[1], R.b_zt[0], R.b_zt[1], R.b_tmp[0]]
            for k, src in enumerate((gb, vp, vv, vn, yb)):
                P.op("sp", lambda e, k=k, src=src, j=j, w=w, c0=c0, c1=c1, tl=tl: e.dma_start(
                    out=tl[k][:, :w], in_=src[j * 128:(j + 1) * 128, c0:c1]), writes=[tb[k]], dma=f"m{k}")
            a, g_, v0, v1, yt = tl
            P.op("dve", lambda e, a=a, j=j, w=w: None or e.tensor_scalar_mul(out=R.sg[1][:, :w], in0=R.sg[1][:, :w],
                                                                            scalar1=R.vc(L0 + 8, j)),
                 reads=[R.b_vecs], writes=[tb[1]])
            P.op("dve", lambda e, j=j, w=w: e.scalar_tensor_tensor(out=R.sg[1][:, :w], in0=R.zt[0][:, :w],
                                                                   scalar=R.vc(L0 + 9, j), in1=R.sg[1][:, :w],
                                                                   op0=ALU.mult, op1=ALU.add),
                 reads=[R.b_vecs, tb[2]], writes=[tb[1]])
            P.op("dve", lambda e, j=j, w=w: e.scalar_tensor_tensor(out=R.sg[1][:, :w], in0=R.zt[1][:, :w],
                                                                   scalar=R.vc(L0 + 10, j), in1=R.sg[1][:, :w],
                                                                   op0=ALU.mult, op1=ALU.add),
                 reads=[R.b_vecs, tb[3]], writes=[tb[1]])
            P.op("dve", lambda e, j=j, w=w: e.tensor_tensor(out=R.hb[:, j, :w], in0=R.sg[1][:, :w], in1=R.sg[0][:, :w],
                                                            op=ALU.mult),
                 reads=[tb[0], tb[1]], cowrites=[R.b_hb])
            P.op("act", lambda e, j=j, w=w: e.activation(out=R.hb[:, 8 + j, :w], in_=R.tmp[0][:, :w], func=AF.Copy),
                 reads=[tb[4]], cowrites=[R.b_hb])
        proj_residual(R, wout, z2, b_z2, bi, c0, c1, SL(0))
    for bi, (c0, c1) in enumerate(R.blocks):
        R.load_xn(z2, b_z2[bi], c0, c1)
        R.layernorm(c1 - c0, L0 + 2, L0 + 3)
        R.modulate(c0, c1, SL(1), SL(2))
        R.ffn(bi, c0, c1, wg2, wu2, wd2, z3, b_z3, SL(3))
    for bi, (c0, c1) in enumerate(R.blocks):
        R.load_xn(z3, b_z3[bi], c0, c1)
        R.layernorm(c1 - c0, L0 + 4, L0 + 5)
        R.modulate(c0, c1, SL(4), SL(5))
        R.ffn(bi, c0, c1, wg3, wu3, wd3, z4, b_z4, SL(6))
    for bi, (c0, c1) in enumerate(R.blocks):
        w = c1 - c0
        R.load_xn(z4, b_z4[bi], c0, c1)
        R.layernorm(w, L0 + 6, L0 + 7)
        R.modulate(c0, c1, SL(7), SL(8))
        ld = lambda g: (R.load_wa(wqkv, g * 256, 256), R.load_wa(wperm, g * 256, 256) if g < 9 else None)
        nxt = ld(0)
        for g in range(10):
            iw, ip = nxt
            if g + 1 < 10:
                nxt = ld(g + 1)
            for j in range(2):
                pb = j
                R.mm(R.pg[pb], R.b_pg[pb], R.wa[iw], R.b_wa[iw], j * 128, NCH, R.hb, [R.b_hb], w)
                i = R.zt_i
                R.zt_i ^= 1
                zt, tmp, sg, pg, pu = R.zt[i], R.tmp[i], R.sg[pb], R.pg[pb], R.pu[pb]
                if g < 9:
                    R.mm(R.pu[pb], R.b_pu[pb], R.wa[ip], R.b_wa[ip], j * 128, NCH, R.hb, [R.b_hb], w)
                    P.op("dve", lambda e, sg=sg, pg=pg, w=w, c0=c0, c1=c1: e.tensor_tensor(
                        out=sg[:, :w], in0=pg[:, :w], in1=cosT[:, c0:c1], op=ALU.mult),
                        reads=[R.b_pg[pb], b_rope], writes=[R.b_sg[pb]])
                    P.op("dve", lambda e, tmp=tmp, pu=pu, w=w, c0=c0, c1=c1: e.tensor_tensor(
                        out=tmp[:, :w], in0=pu[:, :w], in1=sinT[:, c0:c1], op=ALU.mult),
                        reads=[R.b_pu[pb], b_rope], writes=[R.b_tmp[i]])
                    P.op("dve", lambda e, zt=zt, sg=sg, tmp=tmp, w=w: e.tensor_tensor(
                        out=zt[:, :w], in0=sg[:, :w], in1=tmp[:, :w], op=ALU.add),
                        reads=[R.b_sg[pb], R.b_tmp[i]], writes=[R.b_zt[i]])
                    dst, row = (qr, (2 * g + j) * 128) if g < 8 else (kr, j * 128)
                else:
                    P.op("act", lambda e, zt=zt, pg=pg, w=w: e.activation(out=zt[:, :w], in_=pg[:, :w], func=AF.Copy),
                         reads=[R.b_pg[pb]], writes=[R.b_zt[i]])
                    dst, row = vo, j * 128
                R.store(dst, b_o, row, c0, c1, zt, R.b_zt[i])
    P.finish()
    P.emit()
    return nc


def build_LD(T):
    nc = bass.Bass("TRN2", target_bir_lowering=False)
    segs = [(0, T)]
    nvec = 10
    z4 = dram_in(nc, "z4", [D, T])
    oT = dram_in(nc, "oT", [D, T])
    vecs_ap = dram_in(nc, "vecs", [128, nvec, NCH])
    wout = dram_in(nc, "wout", [D, D])
    wg = dram_in(nc, "wg", [D, DFF])
    wu = dram_in(nc, "wu", [D, DFF])
    wd = dram_in(nc, "wd", [DFF, D])
    out = dram_out(nc, "out", [D, T])
    z5 = nc.dram_tensor("z5", [D, T], F32).ap()
    z6 = nc.dram_tensor("z6", [D, T], F32).ap()
    P = Prog(nc)
    R = RowLocal(P, T, segs, vecs_ap, nvec)
    R.derive(2, add=1.0)
    R.derive(3, mul=0.5)
    nb = len(R.blocks)
    b_in = Buf()
    b_z5 = [Buf() for _ in range(nb)]
    b_z6 = [Buf() for _ in range(nb)]
    for bi, (c0, c1) in enumerate(R.blocks):
        w = c1 - c0
        R.load_xn(z4, b_in, c0, c1)
        R.layernorm(w, 4, 5)
        P.op("pool", lambda e, w=w, c0=c0, c1=c1: e.dma_start(out=R.hb[:, :, :w], in_=chunked(oT)[:, :, c0:c1]),
             writes=[R.b_hb], dma="oin")
        proj_residual(R, wout, z5, b_z5, bi, c0, c1, [0])
    for bi, (c0, c1) in enumerate(R.blocks):
        R.load_xn(z5, b_z5[bi], c0, c1)
        R.layernorm(c1 - c0, 6, 7)
        R.modulate(c0, c1, [1], [2])
        R.ffn(bi, c0, c1, wg, wu, wd, z6, b_z6, [3])
    for bi, (c0, c1) in enumerate(R.blocks):
        w = c1 - c0
        R.load_xn(z6, b_z6[bi], c0, c1)
        R.layernorm(w, 8, 9)
        P.op("sp", lambda e, w=w, c0=c0, c1=c1: e.dma_start(out=chunked(out)[:, :, c0:c1], in_=R.xn[:, :, :w]),
             reads=[R.b_xn], writes=[Buf()], dma="fin")
    P.finish()
    P.emit()
    return nc


def lay(v):
    v = np.asarray(v, dtype=np.float32)
    if v.shape[0] < D:
        v = np.concatenate([v, np.zeros(D - v.shape[0], np.float32)])
    return v.reshape(NCH, 128).T


def rope_tables(pos):
    nf = 16
    inv = np.power(10000.0, -np.arange(nf, dtype=np.float64) / nf)
    row = (pos // 64).astype(np.float64)
    col = (pos % 64).astype(np.float64)
    d = np.arange(64)
    axis, part, f = d // 32, (d % 32) // 16, d % 16
    ang = np.where(axis[:, None] == 0, row[None, :], col[None, :]) * inv[f][:, None]
    c = np.cos(ang)
    s = np.sin(ang) * np.where(part == 0, -1.0, 1.0)[:, None]
    return np.concatenate([c, c], 0).astype(np.float32), np.concatenate([s, s], 0).astype(np.float32)


def rope_perm():
    d = np.arange(64)
    part = (d % 32) // 16
    p = np.where(part == 0, d + 16, d - 16)
    cols = np.concatenate([h * 64 + p for h in range(36)])
    return cols


_CACHE = {}


def _prog(key, fn):
    if key not in _CACHE:
        _CACHE[key] = fn()
    return _CACHE[key]


def _run(nc, ins):
    res = run_bass_kernel_spmd(nc, ins, core_ids=list(range(NCORE)))
    return res.results


def kernel(x, c, ctx, c_ctx, w_mod, b_mod, ln_g, ln_b, ffn_w_gate, ffn_w_up, ffn_w_down,
           ab_w_in, ab_conv, ab_w_out, attn_w_in, attn_sink, attn_w_out):
    f32 = lambda a: np.ascontiguousarray(np.asarray(a), dtype=np.float32)
    x, c, ctx, c_ctx = f32(x), f32(c), f32(ctx), f32(c_ctx)
    w_mod, b_mod, ln_g, ln_b = f32(w_mod), f32(b_mod), f32(ln_g), f32(ln_b)
    ffn_w_gate, ffn_w_up, ffn_w_down = f32(ffn_w_gate), f32(ffn_w_up), f32(ffn_w_down)
    ab_w_in, ab_conv, ab_w_out = f32(ab_w_in), f32(ab_conv), f32(ab_w_out)
    attn_w_in, attn_sink, attn_w_out = f32(attn_w_in), f32(attn_sink), f32(attn_w_out)
    LT, CT = SEQ // 4, CTX // 4
    T = LT + CT
    segs = [(0, LT), (LT, T)]
    cores = [(r // 4, r % 4) for r in range(NCORE)]

    cv = np.ascontiguousarray(np.stack([lay(c[0]), lay(c[1]), lay(c_ctx)], axis=-1))
    wm = np.concatenate([w_mod[0], w_mod[1]], axis=1)
    bm = b_mod.reshape(-1)
    ins = []
    for r in range(NCORE):
        sl = slice(r * MODC, (r + 1) * MODC)
        ins.append({"cv": cv, "w": np.ascontiguousarray(wm[:, sl]),
                    "b": np.ascontiguousarray(np.broadcast_to(bm[sl], (3, MODC)))})
    res = _run(_prog("L0", build_L0), ins)
    del wm
    mod = np.concatenate([res[r]["mod"] for r in range(NCORE)], axis=1).reshape(3, 2, 9, D)

    def mv(b, s, layer, k):
        return lay(mod[b if s == 0 else 2, layer, k])

    ins = []
    for b, q in cores:
        xT = np.concatenate([x[b, q * LT:(q + 1) * LT].T, ctx[b, q * CT:(q + 1) * CT].T], axis=1)
        vecs = [mv(b, s, 0, k) for s in range(2) for k in range(5)] + [lay(ln_g[0, 0]), lay(ln_b[0, 0])]
        ins.append({"xT": np.ascontiguousarray(xT), "vecs": np.ascontiguousarray(np.stack(vecs, axis=1)),
                    "wg": ffn_w_gate[0, 0], "wu": ffn_w_up[0, 0], "wd": ffn_w_down[0, 0], "win": ab_w_in[0]})
    resA = _run(_prog("LA", lambda: build_LA(T, segs)), ins)

    def gather(res, name, nrow):
        lat = np.empty((2, nrow, SEQ), np.float32)
        cx = np.empty((2, nrow, CTX), np.float32)
        for r, (b, q) in enumerate(cores):
            a = res[r][name]
            lat[b, :, q * LT:(q + 1) * LT] = a[:, :LT]
            cx[b, :, q * CT:(q + 1) * CT] = a[:, LT:]
        return lat, cx

    UFl, UFc = gather(resA, "uf", 1024)
    VVl, VVc = gather(resA, "vv", 1024)

    ftw, fc, c64, c256 = fft_tables()
    ins = []
    for b, q in cores:
        gs = [2 * q, 2 * q + 1]
        xl = np.stack([UFl[b, g * 128:(g + 1) * 128].T for g in gs])
        xc = np.stack([UFc[b, g * 128:(g + 1) * 128] for g in gs])
        ins.append({"xl": np.ascontiguousarray(xl), "xc": np.ascontiguousarray(xc),
                    "ftw": ftw, "fc": fc, "c64": c64, "c256": c256})
    resF = _run(_prog("LF", build_LF), ins)
    YBl = np.empty((2, 1024, SEQ), np.float32)
    YBc = np.empty((2, 1024, CTX), np.float32)
    for r, (b, q) in enumerate(cores):
        for u in range(2):
            g = 2 * q + u
            YBl[b, g * 128:(g + 1) * 128] = resF[r]["yl"][u]
            YBc[b, g * 128:(g + 1) * 128] = resF[r]["yc"][u]
    del UFl, UFc

    def shift(a, k):
        o = np.zeros_like(a)
        if k > 0:
            o[..., k:] = a[..., :-k]
        else:
            o[..., :k] = a[..., -k:]
        return o

    VPl, VPc, VNl, VNc = shift(VVl, 1), shift(VVc, 1), shift(VVl, -1), shift(VVc, -1)

    def cols(lat, cx, b, q):
        return np.ascontiguousarray(np.concatenate([lat[b][:, q * LT:(q + 1) * LT], cx[b][:, q * CT:(q + 1) * CT]], axis=1))

    perm = rope_perm()
    wperm = np.ascontiguousarray(attn_w_in[0][:, perm])
    ins = []
    for r, (b, q) in enumerate(cores):
        vecs = []
        for s in range(2):
            vecs += [mv(b, s, 0, 5), mv(b, s, 0, 6), mv(b, s, 0, 7), mv(b, s, 0, 8),
                     mv(b, s, 1, 0), mv(b, s, 1, 1), mv(b, s, 1, 2), mv(b, s, 1, 3), mv(b, s, 1, 4)]
        vecs += [lay(ln_g[0, 0]), lay(ln_b[0, 0]), lay(ln_g[0, 1]), lay(ln_b[0, 1]), lay(ln_g[0, 2]), lay(ln_b[0, 2]),
                 lay(ln_g[1, 0]), lay(ln_b[1, 0]), lay(ab_conv[0, 0]), lay(ab_conv[0, 1]), lay(ab_conv[0, 2])]
        cl, sl_ = rope_tables(np.arange(q * LT, (q + 1) * LT))
        cosT = np.concatenate([cl, np.ones((128, CT), np.float32)], axis=1)
        sinT = np.concatenate([sl_, np.zeros((128, CT), np.float32)], axis=1)
        ins.append({"z1": resA[r]["z1"], "gb": resA[r]["gb"], "vp": cols(VPl, VPc, b, q), "vv": resA[r]["vv"],
                    "vn": cols(VNl, VNc, b, q), "yb": cols(YBl, YBc, b, q),
                    "vecs": np.ascontiguousarray(np.stack(vecs, axis=1)), "wout": ab_w_out[0],
                    "wg2": ffn_w_gate[0, 1], "wu2": ffn_w_up[0, 1], "wd2": ffn_w_down[0, 1],
                    "wg3": ffn_w_gate[1, 0], "wu3": ffn_w_up[1, 0], "wd3": ffn_w_down[1, 0],
                    "wqkv": attn_w_in[0], "wperm": wperm,
                    "cosT": np.ascontiguousarray(cosT), "sinT": np.ascontiguousarray(sinT)})
    resB = _run(_prog("LB", lambda: build_LB(T, segs)), ins)
    del resA, VPl, VNl, YBl, VVl
    Ql, _ = gather(resB, "qr", D)
    Kl, Kc = gather(resB, "kr", 256)
    Vl, Vc = gather(resB, "vo", 256)

    mask = attn_mask()
    ident = np.eye(128, dtype=np.float32)
    ins = []
    for r in range(NCORE):
        b, g = r // 4, r % 4
        qt = Ql[b, g * 512:(g + 1) * 512].reshape(2, 4, 64, SEQ).transpose(0, 2, 1, 3).reshape(128, 4, SEQ)
        k1 = np.concatenate([Kl[b, g * 64:(g + 1) * 64], Kc[b, g * 64:(g + 1) * 64]], axis=1)
        vl = Vl[b, g * 64:(g + 1) * 64].T.reshape(NBLK, 128, 64)
        vc_ = Vc[b, g * 64:(g + 1) * 64].T.reshape(2, 128, 64)
        vall = np.concatenate([vl, vc_], axis=0).transpose(1, 0, 2)
        ins.append({"qt": np.ascontiguousarray(qt), "kt": np.ascontiguousarray(np.concatenate([k1, k1], axis=0)),
                    "v": np.ascontiguousarray(vall), "mask": mask,
                    "sink": np.ascontiguousarray(np.broadcast_to(attn_sink[0, g * 8:(g + 1) * 8], (128, 8))),
                    "ident": ident})
    resC = _run(_prog("LC", build_LC), ins)
    del Ql
    O = np.empty((2, D, SEQ), np.float32)
    for r in range(NCORE):
        b, g = r // 4, r % 4
        O[b, g * 512:(g + 1) * 512] = resC[r]["o"].T
    del resC

    ins = []
    for r, (b, q) in enumerate(cores):
        vecs = [mv(b, 0, 1, 5), mv(b, 0, 1, 6), mv(b, 0, 1, 7), mv(b, 0, 1, 8),
                lay(ln_g[1, 0]), lay(ln_b[1, 0]), lay(ln_g[1, 1]), lay(ln_b[1, 1]), lay(ln_g[1, 2]), lay(ln_b[1, 2])]
        ins.append({"z4": np.ascontiguousarray(resB[r]["z4"][:, :LT]), "oT": np.ascontiguousarray(O[b][:, q * LT:(q + 1) * LT]),
                    "vecs": np.ascontiguousarray(np.stack(vecs, axis=1)), "wout": attn_w_out[0],
                    "wg": ffn_w_gate[1, 1], "wu": ffn_w_up[1, 1], "wd": ffn_w_down[1, 1]})
    resD = _run(_prog("LD", lambda: build_LD(LT)), ins)
    out = np.empty((2, SEQ, D), np.float32)
    for r, (b, q) in enumerate(cores):
        out[b, q * LT:(q + 1) * LT] = resD[r]["out"].T
    return out
```

```python
import numpy as np
from contextlib import ExitStack
import concourse.bass as bass
import concourse.mybir as mybir
from concourse.bass_utils import run_bass_kernel_spmd

F32 = mybir.dt.float32
BF16 = mybir.dt.bfloat16
AF = mybir.ActivationFunctionType
ALU = mybir.AluOpType
AX = mybir.AxisListType

D = 2048
DFF = 5632
NCH = 16
FCH = 44
SEQ = 8192
CTX = 256
NCORE = 8
ALPHA = 4.0 ** 0.25
LN_EPS = 1e-5
TB = 512


class Buf:
    __slots__ = ("w", "r", "name")

    def __init__(self, name=""):
        self.w = []
        self.r = []
        self.name = name


class Ctr:
    LIMIT = 30000

    def __init__(self, P, name, step):
        self.P, self.name, self.step = P, name, step
        self.k = 0
        self.done = []
        self._new()

    def _new(self):
        self.sem = self.P.stack.enter_context(self.P.nc.semaphore(f"{self.name}_{self.k}"))
        self.k += 1
        self.val = 0

    def next(self):
        if self.val + self.step > self.LIMIT:
            self.done.append((self.sem, self.val))
            self._new()
        self.val += self.step
        return (self.sem, self.val)


class Eng:
    def __init__(self, P, name):
        self.name = name
        self.ops = []
        self.waited = {}
        self.ctr = Ctr(P, "e" + name, 1)


class Prog:
    def __init__(self, nc):
        self.nc = nc
        self.stack = ExitStack()
        self.engs = {n: Eng(self, n) for n in ("pe", "act", "dve", "pool", "sp")}
        self.dctr = {}
        self.fuzzy = {}
        self.n = 0

    def sb(self, name, shape, dt):
        return self.stack.enter_context(self.nc.sbuf_tensor("s_" + name, list(shape), dt))

    def ps(self, name, shape, dt=F32):
        return self.stack.enter_context(self.nc.psum_tensor("p_" + name, list(shape), dt))

    def op(self, eng, fn, reads=(), writes=(), dma=None, cowrites=()):
        E = self.engs[eng]
        deps = {}

        def add(tok):
            s, v = tok
            k = id(s)
            if k in self.fuzzy:
                v = max(v, self.fuzzy[k].val if self.fuzzy[k].sem is s else v)
            if k not in deps or deps[k][1] < v:
                deps[k] = (s, v)

        for b in reads:
            for t in b.w:
                add(t)
        for b in writes:
            for t in b.w:
                add(t)
            for t in b.r:
                add(t)
        for b in cowrites:
            for t in b.r:
                add(t)
        if dma is not None:
            if dma not in self.dctr:
                self.dctr[dma] = Ctr(self, "d" + dma, 16)
            ctr = self.dctr[dma]
            if dma == "st":
                self.fuzzy[id(ctr.sem)] = ctr
        else:
            ctr = E.ctr
        waits = []
        for k, (s, v) in deps.items():
            if eng == "pe" and dma is None and s is E.ctr.sem:
                continue
            if E.waited.get(k, 0) >= v:
                continue
            E.waited[k] = v
            waits.append((s, v))
        tok = ctr.next()
        E.ops.append((waits, fn, tok[0], ctr.step))
        for b in reads:
            b.r.append(tok)
        for b in writes:
            b.w = [tok]
            b.r = []
        for b in cowrites:
            b.w.append(tok)
        self.n += 1
        return tok

    def finish(self):
        E = self.engs["sp"]
        waits = []
        for ctr in self.dctr.values():
            for s, v in ctr.done + [(ctr.sem, ctr.val)]:
                if v > 0 and E.waited.get(id(s), 0) < v:
                    waits.append((s, v))
        E.ops.append((waits, None, None, 0))

    def emit(self):
        nc = self.nc

        def mk(E):
            def run(e):
                for waits, fn, sem, step in E.ops:
                    for ws, wv in waits:
                        e.wait_ge(ws, wv)
                    if fn is not None:
                        fn(e).then_inc(sem, step)
            return run

        with nc.Block() as block:
            block.tensor(mk(self.engs["pe"]))
            block.scalar(mk(self.engs["act"]))
            block.vector(mk(self.engs["dve"]))
            block.gpsimd(mk(self.engs["pool"]))
            block.sync(mk(self.engs["sp"]))
        self.stack.close()


def blocks_of(T, tb=TB):
    return [(c, min(c + tb, T)) for c in range(0, T, tb)]


def chunked(ap):
    return ap.rearrange("(c p) t -> p c t", p=128)


class RL:
    def __init__(self, P, T, segs, vecs_ap, nvec):
        self.P, self.T, self.segs = P, T, segs
        self.blocks = blocks_of(T)
        nb = len(self.blocks)
        nc = P.nc
        self.xn = P.sb("xn", [128, NCH, TB], F32)
        self.big = P.sb("big", [128, max(NCH * T, FCH * TB)], BF16)
        self.hb = self.big[:, 0:NCH * T].rearrange("p (c t) -> p c t", c=NCH)
        self.ab = self.big[:, 0:FCH * TB].rearrange("p (c t) -> p c t", c=FCH)
        self.wa = [P.sb(f"wa{i}", [128, NCH, 256], BF16) for i in range(4)]
        self.wbig = P.sb("wbig", [128, 2 * FCH * 128], BF16)
        self.wb = [self.wbig[:, i * FCH * 128:(i + 1) * FCH * 128].rearrange("p (c t) -> p c t", c=FCH) for i in range(2)]
        self.sq = self.wbig[:, 0:NCH * TB].rearrange("p (c t) -> p c t", c=NCH)
        self.mu = P.sb("mu", [128, TB], F32)
        self.msq = P.sb("msq", [128, TB], F32)
        self.rstd = P.sb("rstd", [128, TB], F32)
        self.sg = [P.sb(f"sg{i}", [128, TB], F32) for i in range(2)]
        self.zt = [P.sb(f"zt{i}", [128, TB], F32) for i in range(2)]
        self.tmp = [P.sb(f"tmp{i}", [128, TB], F32) for i in range(2)]
        self.xr = [P.sb(f"xr{i}", [128, TB], F32) for i in range(2)]
        self.at = [P.sb(f"at{i}", [128, TB], BF16) for i in range(2)]
        self.ones = P.sb("ones", [128, 128], BF16)
        self.vecs = P.sb("vecs", [128, nvec, NCH], F32)
        self.s1 = P.ps("s1", [128, TB])
        self.s2 = P.ps("s2", [128, TB])
        self.pg = [P.ps(f"pg{i}", [128, TB]) for i in range(2)]
        self.pu = [P.ps(f"pu{i}", [128, TB]) for i in range(2)]
        self.py = [P.ps(f"py{i}", [128, TB]) for i in range(2)]
        self.xs = nc.dram_tensor("xs", [nb, 128, NCH, TB], F32).ap()
        self.A = nc.dram_tensor("Asp", [nb, 128, FCH, TB], BF16).ap()
        B = Buf
        self.b_xn, self.b_hb = B("xn"), B("hb")
        self.b_abc = [B(f"ab{c}") for c in range(FCH)]
        self.b_wa = [B() for _ in range(4)]
        self.b_wb = [B() for _ in range(2)]
        self.b_mu, self.b_msq, self.b_rstd = B(), B(), B()
        self.b_sg, self.b_zt, self.b_tmp, self.b_xr, self.b_at = [B(), B()], [B(), B()], [B(), B()], [B(), B()], [B(), B()]
        self.b_s1, self.b_s2 = B(), B()
        self.b_pg, self.b_pu, self.b_py = [B(), B()], [B(), B()], [B(), B()]
        self.b_ones, self.b_vecs = B(), B()
        self.b_xs = [B() for _ in range(nb)]
        self.b_A = [B() for _ in range(nb)]
        self.W_HB = [self.b_hb] + self.b_abc
        self.wa_i = self.wb_i = self.zt_i = self.xr_i = self.at_i = self.pp_i = 0
        ones, vecs = self.ones, self.vecs
        P.op("dve", lambda e: e.memset(ones[:], 1.0), writes=[self.b_ones])
        P.op("sp", lambda e: e.dma_start(out=vecs[:], in_=vecs_ap), writes=[self.b_vecs], dma="vecs")

    def vc(self, v, c):
        return self.vecs[:, v, c:c + 1]

    def derive(self, v, mul=None, add=None):
        vecs = self.vecs
        if add is not None:
            self.P.op("dve", lambda e: e.tensor_scalar_add(out=vecs[:, v, :], in0=vecs[:, v, :], scalar1=float(add)),
                      reads=[self.b_vecs], writes=[self.b_vecs])
        if mul is not None:
            self.P.op("dve", lambda e: e.tensor_scalar_mul(out=vecs[:, v, :], in0=vecs[:, v, :], scalar1=float(mul)),
                      reads=[self.b_vecs], writes=[self.b_vecs])

    def segparts(self, c0, c1):
        out = []
        for si, (s0, s1) in enumerate(self.segs):
            a, b = max(c0, s0), min(c1, s1)
            if a < b:
                out.append((si, a - c0, b - c0))
        return out

    def load_xn(self, src_view, src_buf, w):
        xn = self.xn
        self.P.op("sp", lambda e: e.dma_start(out=xn[:, :, :w], in_=src_view), reads=[src_buf], writes=[self.b_xn], dma="xin")

    def layernorm(self, c0, c1, vg, vb):
        P = self.P
        w = c1 - c0
        xn, hb, sq, ones = self.xn, self.hb, self.sq, self.ones
        mu, msq, rstd, s1, s2 = self.mu, self.msq, self.rstd, self.s1, self.s2
        P.op("act", lambda e: e.activation(out=hb[:, :, c0:c1], in_=xn[:, :, :w], func=AF.Copy),
             reads=[self.b_xn], writes=self.W_HB)
        P.op("act", lambda e: e.activation(out=sq[:, :, :w], in_=xn[:, :, :w], func=AF.Square),
             reads=[self.b_xn], writes=self.b_wb)

        def st(src, lo, hi, dst):
            def f(e):
                for c in range(NCH):
                    ins = e.matmul(dst[:, :w], lhsT=ones[:], rhs=src[:, c, lo:hi], start=(c == 0), stop=(c == NCH - 1))
                return ins
            return f
        P.op("pe", st(hb, c0, c1, s1), reads=[self.b_hb, self.b_ones], writes=[self.b_s1])
        P.op("pe", st(sq, 0, w, s2), reads=self.b_wb + [self.b_ones], writes=[self.b_s2])
        P.op("dve", lambda e: e.tensor_scalar_mul(out=mu[:, :w], in0=s1[:, :w], scalar1=1.0 / D),
             reads=[self.b_s1], writes=[self.b_mu])
        P.op("dve", lambda e: e.tensor_tensor(out=msq[:, :w], in0=mu[:, :w], in1=mu[:, :w], op=ALU.mult),
             reads=[self.b_mu], writes=[self.b_msq])
        P.op("dve", lambda e: e.scalar_tensor_tensor(out=rstd[:, :w], in0=s2[:, :w], scalar=1.0 / D, in1=msq[:, :w],
                                                     op0=ALU.mult, op1=ALU.subtract),
             reads=[self.b_s2, self.b_msq], writes=[self.b_rstd])
        P.op("dve", lambda e: e.tensor_scalar_add(out=rstd[:, :w], in0=rstd[:, :w], scalar1=LN_EPS),
             reads=[self.b_rstd], writes=[self.b_rstd])
        P.op("act", lambda e: e.activation(out=rstd[:, :w], in_=rstd[:, :w], func=AF.Sqrt),
             reads=[self.b_rstd], writes=[self.b_rstd])
        P.op("dve", lambda e: e.reciprocal(out=rstd[:, :w], in_=rstd[:, :w]),
             reads=[self.b_rstd], writes=[self.b_rstd])
        for c in range(NCH):
            P.op("dve", (lambda c: lambda e: e.tensor_tensor(out=xn[:, c, :w], in0=xn[:, c, :w], in1=mu[:, :w],
                                                             op=ALU.subtract))(c),
                 reads=[self.b_mu, self.b_xn], writes=[self.b_xn])
            P.op("dve", (lambda c: lambda e: e.tensor_tensor(out=xn[:, c, :w], in0=xn[:, c, :w], in1=rstd[:, :w],
                                                             op=ALU.mult))(c),
                 reads=[self.b_rstd, self.b_xn], writes=[self.b_xn])
            P.op("act", (lambda c: lambda e: e.activation(out=xn[:, c, :w], in_=xn[:, c, :w], func=AF.Identity,
                                                          scale=self.vc(vg, c), bias=self.vc(vb, c)))(c),
                 reads=[self.b_vecs, self.b_xn], writes=[self.b_xn])

    def modulate(self, c0, c1, vshift, vscale1):
        P = self.P
        xn, hb = self.xn, self.hb
        for si, a, b in self.segparts(c0, c1):
            for c in range(NCH):
                P.op("act", (lambda c, si, a, b: lambda e: e.activation(
                    out=hb[:, c, c0 + a:c0 + b], in_=xn[:, c, a:b], func=AF.Identity,
                    scale=self.vc(vscale1[si], c), bias=self.vc(vshift[si], c)))(c, si, a, b),
                    reads=[self.b_vecs, self.b_xn], writes=self.W_HB)

    def save_xs(self, bi, w):
        xn, xs = self.xn, self.xs
        self.P.op("sp", lambda e: e.dma_start(out=xs[bi, :, :, :w], in_=xn[:, :, :w]),
                  reads=[self.b_xn], writes=[self.b_xs[bi]], dma="xs")

    def prologue_all(self, src_ap, src_bufs, ln=None, mod=None, save=True, fill=None):
        for bi, (c0, c1) in enumerate(self.blocks):
            w = c1 - c0
            sb_ = src_bufs[bi] if isinstance(src_bufs, list) else src_bufs
            self.load_xn(chunked(src_ap)[:, :, c0:c1], sb_, w)
            if ln is not None:
                self.layernorm(c0, c1, ln[0], ln[1])
            if save:
                self.save_xs(bi, w)
            if mod is not None:
                self.modulate(c0, c1, mod[0], mod[1])
            if fill is not None:
                fill(bi, c0, c1, w)

    def load_wa(self, W_ap, g, ncols):
        i = self.wa_i
        self.wa_i = (i + 1) % 4
        t = self.wa[i]
        self.P.op("pool", lambda e: e.dma_start(out=t[:, :, :ncols], in_=W_ap[g]), writes=[self.b_wa[i]], dma=f"wa{i}")
        return i

    def load_wb(self, W_ap, g):
        i = self.wb_i
        self.wb_i = (i + 1) % 2
        t = self.wb[i]
        self.P.op("pool", lambda e: e.dma_start(out=t, in_=W_ap[g]), writes=[self.b_wb[i]], dma=f"wb{i}")
        return i

    def mm(self, dst, dst_buf, wt, wbuf, j0, kc, X, lo, hi, xbufs):
        w = hi - lo

        def f(e):
            for k in range(kc):
                ins = e.matmul(dst[:, :w], lhsT=wt[:, k, j0:j0 + 128], rhs=X[:, k, lo:hi], start=(k == 0), stop=(k == kc - 1))
            return ins
        self.P.op("pe", f, reads=[wbuf] + list(xbufs), writes=[dst_buf])

    def store(self, dst_ap, dst_buf, row0, c0, c1, tile, tbuf):
        w = c1 - c0
        self.P.op("sp", lambda e: e.dma_start(out=dst_ap[row0:row0 + 128, c0:c1], in_=tile[:, :w]),
                  reads=[tbuf], cowrites=[dst_buf], dma="st")

    def residual_out(self, dst_ap, dst_buf, dc, c0, c1, psum, pbuf, vgate, xres, xres_buf):
        P = self.P
        w = c1 - c0
        i = self.zt_i
        self.zt_i ^= 1
        tmp, zt = self.tmp[i], self.zt[i]
        for si, a, b in self.segparts(c0, c1):
            P.op("act", (lambda si, a, b: lambda e: e.activation(out=tmp[:, a:b], in_=psum[:, a:b], func=AF.Copy,
                                                                 scale=self.vc(vgate[si], dc)))(si, a, b),
                 reads=[pbuf, self.b_vecs], writes=[self.b_tmp[i]])
        P.op("dve", lambda e: e.scalar_tensor_tensor(out=zt[:, :w], in0=xres, scalar=ALPHA, in1=tmp[:, :w],
                                                     op0=ALU.mult, op1=ALU.add),
             reads=[xres_buf, self.b_tmp[i]], writes=[self.b_zt[i]])
        self.store(dst_ap, dst_buf, dc * 128, c0, c1, zt, self.b_zt[i])

    def ffn_all(self, wg, wu, wd, dst_ap, dst_bufs, vgate, res_ap=None, res_buf=None):
        P = self.P
        hb, ab, A = self.hb, self.ab, self.A
        nfg = FCH // 2
        ld = lambda fg: (self.load_wa(wg, fg, 256), self.load_wa(wu, fg, 256))
        nxt = ld(0)
        for fg in range(nfg):
            ig, iu = nxt
            if fg + 1 < nfg:
                nxt = ld(fg + 1)
            for bi, (c0, c1) in enumerate(self.blocks):
                w = c1 - c0
                for j in range(2):
                    fc = 2 * fg + j
                    pb = self.pp_i
                    self.pp_i ^= 1
                    self.mm(self.pg[pb], self.b_pg[pb], self.wa[ig], self.b_wa[ig], j * 128, NCH, hb, c0, c1, [self.b_hb])
                    self.mm(self.pu[pb], self.b_pu[pb], self.wa[iu], self.b_wa[iu], j * 128, NCH, hb, c0, c1, [self.b_hb])
                    sg, pg, pu = self.sg[pb], self.pg[pb], self.pu[pb]
                    ai = self.at_i
                    self.at_i ^= 1
                    at = self.at[ai]
                    P.op("act", lambda e, sg=sg, pg=pg, w=w: e.activation(out=sg[:, :w], in_=pg[:, :w], func=AF.Silu),
                         reads=[self.b_pg[pb]], writes=[self.b_sg[pb]])
                    P.op("dve", lambda e, sg=sg, pu=pu, at=at, w=w: e.tensor_tensor(out=at[:, :w], in0=sg[:, :w], in1=pu[:, :w],
                                                                                   op=ALU.mult),
                         reads=[self.b_sg[pb], self.b_pu[pb]], writes=[self.b_at[ai]])
                    P.op("sp", lambda e, at=at, bi=bi, fc=fc, w=w: e.dma_start(out=A[bi, :, fc, :w], in_=at[:, :w]),
                         reads=[self.b_at[ai]], cowrites=[self.b_A[bi]], dma="sta")
        for bi, (c0, c1) in enumerate(self.blocks):
            w = c1 - c0
            P.op("sp", lambda e, bi=bi, w=w: e.dma_start(out=ab[:, :, :w], in_=A[bi, :, :, :w]),
                 reads=[self.b_A[bi]], writes=self.W_HB, dma="lda")
            if res_ap is None:
                self.load_xn(self.xs[bi, :, :, :w], self.b_xs[bi], w)
            else:
                self.load_xn(chunked(res_ap)[:, :, c0:c1], res_buf, w)
            nxt = self.load_wb(wd, 0)
            for dc in range(NCH):
                iw = nxt
                if dc + 1 < NCH:
                    nxt = self.load_wb(wd, dc + 1)
                pb = dc % 2
                self.mm(self.py[pb], self.b_py[pb], self.wb[iw], self.b_wb[iw], 0, FCH, ab, 0, w, self.b_abc)
                self.residual_out(dst_ap, dst_bufs[bi], dc, c0, c1, self.py[pb], self.b_py[pb], vgate,
                                  self.xn[:, dc, :w], self.b_xn)
            self.b_A[bi].w = []

    def proj_residual_all(self, W, dst_ap, dst_bufs, vgate):
        nxt = self.load_wa(W, 0, 256)
        for g in range(8):
            iw = nxt
            if g + 1 < 8:
                nxt = self.load_wa(W, g + 1, 256)
            for bi, (c0, c1) in enumerate(self.blocks):
                w = c1 - c0
                for j in range(2):
                    dc = 2 * g + j
                    pb = self.pp_i
                    self.pp_i ^= 1
                    self.mm(self.py[pb], self.b_py[pb], self.wa[iw], self.b_wa[iw], j * 128, NCH, self.hb, c0, c1, [self.b_hb])
                    xi = self.xr_i
                    self.xr_i ^= 1
                    xr, xs = self.xr[xi], self.xs
                    self.P.op("sp", lambda e, xr=xr, bi=bi, dc=dc, w=w: e.dma_start(out=xr[:, :w], in_=xs[bi, :, dc, :w]),
                              reads=[self.b_xs[bi]], writes=[self.b_xr[xi]], dma=f"xr{xi}")
                    self.residual_out(dst_ap, dst_bufs[bi], dc, c0, c1, self.py[pb], self.b_py[pb], vgate,
                                      xr[:, :w], self.b_xr[xi])


def tile_w(W, gcols):
    K, N = W.shape
    return np.ascontiguousarray(W.reshape(K // 128, 128, N // gcols, gcols).transpose(2, 1, 0, 3))


def dram_in(nc, name, shape, dt=F32):
    return nc.dram_tensor(name, list(shape), dt, kind="ExternalInput").ap()


def dram_out(nc, name, shape, dt=F32):
    return nc.dram_tensor(name, list(shape), dt, kind="ExternalOutput").ap()


def ffn_w_in(nc, sfx):
    return (dram_in(nc, "wg" + sfx, [FCH // 2, 128, NCH, 256]), dram_in(nc, "wu" + sfx, [FCH // 2, 128, NCH, 256]),
            dram_in(nc, "wd" + sfx, [NCH, 128, FCH, 128]))


MODC = 2 * 9 * D // NCORE


def build_L0():
    nc = bass.Bass("TRN2", target_bir_lowering=False)
    cv_ap = dram_in(nc, "cv", [128, NCH, 3])
    w_ap = dram_in(nc, "w", [D, MODC])
    b_ap = dram_in(nc, "b", [3, MODC])
    o_ap = dram_out(nc, "mod", [3, MODC])
    P = Prog(nc)
    cv = P.sb("cv", [128, NCH, 3], F32)
    cb = P.sb("cb", [128, NCH, 3], BF16)
    bt = P.sb("bt", [3, MODC], F32)
    ot = P.sb("ot", [3, MODC], F32)
    wt = [P.sb(f"w{i}", [128, NCH, 512], BF16) for i in range(2)]
    ps = [P.ps(f"ps{i}", [128, 512]) for i in range(2)]
    b_cv, b_cb, b_bt, b_ot = Buf(), Buf(), Buf(), Buf()
    b_w, b_ps = [Buf(), Buf()], [Buf(), Buf()]
    P.op("sp", lambda e: e.dma_start(out=cv[:], in_=cv_ap), writes=[b_cv], dma="cv")
    P.op("sp", lambda e: e.dma_start(out=bt[:], in_=b_ap), writes=[b_bt], dma="bt")
    P.op("act", lambda e: e.activation(out=cb[:], in_=cv[:], func=AF.Silu), reads=[b_cv], writes=[b_cb])
    wv = w_ap.rearrange("(k p) f -> p k f", p=128)
    for t in range(MODC // 512):
        i = t % 2
        P.op("pool", lambda e, t=t, i=i: e.dma_start(out=wt[i][:], in_=wv[:, :, t * 512:(t + 1) * 512]),
             writes=[b_w[i]], dma=f"w{i}")

        def f(e, i=i):
            for k in range(NCH):
                ins = e.matmul(ps[i][0:3, :], lhsT=cb[:, k, :], rhs=wt[i][:, k, :], start=(k == 0), stop=(k == NCH - 1))
            return ins
        P.op("pe", f, reads=[b_cb, b_w[i]], writes=[b_ps[i]])
        P.op("dve", lambda e, t=t, i=i: e.tensor_tensor(out=ot[:, t * 512:(t + 1) * 512], in0=ps[i][0:3, :],
                                                        in1=bt[:, t * 512:(t + 1) * 512], op=ALU.add),
             reads=[b_ps[i], b_bt], writes=[b_ot])
    P.op("sp", lambda e: e.dma_start(out=o_ap, in_=ot[:]), reads=[b_ot], writes=[Buf()], dma="st")
    P.finish()
    P.emit()
    return nc


def simple_proj(R, W, ng, dst, b_o):
    P = R.P
    nxt = R.load_wa(W, 0, 256)
    for g in range(ng):
        iw = nxt
        if g + 1 < ng:
            nxt = R.load_wa(W, g + 1, 256)
        for bi, (c0, c1) in enumerate(R.blocks):
            w = c1 - c0
            for j in range(2):
                pb = R.pp_i
                R.pp_i ^= 1
                R.mm(R.pg[pb], R.b_pg[pb], R.wa[iw], R.b_wa[iw], j * 128, NCH, R.hb, c0, c1, [R.b_hb])
                i = R.zt_i
                R.zt_i ^= 1
                zt, pg = R.zt[i], R.pg[pb]
                P.op("act", lambda e, pg=pg, zt=zt, w=w: e.activation(out=zt[:, :w], in_=pg[:, :w], func=AF.Copy),
                     reads=[R.b_pg[pb]], writes=[R.b_zt[i]])
                R.store(dst, b_o, (2 * g + j) * 128, c0, c1, zt, R.b_zt[i])


def build_LA(T, segs):
    nc = bass.Bass("TRN2", target_bir_lowering=False)
    nseg = len(segs)
    nvec = 5 * nseg + 2
    x_ap = dram_in(nc, "xT", [D, T])
    vecs_ap = dram_in(nc, "vecs", [128, nvec, NCH])
    wg, wu, wd = ffn_w_in(nc, "")
    w_gb = dram_in(nc, "w_gb", [4, 128, NCH, 256])
    w_gc = dram_in(nc, "w_gc", [8, 128, NCH, 128])
    w_xi = dram_in(nc, "w_xi", [8, 128, NCH, 128])
    w_uf = dram_in(nc, "w_uf", [4, 128, NCH, 256])
    z1 = dram_out(nc, "z1", [D, T])
    gb = dram_out(nc, "gb", [1024, T])
    vv = dram_out(nc, "vv", [1024, T])
    uf = dram_out(nc, "uf", [1024, T])
    P = Prog(nc)
    R = RL(P, T, segs, vecs_ap, nvec)
    V = lambda s, k: 5 * s + k
    SL = lambda k: [V(s, k) for s in range(nseg)]
    VG, VB = 5 * nseg, 5 * nseg + 1
    for s in range(nseg):
        R.derive(V(s, 1), add=1.0)
        R.derive(V(s, 2), mul=0.5)
        R.derive(V(s, 4), add=1.0)
    b_x = Buf()
    b_z1 = [Buf() for _ in R.blocks]
    b_o = Buf()
    R.prologue_all(x_ap, b_x, ln=None, mod=(SL(0), SL(1)), save=False)
    R.ffn_all(wg, wu, wd, z1, b_z1, SL(2), res_ap=x_ap, res_buf=b_x)
    R.prologue_all(z1, b_z1, ln=(VG, VB), mod=(SL(3), SL(4)), save=False)
    simple_proj(R, w_gb, 4, gb, b_o)
    ld = lambda jj: (R.load_wa(w_gc, jj, 128), R.load_wa(w_xi, jj, 128))
    nxt = ld(0)
    for jj in range(8):
        ic, ix = nxt
        if jj + 1 < 8:
            nxt = ld(jj + 1)
        for bi, (c0, c1) in enumerate(R.blocks):
            w = c1 - c0
            pb = R.pp_i
            R.pp_i ^= 1
            R.mm(R.pg[pb], R.b_pg[pb], R.wa[ic], R.b_wa[ic], 0, NCH, R.hb, c0, c1, [R.b_hb])
            R.mm(R.pu[pb], R.b_pu[pb], R.wa[ix], R.b_wa[ix], 0, NCH, R.hb, c0, c1, [R.b_hb])
            i = R.zt_i
            R.zt_i ^= 1
            zt, sg, pg, pu = R.zt[i], R.sg[pb], R.pg[pb], R.pu[pb]
            P.op("act", lambda e, pg=pg, sg=sg, w=w: e.activation(out=sg[:, :w], in_=pg[:, :w], func=AF.Copy),
                 reads=[R.b_pg[pb]], writes=[R.b_sg[pb]])
            P.op("dve", lambda e, pu=pu, sg=sg, zt=zt, w=w: e.tensor_tensor(out=zt[:, :w], in0=pu[:, :w], in1=sg[:, :w],
                                                                           op=ALU.mult),
                 reads=[R.b_pu[pb], R.b_sg[pb]], writes=[R.b_zt[i]])
            R.store(vv, b_o, jj * 128, c0, c1, zt, R.b_zt[i])
    simple_proj(R, w_uf, 4, uf, b_o)
    P.finish()
    P.emit()
    return nc


def build_LF():
    nc = bass.Bass("TRN2", target_bir_lowering=False)
    xl = dram_in(nc, "xl", [2, SEQ, 128])
    xc = dram_in(nc, "xc", [2, 128, CTX])
    ftw_ap = dram_in(nc, "ftw", [128, 64, 256])
    fc_ap = dram_in(nc, "fc", [128, 2, 256])
    c64_ap = dram_in(nc, "c64", [64, 2, 64])
    c256_ap = dram_in(nc, "c256", [128, 4, 256])
    yl = dram_out(nc, "yl", [2, 128, SEQ])
    yc = dram_out(nc, "yc", [2, 128, CTX])
    P = Prog(nc)
    ftw = P.sb("ftw", [128, 64, 256], BF16)
    fc = P.sb("fc", [128, 2, 256], BF16)
    c64 = P.sb("c64", [64, 2, 64], BF16)
    c256 = P.sb("c256", [128, 4, 256], BF16)
    XA = P.sb("XA", [128, 64, 128], BF16)
    D1 = P.sb("D1", [128, 64, 256], BF16)
    D2 = P.sb("D2", [64, 128, 256], BF16)
    Y = P.sb("Y", [128, SEQ], F32)
    XC = P.sb("XC", [128, CTX], BF16)
    DA = P.sb("DA", [128, 2, 256], BF16)
    YC = P.sb("YC", [128, CTX], F32)
    pp = [P.ps(f"pp{i}", [128, 512]) for i in range(4)]
    b_pp = [Buf() for _ in range(4)]
    b_t, b_XA, b_D1, b_D2, b_Y, b_XC, b_DA, b_YC = (Buf() for _ in range(8))
    for t, ap, k in ((ftw, ftw_ap, "t0"), (fc, fc_ap, "t1"), (c64, c64_ap, "t2"), (c256, c256_ap, "t3")):
        P.op("pool", lambda e, t=t, ap=ap: e.dma_start(out=t[:], in_=ap), cowrites=[b_t], dma=k)
    pi = [0]

    def nextp():
        pi[0] = (pi[0] + 1) % 4
        return pi[0]
    ev = [0]

    def evac(out_ap, in_ap, rd, wr, co=False):
        ev[0] ^= 1
        kw = dict(cowrites=[wr]) if co else dict(writes=[wr])
        if ev[0]:
            P.op("act", lambda e: e.activation(out=out_ap, in_=in_ap, func=AF.Copy), reads=[rd], **kw)
        else:
            P.op("dve", lambda e: e.tensor_copy(out=out_ap, in_=in_ap), reads=[rd], **kw)

    for u in range(2):
        src = xl[u].rearrange("(a r) c -> a r c", r=64)
        P.op("pool", lambda e, src=src: e.dma_start(out=XA[:], in_=src), writes=[b_XA], dma="xa")
        P.op("pool", lambda e, u=u: e.dma_start(out=XC[:], in_=xc[u]), writes=[b_XC], dma="xc")
        for b0 in range(0, 64, 2):
            i = nextp()

            def f(e, b0=b0, i=i):
                for q in range(2):
                    ins = e.matmul(pp[i][:, q * 256:(q + 1) * 256], lhsT=XA[:, b0 + q, :], rhs=ftw[:, b0 + q, :],
                                   start=True, stop=True)
                return ins
            P.op("pe", f, reads=[b_XA, b_t], writes=[b_pp[i]])
            evac(D1[:, b0:b0 + 2, :], pp[i][:].rearrange("p (q n) -> p q n", q=2), b_pp[i], b_D1, co=True)
        for a0 in range(0, 128, 2):
            i = nextp()

            def f(e, a0=a0, i=i):
                for q in range(2):
                    e.matmul(pp[i][0:64, q * 256:(q + 1) * 256], lhsT=D1[:, :, a0 + q], rhs=fc[:, 0, :], start=True, stop=False)
                    ins = e.matmul(pp[i][0:64, q * 256:(q + 1) * 256], lhsT=D1[:, :, 128 + a0 + q], rhs=fc[:, 1, :],
                                   start=False, stop=True)
                return ins
            P.op("pe", f, reads=[b_D1, b_t], writes=[b_pp[i]])
            evac(D2[:, a0:a0 + 2, :], pp[i][0:64, :].rearrange("p (q n) -> p q n", q=2), b_pp[i], b_D2, co=True)
        Yv = Y[:].rearrange("p (b a) -> p a b", a=128)
        for a0 in range(0, 128, 8):
            i = nextp()

            def f(e, a0=a0, i=i):
                for q in range(8):
                    e.matmul(pp[i][:, q * 64:(q + 1) * 64], lhsT=D2[:, a0 + q, 0:128], rhs=c64[:, 0, :], start=True, stop=False)
                    ins = e.matmul(pp[i][:, q * 64:(q + 1) * 64], lhsT=D2[:, a0 + q, 128:256], rhs=c64[:, 1, :],
                                   start=False, stop=True)
                return ins
            P.op("pe", f, reads=[b_D2, b_t], writes=[b_pp[i]])
            evac(Yv[:, a0:a0 + 8, :], pp[i][:].rearrange("p (a b) -> p a b", a=8), b_pp[i], b_Y, co=True)
        P.op("sp", lambda e, u=u: e.dma_start(out=yl[u], in_=Y[:]), reads=[b_Y], writes=[Buf()], dma="sty")
        i = nextp()

        def f(e, i=i):
            for j in range(2):
                ins = e.matmul(pp[i][:, j * 256:(j + 1) * 256], lhsT=XC[:, j * 128:(j + 1) * 128], rhs=fc[:, 0, :],
                               start=True, stop=True)
            return ins
        P.op("pe", f, reads=[b_XC, b_t], writes=[b_pp[i]])
        evac(DA[:], pp[i][:].rearrange("p (j n) -> p j n", j=2), b_pp[i], b_DA)
        i = nextp()

        def f(e, i=i):
            n = 0
            for j in range(2):
                for ri in range(2):
                    ins = e.matmul(pp[i][:, 0:256], lhsT=DA[:, j, ri * 128:(ri + 1) * 128], rhs=c256[:, 2 * ri + j, :],
                                   start=(n == 0), stop=(n == 3))
                    n += 1
            return ins
        P.op("pe", f, reads=[b_DA, b_t], writes=[b_pp[i]])
        evac(YC[:], pp[i][:, 0:256], b_pp[i], b_YC)
        P.op("sp", lambda e, u=u: e.dma_start(out=yc[u], in_=YC[:]), reads=[b_YC], writes=[Buf()], dma="styc")
    P.finish()
    P.emit()
    return nc


def fft_tables():
    a = np.arange(128)[:, None, None]
    b = np.arange(64)[None, :, None]
    ap = np.arange(128)[None, None, :]
    th = 2 * np.pi * (a * ap / 128.0 + b * ap / 8192.0)
    ftw = np.concatenate([np.cos(th), -np.sin(th)], axis=-1) / np.sqrt(128.0)
    c = np.arange(128)[:, None]
    cp = np.arange(128)[None, :]
    th = 2 * np.pi * c * cp / 128.0
    cr, ci = np.cos(th) / np.sqrt(128.0), -np.sin(th) / np.sqrt(128.0)
    fc = np.stack([np.concatenate([cr, ci], 1), np.concatenate([-ci, cr], 1)], axis=1)
    bb = np.arange(64)[:, None]
    bp = np.arange(64)[None, :]
    th = 2 * np.pi * bb * bp / 64.0
    c64 = np.stack([np.cos(th), np.sin(th)], axis=1) / 8.0
    l = np.arange(256)[:, None]
    lp = np.arange(256)[None, :]
    th = 2 * np.pi * l * lp / 256.0
    C, S = np.cos(th) / 16.0, np.sin(th) / 16.0
    c256 = np.stack([C[0:128], C[128:256], S[0:128], S[128:256]], axis=1)
    f = lambda x: np.ascontiguousarray(x, dtype=np.float32)
    return f(ftw), f(fc), f(c64), f(c256)


NBLK = SEQ // 128


def build_LC(nblk=NBLK):
    nc = bass.Bass("TRN2", target_bir_lowering=False)
    S = nblk * 128
    qt_ap = dram_in(nc, "qt", [128, 4, S])
    kt_ap = dram_in(nc, "kt", [128, S + CTX])
    v_ap = dram_in(nc, "v", [128, nblk + 2, 64])
    mask_ap = dram_in(nc, "mask", [128, 384])
    sink_ap = dram_in(nc, "sink", [128, 8])
    id_ap = dram_in(nc, "ident", [128, 128])
    o_ap = dram_out(nc, "o", [S, 512])
    P = Prog(nc)
    QT = P.sb("QT", [128, 4, S], BF16)
    KT = P.sb("KT", [128, S + CTX], BF16)
    V = P.sb("V", [128, nblk + 2, 64], BF16)
    mask = P.sb("mask", [128, 384], F32)
    sink = P.sb("sink", [128, 8], F32)
    ident = P.sb("ident", [128, 128], BF16)
    sc = [P.sb(f"sc{i}", [128, 640], F32) for i in range(2)]
    pb = [P.sb(f"pb{i}", [128, 640], BF16) for i in range(2)]
    pTs = [P.sb(f"pTs{i}", [128, 5, 128], BF16) for i in range(2)]
    sm = [P.sb(f"sm{i}", [128, 8], F32) for i in range(2)]
    ot = [P.sb(f"ot{i}", [128, 512], F32) for i in range(2)]
    scA = [P.ps(f"scA{i}", [128, 512]) for i in range(2)]
    scB = [P.ps(f"scB{i}", [128, 512]) for i in range(2)]
    pTt = [P.ps(f"pTt{i}", [128, 8, 128], BF16) for i in range(2)]
    ops = [P.ps(f"ops{i}", [128, 512]) for i in range(2)]
    B2 = lambda: [Buf(), Buf()]
    b_c = Buf()
    b_sc, b_pb, b_pTs, b_sm, b_ot, b_scA, b_scB, b_pTt, b_ops = (B2() for _ in range(9))
    for t, ap, k in ((QT, qt_ap, "c0"), (KT, kt_ap, "c1"), (V, v_ap, "c2"), (ident, id_ap, "c3")):
        P.op("pool", lambda e, t=t, ap=ap: e.dma_start(out=t[:], in_=ap), cowrites=[b_c], dma=k)
    for t, ap, k in ((mask, mask_ap, "c4"), (sink, sink_ap, "c5")):
        P.op("sp", lambda e, t=t, ap=ap: e.dma_start(out=t[:], in_=ap), cowrites=[b_c], dma=k)
    n = 0
    for i in range(nblk):
        kb0, kb1 = max(i - 1, 0), min(i + 1, nblk - 1)
        nloc = kb1 - kb0 + 1
        nk = nloc * 128
        mo = (kb0 - (i - 1)) * 128
        nt = nloc + 2
        L = nk + 256
        oi = i % 2
        for r in range(8):
            s = n % 2
            n += 1
            hh, j = r // 4, r % 4
            p0, p1 = hh * 64, hh * 64 + 64
            q = QT[p0:p1, j, i * 128:(i + 1) * 128]

            def f(e, q=q, s=s, kb0=kb0, nk=nk, p0=p0, p1=p1):
                e.matmul(scA[s][:, 0:nk], lhsT=q, rhs=KT[p0:p1, kb0 * 128:kb0 * 128 + nk], start=True, stop=True)
                return e.matmul(scB[s][:, 0:256], lhsT=q, rhs=KT[p0:p1, S:S + CTX], start=True, stop=True)
            P.op("pe", f, reads=[b_c], writes=[b_scA[s], b_scB[s]])
            P.op("dve", lambda e, s=s, nk=nk, mo=mo: e.tensor_tensor(out=sc[s][:, 0:nk], in0=scA[s][:, 0:nk],
                                                                    in1=mask[:, mo:mo + nk], op=ALU.add),
                 reads=[b_scA[s], b_c], writes=[b_sc[s]])
            P.op("act", lambda e, s=s, nk=nk: e.activation(out=sc[s][:, nk:nk + 256], in_=scB[s][:, 0:256], func=AF.Copy),
                 reads=[b_scB[s]], cowrites=[b_sc[s]])
            m = sm[s]
            P.op("dve", lambda e, s=s, L=L, m=m: e.reduce_max(out=m[:, 0:1], in_=sc[s][:, 0:L], axis=AX.X),
                 reads=[b_sc[s]], writes=[b_sm[s]])
            P.op("dve", lambda e, m=m: e.tensor_scalar_mul(out=m[:, 1:2], in0=m[:, 0:1], scalar1=0.125),
                 reads=[b_sm[s]], writes=[b_sm[s]])
            P.op("dve", lambda e, m=m, r=r: e.tensor_tensor(out=m[:, 2:3], in0=m[:, 1:2], in1=sink[:, r:r + 1], op=ALU.max),
                 reads=[b_sm[s], b_c], writes=[b_sm[s]])
            P.op("dve", lambda e, m=m: e.tensor_scalar_mul(out=m[:, 3:4], in0=m[:, 2:3], scalar1=-1.0),
                 reads=[b_sm[s]], writes=[b_sm[s]])
            P.op("act", lambda e, s=s, L=L, m=m: e.activation(out=pb[s][:, 0:L], in_=sc[s][:, 0:L], func=AF.Exp,
                                                             scale=0.125, bias=m[:, 3:4], accum_out=m[:, 4:5]),
                 reads=[b_sc[s], b_sm[s]], writes=[b_pb[s], b_sm[s]])
            P.op("act", lambda e, m=m, r=r: e.activation(out=m[:, 5:6], in_=sink[:, r:r + 1], func=AF.Exp, bias=m[:, 3:4]),
                 reads=[b_sm[s], b_c], writes=[b_sm[s]])
            P.op("dve", lambda e, m=m: e.tensor_tensor(out=m[:, 6:7], in0=m[:, 4:5], in1=m[:, 5:6], op=ALU.add),
                 reads=[b_sm[s]], writes=[b_sm[s]])
            P.op("dve", lambda e, m=m: e.reciprocal(out=m[:, 7:8], in_=m[:, 6:7]), reads=[b_sm[s]], writes=[b_sm[s]])

            def f(e, s=s, nt=nt):
                for t in range(nt):
                    ins = e.transpose(out=pTt[s][:, t, :], in_=pb[s][:, t * 128:(t + 1) * 128], identity=ident[:])
                return ins
            P.op("pe", f, reads=[b_pb[s], b_c], writes=[b_pTt[s]])
            P.op("dve", lambda e, s=s, nt=nt: e.tensor_copy(out=pTs[s][:, 0:nt, :], in_=pTt[s][:, 0:nt, :]),
                 reads=[b_pTt[s]], writes=[b_pTs[s]])

            def f(e, s=s, nt=nt, nloc=nloc, kb0=kb0):
                for t in range(nt):
                    vb = kb0 + t if t < nloc else nblk + (t - nloc)
                    ins = e.matmul(ops[s][:, 0:64], lhsT=pTs[s][:, t, :], rhs=V[:, vb, :], start=(t == 0), stop=(t == nt - 1))
                return ins
            P.op("pe", f, reads=[b_pTs[s], b_c], writes=[b_ops[s]])
            P.op("act", lambda e, s=s, m=m, r=r, oi=oi: e.activation(out=ot[oi][:, r * 64:(r + 1) * 64], in_=ops[s][:, 0:64],
                                                                    func=AF.Copy, scale=m[:, 7:8]),
                 reads=[b_ops[s], b_sm[s]], cowrites=[b_ot[oi]])
        P.op("sp", lambda e, i=i, oi=oi: e.dma_start(out=o_ap[i * 128:(i + 1) * 128, :], in_=ot[oi][:]),
             reads=[b_ot[oi]], writes=[Buf()], dma=f"so{oi}")
        b_ot[oi].w = []
    P.finish()
    P.emit()
    return nc


def attn_mask():
    a = np.arange(128)[:, None]
    j = np.arange(384)[None, :]
    return np.where(np.abs(j - 128 - a) <= 128, 0.0, -1e30).astype(np.float32)


def build_LB(T, segs):
    nc = bass.Bass("TRN2", target_bir_lowering=False)
    nseg = len(segs)
    nvec = 9 * nseg + 11
    z1 = dram_in(nc, "z1", [D, T])
    gb = dram_in(nc, "gb", [1024, T])
    vp = dram_in(nc, "vp", [1024, T])
    vv = dram_in(nc, "vv", [1024, T])
    vn = dram_in(nc, "vn", [1024, T])
    yb = dram_in(nc, "yb", [1024, T])
    vecs_ap = dram_in(nc, "vecs", [128, nvec, NCH])
    wout = dram_in(nc, "wout", [8, 128, NCH, 256])
    wg2, wu2, wd2 = ffn_w_in(nc, "2")
    wg3, wu3, wd3 = ffn_w_in(nc, "3")
    wqkv = dram_in(nc, "wqkv", [10, 128, NCH, 256])
    wperm = dram_in(nc, "wperm", [9, 128, NCH, 256])
    cos_ap = dram_in(nc, "cosT", [128, T])
    sin_ap = dram_in(nc, "sinT", [128, T])
    z4 = dram_out(nc, "z4", [D, T])
    qr = dram_out(nc, "qr", [D, T])
    kr = dram_out(nc, "kr", [256, T])
    vo = dram_out(nc, "vo", [256, T])
    z2 = nc.dram_tensor("z2", [D, T], F32).ap()
    z3 = nc.dram_tensor("z3", [D, T], F32).ap()
    P = Prog(nc)
    R = RL(P, T, segs, vecs_ap, nvec)
    cosT = P.sb("cosT", [128, T], F32)
    sinT = P.sb("sinT", [128, T], F32)
    b_rope = Buf()
    P.op("sp", lambda e: e.dma_start(out=cosT[:], in_=cos_ap), cowrites=[b_rope], dma="r0")
    P.op("sp", lambda e: e.dma_start(out=sinT[:], in_=sin_ap), cowrites=[b_rope], dma="r1")
    V = lambda s, k: 9 * s + k
    L0 = 9 * nseg
    for s in range(nseg):
        R.derive(V(s, 2), add=1.0)
        R.derive(V(s, 3), mul=0.5)
        R.derive(V(s, 5), add=1.0)
        R.derive(V(s, 6), mul=0.5)
        R.derive(V(s, 8), add=1.0)
    SL = lambda k: [V(s, k) for s in range(nseg)]
    b_in = Buf()
    nb = len(R.blocks)
    b_z2 = [Buf() for _ in range(nb)]
    b_z3 = [Buf() for _ in range(nb)]
    b_z4 = [Buf() for _ in range(nb)]
    b_o = Buf()

    def mix_fill(bi, c0, c1, w):
        tl = [R.sg[0], R.sg[1], R.zt[0], R.zt[1], R.tmp[0]]
        tb = [R.b_sg[0], R.b_sg[1], R.b_zt[0], R.b_zt[1], R.b_tmp[0]]
        for j in range(8):
            for k, src in enumerate((gb, vp, vv, vn, yb)):
                P.op("sp", lambda e, k=k, src=src, j=j: e.dma_start(out=tl[k][:, :w], in_=src[j * 128:(j + 1) * 128, c0:c1]),
                     writes=[tb[k]], dma=f"m{k}")
            P.op("dve", lambda e, j=j: e.tensor_scalar_mul(out=tl[1][:, :w], in0=tl[1][:, :w], scalar1=R.vc(L0 + 8, j)),
                 reads=[R.b_vecs], writes=[tb[1]])
            P.op("dve", lambda e, j=j: e.scalar_tensor_tensor(out=tl[1][:, :w], in0=tl[2][:, :w], scalar=R.vc(L0 + 9, j),
                                                              in1=tl[1][:, :w], op0=ALU.mult, op1=ALU.add),
                 reads=[R.b_vecs, tb[2]], writes=[tb[1]])
            P.op("dve", lambda e, j=j: e.scalar_tensor_tensor(out=tl[1][:, :w], in0=tl[3][:, :w], scalar=R.vc(L0 + 10, j),
                                                              in1=tl[1][:, :w], op0=ALU.mult, op1=ALU.add),
                 reads=[R.b_vecs, tb[3]], writes=[tb[1]])
            P.op("dve", lambda e, j=j: e.tensor_tensor(out=R.hb[:, j, c0:c1], in0=tl[1][:, :w], in1=tl[0][:, :w], op=ALU.mult),
                 reads=[tb[0], tb[1]], cowrites=R.W_HB)
            P.op("act", lambda e, j=j: e.activation(out=R.hb[:, 8 + j, c0:c1], in_=tl[4][:, :w], func=AF.Copy),
                 reads=[tb[4]], cowrites=R.W_HB)

    R.prologue_all(z1, b_in, ln=(L0 + 0, L0 + 1), mod=None, save=True, fill=mix_fill)
    R.proj_residual_all(wout, z2, b_z2, SL(0))
    R.prologue_all(z2, b_z2, ln=(L0 + 2, L0 + 3), mod=(SL(1), SL(2)), save=True)
    R.ffn_all(wg2, wu2, wd2, z3, b_z3, SL(3))
    R.prologue_all(z3, b_z3, ln=(L0 + 4, L0 + 5), mod=(SL(4), SL(5)), save=True)
    R.ffn_all(wg3, wu3, wd3, z4, b_z4, SL(6))
    R.prologue_all(z4, b_z4, ln=(L0 + 6, L0 + 7), mod=(SL(7), SL(8)), save=False)
    ld = lambda g: (R.load_wa(wqkv, g, 256), R.load_wa(wperm, g, 256) if g < 9 else None)
    nxt = ld(0)
    for g in range(10):
        iw, ip = nxt
        if g + 1 < 10:
            nxt = ld(g + 1)
        for bi, (c0, c1) in enumerate(R.blocks):
            w = c1 - c0
            for j in range(2):
                pb = R.pp_i
                R.pp_i ^= 1
                R.mm(R.pg[pb], R.b_pg[pb], R.wa[iw], R.b_wa[iw], j * 128, NCH, R.hb, c0, c1, [R.b_hb])
                i = R.zt_i
                R.zt_i ^= 1
                zt, tmp, sg, pg, pu = R.zt[i], R.tmp[i], R.sg[pb], R.pg[pb], R.pu[pb]
                if g < 9:
                    R.mm(R.pu[pb], R.b_pu[pb], R.wa[ip], R.b_wa[ip], j * 128, NCH, R.hb, c0, c1, [R.b_hb])
                    P.op("dve", lambda e, sg=sg, pg=pg, w=w, c0=c0, c1=c1: e.tensor_tensor(
                        out=sg[:, :w], in0=pg[:, :w], in1=cosT[:, c0:c1], op=ALU.mult),
                        reads=[R.b_pg[pb], b_rope], writes=[R.b_sg[pb]])
                    P.op("dve", lambda e, tmp=tmp, pu=pu, w=w, c0=c0, c1=c1: e.tensor_tensor(
                        out=tmp[:, :w], in0=pu[:, :w], in1=sinT[:, c0:c1], op=ALU.mult),
                        reads=[R.b_pu[pb], b_rope], writes=[R.b_tmp[i]])
                    P.op("dve", lambda e, zt=zt, sg=sg, tmp=tmp, w=w: e.tensor_tensor(
                        out=zt[:, :w], in0=sg[:, :w], in1=tmp[:, :w], op=ALU.add),
                        reads=[R.b_sg[pb], R.b_tmp[i]], writes=[R.b_zt[i]])
                    dst, row = (qr, (2 * g + j) * 128) if g < 8 else (kr, j * 128)
                else:
                    P.op("act", lambda e, zt=zt, pg=pg, w=w: e.activation(out=zt[:, :w], in_=pg[:, :w], func=AF.Copy),
                         reads=[R.b_pg[pb]], writes=[R.b_zt[i]])
                    dst, row = vo, j * 128
                R.store(dst, b_o, row, c0, c1, zt, R.b_zt[i])
    P.finish()
    P.emit()
    return nc


def build_LD(T):
    nc = bass.Bass("TRN2", target_bir_lowering=False)
    segs = [(0, T)]
    nvec = 10
    z4 = dram_in(nc, "z4", [D, T])
    oT = dram_in(nc, "oT", [D, T])
    vecs_ap = dram_in(nc, "vecs", [128, nvec, NCH])
    wout = dram_in(nc, "wout", [8, 128, NCH, 256])
    wg, wu, wd = ffn_w_in(nc, "")
    out = dram_out(nc, "out", [D, T])
    z5 = nc.dram_tensor("z5", [D, T], F32).ap()
    z6 = nc.dram_tensor("z6", [D, T], F32).ap()
    P = Prog(nc)
    R = RL(P, T, segs, vecs_ap, nvec)
    R.derive(2, add=1.0)
    R.derive(3, mul=0.5)
    nb = len(R.blocks)
    b_in = Buf()
    b_z5 = [Buf() for _ in range(nb)]
    b_z6 = [Buf() for _ in range(nb)]

    def o_fill(bi, c0, c1, w):
        P.op("pool", lambda e: e.dma_start(out=R.hb[:, :, c0:c1], in_=chunked(oT)[:, :, c0:c1]),
             cowrites=R.W_HB, dma="oin")

    def out_fill(bi, c0, c1, w):
        P.op("sp", lambda e: e.dma_start(out=chunked(out)[:, :, c0:c1], in_=R.xn[:, :, :w]),
             reads=[R.b_xn], writes=[Buf()], dma="fin")

    R.prologue_all(z4, b_in, ln=(4, 5), mod=None, save=True, fill=o_fill)
    R.proj_residual_all(wout, z5, b_z5, [0])
    R.prologue_all(z5, b_z5, ln=(6, 7), mod=([1], [2]), save=True)
    R.ffn_all(wg, wu, wd, z6, b_z6, [3])
    R.prologue_all(z6, b_z6, ln=(8, 9), mod=None, save=False, fill=out_fill)
    P.finish()
    P.emit()
    return nc


def lay(v):
    v = np.asarray(v, dtype=np.float32)
    if v.shape[0] < D:
        v = np.concatenate([v, np.zeros(D - v.shape[0], np.float32)])
    return v.reshape(NCH, 128).T


def rope_tables(pos):
    nf = 16
    inv = np.power(10000.0, -np.arange(nf, dtype=np.float64) / nf)
    row = (pos // 64).astype(np.float64)
    col = (pos % 64).astype(np.float64)
    d = np.arange(64)
    axis, part, f = d // 32, (d % 32) // 16, d % 16
    ang = np.where(axis[:, None] == 0, row[None, :], col[None, :]) * inv[f][:, None]
    c = np.cos(ang)
    s = np.sin(ang) * np.where(part == 0, -1.0, 1.0)[:, None]
    return np.concatenate([c, c], 0).astype(np.float32), np.concatenate([s, s], 0).astype(np.float32)


def rope_perm():
    d = np.arange(64)
    part = (d % 32) // 16
    p = np.where(part == 0, d + 16, d - 16)
    cols = np.concatenate([h * 64 + p for h in range(36)])
    return cols


_CACHE = {}


def _prog(key, fn):
    if key not in _CACHE:
        _CACHE[key] = fn()
    return _CACHE[key]


def _run(nc, ins):
    res = run_bass_kernel_spmd(nc, ins, core_ids=list(range(NCORE)))
    return res.results


def kernel(x, c, ctx, c_ctx, w_mod, b_mod, ln_g, ln_b, ffn_w_gate, ffn_w_up, ffn_w_down,
           ab_w_in, ab_conv, ab_w_out, attn_w_in, attn_sink, attn_w_out):
    f32 = lambda a: np.ascontiguousarray(np.asarray(a), dtype=np.float32)
    x, c, ctx, c_ctx = f32(x), f32(c), f32(ctx), f32(c_ctx)
    w_mod, b_mod, ln_g, ln_b = f32(w_mod), f32(b_mod), f32(ln_g), f32(ln_b)
    ffn_w_gate, ffn_w_up, ffn_w_down = f32(ffn_w_gate), f32(ffn_w_up), f32(ffn_w_down)
    ab_w_in, ab_conv, ab_w_out = f32(ab_w_in), f32(ab_conv), f32(ab_w_out)
    attn_w_in, attn_sink, attn_w_out = f32(attn_w_in), f32(attn_sink), f32(attn_w_out)
    LT, CT = SEQ // 4, CTX // 4
    T = LT + CT
    segs = [(0, LT), (LT, T)]
    cores = [(r // 4, r % 4) for r in range(NCORE)]

    cv = np.ascontiguousarray(np.stack([lay(c[0]), lay(c[1]), lay(c_ctx)], axis=-1))
    wm = np.concatenate([w_mod[0], w_mod[1]], axis=1)
    bm = b_mod.reshape(-1)
    ins = []
    for r in range(NCORE):
        sl = slice(r * MODC, (r + 1) * MODC)
        ins.append({"cv": cv, "w": np.ascontiguousarray(wm[:, sl]),
                    "b": np.ascontiguousarray(np.broadcast_to(bm[sl], (3, MODC)))})
    res = _run(_prog("L0", build_L0), ins)
    del wm
    mod = np.concatenate([res[r]["mod"] for r in range(NCORE)], axis=1).reshape(3, 2, 9, D)

    def mv(b, s, layer, k):
        return lay(mod[b if s == 0 else 2, layer, k])

    _ffw = {}

    def ffw(l, i, sfx):
        if (l, i) not in _ffw:
            _ffw[(l, i)] = (tile_w(ffn_w_gate[l, i], 256), tile_w(ffn_w_up[l, i], 256), tile_w(ffn_w_down[l, i], 128))
        a, b_, c_ = _ffw[(l, i)]
        return {"wg" + sfx: a, "wu" + sfx: b_, "wd" + sfx: c_}

    w_gb, w_gc = tile_w(ab_w_in[0][:, 0:1024], 256), tile_w(ab_w_in[0][:, 1024:2048], 128)
    w_xi, w_uf = tile_w(ab_w_in[0][:, 2048:3072], 128), tile_w(ab_w_in[0][:, 3072:4096], 256)

    ins = []
    for b, q in cores:
        xT = np.concatenate([x[b, q * LT:(q + 1) * LT].T, ctx[b, q * CT:(q + 1) * CT].T], axis=1)
        vecs = [mv(b, s, 0, k) for s in range(2) for k in range(5)] + [lay(ln_g[0, 0]), lay(ln_b[0, 0])]
        ins.append({"xT": np.ascontiguousarray(xT), "vecs": np.ascontiguousarray(np.stack(vecs, axis=1)),
                    **ffw(0, 0, ""), "w_gb": w_gb, "w_gc": w_gc, "w_xi": w_xi, "w_uf": w_uf})
    resA = _run(_prog("LA", lambda: build_LA(T, segs)), ins)

    def gather(res, name, nrow):
        lat = np.empty((2, nrow, SEQ), np.float32)
        cx = np.empty((2, nrow, CTX), np.float32)
        for r, (b, q) in enumerate(cores):
            a = res[r][name]
            lat[b, :, q * LT:(q + 1) * LT] = a[:, :LT]
            cx[b, :, q * CT:(q + 1) * CT] = a[:, LT:]
        return lat, cx

    UFl, UFc = gather(resA, "uf", 1024)
    VVl, VVc = gather(resA, "vv", 1024)

    ftw, fc, c64, c256 = fft_tables()
    ins = []
    for b, q in cores:
        gs = [2 * q, 2 * q + 1]
        xl = np.stack([UFl[b, g * 128:(g + 1) * 128].T for g in gs])
        xc = np.stack([UFc[b, g * 128:(g + 1) * 128] for g in gs])
        ins.append({"xl": np.ascontiguousarray(xl), "xc": np.ascontiguousarray(xc),
                    "ftw": ftw, "fc": fc, "c64": c64, "c256": c256})
    resF = _run(_prog("LF", build_LF), ins)
    YBl = np.empty((2, 1024, SEQ), np.float32)
    YBc = np.empty((2, 1024, CTX), np.float32)
    for r, (b, q) in enumerate(cores):
        for u in range(2):
            g = 2 * q + u
            YBl[b, g * 128:(g + 1) * 128] = resF[r]["yl"][u]
            YBc[b, g * 128:(g + 1) * 128] = resF[r]["yc"][u]
    del UFl, UFc

    def shift(a, k):
        o = np.zeros_like(a)
        if k > 0:
            o[..., k:] = a[..., :-k]
        else:
            o[..., :k] = a[..., -k:]
        return o

    VPl, VPc, VNl, VNc = shift(VVl, 1), shift(VVc, 1), shift(VVl, -1), shift(VVc, -1)

    def cols(lat, cx, b, q):
        return np.ascontiguousarray(np.concatenate([lat[b][:, q * LT:(q + 1) * LT], cx[b][:, q * CT:(q + 1) * CT]], axis=1))

    perm = rope_perm()
    wperm = tile_w(np.ascontiguousarray(attn_w_in[0][:, perm]), 256)
    wqkv_t = tile_w(attn_w_in[0], 256)
    wout_t = tile_w(ab_w_out[0], 256)
    _ffw.pop((0, 0), None)
    ins = []
    for r, (b, q) in enumerate(cores):
        vecs = []
        for s in range(2):
            vecs += [mv(b, s, 0, 5), mv(b, s, 0, 6), mv(b, s, 0, 7), mv(b, s, 0, 8),
                     mv(b, s, 1, 0), mv(b, s, 1, 1), mv(b, s, 1, 2), mv(b, s, 1, 3), mv(b, s, 1, 4)]
        vecs += [lay(ln_g[0, 0]), lay(ln_b[0, 0]), lay(ln_g[0, 1]), lay(ln_b[0, 1]), lay(ln_g[0, 2]), lay(ln_b[0, 2]),
                 lay(ln_g[1, 0]), lay(ln_b[1, 0]), lay(ab_conv[0, 0]), lay(ab_conv[0, 1]), lay(ab_conv[0, 2])]
        cl, sl_ = rope_tables(np.arange(q * LT, (q + 1) * LT))
        cosT = np.concatenate([cl, np.ones((128, CT), np.float32)], axis=1)
        sinT = np.concatenate([sl_, np.zeros((128, CT), np.float32)], axis=1)
        ins.append({"z1": resA[r]["z1"], "gb": resA[r]["gb"], "vp": cols(VPl, VPc, b, q), "vv": resA[r]["vv"],
                    "vn": cols(VNl, VNc, b, q), "yb": cols(YBl, YBc, b, q),
                    "vecs": np.ascontiguousarray(np.stack(vecs, axis=1)), "wout": wout_t,
                    **ffw(0, 1, "2"), **ffw(1, 0, "3"), "wqkv": wqkv_t, "wperm": wperm,
                    "cosT": np.ascontiguousarray(cosT), "sinT": np.ascontiguousarray(sinT)})
    resB = _run(_prog("LB", lambda: build_LB(T, segs)), ins)
    del resA, VPl, VNl, YBl, VVl
    Ql, _ = gather(resB, "qr", D)
    Kl, Kc = gather(resB, "kr", 256)
    Vl, Vc = gather(resB, "vo", 256)

    mask = attn_mask()
    ident = np.eye(128, dtype=np.float32)
    ins = []
    for r in range(NCORE):
        b, g = r // 4, r % 4
        qt = Ql[b, g * 512:(g + 1) * 512].reshape(2, 4, 64, SEQ).transpose(0, 2, 1, 3).reshape(128, 4, SEQ)
        k1 = np.concatenate([Kl[b, g * 64:(g + 1) * 64], Kc[b, g * 64:(g + 1) * 64]], axis=1)
        vl = Vl[b, g * 64:(g + 1) * 64].T.reshape(NBLK, 128, 64)
        vc_ = Vc[b, g * 64:(g + 1) * 64].T.reshape(2, 128, 64)
        vall = np.concatenate([vl, vc_], axis=0).transpose(1, 0, 2)
        ins.append({"qt": np.ascontiguousarray(qt), "kt": np.ascontiguousarray(np.concatenate([k1, k1], axis=0)),
                    "v": np.ascontiguousarray(vall), "mask": mask,
                    "sink": np.ascontiguousarray(np.broadcast_to(attn_sink[0, g * 8:(g + 1) * 8], (128, 8))),
                    "ident": ident})
    resC = _run(_prog("LC", build_LC), ins)
    del Ql
    O = np.empty((2, D, SEQ), np.float32)
    for r in range(NCORE):
        b, g = r // 4, r % 4
        O[b, g * 512:(g + 1) * 512] = resC[r]["o"].T
    del resC

    _ffw.clear()
    awout_t = tile_w(attn_w_out[0], 256)
    ins = []
    for r, (b, q) in enumerate(cores):
        vecs = [mv(b, 0, 1, 5), mv(b, 0, 1, 6), mv(b, 0, 1, 7), mv(b, 0, 1, 8),
                lay(ln_g[1, 0]), lay(ln_b[1, 0]), lay(ln_g[1, 1]), lay(ln_b[1, 1]), lay(ln_g[1, 2]), lay(ln_b[1, 2])]
        ins.append({"z4": np.ascontiguousarray(resB[r]["z4"][:, :LT]), "oT": np.ascontiguousarray(O[b][:, q * LT:(q + 1) * LT]),
                    "vecs": np.ascontiguousarray(np.stack(vecs, axis=1)), "wout": awout_t, **ffw(1, 1, "")})
    resD = _run(_prog("LD", lambda: build_LD(LT)), ins)
    out = np.empty((2, SEQ, D), np.float32)
    for r, (b, q) in enumerate(cores):
        out[b, q * LT:(q + 1) * LT] = resD[r]["out"].T
    return out
```

```python
import numpy as np
from contextlib import ExitStack
import concourse.bass as bass
import concourse.mybir as mybir
from concourse.bass_utils import run_bass_kernel_spmd

F32 = mybir.dt.float32
BF16 = mybir.dt.bfloat16
AF = mybir.ActivationFunctionType
ALU = mybir.AluOpType
AX = mybir.AxisListType

D = 2048
DFF = 5632
NCH = 16
FCH = 44
SEQ = 8192
CTX = 256
NCORE = 8
ALPHA = 4.0 ** 0.25
LN_EPS = 1e-5
TB = 512


class Buf:
    __slots__ = ("w", "r", "name")

    def __init__(self, name=""):
        self.w = []
        self.r = []
        self.name = name


class Ctr:
    LIMIT = 30000

    def __init__(self, P, name, step):
        self.P, self.name, self.step = P, name, step
        self.k = 0
        self.done = []
        self._new()

    def _new(self):
        self.sem = self.P.stack.enter_context(self.P.nc.semaphore(f"{self.name}_{self.k}"))
        self.k += 1
        self.val = 0

    def next(self):
        if self.val + self.step > self.LIMIT:
            self.done.append((self.sem, self.val))
            self._new()
        self.val += self.step
        return (self.sem, self.val)


class Eng:
    def __init__(self, P, name):
        self.name = name
        self.ops = []
        self.waited = {}
        self.ctr = Ctr(P, "e" + name, 1)


class Prog:
    def __init__(self, nc):
        self.nc = nc
        self.stack = ExitStack()
        self.engs = {n: Eng(self, n) for n in ("pe", "act", "dve", "pool", "sp")}
        self.dctr = {}
        self.fuzzy = {}
        self.n = 0

    def sb(self, name, shape, dt):
        return self.stack.enter_context(self.nc.sbuf_tensor("s_" + name, list(shape), dt))

    def ps(self, name, shape, dt=F32):
        return self.stack.enter_context(self.nc.psum_tensor("p_" + name, list(shape), dt))

    def op(self, eng, fn, reads=(), writes=(), dma=None, cowrites=()):
        E = self.engs[eng]
        deps = {}

        def add(tok):
            s, v = tok
            k = id(s)
            if k in self.fuzzy:
                v = max(v, self.fuzzy[k].val if self.fuzzy[k].sem is s else v)
            if k not in deps or deps[k][1] < v:
                deps[k] = (s, v)

        for b in reads:
            for t in b.w:
                add(t)
        for b in writes:
            for t in b.w:
                add(t)
            for t in b.r:
                add(t)
        for b in cowrites:
            for t in b.r:
                add(t)
        if dma is not None:
            if dma not in self.dctr:
                self.dctr[dma] = Ctr(self, "d" + dma, 16)
            ctr = self.dctr[dma]
            if dma == "st":
                self.fuzzy[id(ctr.sem)] = ctr
        else:
            ctr = E.ctr
        waits = []
        for k, (s, v) in deps.items():
            if eng == "pe" and dma is None and s is E.ctr.sem:
                continue
            if E.waited.get(k, 0) >= v:
                continue
            E.waited[k] = v
            waits.append((s, v))
        tok = ctr.next()
        E.ops.append((waits, fn, tok[0], ctr.step))
        for b in reads:
            b.r.append(tok)
        for b in writes:
            b.w = [tok]
            b.r = []
        for b in cowrites:
            b.w.append(tok)
        self.n += 1
        return tok

    def finish(self):
        E = self.engs["sp"]
        waits = []
        for ctr in self.dctr.values():
            for s, v in ctr.done + [(ctr.sem, ctr.val)]:
                if v > 0 and E.waited.get(id(s), 0) < v:
                    waits.append((s, v))
        E.ops.append((waits, None, None, 0))

    def emit(self):
        nc = self.nc

        def mk(E):
            def run(e):
                for waits, fn, sem, step in E.ops:
                    for ws, wv in waits:
                        e.wait_ge(ws, wv)
                    if fn is not None:
                        fn(e).then_inc(sem, step)
            return run

        with nc.Block() as block:
            block.tensor(mk(self.engs["pe"]))
            block.scalar(mk(self.engs["act"]))
            block.vector(mk(self.engs["dve"]))
            block.gpsimd(mk(self.engs["pool"]))
            block.sync(mk(self.engs["sp"]))
        self.stack.close()


def blocks_of(T, tb=TB):
    return [(c, min(c + tb, T)) for c in range(0, T, tb)]


def chunked(ap):
    return ap.rearrange("(c p) t -> p c t", p=128)


class RL:
    def __init__(self, P, T, segs, vecs_ap, nvec):
        self.P, self.T, self.segs = P, T, segs
        self.blocks = blocks_of(T)
        nb = len(self.blocks)
        nc = P.nc
        self.xn = P.sb("xn", [128, NCH, TB], F32)
        self.big = P.sb("big", [128, max(NCH * T, FCH * TB)], BF16)
        self.hb = self.big[:, 0:NCH * T].rearrange("p (c t) -> p c t", c=NCH)
        self.ab = self.big[:, 0:FCH * TB].rearrange("p (c t) -> p c t", c=FCH)
        self.wa = [P.sb(f"wa{i}", [128, NCH, 256], BF16) for i in range(4)]
        self.wbig = P.sb("wbig", [128, 2 * FCH * 128], BF16)
        self.wb = [self.wbig[:, i * FCH * 128:(i + 1) * FCH * 128].rearrange("p (c t) -> p c t", c=FCH) for i in range(2)]
        self.sq = self.wbig[:, 0:NCH * TB].rearrange("p (c t) -> p c t", c=NCH)
        self.mu = P.sb("mu", [128, TB], F32)
        self.msq = P.sb("msq", [128, TB], F32)
        self.rstd = P.sb("rstd", [128, TB], F32)
        self.sg = [P.sb(f"sg{i}", [128, TB], F32) for i in range(2)]
        self.zt = [P.sb(f"zt{i}", [128, TB], F32) for i in range(2)]
        self.tmp = [P.sb(f"tmp{i}", [128, TB], F32) for i in range(2)]
        self.xr = [P.sb(f"xr{i}", [128, TB], F32) for i in range(2)]
        self.at = [P.sb(f"at{i}", [128, TB], BF16) for i in range(2)]
        self.ones = P.sb("ones", [128, 128], BF16)
        self.vecs = P.sb("vecs", [128, nvec, NCH], F32)
        self.s1 = P.ps("s1", [128, TB])
        self.s2 = P.ps("s2", [128, TB])
        self.pg = [P.ps(f"pg{i}", [128, TB]) for i in range(2)]
        self.pu = [P.ps(f"pu{i}", [128, TB]) for i in range(2)]
        self.py = [P.ps(f"py{i}", [128, TB]) for i in range(2)]
        self.xs = nc.dram_tensor("xs", [nb, 128, NCH, TB], F32).ap()
        self.A = nc.dram_tensor("Asp", [nb, 128, FCH, TB], BF16).ap()
        B = Buf
        self.b_xn, self.b_hb = B("xn"), B("hb")
        self.b_abc = [B(f"ab{c}") for c in range(FCH)]
        self.b_wa = [B() for _ in range(4)]
        self.b_wb = [B() for _ in range(2)]
        self.b_mu, self.b_msq, self.b_rstd = B(), B(), B()
        self.b_sg, self.b_zt, self.b_tmp, self.b_xr, self.b_at = [B(), B()], [B(), B()], [B(), B()], [B(), B()], [B(), B()]
        self.b_s1, self.b_s2 = B(), B()
        self.b_pg, self.b_pu, self.b_py = [B(), B()], [B(), B()], [B(), B()]
        self.b_ones, self.b_vecs = B(), B()
        self.b_xs = [B() for _ in range(nb)]
        self.b_A = [B() for _ in range(nb)]
        self.W_HB = [self.b_hb] + self.b_abc
        self.wa_i = self.wb_i = self.zt_i = self.xr_i = self.at_i = self.pp_i = 0
        ones, vecs = self.ones, self.vecs
        P.op("dve", lambda e: e.memset(ones[:], 1.0), writes=[self.b_ones])
        P.op("sp", lambda e: e.dma_start(out=vecs[:], in_=vecs_ap), writes=[self.b_vecs], dma="vecs")

    def vc(self, v, c):
        return self.vecs[:, v, c:c + 1]

    def derive(self, v, mul=None, add=None):
        vecs = self.vecs
        if add is not None:
            self.P.op("dve", lambda e: e.tensor_scalar_add(out=vecs[:, v, :], in0=vecs[:, v, :], scalar1=float(add)),
                      reads=[self.b_vecs], writes=[self.b_vecs])
        if mul is not None:
            self.P.op("dve", lambda e: e.tensor_scalar_mul(out=vecs[:, v, :], in0=vecs[:, v, :], scalar1=float(mul)),
                      reads=[self.b_vecs], writes=[self.b_vecs])

    def segparts(self, c0, c1):
        out = []
        for si, (s0, s1) in enumerate(self.segs):
            a, b = max(c0, s0), min(c1, s1)
            if a < b:
                out.append((si, a - c0, b - c0))
        return out

    def load_xn(self, src_view, src_buf, w):
        xn = self.xn
        self.P.op("sp", lambda e: e.dma_start(out=xn[:, :, :w], in_=src_view), reads=[src_buf], writes=[self.b_xn], dma="xin")

    def layernorm(self, c0, c1, vg, vb):
        P = self.P
        w = c1 - c0
        xn, hb, sq, ones = self.xn, self.hb, self.sq, self.ones
        mu, msq, rstd, s1, s2 = self.mu, self.msq, self.rstd, self.s1, self.s2
        P.op("act", lambda e: e.activation(out=hb[:, :, c0:c1], in_=xn[:, :, :w], func=AF.Copy),
             reads=[self.b_xn], writes=self.W_HB)
        P.op("act", lambda e: e.activation(out=sq[:, :, :w], in_=xn[:, :, :w], func=AF.Square),
             reads=[self.b_xn], writes=self.b_wb)

        def st(src, lo, hi, dst):
            def f(e):
                for c in range(NCH):
                    ins = e.matmul(dst[:, :w], lhsT=ones[:], rhs=src[:, c, lo:hi], start=(c == 0), stop=(c == NCH - 1))
                return ins
            return f
        P.op("pe", st(hb, c0, c1, s1), reads=[self.b_hb, self.b_ones], writes=[self.b_s1])
        P.op("pe", st(sq, 0, w, s2), reads=self.b_wb + [self.b_ones], writes=[self.b_s2])
        P.op("dve", lambda e: e.tensor_scalar_mul(out=mu[:, :w], in0=s1[:, :w], scalar1=1.0 / D),
             reads=[self.b_s1], writes=[self.b_mu])
        P.op("dve", lambda e: e.tensor_tensor(out=msq[:, :w], in0=mu[:, :w], in1=mu[:, :w], op=ALU.mult),
             reads=[self.b_mu], writes=[self.b_msq])
        P.op("dve", lambda e: e.scalar_tensor_tensor(out=rstd[:, :w], in0=s2[:, :w], scalar=1.0 / D, in1=msq[:, :w],
                                                     op0=ALU.mult, op1=ALU.subtract),
             reads=[self.b_s2, self.b_msq], writes=[self.b_rstd])
        P.op("dve", lambda e: e.tensor_scalar_add(out=rstd[:, :w], in0=rstd[:, :w], scalar1=LN_EPS),
             reads=[self.b_rstd], writes=[self.b_rstd])
        P.op("act", lambda e: e.activation(out=rstd[:, :w], in_=rstd[:, :w], func=AF.Sqrt),
             reads=[self.b_rstd], writes=[self.b_rstd])
        P.op("dve", lambda e: e.reciprocal(out=rstd[:, :w], in_=rstd[:, :w]),
             reads=[self.b_rstd], writes=[self.b_rstd])
        for c in range(NCH):
            P.op("dve", (lambda c: lambda e: e.tensor_tensor(out=xn[:, c, :w], in0=xn[:, c, :w], in1=mu[:, :w],
                                                             op=ALU.subtract))(c),
                 reads=[self.b_mu, self.b_xn], writes=[self.b_xn])
            P.op("dve", (lambda c: lambda e: e.tensor_tensor(out=xn[:, c, :w], in0=xn[:, c, :w], in1=rstd[:, :w],
                                                             op=ALU.mult))(c),
                 reads=[self.b_rstd, self.b_xn], writes=[self.b_xn])
            P.op("act", (lambda c: lambda e: e.activation(out=xn[:, c, :w], in_=xn[:, c, :w], func=AF.Identity,
                                                          scale=self.vc(vg, c), bias=self.vc(vb, c)))(c),
                 reads=[self.b_vecs, self.b_xn], writes=[self.b_xn])

    def modulate(self, c0, c1, vshift, vscale1):
        P = self.P
        xn, hb = self.xn, self.hb
        for si, a, b in self.segparts(c0, c1):
            for c in range(NCH):
                P.op("act", (lambda c, si, a, b: lambda e: e.activation(
                    out=hb[:, c, c0 + a:c0 + b], in_=xn[:, c, a:b], func=AF.Identity,
                    scale=self.vc(vscale1[si], c), bias=self.vc(vshift[si], c)))(c, si, a, b),
                    reads=[self.b_vecs, self.b_xn], writes=self.W_HB)

    def save_xs(self, bi, w):
        xn, xs = self.xn, self.xs
        self.P.op("sp", lambda e: e.dma_start(out=xs[bi, :, :, :w], in_=xn[:, :, :w]),
                  reads=[self.b_xn], writes=[self.b_xs[bi]], dma="xs")

    def prologue_all(self, src_ap, src_bufs, ln=None, mod=None, save=True, fill=None):
        for bi, (c0, c1) in enumerate(self.blocks):
            w = c1 - c0
            sb_ = src_bufs[bi] if isinstance(src_bufs, list) else src_bufs
            self.load_xn(chunked(src_ap)[:, :, c0:c1], sb_, w)
            if ln is not None:
                self.layernorm(c0, c1, ln[0], ln[1])
            if save:
                self.save_xs(bi, w)
            if mod is not None:
                self.modulate(c0, c1, mod[0], mod[1])
            if fill is not None:
                fill(bi, c0, c1, w)

    def load_wa(self, W_ap, g, ncols):
        i = self.wa_i
        self.wa_i = (i + 1) % 4
        t = self.wa[i]
        self.P.op("pool", lambda e: e.dma_start(out=t[:, :, :ncols], in_=W_ap[g]), writes=[self.b_wa[i]], dma=f"wa{i}")
        return i

    def load_wb(self, W_ap, g):
        i = self.wb_i
        self.wb_i = (i + 1) % 2
        t = self.wb[i]
        self.P.op("pool", lambda e: e.dma_start(out=t, in_=W_ap[g]), writes=[self.b_wb[i]], dma=f"wb{i}")
        return i

    def mm(self, dst, dst_buf, wt, wbuf, j0, kc, X, lo, hi, xbufs):
        w = hi - lo

        def f(e):
            for k in range(kc):
                ins = e.matmul(dst[:, :w], lhsT=wt[:, k, j0:j0 + 128], rhs=X[:, k, lo:hi], start=(k == 0), stop=(k == kc - 1))
            return ins
        self.P.op("pe", f, reads=[wbuf] + list(xbufs), writes=[dst_buf])

    def store(self, dst_ap, dst_buf, row0, c0, c1, tile, tbuf):
        w = c1 - c0
        self.P.op("sp", lambda e: e.dma_start(out=dst_ap[row0:row0 + 128, c0:c1], in_=tile[:, :w]),
                  reads=[tbuf], cowrites=[dst_buf], dma="st")

    def residual_out(self, dst_ap, dst_buf, dc, c0, c1, psum, pbuf, vgate, xres, xres_buf):
        P = self.P
        w = c1 - c0
        i = self.zt_i
        self.zt_i ^= 1
        tmp, zt = self.tmp[i], self.zt[i]
        for si, a, b in self.segparts(c0, c1):
            P.op("act", (lambda si, a, b: lambda e: e.activation(out=tmp[:, a:b], in_=psum[:, a:b], func=AF.Copy,
                                                                 scale=self.vc(vgate[si], dc)))(si, a, b),
                 reads=[pbuf, self.b_vecs], writes=[self.b_tmp[i]])
        P.op("dve", lambda e: e.scalar_tensor_tensor(out=zt[:, :w], in0=xres, scalar=ALPHA, in1=tmp[:, :w],
                                                     op0=ALU.mult, op1=ALU.add),
             reads=[xres_buf, self.b_tmp[i]], writes=[self.b_zt[i]])
        self.store(dst_ap, dst_buf, dc * 128, c0, c1, zt, self.b_zt[i])

    def ffn_all(self, wg, wu, wd, dst_ap, dst_bufs, vgate, res_ap=None, res_buf=None):
        P = self.P
        hb, ab, A = self.hb, self.ab, self.A
        nfg = FCH // 2
        ld = lambda fg: (self.load_wa(wg, fg, 256), self.load_wa(wu, fg, 256))
        nxt = ld(0)
        for fg in range(nfg):
            ig, iu = nxt
            if fg + 1 < nfg:
                nxt = ld(fg + 1)
            for bi, (c0, c1) in enumerate(self.blocks):
                w = c1 - c0
                for j in range(2):
                    fc = 2 * fg + j
                    pb = self.pp_i
                    self.pp_i ^= 1
                    self.mm(self.pg[pb], self.b_pg[pb], self.wa[ig], self.b_wa[ig], j * 128, NCH, hb, c0, c1, [self.b_hb])
                    self.mm(self.pu[pb], self.b_pu[pb], self.wa[iu], self.b_wa[iu], j * 128, NCH, hb, c0, c1, [self.b_hb])
                    sg, pg, pu = self.sg[pb], self.pg[pb], self.pu[pb]
                    ai = self.at_i
                    self.at_i ^= 1
                    at = self.at[ai]
                    P.op("act", lambda e, sg=sg, pg=pg, w=w: e.activation(out=sg[:, :w], in_=pg[:, :w], func=AF.Silu),
                         reads=[self.b_pg[pb]], writes=[self.b_sg[pb]])
                    P.op("dve", lambda e, sg=sg, pu=pu, at=at, w=w: e.tensor_tensor(out=at[:, :w], in0=sg[:, :w], in1=pu[:, :w],
                                                                                   op=ALU.mult),
                         reads=[self.b_sg[pb], self.b_pu[pb]], writes=[self.b_at[ai]])
                    P.op("sp", lambda e, at=at, bi=bi, fc=fc, w=w: e.dma_start(out=A[bi, :, fc, :w], in_=at[:, :w]),
                         reads=[self.b_at[ai]], cowrites=[self.b_A[bi]], dma="sta")
        for bi, (c0, c1) in enumerate(self.blocks):
            w = c1 - c0
            P.op("sp", lambda e, bi=bi, w=w: e.dma_start(out=ab[:, :, :w], in_=A[bi, :, :, :w]),
                 reads=[self.b_A[bi]], writes=self.W_HB, dma="lda")
            if res_ap is None:
                self.load_xn(self.xs[bi, :, :, :w], self.b_xs[bi], w)
            else:
                self.load_xn(chunked(res_ap)[:, :, c0:c1], res_buf, w)
            nxt = self.load_wb(wd, 0)
            for dc in range(NCH):
                iw = nxt
                if dc + 1 < NCH:
                    nxt = self.load_wb(wd, dc + 1)
                pb = dc % 2
                self.mm(self.py[pb], self.b_py[pb], self.wb[iw], self.b_wb[iw], 0, FCH, ab, 0, w, self.b_abc)
                self.residual_out(dst_ap, dst_bufs[bi], dc, c0, c1, self.py[pb], self.b_py[pb], vgate,
                                  self.xn[:, dc, :w], self.b_xn)
            self.b_A[bi].w = []

    def proj_residual_all(self, W, dst_ap, dst_bufs, vgate):
        nxt = self.load_wa(W, 0, 256)
        for g in range(8):
            iw = nxt
            if g + 1 < 8:
                nxt = self.load_wa(W, g + 1, 256)
            for bi, (c0, c1) in enumerate(self.blocks):
                w = c1 - c0
                for j in range(2):
                    dc = 2 * g + j
                    pb = self.pp_i
                    self.pp_i ^= 1
                    self.mm(self.py[pb], self.b_py[pb], self.wa[iw], self.b_wa[iw], j * 128, NCH, self.hb, c0, c1, [self.b_hb])
                    xi = self.xr_i
                    self.xr_i ^= 1
                    xr, xs = self.xr[xi], self.xs
                    self.P.op("sp", lambda e, xr=xr, bi=bi, dc=dc, w=w: e.dma_start(out=xr[:, :w], in_=xs[bi, :, dc, :w]),
                              reads=[self.b_xs[bi]], writes=[self.b_xr[xi]], dma=f"xr{xi}")
                    self.residual_out(dst_ap, dst_bufs[bi], dc, c0, c1, self.py[pb], self.b_py[pb], vgate,
                                      xr[:, :w], self.b_xr[xi])


def tile_w(W, gcols):
    K, N = W.shape
    return np.ascontiguousarray(W.reshape(K // 128, 128, N // gcols, gcols).transpose(2, 1, 0, 3))


def dram_in(nc, name, shape, dt=F32):
    return nc.dram_tensor(name, list(shape), dt, kind="ExternalInput").ap()


def dram_out(nc, name, shape, dt=F32):
    return nc.dram_tensor(name, list(shape), dt, kind="ExternalOutput").ap()


def ffn_w_in(nc, sfx):
    return (dram_in(nc, "wg" + sfx, [FCH // 2, 128, NCH, 256]), dram_in(nc, "wu" + sfx, [FCH // 2, 128, NCH, 256]),
            dram_in(nc, "wd" + sfx, [NCH, 128, FCH, 128]))


MODC = 2 * 9 * D // NCORE


def build_L0():
    nc = bass.Bass("TRN2", target_bir_lowering=False)
    cv_ap = dram_in(nc, "cv", [128, NCH, 3])
    w_ap = dram_in(nc, "w", [D, MODC])
    b_ap = dram_in(nc, "b", [3, MODC])
    o_ap = dram_out(nc, "mod", [3, MODC])
    P = Prog(nc)
    cv = P.sb("cv", [128, NCH, 3], F32)
    cb = P.sb("cb", [128, NCH, 3], BF16)
    bt = P.sb("bt", [3, MODC], F32)
    ot = P.sb("ot", [3, MODC], F32)
    wt = [P.sb(f"w{i}", [128, NCH, 512], BF16) for i in range(2)]
    ps = [P.ps(f"ps{i}", [128, 512]) for i in range(2)]
    b_cv, b_cb, b_bt, b_ot = Buf(), Buf(), Buf(), Buf()
    b_w, b_ps = [Buf(), Buf()], [Buf(), Buf()]
    P.op("sp", lambda e: e.dma_start(out=cv[:], in_=cv_ap), writes=[b_cv], dma="cv")
    P.op("sp", lambda e: e.dma_start(out=bt[:], in_=b_ap), writes=[b_bt], dma="bt")
    P.op("act", lambda e: e.activation(out=cb[:], in_=cv[:], func=AF.Silu), reads=[b_cv], writes=[b_cb])
    wv = w_ap.rearrange("(k p) f -> p k f", p=128)
    for t in range(MODC // 512):
        i = t % 2
        P.op("pool", lambda e, t=t, i=i: e.dma_start(out=wt[i][:], in_=wv[:, :, t * 512:(t + 1) * 512]),
             writes=[b_w[i]], dma=f"w{i}")

        def f(e, i=i):
            for k in range(NCH):
                ins = e.matmul(ps[i][0:3, :], lhsT=cb[:, k, :], rhs=wt[i][:, k, :], start=(k == 0), stop=(k == NCH - 1))
            return ins
        P.op("pe", f, reads=[b_cb, b_w[i]], writes=[b_ps[i]])
        P.op("dve", lambda e, t=t, i=i: e.tensor_tensor(out=ot[:, t * 512:(t + 1) * 512], in0=ps[i][0:3, :],
                                                        in1=bt[:, t * 512:(t + 1) * 512], op=ALU.add),
             reads=[b_ps[i], b_bt], writes=[b_ot])
    P.op("sp", lambda e: e.dma_start(out=o_ap, in_=ot[:]), reads=[b_ot], writes=[Buf()], dma="st")
    P.finish()
    P.emit()
    return nc


def simple_proj(R, W, ng, dst, b_o):
    P = R.P
    nxt = R.load_wa(W, 0, 256)
    for g in range(ng):
        iw = nxt
        if g + 1 < ng:
            nxt = R.load_wa(W, g + 1, 256)
        for bi, (c0, c1) in enumerate(R.blocks):
            w = c1 - c0
            for j in range(2):
                pb = R.pp_i
                R.pp_i ^= 1
                R.mm(R.pg[pb], R.b_pg[pb], R.wa[iw], R.b_wa[iw], j * 128, NCH, R.hb, c0, c1, [R.b_hb])
                i = R.zt_i
                R.zt_i ^= 1
                zt, pg = R.zt[i], R.pg[pb]
                P.op("act", lambda e, pg=pg, zt=zt, w=w: e.activation(out=zt[:, :w], in_=pg[:, :w], func=AF.Copy),
                     reads=[R.b_pg[pb]], writes=[R.b_zt[i]])
                R.store(dst, b_o, (2 * g + j) * 128, c0, c1, zt, R.b_zt[i])


def build_LA(T, segs):
    nc = bass.Bass("TRN2", target_bir_lowering=False)
    nseg = len(segs)
    nvec = 5 * nseg + 2
    x_ap = dram_in(nc, "xT", [D, T])
    vecs_ap = dram_in(nc, "vecs", [128, nvec, NCH])
    wg, wu, wd = ffn_w_in(nc, "")
    w_gb = dram_in(nc, "w_gb", [4, 128, NCH, 256])
    w_gc = dram_in(nc, "w_gc", [8, 128, NCH, 128])
    w_xi = dram_in(nc, "w_xi", [8, 128, NCH, 128])
    w_uf = dram_in(nc, "w_uf", [4, 128, NCH, 256])
    z1 = dram_out(nc, "z1", [D, T])
    gb = dram_out(nc, "gb", [1024, T])
    vv = dram_out(nc, "vv", [1024, T])
    uf = dram_out(nc, "uf", [1024, T])
    P = Prog(nc)
    R = RL(P, T, segs, vecs_ap, nvec)
    V = lambda s, k: 5 * s + k
    SL = lambda k: [V(s, k) for s in range(nseg)]
    VG, VB = 5 * nseg, 5 * nseg + 1
    for s in range(nseg):
        R.derive(V(s, 1), add=1.0)
        R.derive(V(s, 2), mul=0.5)
        R.derive(V(s, 4), add=1.0)
    b_x = Buf()
    b_z1 = [Buf() for _ in R.blocks]
    b_o = Buf()
    R.prologue_all(x_ap, b_x, ln=None, mod=(SL(0), SL(1)), save=False)
    R.ffn_all(wg, wu, wd, z1, b_z1, SL(2), res_ap=x_ap, res_buf=b_x)
    R.prologue_all(z1, b_z1, ln=(VG, VB), mod=(SL(3), SL(4)), save=False)
    simple_proj(R, w_gb, 4, gb, b_o)
    ld = lambda jj: (R.load_wa(w_gc, jj, 128), R.load_wa(w_xi, jj, 128))
    nxt = ld(0)
    for jj in range(8):
        ic, ix = nxt
        if jj + 1 < 8:
            nxt = ld(jj + 1)
        for bi, (c0, c1) in enumerate(R.blocks):
            w = c1 - c0
            pb = R.pp_i
            R.pp_i ^= 1
            R.mm(R.pg[pb], R.b_pg[pb], R.wa[ic], R.b_wa[ic], 0, NCH, R.hb, c0, c1, [R.b_hb])
            R.mm(R.pu[pb], R.b_pu[pb], R.wa[ix], R.b_wa[ix], 0, NCH, R.hb, c0, c1, [R.b_hb])
            i = R.zt_i
            R.zt_i ^= 1
            zt, sg, pg, pu = R.zt[i], R.sg[pb], R.pg[pb], R.pu[pb]
            P.op("act", lambda e, pg=pg, sg=sg, w=w: e.activation(out=sg[:, :w], in_=pg[:, :w], func=AF.Copy),
                 reads=[R.b_pg[pb]], writes=[R.b_sg[pb]])
            P.op("dve", lambda e, pu=pu, sg=sg, zt=zt, w=w: e.tensor_tensor(out=zt[:, :w], in0=pu[:, :w], in1=sg[:, :w],
                                                                           op=ALU.mult),
                 reads=[R.b_pu[pb], R.b_sg[pb]], writes=[R.b_zt[i]])
            R.store(vv, b_o, jj * 128, c0, c1, zt, R.b_zt[i])
    simple_proj(R, w_uf, 4, uf, b_o)
    P.finish()
    P.emit()
    return nc


def build_LF():
    nc = bass.Bass("TRN2", target_bir_lowering=False)
    xl = dram_in(nc, "xl", [2, SEQ, 128])
    xc = dram_in(nc, "xc", [2, 128, CTX])
    ftw_ap = dram_in(nc, "ftw", [128, 64, 256])
    fc_ap = dram_in(nc, "fc", [128, 2, 256])
    c64_ap = dram_in(nc, "c64", [64, 2, 64])
    c256_ap = dram_in(nc, "c256", [128, 4, 256])
    yl = dram_out(nc, "yl", [2, 128, SEQ])
    yc = dram_out(nc, "yc", [2, 128, CTX])
    P = Prog(nc)
    ftw = P.sb("ftw", [128, 64, 256], BF16)
    fc = P.sb("fc", [128, 2, 256], BF16)
    c64 = P.sb("c64", [64, 2, 64], BF16)
    c256 = P.sb("c256", [128, 4, 256], BF16)
    XA = P.sb("XA", [128, 64, 128], BF16)
    D1 = P.sb("D1", [128, 64, 256], BF16)
    D2 = P.sb("D2", [64, 128, 256], BF16)
    Y = P.sb("Y", [128, SEQ], F32)
    XC = P.sb("XC", [128, CTX], BF16)
    DA = P.sb("DA", [128, 2, 256], BF16)
    YC = P.sb("YC", [128, CTX], F32)
    pp = [P.ps(f"pp{i}", [128, 512]) for i in range(4)]
    b_pp = [Buf() for _ in range(4)]
    b_t, b_XA, b_D1, b_D2, b_Y, b_XC, b_DA, b_YC = (Buf() for _ in range(8))
    for t, ap, k in ((ftw, ftw_ap, "t0"), (fc, fc_ap, "t1"), (c64, c64_ap, "t2"), (c256, c256_ap, "t3")):
        P.op("pool", lambda e, t=t, ap=ap: e.dma_start(out=t[:], in_=ap), cowrites=[b_t], dma=k)
    pi = [0]

    def nextp():
        pi[0] = (pi[0] + 1) % 4
        return pi[0]
    ev = [0]

    def evac(out_ap, in_ap, rd, wr, co=False):
        ev[0] ^= 1
        kw = dict(cowrites=[wr]) if co else dict(writes=[wr])
        if ev[0]:
            P.op("act", lambda e: e.activation(out=out_ap, in_=in_ap, func=AF.Copy), reads=[rd], **kw)
        else:
            P.op("dve", lambda e: e.tensor_copy(out=out_ap, in_=in_ap), reads=[rd], **kw)

    for u in range(2):
        src = xl[u].rearrange("(a r) c -> a r c", r=64)
        P.op("pool", lambda e, src=src: e.dma_start(out=XA[:], in_=src), writes=[b_XA], dma="xa")
        P.op("pool", lambda e, u=u: e.dma_start(out=XC[:], in_=xc[u]), writes=[b_XC], dma="xc")
        for b0 in range(0, 64, 2):
            i = nextp()

            def f(e, b0=b0, i=i):
                for q in range(2):
                    ins = e.matmul(pp[i][:, q * 256:(q + 1) * 256], lhsT=XA[:, b0 + q, :], rhs=ftw[:, b0 + q, :],
                                   start=True, stop=True)
                return ins
            P.op("pe", f, reads=[b_XA, b_t], writes=[b_pp[i]])
            evac(D1[:, b0:b0 + 2, :], pp[i][:].rearrange("p (q n) -> p q n", q=2), b_pp[i], b_D1, co=True)
        for a0 in range(0, 128, 2):
            i = nextp()

            def f(e, a0=a0, i=i):
                for q in range(2):
                    e.matmul(pp[i][0:64, q * 256:(q + 1) * 256], lhsT=D1[:, :, a0 + q], rhs=fc[:, 0, :], start=True, stop=False)
                    ins = e.matmul(pp[i][0:64, q * 256:(q + 1) * 256], lhsT=D1[:, :, 128 + a0 + q], rhs=fc[:, 1, :],
                                   start=False, stop=True)
                return ins
            P.op("pe", f, reads=[b_D1, b_t], writes=[b_pp[i]])
            evac(D2[:, a0:a0 + 2, :], pp[i][0:64, :].rearrange("p (q n) -> p q n", q=2), b_pp[i], b_D2, co=True)
        Yv = Y[:].rearrange("p (b a) -> p a b", a=128)
        for a0 in range(0, 128, 8):
            i = nextp()

            def f(e, a0=a0, i=i):
                for q in range(8):
                    e.matmul(pp[i][:, q * 64:(q + 1) * 64], lhsT=D2[:, a0 + q, 0:128], rhs=c64[:, 0, :], start=True, stop=False)
                    ins = e.matmul(pp[i][:, q * 64:(q + 1) * 64], lhsT=D2[:, a0 + q, 128:256], rhs=c64[:, 1, :],
                                   start=False, stop=True)
                return ins
            P.op("pe", f, reads=[b_D2, b_t], writes=[b_pp[i]])
            evac(Yv[:, a0:a0 + 8, :], pp[i][:].rearrange("p (a b) -> p a b", a=8), b_pp[i], b_Y, co=True)
        P.op("sp", lambda e, u=u: e.dma_start(out=yl[u], in_=Y[:]), reads=[b_Y], writes=[Buf()], dma="sty")
        i = nextp()

        def f(e, i=i):
            for j in range(2):
                ins = e.matmul(pp[i][:, j * 256:(j + 1) * 256], lhsT=XC[:, j * 128:(j + 1) * 128], rhs=fc[:, 0, :],
                               start=True, stop=True)
            return ins
        P.op("pe", f, reads=[b_XC, b_t], writes=[b_pp[i]])
        evac(DA[:], pp[i][:].rearrange("p (j n) -> p j n", j=2), b_pp[i], b_DA)
        i = nextp()

        def f(e, i=i):
            n = 0
            for j in range(2):
                for ri in range(2):
                    ins = e.matmul(pp[i][:, 0:256], lhsT=DA[:, j, ri * 128:(ri + 1) * 128], rhs=c256[:, 2 * ri + j, :],
                                   start=(n == 0), stop=(n == 3))
                    n += 1
            return ins
        P.op("pe", f, reads=[b_DA, b_t], writes=[b_pp[i]])
        evac(YC[:], pp[i][:, 0:256], b_pp[i], b_YC)
        P.op("sp", lambda e, u=u: e.dma_start(out=yc[u], in_=YC[:]), reads=[b_YC], writes=[Buf()], dma="styc")
    P.finish()
    P.emit()
    return nc


def fft_tables():
    a = np.arange(128)[:, None, None]
    b = np.arange(64)[None, :, None]
    ap = np.arange(128)[None, None, :]
    th = 2 * np.pi * (a * ap / 128.0 + b * ap / 8192.0)
    ftw = np.concatenate([np.cos(th), -np.sin(th)], axis=-1) / np.sqrt(128.0)
    c = np.arange(128)[:, None]
    cp = np.arange(128)[None, :]
    th = 2 * np.pi * c * cp / 128.0
    cr, ci = np.cos(th) / np.sqrt(128.0), -np.sin(th) / np.sqrt(128.0)
    fc = np.stack([np.concatenate([cr, ci], 1), np.concatenate([-ci, cr], 1)], axis=1)
    bb = np.arange(64)[:, None]
    bp = np.arange(64)[None, :]
    th = 2 * np.pi * bb * bp / 64.0
    c64 = np.stack([np.cos(th), np.sin(th)], axis=1) / 8.0
    l = np.arange(256)[:, None]
    lp = np.arange(256)[None, :]
    th = 2 * np.pi * l * lp / 256.0
    C, S = np.cos(th) / 16.0, np.sin(th) / 16.0
    c256 = np.stack([C[0:128], C[128:256], S[0:128], S[128:256]], axis=1)
    f = lambda x: np.ascontiguousarray(x, dtype=np.float32)
    return f(ftw), f(fc), f(c64), f(c256)


NBLK = SEQ // 128


def build_LC(nblk=NBLK):
    nc = bass.Bass("TRN2", target_bir_lowering=False)
    S = nblk * 128
    NS = 4
    qt_ap = dram_in(nc, "qt", [128, 4, S])
    kt_ap = dram_in(nc, "kt", [128, S + CTX])
    v_ap = dram_in(nc, "v", [128, nblk + 2, 64])
    mask_ap = dram_in(nc, "mask", [128, 384])
    sink_ap = dram_in(nc, "sink", [128, 8])
    id_ap = dram_in(nc, "ident", [128, 128])
    o_ap = dram_out(nc, "o", [S, 512])
    P = Prog(nc)
    QT = P.sb("QT", [128, 4, S], BF16)
    KT = P.sb("KT", [128, S + CTX], BF16)
    V = P.sb("V", [128, nblk + 2, 64], BF16)
    mask = P.sb("mask", [128, 384], F32)
    sink = P.sb("sink", [128, 8], F32)
    sink8 = P.sb("sink8", [128, 8], F32)
    ident = P.sb("ident", [128, 128], BF16)
    sc = [P.sb(f"sc{i}", [128, 648], F32) for i in range(NS)]
    pb = [P.sb(f"pb{i}", [128, 648], BF16) for i in range(NS)]
    pTs = [P.sb(f"pTs{i}", [128, 5, 128], BF16) for i in range(NS)]
    sm = [P.sb(f"sm{i}", [128, 4], F32) for i in range(NS)]
    ot = [P.sb(f"ot{i}", [128, 512], F32) for i in range(2)]
    scA = [P.ps(f"scA{i}", [128, 512]) for i in range(2)]
    scB = [P.ps(f"scB{i}", [128, 512]) for i in range(2)]
    pTt = [P.ps(f"pTt{i}", [128, 8, 128], BF16) for i in range(2)]
    ops = [P.ps(f"ops{i}", [128, 512]) for i in range(2)]
    BL = lambda n: [Buf() for _ in range(n)]
    b_c, b_s8 = Buf(), Buf()
    b_sc, b_pb, b_pTs, b_sm = BL(NS), BL(NS), BL(NS), BL(NS)
    b_ot, b_scA, b_scB, b_pTt, b_ops = BL(2), BL(2), BL(2), BL(2), BL(2)
    for t, ap, k in ((QT, qt_ap, "c0"), (KT, kt_ap, "c1"), (V, v_ap, "c2"), (ident, id_ap, "c3")):
        P.op("pool", lambda e, t=t, ap=ap: e.dma_start(out=t[:], in_=ap), cowrites=[b_c], dma=k)
    for t, ap, k in ((mask, mask_ap, "c4"), (sink, sink_ap, "c5")):
        P.op("sp", lambda e, t=t, ap=ap: e.dma_start(out=t[:], in_=ap), cowrites=[b_c], dma=k)
    P.op("dve", lambda e: e.tensor_scalar_mul(out=sink8[:], in0=sink[:], scalar1=8.0), reads=[b_c], writes=[b_s8])
    units = [(i, r) for i in range(nblk) for r in range(8)]
    N = len(units)

    def geo(n):
        i, r = units[n]
        kb0, kb1 = max(i - 1, 0), min(i + 1, nblk - 1)
        nloc = kb1 - kb0 + 1
        nk = nloc * 128
        mo = (kb0 - (i - 1)) * 128
        return i, r, kb0, nloc, nk, mo, nk + 256

    def S1(n):
        i, r, kb0, nloc, nk, mo, L = geo(n)
        s, p = n % NS, n % 2
        hh, j = r // 4, r % 4
        p0, p1 = hh * 64, hh * 64 + 64
        q = QT[p0:p1, j, i * 128:(i + 1) * 128]

        def f(e):
            e.matmul(scA[p][:, 0:nk], lhsT=q, rhs=KT[p0:p1, kb0 * 128:kb0 * 128 + nk], start=True, stop=True)
            return e.matmul(scB[p][:, 0:256], lhsT=q, rhs=KT[p0:p1, S:S + CTX], start=True, stop=True)
        P.op("pe", f, reads=[b_c], writes=[b_scA[p], b_scB[p]])
        P.op("dve", lambda e: e.tensor_tensor(out=sc[s][:, 0:nk], in0=scA[p][:, 0:nk], in1=mask[:, mo:mo + nk], op=ALU.add),
             reads=[b_scA[p], b_c], writes=[b_sc[s]])
        P.op("act", lambda e: e.activation(out=sc[s][:, nk:L], in_=scB[p][:, 0:256], func=AF.Copy),
             reads=[b_scB[p]], cowrites=[b_sc[s]])
        P.op("pool", lambda e: e.tensor_copy(out=sc[s][:, L:L + 1], in_=sink8[:, r:r + 1]),
             reads=[b_s8], cowrites=[b_sc[s]])
        m = sm[s]
        P.op("dve", lambda e: e.reduce_max(out=m[:, 0:1], in_=sc[s][:, 0:L + 1], axis=AX.X),
             reads=[b_sc[s]], writes=[b_sm[s]])
        P.op("dve", lambda e: e.tensor_scalar_mul(out=m[:, 1:2], in0=m[:, 0:1], scalar1=-0.125),
             reads=[b_sm[s]], writes=[b_sm[s]])

    def S2(n):
        i, r, kb0, nloc, nk, mo, L = geo(n)
        s, p = n % NS, n % 2
        nt = nloc + 2
        m = sm[s]
        P.op("act", lambda e: e.activation(out=pb[s][:, 0:L + 1], in_=sc[s][:, 0:L + 1], func=AF.Exp,
                                           scale=0.125, bias=m[:, 1:2], accum_out=m[:, 2:3]),
             reads=[b_sc[s], b_sm[s]], writes=[b_pb[s], b_sm[s]])
        P.op("dve", lambda e: e.reciprocal(out=m[:, 3:4], in_=m[:, 2:3]), reads=[b_sm[s]], writes=[b_sm[s]])

        def f(e):
            for t in range(nt):
                ins = e.transpose(out=pTt[p][:, t, :], in_=pb[s][:, t * 128:(t + 1) * 128], identity=ident[:])
            return ins
        P.op("pe", f, reads=[b_pb[s], b_c], writes=[b_pTt[p]])
        if n % 2:
            P.op("act", lambda e: e.activation(out=pTs[s][:, 0:nt, :], in_=pTt[p][:, 0:nt, :], func=AF.Copy),
                 reads=[b_pTt[p]], writes=[b_pTs[s]])
        else:
            P.op("dve", lambda e: e.tensor_copy(out=pTs[s][:, 0:nt, :], in_=pTt[p][:, 0:nt, :]),
                 reads=[b_pTt[p]], writes=[b_pTs[s]])

    def S3(n):
        i, r, kb0, nloc, nk, mo, L = geo(n)
        s, p = n % NS, n % 2
        nt = nloc + 2
        oi = i % 2
        m = sm[s]

        def f(e):
            for t in range(nt):
                vb = kb0 + t if t < nloc else nblk + (t - nloc)
                ins = e.matmul(ops[p][:, 0:64], lhsT=pTs[s][:, t, :], rhs=V[:, vb, :], start=(t == 0), stop=(t == nt - 1))
            return ins
        P.op("pe", f, reads=[b_pTs[s], b_c], writes=[b_ops[p]])
        P.op("act", lambda e: e.activation(out=ot[oi][:, r * 64:(r + 1) * 64], in_=ops[p][:, 0:64], func=AF.Copy,
                                           scale=m[:, 3:4]),
             reads=[b_ops[p], b_sm[s]], cowrites=[b_ot[oi]])
        if r == 7:
            P.op("sp", lambda e: e.dma_start(out=o_ap[i * 128:(i + 1) * 128, :], in_=ot[oi][:]),
                 reads=[b_ot[oi]], writes=[Buf()], dma=f"so{oi}")
            b_ot[oi].w = []

    for k in range(N + 2):
        if k < N:
            S1(k)
        if 0 <= k - 1 < N:
            S2(k - 1)
        if 0 <= k - 2 < N:
            S3(k - 2)
    P.finish()
    P.emit()
    return nc


def attn_mask():
    a = np.arange(128)[:, None]
    j = np.arange(384)[None, :]
    return np.where(np.abs(j - 128 - a) <= 128, 0.0, -1e30).astype(np.float32)


def build_LB(T, segs):
    nc = bass.Bass("TRN2", target_bir_lowering=False)
    nseg = len(segs)
    nvec = 9 * nseg + 11
    z1 = dram_in(nc, "z1", [D, T])
    gb = dram_in(nc, "gb", [1024, T])
    vp = dram_in(nc, "vp", [1024, T])
    vv = dram_in(nc, "vv", [1024, T])
    vn = dram_in(nc, "vn", [1024, T])
    yb = dram_in(nc, "yb", [1024, T])
    vecs_ap = dram_in(nc, "vecs", [128, nvec, NCH])
    wout = dram_in(nc, "wout", [8, 128, NCH, 256])
    wg2, wu2, wd2 = ffn_w_in(nc, "2")
    wg3, wu3, wd3 = ffn_w_in(nc, "3")
    wqkv = dram_in(nc, "wqkv", [10, 128, NCH, 256])
    wperm = dram_in(nc, "wperm", [9, 128, NCH, 256])
    cos_ap = dram_in(nc, "cosT", [128, T])
    sin_ap = dram_in(nc, "sinT", [128, T])
    z4 = dram_out(nc, "z4", [D, T])
    qr = dram_out(nc, "qr", [D, T])
    kr = dram_out(nc, "kr", [256, T])
    vo = dram_out(nc, "vo", [256, T])
    z2 = nc.dram_tensor("z2", [D, T], F32).ap()
    z3 = nc.dram_tensor("z3", [D, T], F32).ap()
    P = Prog(nc)
    R = RL(P, T, segs, vecs_ap, nvec)
    cosT = P.sb("cosT", [128, T], F32)
    sinT = P.sb("sinT", [128, T], F32)
    b_rope = Buf()
    P.op("sp", lambda e: e.dma_start(out=cosT[:], in_=cos_ap), cowrites=[b_rope], dma="r0")
    P.op("sp", lambda e: e.dma_start(out=sinT[:], in_=sin_ap), cowrites=[b_rope], dma="r1")
    V = lambda s, k: 9 * s + k
    L0 = 9 * nseg
    for s in range(nseg):
        R.derive(V(s, 2), add=1.0)
        R.derive(V(s, 3), mul=0.5)
        R.derive(V(s, 5), add=1.0)
        R.derive(V(s, 6), mul=0.5)
        R.derive(V(s, 8), add=1.0)
    SL = lambda k: [V(s, k) for s in range(nseg)]
    b_in = Buf()
    nb = len(R.blocks)
    b_z2 = [Buf() for _ in range(nb)]
    b_z3 = [Buf() for _ in range(nb)]
    b_z4 = [Buf() for _ in range(nb)]
    b_o = Buf()

    def mix_fill(bi, c0, c1, w):
        tl = [R.sg[0], R.sg[1], R.zt[0], R.zt[1], R.tmp[0]]
        tb = [R.b_sg[0], R.b_sg[1], R.b_zt[0], R.b_zt[1], R.b_tmp[0]]
        for j in range(8):
            for k, src in enumerate((gb, vp, vv, vn, yb)):
                P.op("sp", lambda e, k=k, src=src, j=j: e.dma_start(out=tl[k][:, :w], in_=src[j * 128:(j + 1) * 128, c0:c1]),
                     writes=[tb[k]], dma=f"m{k}")
            P.op("dve", lambda e, j=j: e.tensor_scalar_mul(out=tl[1][:, :w], in0=tl[1][:, :w], scalar1=R.vc(L0 + 8, j)),
                 reads=[R.b_vecs], writes=[tb[1]])
            P.op("dve", lambda e, j=j: e.scalar_tensor_tensor(out=tl[1][:, :w], in0=tl[2][:, :w], scalar=R.vc(L0 + 9, j),
                                                              in1=tl[1][:, :w], op0=ALU.mult, op1=ALU.add),
                 reads=[R.b_vecs, tb[2]], writes=[tb[1]])
            P.op("dve", lambda e, j=j: e.scalar_tensor_tensor(out=tl[1][:, :w], in0=tl[3][:, :w], scalar=R.vc(L0 + 10, j),
                                                              in1=tl[1][:, :w], op0=ALU.mult, op1=ALU.add),
                 reads=[R.b_vecs, tb[3]], writes=[tb[1]])
            P.op("dve", lambda e, j=j: e.tensor_tensor(out=R.hb[:, j, c0:c1], in0=tl[1][:, :w], in1=tl[0][:, :w], op=ALU.mult),
                 reads=[tb[0], tb[1]], cowrites=R.W_HB)
            P.op("act", lambda e, j=j: e.activation(out=R.hb[:, 8 + j, c0:c1], in_=tl[4][:, :w], func=AF.Copy),
                 reads=[tb[4]], cowrites=R.W_HB)

    R.prologue_all(z1, b_in, ln=(L0 + 0, L0 + 1), mod=None, save=True, fill=mix_fill)
    R.proj_residual_all(wout, z2, b_z2, SL(0))
    R.prologue_all(z2, b_z2, ln=(L0 + 2, L0 + 3), mod=(SL(1), SL(2)), save=True)
    R.ffn_all(wg2, wu2, wd2, z3, b_z3, SL(3))
    R.prologue_all(z3, b_z3, ln=(L0 + 4, L0 + 5), mod=(SL(4), SL(5)), save=True)
    R.ffn_all(wg3, wu3, wd3, z4, b_z4, SL(6))
    R.prologue_all(z4, b_z4, ln=(L0 + 6, L0 + 7), mod=(SL(7), SL(8)), save=False)
    ld = lambda g: (R.load_wa(wqkv, g, 256), R.load_wa(wperm, g, 256) if g < 9 else None)
    nxt = ld(0)
    for g in range(10):
        iw, ip = nxt
        if g + 1 < 10:
            nxt = ld(g + 1)
        for bi, (c0, c1) in enumerate(R.blocks):
            w = c1 - c0
            for j in range(2):
                pb = R.pp_i
                R.pp_i ^= 1
                R.mm(R.pg[pb], R.b_pg[pb], R.wa[iw], R.b_wa[iw], j * 128, NCH, R.hb, c0, c1, [R.b_hb])
                i = R.zt_i
                R.zt_i ^= 1
                zt, tmp, sg, pg, pu = R.zt[i], R.tmp[i], R.sg[pb], R.pg[pb], R.pu[pb]
                if g < 9:
                    R.mm(R.pu[pb], R.b_pu[pb], R.wa[ip], R.b_wa[ip], j * 128, NCH, R.hb, c0, c1, [R.b_hb])
                    P.op("dve", lambda e, sg=sg, pg=pg, w=w, c0=c0, c1=c1: e.tensor_tensor(
                        out=sg[:, :w], in0=pg[:, :w], in1=cosT[:, c0:c1], op=ALU.mult),
                        reads=[R.b_pg[pb], b_rope], writes=[R.b_sg[pb]])
                    P.op("dve", lambda e, tmp=tmp, pu=pu, w=w, c0=c0, c1=c1: e.tensor_tensor(
                        out=tmp[:, :w], in0=pu[:, :w], in1=sinT[:, c0:c1], op=ALU.mult),
                        reads=[R.b_pu[pb], b_rope], writes=[R.b_tmp[i]])
                    P.op("dve", lambda e, zt=zt, sg=sg, tmp=tmp, w=w: e.tensor_tensor(
                        out=zt[:, :w], in0=sg[:, :w], in1=tmp[:, :w], op=ALU.add),
                        reads=[R.b_sg[pb], R.b_tmp[i]], writes=[R.b_zt[i]])
                    dst, row = (qr, (2 * g + j) * 128) if g < 8 else (kr, j * 128)
                else:
                    P.op("act", lambda e, zt=zt, pg=pg, w=w: e.activation(out=zt[:, :w], in_=pg[:, :w], func=AF.Copy),
                         reads=[R.b_pg[pb]], writes=[R.b_zt[i]])
                    dst, row = vo, j * 128
                R.store(dst, b_o, row, c0, c1, zt, R.b_zt[i])
    P.finish()
    P.emit()
    return nc


def build_LD(T):
    nc = bass.Bass("TRN2", target_bir_lowering=False)
    segs = [(0, T)]
    nvec = 10
    z4 = dram_in(nc, "z4", [D, T])
    oT = dram_in(nc, "oT", [D, T])
    vecs_ap = dram_in(nc, "vecs", [128, nvec, NCH])
    wout = dram_in(nc, "wout", [8, 128, NCH, 256])
    wg, wu, wd = ffn_w_in(nc, "")
    out = dram_out(nc, "out", [D, T])
    z5 = nc.dram_tensor("z5", [D, T], F32).ap()
    z6 = nc.dram_tensor("z6", [D, T], F32).ap()
    P = Prog(nc)
    R = RL(P, T, segs, vecs_ap, nvec)
    R.derive(2, add=1.0)
    R.derive(3, mul=0.5)
    nb = len(R.blocks)
    b_in = Buf()
    b_z5 = [Buf() for _ in range(nb)]
    b_z6 = [Buf() for _ in range(nb)]

    def o_fill(bi, c0, c1, w):
        P.op("pool", lambda e: e.dma_start(out=R.hb[:, :, c0:c1], in_=chunked(oT)[:, :, c0:c1]),
             cowrites=R.W_HB, dma="oin")

    def out_fill(bi, c0, c1, w):
        P.op("sp", lambda e: e.dma_start(out=chunked(out)[:, :, c0:c1], in_=R.xn[:, :, :w]),
             reads=[R.b_xn], writes=[Buf()], dma="fin")

    R.prologue_all(z4, b_in, ln=(4, 5), mod=None, save=True, fill=o_fill)
    R.proj_residual_all(wout, z5, b_z5, [0])
    R.prologue_all(z5, b_z5, ln=(6, 7), mod=([1], [2]), save=True)
    R.ffn_all(wg, wu, wd, z6, b_z6, [3])
    R.prologue_all(z6, b_z6, ln=(8, 9), mod=None, save=False, fill=out_fill)
    P.finish()
    P.emit()
    return nc


def lay(v):
    v = np.asarray(v, dtype=np.float32)
    if v.shape[0] < D:
        v = np.concatenate([v, np.zeros(D - v.shape[0], np.float32)])
    return v.reshape(NCH, 128).T


def rope_tables(pos):
    nf = 16
    inv = np.power(10000.0, -np.arange(nf, dtype=np.float64) / nf)
    row = (pos // 64).astype(np.float64)
    col = (pos % 64).astype(np.float64)
    d = np.arange(64)
    axis, part, f = d // 32, (d % 32) // 16, d % 16
    ang = np.where(axis[:, None] == 0, row[None, :], col[None, :]) * inv[f][:, None]
    c = np.cos(ang)
    s = np.sin(ang) * np.where(part == 0, -1.0, 1.0)[:, None]
    return np.concatenate([c, c], 0).astype(np.float32), np.concatenate([s, s], 0).astype(np.float32)


def rope_perm():
    d = np.arange(64)
    part = (d % 32) // 16
    p = np.where(part == 0, d + 16, d - 16)
    cols = np.concatenate([h * 64 + p for h in range(36)])
    return cols


_CACHE = {}


def _prog(key, fn):
    if key not in _CACHE:
        _CACHE[key] = fn()
    return _CACHE[key]


def _run(nc, ins):
    res = run_bass_kernel_spmd(nc, ins, core_ids=list(range(NCORE)))
    return res.results


def kernel(x, c, ctx, c_ctx, w_mod, b_mod, ln_g, ln_b, ffn_w_gate, ffn_w_up, ffn_w_down,
           ab_w_in, ab_conv, ab_w_out, attn_w_in, attn_sink, attn_w_out):
    f32 = lambda a: np.ascontiguousarray(np.asarray(a), dtype=np.float32)
    x, c, ctx, c_ctx = f32(x), f32(c), f32(ctx), f32(c_ctx)
    w_mod, b_mod, ln_g, ln_b = f32(w_mod), f32(b_mod), f32(ln_g), f32(ln_b)
    ffn_w_gate, ffn_w_up, ffn_w_down = f32(ffn_w_gate), f32(ffn_w_up), f32(ffn_w_down)
    ab_w_in, ab_conv, ab_w_out = f32(ab_w_in), f32(ab_conv), f32(ab_w_out)
    attn_w_in, attn_sink, attn_w_out = f32(attn_w_in), f32(attn_sink), f32(attn_w_out)
    LT, CT = SEQ // 4, CTX // 4
    T = LT + CT
    segs = [(0, LT), (LT, T)]
    cores = [(r // 4, r % 4) for r in range(NCORE)]

    cv = np.ascontiguousarray(np.stack([lay(c[0]), lay(c[1]), lay(c_ctx)], axis=-1))
    wm = np.concatenate([w_mod[0], w_mod[1]], axis=1)
    bm = b_mod.reshape(-1)
    ins = []
    for r in range(NCORE):
        sl = slice(r * MODC, (r + 1) * MODC)
        ins.append({"cv": cv, "w": np.ascontiguousarray(wm[:, sl]),
                    "b": np.ascontiguousarray(np.broadcast_to(bm[sl], (3, MODC)))})
    res = _run(_prog("L0", build_L0), ins)
    del wm
    mod = np.concatenate([res[r]["mod"] for r in range(NCORE)], axis=1).reshape(3, 2, 9, D)

    def mv(b, s, layer, k):
        return lay(mod[b if s == 0 else 2, layer, k])

    _ffw = {}

    def ffw(l, i, sfx):
        if (l, i) not in _ffw:
            _ffw[(l, i)] = (tile_w(ffn_w_gate[l, i], 256), tile_w(ffn_w_up[l, i], 256), tile_w(ffn_w_down[l, i], 128))
        a, b_, c_ = _ffw[(l, i)]
        return {"wg" + sfx: a, "wu" + sfx: b_, "wd" + sfx: c_}

    w_gb, w_gc = tile_w(ab_w_in[0][:, 0:1024], 256), tile_w(ab_w_in[0][:, 1024:2048], 128)
    w_xi, w_uf = tile_w(ab_w_in[0][:, 2048:3072], 128), tile_w(ab_w_in[0][:, 3072:4096], 256)

    ins = []
    for b, q in cores:
        xT = np.concatenate([x[b, q * LT:(q + 1) * LT].T, ctx[b, q * CT:(q + 1) * CT].T], axis=1)
        vecs = [mv(b, s, 0, k) for s in range(2) for k in range(5)] + [lay(ln_g[0, 0]), lay(ln_b[0, 0])]
        ins.append({"xT": np.ascontiguousarray(xT), "vecs": np.ascontiguousarray(np.stack(vecs, axis=1)),
                    **ffw(0, 0, ""), "w_gb": w_gb, "w_gc": w_gc, "w_xi": w_xi, "w_uf": w_uf})
    resA = _run(_prog("LA", lambda: build_LA(T, segs)), ins)

    def gather(res, name, nrow):
        lat = np.empty((2, nrow, SEQ), np.float32)
        cx = np.empty((2, nrow, CTX), np.float32)
        for r, (b, q) in enumerate(cores):
            a = res[r][name]
            lat[b, :, q * LT:(q + 1) * LT] = a[:, :LT]
            cx[b, :, q * CT:(q + 1) * CT] = a[:, LT:]
        return lat, cx

    UFl, UFc = gather(resA, "uf", 1024)
    VVl, VVc = gather(resA, "vv", 1024)

    ftw, fc, c64, c256 = fft_tables()
    ins = []
    for b, q in cores:
        gs = [2 * q, 2 * q + 1]
        xl = np.stack([UFl[b, g * 128:(g + 1) * 128].T for g in gs])
        xc = np.stack([UFc[b, g * 128:(g + 1) * 128] for g in gs])
        ins.append({"xl": np.ascontiguousarray(xl), "xc": np.ascontiguousarray(xc),
                    "ftw": ftw, "fc": fc, "c64": c64, "c256": c256})
    resF = _run(_prog("LF", build_LF), ins)
    YBl = np.empty((2, 1024, SEQ), np.float32)
    YBc = np.empty((2, 1024, CTX), np.float32)
    for r, (b, q) in enumerate(cores):
        for u in range(2):
            g = 2 * q + u
            YBl[b, g * 128:(g + 1) * 128] = resF[r]["yl"][u]
            YBc[b, g * 128:(g + 1) * 128] = resF[r]["yc"][u]
    del UFl, UFc

    def shift(a, k):
        o = np.zeros_like(a)
        if k > 0:
            o[..., k:] = a[..., :-k]
        else:
            o[..., :k] = a[..., -k:]
        return o

    VPl, VPc, VNl, VNc = shift(VVl, 1), shift(VVc, 1), shift(VVl, -1), shift(VVc, -1)

    def cols(lat, cx, b, q):
        return np.ascontiguousarray(np.concatenate([lat[b][:, q * LT:(q + 1) * LT], cx[b][:, q * CT:(q + 1) * CT]], axis=1))

    perm = rope_perm()
    wperm = tile_w(np.ascontiguousarray(attn_w_in[0][:, perm]), 256)
    wqkv_t = tile_w(attn_w_in[0], 256)
    wout_t = tile_w(ab_w_out[0], 256)
    _ffw.pop((0, 0), None)
    ins = []
    for r, (b, q) in enumerate(cores):
        vecs = []
        for s in range(2):
            vecs += [mv(b, s, 0, 5), mv(b, s, 0, 6), mv(b, s, 0, 7), mv(b, s, 0, 8),
                     mv(b, s, 1, 0), mv(b, s, 1, 1), mv(b, s, 1, 2), mv(b, s, 1, 3), mv(b, s, 1, 4)]
        vecs += [lay(ln_g[0, 0]), lay(ln_b[0, 0]), lay(ln_g[0, 1]), lay(ln_b[0, 1]), lay(ln_g[0, 2]), lay(ln_b[0, 2]),
                 lay(ln_g[1, 0]), lay(ln_b[1, 0]), lay(ab_conv[0, 0]), lay(ab_conv[0, 1]), lay(ab_conv[0, 2])]
        cl, sl_ = rope_tables(np.arange(q * LT, (q + 1) * LT))
        cosT = np.concatenate([cl, np.ones((128, CT), np.float32)], axis=1)
        sinT = np.concatenate([sl_, np.zeros((128, CT), np.float32)], axis=1)
        ins.append({"z1": resA[r]["z1"], "gb": resA[r]["gb"], "vp": cols(VPl, VPc, b, q), "vv": resA[r]["vv"],
                    "vn": cols(VNl, VNc, b, q), "yb": cols(YBl, YBc, b, q),
                    "vecs": np.ascontiguousarray(np.stack(vecs, axis=1)), "wout": wout_t,
                    **ffw(0, 1, "2"), **ffw(1, 0, "3"), "wqkv": wqkv_t, "wperm": wperm,
                    "cosT": np.ascontiguousarray(cosT), "sinT": np.ascontiguousarray(sinT)})
    resB = _run(_prog("LB", lambda: build_LB(T, segs)), ins)
    del resA, VPl, VNl, YBl, VVl
    Ql, _ = gather(resB, "qr", D)
    Kl, Kc = gather(resB, "kr", 256)
    Vl, Vc = gather(resB, "vo", 256)

    mask = attn_mask()
    ident = np.eye(128, dtype=np.float32)
    ins = []
    for r in range(NCORE):
        b, g = r // 4, r % 4
        qt = Ql[b, g * 512:(g + 1) * 512].reshape(2, 4, 64, SEQ).transpose(0, 2, 1, 3).reshape(128, 4, SEQ)
        k1 = np.concatenate([Kl[b, g * 64:(g + 1) * 64], Kc[b, g * 64:(g + 1) * 64]], axis=1)
        vl = Vl[b, g * 64:(g + 1) * 64].T.reshape(NBLK, 128, 64)
        vc_ = Vc[b, g * 64:(g + 1) * 64].T.reshape(2, 128, 64)
        vall = np.concatenate([vl, vc_], axis=0).transpose(1, 0, 2)
        ins.append({"qt": np.ascontiguousarray(qt), "kt": np.ascontiguousarray(np.concatenate([k1, k1], axis=0)),
                    "v": np.ascontiguousarray(vall), "mask": mask,
                    "sink": np.ascontiguousarray(np.broadcast_to(attn_sink[0, g * 8:(g + 1) * 8], (128, 8))),
                    "ident": ident})
    resC = _run(_prog("LC", build_LC), ins)
    del Ql
    O = np.empty((2, D, SEQ), np.float32)
    for r in range(NCORE):
        b, g = r // 4, r % 4
        O[b, g * 512:(g + 1) * 512] = resC[r]["o"].T
    del resC

    _ffw.clear()
    awout_t = tile_w(attn_w_out[0], 256)
    ins = []
    for r, (b, q) in enumerate(cores):
        vecs = [mv(b, 0, 1, 5), mv(b, 0, 1, 6), mv(b, 0, 1, 7), mv(b, 0, 1, 8),
                lay(ln_g[1, 0]), lay(ln_b[1, 0]), lay(ln_g[1, 1]), lay(ln_b[1, 1]), lay(ln_g[1, 2]), lay(ln_b[1, 2])]
        ins.append({"z4": np.ascontiguousarray(resB[r]["z4"][:, :LT]), "oT": np.ascontiguousarray(O[b][:, q * LT:(q + 1) * LT]),
                    "vecs": np.ascontiguousarray(np.stack(vecs, axis=1)), "wout": awout_t, **ffw(1, 1, "")})
    resD = _run(_prog("LD", lambda: build_LD(LT)), ins)
    out = np.empty((2, SEQ, D), np.float32)
    for r, (b, q) in enumerate(cores):
        out[b, q * LT:(q + 1) * LT] = resD[r]["out"].T
    return out
```

```python
import numpy as np
from contextlib import ExitStack
import concourse.bass as bass
import concourse.mybir as mybir
from concourse.bass_utils import run_bass_kernel_spmd

F32 = mybir.dt.float32
BF16 = mybir.dt.bfloat16
AF = mybir.ActivationFunctionType
ALU = mybir.AluOpType
AX = mybir.AxisListType

D = 2048
DFF = 5632
NCH = 16
FCH = 44
SEQ = 8192
CTX = 256
NCORE = 8
ALPHA = 4.0 ** 0.25
LN_EPS = 1e-5
TB = 512


class Buf:
    __slots__ = ("w", "r", "name")

    def __init__(self, name=""):
        self.w = []
        self.r = []
        self.name = name


class Ctr:
    LIMIT = 30000

    def __init__(self, P, name, step):
        self.P, self.name, self.step = P, name, step
        self.k = 0
        self.done = []
        self._new()

    def _new(self):
        self.sem = self.P.stack.enter_context(self.P.nc.semaphore(f"{self.name}_{self.k}"))
        self.k += 1
        self.val = 0

    def next(self):
        if self.val + self.step > self.LIMIT:
            self.done.append((self.sem, self.val))
            self._new()
        self.val += self.step
        return (self.sem, self.val)


class Eng:
    def __init__(self, P, name):
        self.name = name
        self.ops = []
        self.waited = {}
        self.ctr = Ctr(P, "e" + name, 1)


class Prog:
    def __init__(self, nc):
        self.nc = nc
        self.stack = ExitStack()
        self.engs = {n: Eng(self, n) for n in ("pe", "act", "dve", "pool", "sp")}
        self.dctr = {}
        self.fuzzy = {}
        self.n = 0

    def sb(self, name, shape, dt):
        return self.stack.enter_context(self.nc.sbuf_tensor("s_" + name, list(shape), dt))

    def ps(self, name, shape, dt=F32):
        return self.stack.enter_context(self.nc.psum_tensor("p_" + name, list(shape), dt))

    def op(self, eng, fn, reads=(), writes=(), dma=None, cowrites=()):
        E = self.engs[eng]
        deps = {}

        def add(tok):
            s, v = tok
            k = id(s)
            if k in self.fuzzy:
                v = max(v, self.fuzzy[k].val if self.fuzzy[k].sem is s else v)
            if k not in deps or deps[k][1] < v:
                deps[k] = (s, v)

        for b in reads:
            for t in b.w:
                add(t)
        for b in writes:
            for t in b.w:
                add(t)
            for t in b.r:
                add(t)
        for b in cowrites:
            for t in b.r:
                add(t)
        if dma is not None:
            if dma not in self.dctr:
                self.dctr[dma] = Ctr(self, "d" + dma, 16)
            ctr = self.dctr[dma]
            if dma.startswith("st") or dma.startswith("xo"):
                self.fuzzy[id(ctr.sem)] = ctr
        else:
            ctr = E.ctr
        waits = []
        for k, (s, v) in deps.items():
            if eng == "pe" and dma is None and s is E.ctr.sem:
                continue
            if E.waited.get(k, 0) >= v:
                continue
            E.waited[k] = v
            waits.append((s, v))
        tok = ctr.next()
        E.ops.append((waits, fn, tok[0], ctr.step))
        for b in reads:
            b.r.append(tok)
        for b in writes:
            b.w = [tok]
            b.r = []
        for b in cowrites:
            b.w.append(tok)
        self.n += 1
        return tok

    def finish(self):
        E = self.engs["sp"]
        waits = []
        for ctr in self.dctr.values():
            for s, v in ctr.done + [(ctr.sem, ctr.val)]:
                if v > 0 and E.waited.get(id(s), 0) < v:
                    waits.append((s, v))
        E.ops.append((waits, None, None, 0))

    def emit(self):
        nc = self.nc

        def mk(E):
            def run(e):
                for waits, fn, sem, step in E.ops:
                    for ws, wv in waits:
                        e.wait_ge(ws, wv)
                    if fn is not None:
                        fn(e).then_inc(sem, step)
            return run

        with nc.Block() as block:
            block.tensor(mk(self.engs["pe"]))
            block.scalar(mk(self.engs["act"]))
            block.vector(mk(self.engs["dve"]))
            block.gpsimd(mk(self.engs["pool"]))
            block.sync(mk(self.engs["sp"]))
        self.stack.close()


def blocks_of(T, tb=TB):
    return [(c, min(c + tb, T)) for c in range(0, T, tb)]


def chunked(ap):
    return ap.rearrange("(c p) t -> p c t", p=128)


class RL:
    def __init__(self, P, T, segs, vecs_ap, nvec):
        self.P, self.T, self.segs = P, T, segs
        self.blocks = blocks_of(T)
        nb = len(self.blocks)
        nc = P.nc
        self.xn2 = [P.sb(f"xn{i}", [128, NCH, TB], F32) for i in range(2)]
        self.big = P.sb("big", [128, max(NCH * T, FCH * TB)], BF16)
        self.hb = self.big[:, 0:NCH * T].rearrange("p (c t) -> p c t", c=NCH)
        self.ab = self.big[:, 0:FCH * TB].rearrange("p (c t) -> p c t", c=FCH)
        self.wa = [P.sb(f"wa{i}", [128, NCH, 128], BF16) for i in range(4)]
        self.wb = [P.sb(f"wb{i}", [128, FCH, 128], BF16) for i in range(2)]
        self.mu = P.sb("mu", [128, TB], F32)
        self.msq = P.sb("msq", [128, TB], F32)
        self.rstd = P.sb("rstd", [128, TB], F32)
        self.sg = [P.sb(f"sg{i}", [128, TB], F32) for i in range(2)]
        self.zt = [P.sb(f"zt{i}", [128, TB], F32) for i in range(2)]
        self.tmp = [P.sb(f"tmp{i}", [128, TB], F32) for i in range(2)]
        self.xr = [P.sb(f"xr{i}", [128, TB], F32) for i in range(2)]
        self.at = [P.sb(f"at{i}", [128, TB], BF16) for i in range(2)]
        self.zb = [P.sb(f"zb{i}", [128, TB], BF16) for i in range(2)]
        self.zq = [P.sb(f"zq{i}", [128, TB], BF16) for i in range(2)]
        self.ones = P.sb("ones", [128, 128], BF16)
        self.vecs = P.sb("vecs", [128, nvec, NCH], F32)
        self.s1 = P.ps("s1", [128, TB])
        self.s2 = P.ps("s2", [128, TB])
        self.pg = [P.ps(f"pg{i}", [128, TB]) for i in range(2)]
        self.pu = [P.ps(f"pu{i}", [128, TB]) for i in range(2)]
        self.py = [P.ps(f"py{i}", [128, TB]) for i in range(2)]
        self.xs = [nc.dram_tensor(f"xs{i}", [nb, 128, NCH, TB], F32).ap() for i in range(2)]
        self.hs = nc.dram_tensor("hs", [nb, 128, NCH, TB], BF16).ap()
        self.A = nc.dram_tensor("Asp", [nb, 128, FCH, TB], BF16).ap()
        B = Buf
        BL = lambda n: [B() for _ in range(n)]
        self.b_xn2 = [BL(NCH), BL(NCH)]
        self.b_hb = B("hb")
        self.b_abc = BL(FCH)
        self.b_wa, self.b_wb = BL(4), BL(2)
        self.b_mu, self.b_msq, self.b_rstd = B(), B(), B()
        self.b_sg, self.b_zt, self.b_tmp, self.b_xr, self.b_at = BL(2), BL(2), BL(2), BL(2), BL(2)
        self.b_zb, self.b_zq = BL(2), BL(2)
        self.b_s1, self.b_s2 = B(), B()
        self.b_pg, self.b_pu, self.b_py = BL(2), BL(2), BL(2)
        self.b_ones, self.b_vecs = B(), B()
        self.b_xs = [BL(nb), BL(nb)]
        self.b_hs = BL(nb)
        self.b_A = BL(nb)
        self.W_HB = [self.b_hb] + self.b_abc
        self.wa_i = self.wb_i = self.zt_i = self.xr_i = self.at_i = self.pp_i = self.zb_i = 0
        self.xs_cur = 0
        ones, vecs = self.ones, self.vecs
        P.op("dve", lambda e: e.memset(ones[:], 1.0), writes=[self.b_ones])
        P.op("sp", lambda e: e.dma_start(out=vecs[:], in_=vecs_ap), writes=[self.b_vecs], dma="vecs")

    def vc(self, v, c):
        return self.vecs[:, v, c:c + 1]

    def derive(self, v, mul=None, add=None):
        vecs = self.vecs
        if add is not None:
            self.P.op("dve", lambda e: e.tensor_scalar_add(out=vecs[:, v, :], in0=vecs[:, v, :], scalar1=float(add)),
                      reads=[self.b_vecs], writes=[self.b_vecs])
        if mul is not None:
            self.P.op("dve", lambda e: e.tensor_scalar_mul(out=vecs[:, v, :], in0=vecs[:, v, :], scalar1=float(mul)),
                      reads=[self.b_vecs], writes=[self.b_vecs])

    def segparts(self, c0, c1):
        out = []
        for si, (s0, s1) in enumerate(self.segs):
            a, b = max(c0, s0), min(c1, s1)
            if a < b:
                out.append((si, a - c0, b - c0))
        return out

    def first_prologue(self, x_ap, x_buf, vshift, vscale1):
        P = self.P
        hb = self.hb
        for bi, (c0, c1) in enumerate(self.blocks):
            w = c1 - c0
            s = bi % 2
            xn = self.xn2[s]
            P.op("sp", lambda e, xn=xn, w=w, c0=c0, c1=c1: e.dma_start(out=xn[:, :, :w], in_=chunked(x_ap)[:, :, c0:c1]),
                 reads=[x_buf], writes=self.b_xn2[s], dma=f"xin{s}")
            for si, a, b in self.segparts(c0, c1):
                for c in range(NCH):
                    P.op("act", lambda e, xn=xn, c=c, si=si, a=a, b=b, c0=c0: e.activation(
                        out=hb[:, c, c0 + a:c0 + b], in_=xn[:, c, a:b], func=AF.Identity,
                        scale=self.vc(vscale1[si], c), bias=self.vc(vshift[si], c)),
                        reads=[self.b_vecs, self.b_xn2[s][c]], cowrites=self.W_HB)

    def load_hb(self):
        hb, hs = self.hb, self.hs
        for bi, (c0, c1) in enumerate(self.blocks):
            w = c1 - c0
            self.P.op("sp", lambda e, bi=bi, c0=c0, c1=c1, w=w: e.dma_start(out=hb[:, :, c0:c1], in_=hs[bi, :, :, :w]),
                      reads=[self.b_hs[bi]], cowrites=self.W_HB, dma="ldh")
            self.b_hs[bi].w = []

    def load_wa(self, W_ap, g):
        i = self.wa_i
        self.wa_i = (i + 1) % 4
        t = self.wa[i]
        self.P.op("pool", lambda e: e.dma_start(out=t[:], in_=W_ap[g]), writes=[self.b_wa[i]], dma=f"wa{i}")
        return i

    def load_wb(self, W_ap, g):
        i = self.wb_i
        self.wb_i = (i + 1) % 2
        t = self.wb[i]
        self.P.op("pool", lambda e: e.dma_start(out=t[:], in_=W_ap[g]), writes=[self.b_wb[i]], dma=f"wb{i}")
        return i

    def mm(self, dst, dst_buf, wt, wbuf, kc, X, lo, hi, xbufs):
        w = hi - lo

        def f(e):
            for k in range(kc):
                ins = e.matmul(dst[:, :w], lhsT=wt[:, k, :], rhs=X[:, k, lo:hi], start=(k == 0), stop=(k == kc - 1))
            return ins
        self.P.op("pe", f, reads=[wbuf] + list(xbufs), writes=[dst_buf])

    def store(self, dst_ap, dst_buf, row0, c0, c1, tile, tbuf):
        w = c1 - c0
        self.P.op("sp", lambda e: e.dma_start(out=dst_ap[row0:row0 + 128, c0:c1], in_=tile[:, :w]),
                  reads=[tbuf], cowrites=[dst_buf], dma="st")

    def phase1(self, wg, wu):
        P = self.P
        hb, A = self.hb, self.A
        ld = lambda fc: (self.load_wa(wg, fc), self.load_wa(wu, fc))
        nxt = ld(0)
        for fc in range(FCH):
            ig, iu = nxt
            if fc + 1 < FCH:
                nxt = ld(fc + 1)
            for bi, (c0, c1) in enumerate(self.blocks):
                w = c1 - c0
                pb = self.pp_i
                self.pp_i ^= 1
                self.mm(self.pg[pb], self.b_pg[pb], self.wa[ig], self.b_wa[ig], NCH, hb, c0, c1, [self.b_hb])
                self.mm(self.pu[pb], self.b_pu[pb], self.wa[iu], self.b_wa[iu], NCH, hb, c0, c1, [self.b_hb])
                sg, pg, pu = self.sg[pb], self.pg[pb], self.pu[pb]
                ai = self.at_i
                self.at_i ^= 1
                at = self.at[ai]
                P.op("act", lambda e, sg=sg, pg=pg, w=w: e.activation(out=sg[:, :w], in_=pg[:, :w], func=AF.Silu),
                     reads=[self.b_pg[pb]], writes=[self.b_sg[pb]])
                P.op("dve", lambda e, sg=sg, pu=pu, at=at, w=w: e.tensor_tensor(out=at[:, :w], in0=sg[:, :w], in1=pu[:, :w],
                                                                               op=ALU.mult),
                     reads=[self.b_sg[pb], self.b_pu[pb]], writes=[self.b_at[ai]])
                P.op("sp", lambda e, at=at, bi=bi, fc=fc, w=w: e.dma_start(out=A[bi, :, fc, :w], in_=at[:, :w]),
                     reads=[self.b_at[ai]], cowrites=[self.b_A[bi]], dma="sta")

    def phase2(self, kind, W, res, vgate, ln, mod=None, out_ap=None):
        P = self.P
        hb, ab, A, ones = self.hb, self.ab, self.A, self.ones
        s1, s2, mu, msq, rstd = self.s1, self.s2, self.mu, self.msq, self.rstd
        vg, vb = ln
        xs_in = self.xs[self.xs_cur]
        b_xs_in = self.b_xs[self.xs_cur]
        xs_out = self.xs[self.xs_cur ^ 1]
        b_xs_out = self.b_xs[self.xs_cur ^ 1]
        hs = self.hs
        pending = []

        def ln_apply(bi, c0, c1, s, c):
            w = c1 - c0
            xn = self.xn2[s]
            bx = self.b_xn2[s][c]
            P.op("dve", lambda e: e.tensor_tensor(out=xn[:, c, :w], in0=xn[:, c, :w], in1=mu[:, :w], op=ALU.subtract),
                 reads=[self.b_mu], writes=[bx])
            P.op("dve", lambda e: e.tensor_tensor(out=xn[:, c, :w], in0=xn[:, c, :w], in1=rstd[:, :w], op=ALU.mult),
                 reads=[self.b_rstd], writes=[bx])
            P.op("act", lambda e: e.activation(out=xn[:, c, :w], in_=xn[:, c, :w], func=AF.Identity,
                                               scale=self.vc(vg, c), bias=self.vc(vb, c)),
                 reads=[self.b_vecs], writes=[bx])
            if mod is not None:
                ai = self.at_i
                self.at_i ^= 1
                at = self.at[ai]
                for si, a, b in self.segparts(c0, c1):
                    P.op("act", lambda e, si=si, a=a, b=b: e.activation(
                        out=at[:, a:b], in_=xn[:, c, a:b], func=AF.Identity,
                        scale=self.vc(mod[1][si], c), bias=self.vc(mod[0][si], c)),
                        reads=[self.b_vecs, bx], cowrites=[self.b_at[ai]])
                P.op("sp", lambda e: e.dma_start(out=hs[bi, :, c, :w], in_=at[:, :w]),
                     reads=[self.b_at[ai]], cowrites=[self.b_hs[bi]], dma="sth")
                self.b_at[ai].w = []
            if c == NCH - 1:
                if out_ap is not None:
                    P.op("sp", lambda e: e.dma_start(out=chunked(out_ap)[:, :, c0:c1], in_=xn[:, :, :w]),
                         reads=self.b_xn2[s], writes=[Buf()], dma=f"xo{s}")
                else:
                    P.op("sp", lambda e: e.dma_start(out=xs_out[bi, :, :, :w], in_=xn[:, :, :w]),
                         reads=self.b_xn2[s], writes=[b_xs_out[bi]], dma=f"xo{s}")

        for bi, (c0, c1) in enumerate(self.blocks):
            w = c1 - c0
            s = bi % 2
            xn = self.xn2[s]
            if kind == "ffn":
                P.op("sp", lambda e, bi=bi, w=w: e.dma_start(out=ab[:, :, :w], in_=A[bi, :, :, :w]),
                     reads=[self.b_A[bi]], writes=self.W_HB, dma="lda")
                self.b_A[bi].w = []
                ldw = lambda dc: self.load_wb(W, dc)
            else:
                ldw = lambda dc: self.load_wa(W, dc)
            nxt = ldw(0)
            stats_prev = None
            for dc in range(NCH):
                iw = nxt
                if dc + 1 < NCH:
                    nxt = ldw(dc + 1)
                pb = dc % 2
                xi = self.xr_i
                self.xr_i ^= 1
                xr = self.xr[xi]
                if res[0] == "xs":
                    rv, rb = xs_in[bi, :, dc, :w], b_xs_in[bi]
                else:
                    rv, rb = res[1][dc * 128:(dc + 1) * 128, c0:c1], res[2]
                P.op("sp", lambda e, xr=xr, rv=rv, w=w: e.dma_start(out=xr[:, :w], in_=rv),
                     reads=[rb], writes=[self.b_xr[xi]], dma=f"xr{xi}")
                if kind == "ffn":
                    self.mm(self.py[pb], self.b_py[pb], self.wb[iw], self.b_wb[iw], FCH, ab, 0, w, self.b_abc)
                else:
                    self.mm(self.py[pb], self.b_py[pb], self.wa[iw], self.b_wa[iw], NCH, hb, c0, c1, [self.b_hb])
                ti = self.zt_i
                self.zt_i ^= 1
                tmp, py = self.tmp[ti], self.py[pb]
                for si, a, b in self.segparts(c0, c1):
                    P.op("act", lambda e, si=si, a=a, b=b, tmp=tmp, py=py, dc=dc: e.activation(
                        out=tmp[:, a:b], in_=py[:, a:b], func=AF.Copy, scale=self.vc(vgate[si], dc)),
                        reads=[self.b_py[pb], self.b_vecs], cowrites=[self.b_tmp[ti]])
                bx = self.b_xn2[s][dc]
                P.op("dve", lambda e, xn=xn, xr=xr, tmp=tmp, dc=dc, w=w: e.scalar_tensor_tensor(
                    out=xn[:, dc, :w], in0=xr[:, :w], scalar=ALPHA, in1=tmp[:, :w], op0=ALU.mult, op1=ALU.add),
                    reads=[self.b_xr[xi], self.b_tmp[ti]], writes=[bx])
                self.b_tmp[ti].w = []
                zi = self.zb_i
                self.zb_i ^= 1
                zb, zq = self.zb[zi], self.zq[zi]
                P.op("act", lambda e, xn=xn, zb=zb, dc=dc, w=w: e.activation(out=zb[:, :w], in_=xn[:, dc, :w], func=AF.Copy),
                     reads=[bx], writes=[self.b_zb[zi]])
                P.op("dve", lambda e, xn=xn, zq=zq, dc=dc, w=w: e.tensor_tensor(out=zq[:, :w], in0=xn[:, dc, :w],
                                                                               in1=xn[:, dc, :w], op=ALU.mult),
                     reads=[bx], writes=[self.b_zq[zi]])

                def stats(e, zb=zb, zq=zq, dc=dc, w=w):
                    e.matmul(s1[:, :w], lhsT=ones[:], rhs=zb[:, :w], start=(dc == 0), stop=(dc == NCH - 1))
                    return e.matmul(s2[:, :w], lhsT=ones[:], rhs=zq[:, :w], start=(dc == 0), stop=(dc == NCH - 1))
                this_stats = (stats, [self.b_zb[zi], self.b_zq[zi], self.b_ones])
                if stats_prev is not None:
                    P.op("pe", stats_prev[0], reads=stats_prev[1], cowrites=[self.b_s1, self.b_s2])
                stats_prev = this_stats
                if pending:
                    pending.pop(0)()
            P.op("pe", stats_prev[0], reads=stats_prev[1], cowrites=[self.b_s1, self.b_s2])
            while pending:
                pending.pop(0)()
            P.op("dve", lambda e, w=w: e.tensor_scalar_mul(out=mu[:, :w], in0=s1[:, :w], scalar1=1.0 / D),
                 reads=[self.b_s1], writes=[self.b_mu])
            P.op("dve", lambda e, w=w: e.tensor_tensor(out=msq[:, :w], in0=mu[:, :w], in1=mu[:, :w], op=ALU.mult),
                 reads=[self.b_mu], writes=[self.b_msq])
            P.op("dve", lambda e, w=w: e.scalar_tensor_tensor(out=rstd[:, :w], in0=s2[:, :w], scalar=1.0 / D, in1=msq[:, :w],
                                                              op0=ALU.mult, op1=ALU.subtract),
                 reads=[self.b_s2, self.b_msq], writes=[self.b_rstd])
            P.op("dve", lambda e, w=w: e.tensor_scalar_add(out=rstd[:, :w], in0=rstd[:, :w], scalar1=LN_EPS),
                 reads=[self.b_rstd], writes=[self.b_rstd])
            P.op("act", lambda e, w=w: e.activation(out=rstd[:, :w], in_=rstd[:, :w], func=AF.Sqrt),
                 reads=[self.b_rstd], writes=[self.b_rstd])
            P.op("dve", lambda e, w=w: e.reciprocal(out=rstd[:, :w], in_=rstd[:, :w]),
                 reads=[self.b_rstd], writes=[self.b_rstd])
            self.b_s1.w, self.b_s2.w = [], []
            for c in range(NCH):
                pending.append(lambda bi=bi, c0=c0, c1=c1, s=s, c=c: ln_apply(bi, c0, c1, s, c))
        while pending:
            pending.pop(0)()
        if out_ap is None:
            self.xs_cur ^= 1


def tile_w(W, gcols):
    K, N = W.shape
    return np.ascontiguousarray(W.reshape(K // 128, 128, N // gcols, gcols).transpose(2, 1, 0, 3))


def dram_in(nc, name, shape, dt=F32):
    return nc.dram_tensor(name, list(shape), dt, kind="ExternalInput").ap()


def dram_out(nc, name, shape, dt=F32):
    return nc.dram_tensor(name, list(shape), dt, kind="ExternalOutput").ap()


def ffn_w_in(nc, sfx):
    return (dram_in(nc, "wg" + sfx, [FCH, 128, NCH, 128]), dram_in(nc, "wu" + sfx, [FCH, 128, NCH, 128]),
            dram_in(nc, "wd" + sfx, [NCH, 128, FCH, 128]))


MODC = 2 * 9 * D // NCORE


def build_L0():
    nc = bass.Bass("TRN2", target_bir_lowering=False)
    cv_ap = dram_in(nc, "cv", [128, NCH, 3])
    w_ap = dram_in(nc, "w", [D, MODC])
    b_ap = dram_in(nc, "b", [3, MODC])
    o_ap = dram_out(nc, "mod", [3, MODC])
    P = Prog(nc)
    cv = P.sb("cv", [128, NCH, 3], F32)
    cb = P.sb("cb", [128, NCH, 3], BF16)
    bt = P.sb("bt", [3, MODC], F32)
    ot = P.sb("ot", [3, MODC], F32)
    wt = [P.sb(f"w{i}", [128, NCH, 512], BF16) for i in range(2)]
    ps = [P.ps(f"ps{i}", [128, 512]) for i in range(2)]
    b_cv, b_cb, b_bt, b_ot = Buf(), Buf(), Buf(), Buf()
    b_w, b_ps = [Buf(), Buf()], [Buf(), Buf()]
    P.op("sp", lambda e: e.dma_start(out=cv[:], in_=cv_ap), writes=[b_cv], dma="cv")
    P.op("sp", lambda e: e.dma_start(out=bt[:], in_=b_ap), writes=[b_bt], dma="bt")
    P.op("act", lambda e: e.activation(out=cb[:], in_=cv[:], func=AF.Silu), reads=[b_cv], writes=[b_cb])
    wv = w_ap.rearrange("(k p) f -> p k f", p=128)
    for t in range(MODC // 512):
        i = t % 2
        P.op("pool", lambda e, t=t, i=i: e.dma_start(out=wt[i][:], in_=wv[:, :, t * 512:(t + 1) * 512]),
             writes=[b_w[i]], dma=f"w{i}")

        def f(e, i=i):
            for k in range(NCH):
                ins = e.matmul(ps[i][0:3, :], lhsT=cb[:, k, :], rhs=wt[i][:, k, :], start=(k == 0), stop=(k == NCH - 1))
            return ins
        P.op("pe", f, reads=[b_cb, b_w[i]], writes=[b_ps[i]])
        P.op("dve", lambda e, t=t, i=i: e.tensor_tensor(out=ot[:, t * 512:(t + 1) * 512], in0=ps[i][0:3, :],
                                                        in1=bt[:, t * 512:(t + 1) * 512], op=ALU.add),
             reads=[b_ps[i], b_bt], writes=[b_ot])
    P.op("sp", lambda e: e.dma_start(out=o_ap, in_=ot[:]), reads=[b_ot], writes=[Buf()], dma="st")
    P.finish()
    P.emit()
    return nc


def simple_proj(R, W, ng, dst, b_o):
    P = R.P
    nxt = R.load_wa(W, 0)
    for g in range(ng):
        iw = nxt
        if g + 1 < ng:
            nxt = R.load_wa(W, g + 1)
        for bi, (c0, c1) in enumerate(R.blocks):
            w = c1 - c0
            pb = R.pp_i
            R.pp_i ^= 1
            R.mm(R.pg[pb], R.b_pg[pb], R.wa[iw], R.b_wa[iw], NCH, R.hb, c0, c1, [R.b_hb])
            i = R.zt_i
            R.zt_i ^= 1
            zt, pg = R.zt[i], R.pg[pb]
            P.op("act", lambda e, pg=pg, zt=zt, w=w: e.activation(out=zt[:, :w], in_=pg[:, :w], func=AF.Copy),
                 reads=[R.b_pg[pb]], writes=[R.b_zt[i]])
            R.store(dst, b_o, g * 128, c0, c1, zt, R.b_zt[i])


def build_LA(T, segs):
    nc = bass.Bass("TRN2", target_bir_lowering=False)
    nseg = len(segs)
    nvec = 5 * nseg + 2
    x_ap = dram_in(nc, "xT", [D, T])
    vecs_ap = dram_in(nc, "vecs", [128, nvec, NCH])
    wg, wu, wd = ffn_w_in(nc, "")
    w_gb = dram_in(nc, "w_gb", [8, 128, NCH, 128])
    w_gc = dram_in(nc, "w_gc", [8, 128, NCH, 128])
    w_xi = dram_in(nc, "w_xi", [8, 128, NCH, 128])
    w_uf = dram_in(nc, "w_uf", [8, 128, NCH, 128])
    xn1 = dram_out(nc, "xn1", [D, T])
    gb = dram_out(nc, "gb", [1024, T])
    vv = dram_out(nc, "vv", [1024, T])
    uf = dram_out(nc, "uf", [1024, T])
    P = Prog(nc)
    R = RL(P, T, segs, vecs_ap, nvec)
    V = lambda s, k: 5 * s + k
    SL = lambda k: [V(s, k) for s in range(nseg)]
    VG, VB = 5 * nseg, 5 * nseg + 1
    for s in range(nseg):
        R.derive(V(s, 1), add=1.0)
        R.derive(V(s, 2), mul=0.5)
        R.derive(V(s, 4), add=1.0)
    b_x = Buf()
    b_o = Buf()
    R.first_prologue(x_ap, b_x, SL(0), SL(1))
    R.phase1(wg, wu)
    R.phase2("ffn", wd, ("ap", x_ap, b_x), SL(2), ln=(VG, VB), mod=(SL(3), SL(4)), out_ap=xn1)
    R.load_hb()
    simple_proj(R, w_gb, 8, gb, b_o)
    ld = lambda jj: (R.load_wa(w_gc, jj), R.load_wa(w_xi, jj))
    nxt = ld(0)
    for jj in range(8):
        ic, ix = nxt
        if jj + 1 < 8:
            nxt = ld(jj + 1)
        for bi, (c0, c1) in enumerate(R.blocks):
            w = c1 - c0
            pb = R.pp_i
            R.pp_i ^= 1
            R.mm(R.pg[pb], R.b_pg[pb], R.wa[ic], R.b_wa[ic], NCH, R.hb, c0, c1, [R.b_hb])
            R.mm(R.pu[pb], R.b_pu[pb], R.wa[ix], R.b_wa[ix], NCH, R.hb, c0, c1, [R.b_hb])
            i = R.zt_i
            R.zt_i ^= 1
            zt, sg, pg, pu = R.zt[i], R.sg[pb], R.pg[pb], R.pu[pb]
            P.op("act", lambda e, pg=pg, sg=sg, w=w: e.activation(out=sg[:, :w], in_=pg[:, :w], func=AF.Copy),
                 reads=[R.b_pg[pb]], writes=[R.b_sg[pb]])
            P.op("dve", lambda e, pu=pu, sg=sg, zt=zt, w=w: e.tensor_tensor(out=zt[:, :w], in0=pu[:, :w], in1=sg[:, :w],
                                                                           op=ALU.mult),
                 reads=[R.b_pu[pb], R.b_sg[pb]], writes=[R.b_zt[i]])
            R.store(vv, b_o, jj * 128, c0, c1, zt, R.b_zt[i])
    simple_proj(R, w_uf, 8, uf, b_o)
    P.finish()
    P.emit()
    return nc


def build_LF():
    nc = bass.Bass("TRN2", target_bir_lowering=False)
    xl = dram_in(nc, "xl", [2, SEQ, 128])
    xc = dram_in(nc, "xc", [2, 128, CTX])
    ftw_ap = dram_in(nc, "ftw", [128, 64, 256])
    fc_ap = dram_in(nc, "fc", [128, 2, 256])
    c64_ap = dram_in(nc, "c64", [64, 2, 64])
    c256_ap = dram_in(nc, "c256", [128, 4, 256])
    yl = dram_out(nc, "yl", [2, 128, SEQ])
    yc = dram_out(nc, "yc", [2, 128, CTX])
    P = Prog(nc)
    ftw = P.sb("ftw", [128, 64, 256], BF16)
    fc = P.sb("fc", [128, 2, 256], BF16)
    c64 = P.sb("c64", [64, 2, 64], BF16)
    c256 = P.sb("c256", [128, 4, 256], BF16)
    XA = P.sb("XA", [128, 64, 128], BF16)
    D1 = P.sb("D1", [128, 64, 256], BF16)
    D2 = P.sb("D2", [64, 128, 256], BF16)
    Y = P.sb("Y", [128, SEQ], F32)
    XC = P.sb("XC", [128, CTX], BF16)
    DA = P.sb("DA", [128, 2, 256], BF16)
    YC = P.sb("YC", [128, CTX], F32)
    pp = [P.ps(f"pp{i}", [128, 512]) for i in range(4)]
    b_pp = [Buf() for _ in range(4)]
    b_t, b_XA, b_D1, b_D2, b_Y, b_XC, b_DA, b_YC = (Buf() for _ in range(8))
    for t, ap, k in ((ftw, ftw_ap, "t0"), (fc, fc_ap, "t1"), (c64, c64_ap, "t2"), (c256, c256_ap, "t3")):
        P.op("pool", lambda e, t=t, ap=ap: e.dma_start(out=t[:], in_=ap), cowrites=[b_t], dma=k)
    pi = [0]

    def nextp():
        pi[0] = (pi[0] + 1) % 4
        return pi[0]
    ev = [0]

    def evac(out_ap, in_ap, rd, wr, co=False):
        ev[0] ^= 1
        kw = dict(cowrites=[wr]) if co else dict(writes=[wr])
        if ev[0]:
            P.op("act", lambda e: e.activation(out=out_ap, in_=in_ap, func=AF.Copy), reads=[rd], **kw)
        else:
            P.op("dve", lambda e: e.tensor_copy(out=out_ap, in_=in_ap), reads=[rd], **kw)

    for u in range(2):
        src = xl[u].rearrange("(a r) c -> a r c", r=64)
        P.op("pool", lambda e, src=src: e.dma_start(out=XA[:], in_=src), writes=[b_XA], dma="xa")
        P.op("pool", lambda e, u=u: e.dma_start(out=XC[:], in_=xc[u]), writes=[b_XC], dma="xc")
        for b0 in range(0, 64, 2):
            i = nextp()

            def f(e, b0=b0, i=i):
                for q in range(2):
                    ins = e.matmul(pp[i][:, q * 256:(q + 1) * 256], lhsT=XA[:, b0 + q, :], rhs=ftw[:, b0 + q, :],
                                   start=True, stop=True)
                return ins
            P.op("pe", f, reads=[b_XA, b_t], writes=[b_pp[i]])
            evac(D1[:, b0:b0 + 2, :], pp[i][:].rearrange("p (q n) -> p q n", q=2), b_pp[i], b_D1, co=True)
        for a0 in range(0, 128, 2):
            i = nextp()

            def f(e, a0=a0, i=i):
                for q in range(2):
                    e.matmul(pp[i][0:64, q * 256:(q + 1) * 256], lhsT=D1[:, :, a0 + q], rhs=fc[:, 0, :], start=True, stop=False)
                    ins = e.matmul(pp[i][0:64, q * 256:(q + 1) * 256], lhsT=D1[:, :, 128 + a0 + q], rhs=fc[:, 1, :],
                                   start=False, stop=True)
                return ins
            P.op("pe", f, reads=[b_D1, b_t], writes=[b_pp[i]])
            evac(D2[:, a0:a0 + 2, :], pp[i][0:64, :].rearrange("p (q n) -> p q n", q=2), b_pp[i], b_D2, co=True)
        Yv = Y[:].rearrange("p (b a) -> p a b", a=128)
        for a0 in range(0, 128, 8):
            i = nextp()

            def f(e, a0=a0, i=i):
                for q in range(8):
                    e.matmul(pp[i][:, q * 64:(q + 1) * 64], lhsT=D2[:, a0 + q, 0:128], rhs=c64[:, 0, :], start=True, stop=False)
                    ins = e.matmul(pp[i][:, q * 64:(q + 1) * 64], lhsT=D2[:, a0 + q, 128:256], rhs=c64[:, 1, :],
                                   start=False, stop=True)
                return ins
            P.op("pe", f, reads=[b_D2, b_t], writes=[b_pp[i]])
            evac(Yv[:, a0:a0 + 8, :], pp[i][:].rearrange("p (a b) -> p a b", a=8), b_pp[i], b_Y, co=True)
        P.op("sp", lambda e, u=u: e.dma_start(out=yl[u], in_=Y[:]), reads=[b_Y], writes=[Buf()], dma="sty")
        i = nextp()

        def f(e, i=i):
            for j in range(2):
                ins = e.matmul(pp[i][:, j * 256:(j + 1) * 256], lhsT=XC[:, j * 128:(j + 1) * 128], rhs=fc[:, 0, :],
                               start=True, stop=True)
            return ins
        P.op("pe", f, reads=[b_XC, b_t], writes=[b_pp[i]])
        evac(DA[:], pp[i][:].rearrange("p (j n) -> p j n", j=2), b_pp[i], b_DA)
        i = nextp()

        def f(e, i=i):
            n = 0
            for j in range(2):
                for ri in range(2):
                    ins = e.matmul(pp[i][:, 0:256], lhsT=DA[:, j, ri * 128:(ri + 1) * 128], rhs=c256[:, 2 * ri + j, :],
                                   start=(n == 0), stop=(n == 3))
                    n += 1
            return ins
        P.op("pe", f, reads=[b_DA, b_t], writes=[b_pp[i]])
        evac(YC[:], pp[i][:, 0:256], b_pp[i], b_YC)
        P.op("sp", lambda e, u=u: e.dma_start(out=yc[u], in_=YC[:]), reads=[b_YC], writes=[Buf()], dma="styc")
    P.finish()
    P.emit()
    return nc


def fft_tables():
    a = np.arange(128)[:, None, None]
    b = np.arange(64)[None, :, None]
    ap = np.arange(128)[None, None, :]
    th = 2 * np.pi * (a * ap / 128.0 + b * ap / 8192.0)
    ftw = np.concatenate([np.cos(th), -np.sin(th)], axis=-1) / np.sqrt(128.0)
    c = np.arange(128)[:, None]
    cp = np.arange(128)[None, :]
    th = 2 * np.pi * c * cp / 128.0
    cr, ci = np.cos(th) / np.sqrt(128.0), -np.sin(th) / np.sqrt(128.0)
    fc = np.stack([np.concatenate([cr, ci], 1), np.concatenate([-ci, cr], 1)], axis=1)
    bb = np.arange(64)[:, None]
    bp = np.arange(64)[None, :]
    th = 2 * np.pi * bb * bp / 64.0
    c64 = np.stack([np.cos(th), np.sin(th)], axis=1) / 8.0
    l = np.arange(256)[:, None]
    lp = np.arange(256)[None, :]
    th = 2 * np.pi * l * lp / 256.0
    C, S = np.cos(th) / 16.0, np.sin(th) / 16.0
    c256 = np.stack([C[0:128], C[128:256], S[0:128], S[128:256]], axis=1)
    f = lambda x: np.ascontiguousarray(x, dtype=np.float32)
    return f(ftw), f(fc), f(c64), f(c256)


NBLK = SEQ // 128


def build_LC(nblk=NBLK):
    nc = bass.Bass("TRN2", target_bir_lowering=False)
    S = nblk * 128
    NS = 4
    qt_ap = dram_in(nc, "qt", [128, 4, S])
    kt_ap = dram_in(nc, "kt", [128, S + CTX])
    v_ap = dram_in(nc, "v", [128, nblk + 2, 64])
    mask_ap = dram_in(nc, "mask", [128, 384])
    sink_ap = dram_in(nc, "sink", [128, 8])
    id_ap = dram_in(nc, "ident", [128, 128])
    o_ap = dram_out(nc, "o", [S, 512])
    P = Prog(nc)
    QT = P.sb("QT", [128, 4, S], BF16)
    KT = P.sb("KT", [128, S + CTX], BF16)
    V = P.sb("V", [128, nblk + 2, 64], BF16)
    mask = P.sb("mask", [128, 384], F32)
    sink = P.sb("sink", [128, 8], F32)
    sink8 = P.sb("sink8", [128, 8], F32)
    ident = P.sb("ident", [128, 128], BF16)
    sc = [P.sb(f"sc{i}", [128, 648], F32) for i in range(NS)]
    pb = [P.sb(f"pb{i}", [128, 648], BF16) for i in range(NS)]
    pTs = [P.sb(f"pTs{i}", [128, 5, 128], BF16) for i in range(NS)]
    sm = [P.sb(f"sm{i}", [128, 4], F32) for i in range(NS)]
    ot = [P.sb(f"ot{i}", [128, 512], F32) for i in range(2)]
    scA = [P.ps(f"scA{i}", [128, 512]) for i in range(2)]
    scB = [P.ps(f"scB{i}", [128, 512]) for i in range(2)]
    pTt = [P.ps(f"pTt{i}", [128, 8, 128], BF16) for i in range(2)]
    ops = [P.ps(f"ops{i}", [128, 512]) for i in range(2)]
    BL = lambda n: [Buf() for _ in range(n)]
    b_c, b_s8 = Buf(), Buf()
    b_sc, b_pb, b_pTs, b_sm = BL(NS), BL(NS), BL(NS), BL(NS)
    b_ot, b_scA, b_scB, b_pTt, b_ops = BL(2), BL(2), BL(2), BL(2), BL(2)
    for t, ap, k in ((QT, qt_ap, "c0"), (KT, kt_ap, "c1"), (V, v_ap, "c2"), (ident, id_ap, "c3")):
        P.op("pool", lambda e, t=t, ap=ap: e.dma_start(out=t[:], in_=ap), cowrites=[b_c], dma=k)
    for t, ap, k in ((mask, mask_ap, "c4"), (sink, sink_ap, "c5")):
        P.op("sp", lambda e, t=t, ap=ap: e.dma_start(out=t[:], in_=ap), cowrites=[b_c], dma=k)
    P.op("dve", lambda e: e.tensor_scalar_mul(out=sink8[:], in0=sink[:], scalar1=8.0), reads=[b_c], writes=[b_s8])
    units = [(i, r) for i in range(nblk) for r in range(8)]
    N = len(units)

    def geo(n):
        i, r = units[n]
        kb0, kb1 = max(i - 1, 0), min(i + 1, nblk - 1)
        nloc = kb1 - kb0 + 1
        nk = nloc * 128
        mo = (kb0 - (i - 1)) * 128
        return i, r, kb0, nloc, nk, mo, nk + 256

    def S1(n):
        i, r, kb0, nloc, nk, mo, L = geo(n)
        s, p = n % NS, n % 2
        hh, j = r // 4, r % 4
        p0, p1 = hh * 64, hh * 64 + 64
        q = QT[p0:p1, j, i * 128:(i + 1) * 128]

        def f(e):
            e.matmul(scA[p][:, 0:nk], lhsT=q, rhs=KT[p0:p1, kb0 * 128:kb0 * 128 + nk], start=True, stop=True)
            return e.matmul(scB[p][:, 0:256], lhsT=q, rhs=KT[p0:p1, S:S + CTX], start=True, stop=True)
        P.op("pe", f, reads=[b_c], writes=[b_scA[p], b_scB[p]])
        P.op("dve", lambda e: e.tensor_tensor(out=sc[s][:, 0:nk], in0=scA[p][:, 0:nk], in1=mask[:, mo:mo + nk], op=ALU.add),
             reads=[b_scA[p], b_c], writes=[b_sc[s]])
        P.op("act", lambda e: e.activation(out=sc[s][:, nk:L], in_=scB[p][:, 0:256], func=AF.Copy),
             reads=[b_scB[p]], cowrites=[b_sc[s]])
        P.op("pool", lambda e: e.tensor_copy(out=sc[s][:, L:L + 1], in_=sink8[:, r:r + 1]),
             reads=[b_s8], cowrites=[b_sc[s]])
        m = sm[s]
        P.op("dve", lambda e: e.reduce_max(out=m[:, 0:1], in_=sc[s][:, 0:L + 1], axis=AX.X),
             reads=[b_sc[s]], writes=[b_sm[s]])
        P.op("dve", lambda e: e.tensor_scalar_mul(out=m[:, 1:2], in0=m[:, 0:1], scalar1=-0.125),
             reads=[b_sm[s]], writes=[b_sm[s]])

    def S2(n):
        i, r, kb0, nloc, nk, mo, L = geo(n)
        s, p = n % NS, n % 2
        nt = nloc + 2
        m = sm[s]
        P.op("act", lambda e: e.activation(out=pb[s][:, 0:L + 1], in_=sc[s][:, 0:L + 1], func=AF.Exp,
                                           scale=0.125, bias=m[:, 1:2], accum_out=m[:, 2:3]),
             reads=[b_sc[s], b_sm[s]], writes=[b_pb[s], b_sm[s]])
        P.op("dve", lambda e: e.reciprocal(out=m[:, 3:4], in_=m[:, 2:3]), reads=[b_sm[s]], writes=[b_sm[s]])

        def f(e):
            for t in range(nt):
                ins = e.transpose(out=pTt[p][:, t, :], in_=pb[s][:, t * 128:(t + 1) * 128], identity=ident[:])
            return ins
        P.op("pe", f, reads=[b_pb[s], b_c], writes=[b_pTt[p]])
        if n % 2:
            P.op("act", lambda e: e.activation(out=pTs[s][:, 0:nt, :], in_=pTt[p][:, 0:nt, :], func=AF.Copy),
                 reads=[b_pTt[p]], writes=[b_pTs[s]])
        else:
            P.op("dve", lambda e: e.tensor_copy(out=pTs[s][:, 0:nt, :], in_=pTt[p][:, 0:nt, :]),
                 reads=[b_pTt[p]], writes=[b_pTs[s]])

    def S3(n):
        i, r, kb0, nloc, nk, mo, L = geo(n)
        s, p = n % NS, n % 2
        nt = nloc + 2
        oi = i % 2
        m = sm[s]

        def f(e):
            for t in range(nt):
                vb = kb0 + t if t < nloc else nblk + (t - nloc)
                ins = e.matmul(ops[p][:, 0:64], lhsT=pTs[s][:, t, :], rhs=V[:, vb, :], start=(t == 0), stop=(t == nt - 1))
            return ins
        P.op("pe", f, reads=[b_pTs[s], b_c], writes=[b_ops[p]])
        P.op("act", lambda e: e.activation(out=ot[oi][:, r * 64:(r + 1) * 64], in_=ops[p][:, 0:64], func=AF.Copy,
                                           scale=m[:, 3:4]),
             reads=[b_ops[p], b_sm[s]], cowrites=[b_ot[oi]])
        if r == 7:
            P.op("sp", lambda e: e.dma_start(out=o_ap[i * 128:(i + 1) * 128, :], in_=ot[oi][:]),
                 reads=[b_ot[oi]], writes=[Buf()], dma=f"so{oi}")
            b_ot[oi].w = []

    for k in range(N + 2):
        if k < N:
            S1(k)
        if 0 <= k - 1 < N:
            S2(k - 1)
        if 0 <= k - 2 < N:
            S3(k - 2)
    P.finish()
    P.emit()
    return nc


def attn_mask():
    a = np.arange(128)[:, None]
    j = np.arange(384)[None, :]
    return np.where(np.abs(j - 128 - a) <= 128, 0.0, -1e30).astype(np.float32)


def build_LB(T, segs):
    nc = bass.Bass("TRN2", target_bir_lowering=False)
    nseg = len(segs)
    nvec = 9 * nseg + 11
    xn1 = dram_in(nc, "xn1", [D, T])
    gb = dram_in(nc, "gb", [1024, T])
    vp = dram_in(nc, "vp", [1024, T])
    vv = dram_in(nc, "vv", [1024, T])
    vn = dram_in(nc, "vn", [1024, T])
    yb = dram_in(nc, "yb", [1024, T])
    vecs_ap = dram_in(nc, "vecs", [128, nvec, NCH])
    wout = dram_in(nc, "wout", [16, 128, NCH, 128])
    wg2, wu2, wd2 = ffn_w_in(nc, "2")
    wg3, wu3, wd3 = ffn_w_in(nc, "3")
    wqkv = dram_in(nc, "wqkv", [20, 128, NCH, 128])
    wperm = dram_in(nc, "wperm", [18, 128, NCH, 128])
    cos_ap = dram_in(nc, "cosT", [128, T])
    sin_ap = dram_in(nc, "sinT", [128, T])
    xn4 = dram_out(nc, "xn4", [D, T])
    qr = dram_out(nc, "qr", [D, T])
    kr = dram_out(nc, "kr", [256, T])
    vo = dram_out(nc, "vo", [256, T])
    P = Prog(nc)
    R = RL(P, T, segs, vecs_ap, nvec)
    V = lambda s, k: 9 * s + k
    L0 = 9 * nseg
    for s in range(nseg):
        R.derive(V(s, 2), add=1.0)
        R.derive(V(s, 3), mul=0.5)
        R.derive(V(s, 5), add=1.0)
        R.derive(V(s, 6), mul=0.5)
        R.derive(V(s, 8), add=1.0)
    SL = lambda k: [V(s, k) for s in range(nseg)]
    b_in = Buf()
    b_o = Buf()

    def mix_fill(bi, c0, c1, w):
        tl = [R.sg[0], R.sg[1], R.zt[0], R.zt[1], R.tmp[0]]
        tb = [R.b_sg[0], R.b_sg[1], R.b_zt[0], R.b_zt[1], R.b_tmp[0]]
        for j in range(8):
            for k, src in enumerate((gb, vp, vv, vn, yb)):
                P.op("sp", lambda e, k=k, src=src, j=j: e.dma_start(out=tl[k][:, :w], in_=src[j * 128:(j + 1) * 128, c0:c1]),
                     writes=[tb[k]], dma=f"m{k}")
            P.op("dve", lambda e, j=j: e.tensor_scalar_mul(out=tl[1][:, :w], in0=tl[1][:, :w], scalar1=R.vc(L0 + 8, j)),
                 reads=[R.b_vecs], writes=[tb[1]])
            P.op("dve", lambda e, j=j: e.scalar_tensor_tensor(out=tl[1][:, :w], in0=tl[2][:, :w], scalar=R.vc(L0 + 9, j),
                                                              in1=tl[1][:, :w], op0=ALU.mult, op1=ALU.add),
                 reads=[R.b_vecs, tb[2]], writes=[tb[1]])
            P.op("dve", lambda e, j=j: e.scalar_tensor_tensor(out=tl[1][:, :w], in0=tl[3][:, :w], scalar=R.vc(L0 + 10, j),
                                                              in1=tl[1][:, :w], op0=ALU.mult, op1=ALU.add),
                 reads=[R.b_vecs, tb[3]], writes=[tb[1]])
            P.op("dve", lambda e, j=j: e.tensor_tensor(out=R.hb[:, j, c0:c1], in0=tl[1][:, :w], in1=tl[0][:, :w], op=ALU.mult),
                 reads=[tb[0], tb[1]], cowrites=R.W_HB)
            P.op("act", lambda e, j=j: e.activation(out=R.hb[:, 8 + j, c0:c1], in_=tl[4][:, :w], func=AF.Copy),
                 reads=[tb[4]], cowrites=R.W_HB)

    for bi, (c0, c1) in enumerate(R.blocks):
        mix_fill(bi, c0, c1, c1 - c0)
    R.phase2("proj", wout, ("ap", xn1, b_in), SL(0), ln=(L0 + 2, L0 + 3), mod=(SL(1), SL(2)))
    R.load_hb()
    R.phase1(wg2, wu2)
    R.phase2("ffn", wd2, ("xs",), SL(3), ln=(L0 + 4, L0 + 5), mod=(SL(4), SL(5)))
    R.load_hb()
    R.phase1(wg3, wu3)
    R.phase2("ffn", wd3, ("xs",), SL(6), ln=(L0 + 6, L0 + 7), mod=(SL(7), SL(8)), out_ap=xn4)
    R.load_hb()
    rt = R.xn2[0][:, :, :].rearrange("p c t -> p (c t)")
    cosT, sinT = rt[:, 0:T], rt[:, T:2 * T]
    P.op("sp", lambda e: e.dma_start(out=cosT, in_=cos_ap), writes=R.b_xn2[0], dma="r0")
    P.op("sp", lambda e: e.dma_start(out=sinT, in_=sin_ap), cowrites=R.b_xn2[0], dma="r1")
    b_rope = R.b_xn2[0][0]
    ld = lambda g: (R.load_wa(wqkv, g), R.load_wa(wperm, g) if g < 18 else None)
    nxt = ld(0)
    for g in range(20):
        iw, ip = nxt
        if g + 1 < 20:
            nxt = ld(g + 1)
        for bi, (c0, c1) in enumerate(R.blocks):
            w = c1 - c0
            pb = R.pp_i
            R.pp_i ^= 1
            R.mm(R.pg[pb], R.b_pg[pb], R.wa[iw], R.b_wa[iw], NCH, R.hb, c0, c1, [R.b_hb])
            i = R.zt_i
            R.zt_i ^= 1
            zt, tmp, sg, pg, pu = R.zt[i], R.tmp[i], R.sg[pb], R.pg[pb], R.pu[pb]
            if g < 18:
                R.mm(R.pu[pb], R.b_pu[pb], R.wa[ip], R.b_wa[ip], NCH, R.hb, c0, c1, [R.b_hb])
                P.op("dve", lambda e, sg=sg, pg=pg, w=w, c0=c0, c1=c1: e.tensor_tensor(
                    out=sg[:, :w], in0=pg[:, :w], in1=cosT[:, c0:c1], op=ALU.mult),
                    reads=[R.b_pg[pb], b_rope], writes=[R.b_sg[pb]])
                P.op("dve", lambda e, tmp=tmp, pu=pu, w=w, c0=c0, c1=c1: e.tensor_tensor(
                    out=tmp[:, :w], in0=pu[:, :w], in1=sinT[:, c0:c1], op=ALU.mult),
                    reads=[R.b_pu[pb], b_rope], writes=[R.b_tmp[i]])
                P.op("dve", lambda e, zt=zt, sg=sg, tmp=tmp, w=w: e.tensor_tensor(
                    out=zt[:, :w], in0=sg[:, :w], in1=tmp[:, :w], op=ALU.add),
                    reads=[R.b_sg[pb], R.b_tmp[i]], writes=[R.b_zt[i]])
                dst, row = (qr, g * 128) if g < 16 else (kr, (g - 16) * 128)
            else:
                P.op("act", lambda e, zt=zt, pg=pg, w=w: e.activation(out=zt[:, :w], in_=pg[:, :w], func=AF.Copy),
                     reads=[R.b_pg[pb]], writes=[R.b_zt[i]])
                dst, row = vo, (g - 18) * 128
            R.store(dst, b_o, row, c0, c1, zt, R.b_zt[i])
    P.finish()
    P.emit()
    return nc


def build_LD(T):
    nc = bass.Bass("TRN2", target_bir_lowering=False)
    segs = [(0, T)]
    nvec = 10
    xn4 = dram_in(nc, "xn4", [D, T])
    oT = dram_in(nc, "oT", [D, T])
    vecs_ap = dram_in(nc, "vecs", [128, nvec, NCH])
    wout = dram_in(nc, "wout", [16, 128, NCH, 128])
    wg, wu, wd = ffn_w_in(nc, "")
    out = dram_out(nc, "out", [D, T])
    P = Prog(nc)
    R = RL(P, T, segs, vecs_ap, nvec)
    R.derive(2, add=1.0)
    R.derive(3, mul=0.5)
    b_in = Buf()
    for bi, (c0, c1) in enumerate(R.blocks):
        P.op("pool", lambda e, c0=c0, c1=c1: e.dma_start(out=R.hb[:, :, c0:c1], in_=chunked(oT)[:, :, c0:c1]),
             cowrites=R.W_HB, dma="oin")
    R.phase2("proj", wout, ("ap", xn4, b_in), [0], ln=(6, 7), mod=([1], [2]))
    R.load_hb()
    R.phase1(wg, wu)
    R.phase2("ffn", wd, ("xs",), [3], ln=(8, 9), mod=None, out_ap=out)
    P.finish()
    P.emit()
    return nc


def lay(v):
    v = np.asarray(v, dtype=np.float32)
    if v.shape[0] < D:
        v = np.concatenate([v, np.zeros(D - v.shape[0], np.float32)])
    return v.reshape(NCH, 128).T


def rope_tables(pos):
    nf = 16
    inv = np.power(10000.0, -np.arange(nf, dtype=np.float64) / nf)
    row = (pos // 64).astype(np.float64)
    col = (pos % 64).astype(np.float64)
    d = np.arange(64)
    axis, part, f = d // 32, (d % 32) // 16, d % 16
    ang = np.where(axis[:, None] == 0, row[None, :], col[None, :]) * inv[f][:, None]
    c = np.cos(ang)
    s = np.sin(ang) * np.where(part == 0, -1.0, 1.0)[:, None]
    return np.concatenate([c, c], 0).astype(np.float32), np.concatenate([s, s], 0).astype(np.float32)


def rope_perm():
    d = np.arange(64)
    part = (d % 32) // 16
    p = np.where(part == 0, d + 16, d - 16)
    cols = np.concatenate([h * 64 + p for h in range(36)])
    return cols


_CACHE = {}


def _prog(key, fn):
    if key not in _CACHE:
        _CACHE[key] = fn()
    return _CACHE[key]


def _run(nc, ins):
    res = run_bass_kernel_spmd(nc, ins, core_ids=list(range(NCORE)))
    return res.results


def kernel(x, c, ctx, c_ctx, w_mod, b_mod, ln_g, ln_b, ffn_w_gate, ffn_w_up, ffn_w_down,
           ab_w_in, ab_conv, ab_w_out, attn_w_in, attn_sink, attn_w_out):
    f32 = lambda a: np.ascontiguousarray(np.asarray(a), dtype=np.float32)
    x, c, ctx, c_ctx = f32(x), f32(c), f32(ctx), f32(c_ctx)
    w_mod, b_mod, ln_g, ln_b = f32(w_mod), f32(b_mod), f32(ln_g), f32(ln_b)
    ffn_w_gate, ffn_w_up, ffn_w_down = f32(ffn_w_gate), f32(ffn_w_up), f32(ffn_w_down)
    ab_w_in, ab_conv, ab_w_out = f32(ab_w_in), f32(ab_conv), f32(ab_w_out)
    attn_w_in, attn_sink, attn_w_out = f32(attn_w_in), f32(attn_sink), f32(attn_w_out)
    LT, CT = SEQ // 4, CTX // 4
    T = LT + CT
    segs = [(0, LT), (LT, T)]
    cores = [(r // 4, r % 4) for r in range(NCORE)]

    cv = np.ascontiguousarray(np.stack([lay(c[0]), lay(c[1]), lay(c_ctx)], axis=-1))
    wm = np.concatenate([w_mod[0], w_mod[1]], axis=1)
    bm = b_mod.reshape(-1)
    ins = []
    for r in range(NCORE):
        sl = slice(r * MODC, (r + 1) * MODC)
        ins.append({"cv": cv, "w": np.ascontiguousarray(wm[:, sl]),
                    "b": np.ascontiguousarray(np.broadcast_to(bm[sl], (3, MODC)))})
    res = _run(_prog("L0", build_L0), ins)
    del wm
    mod = np.concatenate([res[r]["mod"] for r in range(NCORE)], axis=1).reshape(3, 2, 9, D)

    def mv(b, s, layer, k):
        return lay(mod[b if s == 0 else 2, layer, k])

    _ffw = {}

    def ffw(l, i, sfx):
        if (l, i) not in _ffw:
            _ffw[(l, i)] = (tile_w(ffn_w_gate[l, i], 128), tile_w(ffn_w_up[l, i], 128), tile_w(ffn_w_down[l, i], 128))
        a, b_, c_ = _ffw[(l, i)]
        return {"wg" + sfx: a, "wu" + sfx: b_, "wd" + sfx: c_}

    w_gb, w_gc = tile_w(ab_w_in[0][:, 0:1024], 128), tile_w(ab_w_in[0][:, 1024:2048], 128)
    w_xi, w_uf = tile_w(ab_w_in[0][:, 2048:3072], 128), tile_w(ab_w_in[0][:, 3072:4096], 128)

    ins = []
    for b, q in cores:
        xT = np.concatenate([x[b, q * LT:(q + 1) * LT].T, ctx[b, q * CT:(q + 1) * CT].T], axis=1)
        vecs = [mv(b, s, 0, k) for s in range(2) for k in range(5)] + [lay(ln_g[0, 0]), lay(ln_b[0, 0])]
        ins.append({"xT": np.ascontiguousarray(xT), "vecs": np.ascontiguousarray(np.stack(vecs, axis=1)),
                    **ffw(0, 0, ""), "w_gb": w_gb, "w_gc": w_gc, "w_xi": w_xi, "w_uf": w_uf})
    resA = _run(_prog("LA", lambda: build_LA(T, segs)), ins)

    def gather(res, name, nrow):
        lat = np.empty((2, nrow, SEQ), np.float32)
        cx = np.empty((2, nrow, CTX), np.float32)
        for r, (b, q) in enumerate(cores):
            a = res[r][name]
            lat[b, :, q * LT:(q + 1) * LT] = a[:, :LT]
            cx[b, :, q * CT:(q + 1) * CT] = a[:, LT:]
        return lat, cx

    UFl, UFc = gather(resA, "uf", 1024)
    VVl, VVc = gather(resA, "vv", 1024)

    ftw, fc, c64, c256 = fft_tables()
    ins = []
    for b, q in cores:
        gs = [2 * q, 2 * q + 1]
        xl = np.stack([UFl[b, g * 128:(g + 1) * 128].T for g in gs])
        xc = np.stack([UFc[b, g * 128:(g + 1) * 128] for g in gs])
        ins.append({"xl": np.ascontiguousarray(xl), "xc": np.ascontiguousarray(xc),
                    "ftw": ftw, "fc": fc, "c64": c64, "c256": c256})
    resF = _run(_prog("LF", build_LF), ins)
    YBl = np.empty((2, 1024, SEQ), np.float32)
    YBc = np.empty((2, 1024, CTX), np.float32)
    for r, (b, q) in enumerate(cores):
        for u in range(2):
            g = 2 * q + u
            YBl[b, g * 128:(g + 1) * 128] = resF[r]["yl"][u]
            YBc[b, g * 128:(g + 1) * 128] = resF[r]["yc"][u]
    del UFl, UFc

    def shift(a, k):
        o = np.zeros_like(a)
        if k > 0:
            o[..., k:] = a[..., :-k]
        else:
            o[..., :k] = a[..., -k:]
        return o

    VPl, VPc, VNl, VNc = shift(VVl, 1), shift(VVc, 1), shift(VVl, -1), shift(VVc, -1)

    def cols(lat, cx, b, q):
        return np.ascontiguousarray(np.concatenate([lat[b][:, q * LT:(q + 1) * LT], cx[b][:, q * CT:(q + 1) * CT]], axis=1))

    perm = rope_perm()
    wperm = tile_w(np.ascontiguousarray(attn_w_in[0][:, perm]), 128)
    wqkv_t = tile_w(attn_w_in[0], 128)
    wout_t = tile_w(ab_w_out[0], 128)
    _ffw.pop((0, 0), None)
    ins = []
    for r, (b, q) in enumerate(cores):
        vecs = []
        for s in range(2):
            vecs += [mv(b, s, 0, 5), mv(b, s, 0, 6), mv(b, s, 0, 7), mv(b, s, 0, 8),
                     mv(b, s, 1, 0), mv(b, s, 1, 1), mv(b, s, 1, 2), mv(b, s, 1, 3), mv(b, s, 1, 4)]
        vecs += [lay(ln_g[0, 0]), lay(ln_b[0, 0]), lay(ln_g[0, 1]), lay(ln_b[0, 1]), lay(ln_g[0, 2]), lay(ln_b[0, 2]),
                 lay(ln_g[1, 0]), lay(ln_b[1, 0]), lay(ab_conv[0, 0]), lay(ab_conv[0, 1]), lay(ab_conv[0, 2])]
        cl, sl_ = rope_tables(np.arange(q * LT, (q + 1) * LT))
        cosT = np.concatenate([cl, np.ones((128, CT), np.float32)], axis=1)
        sinT = np.concatenate([sl_, np.zeros((128, CT), np.float32)], axis=1)
        ins.append({"xn1": resA[r]["xn1"], "gb": resA[r]["gb"], "vp": cols(VPl, VPc, b, q), "vv": resA[r]["vv"],
                    "vn": cols(VNl, VNc, b, q), "yb": cols(YBl, YBc, b, q),
                    "vecs": np.ascontiguousarray(np.stack(vecs, axis=1)), "wout": wout_t,
                    **ffw(0, 1, "2"), **ffw(1, 0, "3"), "wqkv": wqkv_t, "wperm": wperm,
                    "cosT": np.ascontiguousarray(cosT), "sinT": np.ascontiguousarray(sinT)})
    resB = _run(_prog("LB", lambda: build_LB(T, segs)), ins)
    del resA, VPl, VNl, YBl, VVl
    Ql, _ = gather(resB, "qr", D)
    Kl, Kc = gather(resB, "kr", 256)
    Vl, Vc = gather(resB, "vo", 256)

    mask = attn_mask()
    ident = np.eye(128, dtype=np.float32)
    ins = []
    for r in range(NCORE):
        b, g = r // 4, r % 4
        qt = Ql[b, g * 512:(g + 1) * 512].reshape(2, 4, 64, SEQ).transpose(0, 2, 1, 3).reshape(128, 4, SEQ)
        k1 = np.concatenate([Kl[b, g * 64:(g + 1) * 64], Kc[b, g * 64:(g + 1) * 64]], axis=1)
        vl = Vl[b, g * 64:(g + 1) * 64].T.reshape(NBLK, 128, 64)
        vc_ = Vc[b, g * 64:(g + 1) * 64].T.reshape(2, 128, 64)
        vall = np.concatenate([vl, vc_], axis=0).transpose(1, 0, 2)
        ins.append({"qt": np.ascontiguousarray(qt), "kt": np.ascontiguousarray(np.concatenate([k1, k1], axis=0)),
                    "v": np.ascontiguousarray(vall), "mask": mask,
                    "sink": np.ascontiguousarray(np.broadcast_to(attn_sink[0, g * 8:(g + 1) * 8], (128, 8))),
                    "ident": ident})
    resC = _run(_prog("LC", build_LC), ins)
    del Ql
    O = np.empty((2, D, SEQ), np.float32)
    for r in range(NCORE):
        b, g = r // 4, r % 4
        O[b, g * 512:(g + 1) * 512] = resC[r]["o"].T
    del resC

    _ffw.clear()
    awout_t = tile_w(attn_w_out[0], 128)
    ins = []
    for r, (b, q) in enumerate(cores):
        vecs = [mv(b, 0, 1, 5), mv(b, 0, 1, 6), mv(b, 0, 1, 7), mv(b, 0, 1, 8),
                lay(ln_g[1, 0]), lay(ln_b[1, 0]), lay(ln_g[1, 1]), lay(ln_b[1, 1]), lay(ln_g[1, 2]), lay(ln_b[1, 2])]
        ins.append({"xn4": np.ascontiguousarray(resB[r]["xn4"][:, :LT]), "oT": np.ascontiguousarray(O[b][:, q * LT:(q + 1) * LT]),
                    "vecs": np.ascontiguousarray(np.stack(vecs, axis=1)), "wout": awout_t, **ffw(1, 1, "")})
    resD = _run(_prog("LD", lambda: build_LD(LT)), ins)
    out = np.empty((2, SEQ, D), np.float32)
    for r, (b, q) in enumerate(cores):
        out[b, q * LT:(q + 1) * LT] = resD[r]["out"].T
    return out
```

```python
import numpy as np
from contextlib import ExitStack
import concourse.bass as bass
import concourse.mybir as mybir
from concourse.bass_utils import run_bass_kernel_spmd

F32 = mybir.dt.float32
BF16 = mybir.dt.bfloat16
AF = mybir.ActivationFunctionType
ALU = mybir.AluOpType
AX = mybir.AxisListType

D = 2048
DFF = 5632
NCH = 16
FCH = 44
SEQ = 8192
CTX = 256
NCORE = 8
ALPHA = 4.0 ** 0.25
LN_EPS = 1e-5
TB = 512


class Buf:
    __slots__ = ("w", "r", "name")

    def __init__(self, name=""):
        self.w = []
        self.r = []
        self.name = name


class Ctr:
    LIMIT = 30000

    def __init__(self, P, name, step):
        self.P, self.name, self.step = P, name, step
        self.k = 0
        self.done = []
        self._new()

    def _new(self):
        self.sem = self.P.stack.enter_context(self.P.nc.semaphore(f"{self.name}_{self.k}"))
        self.k += 1
        self.val = 0

    def next(self):
        if self.val + self.step > self.LIMIT:
            self.done.append((self.sem, self.val))
            self._new()
        self.val += self.step
        return (self.sem, self.val)


class Eng:
    def __init__(self, P, name):
        self.name = name
        self.ops = []
        self.waited = {}
        self.ctr = Ctr(P, "e" + name, 1)


class Prog:
    def __init__(self, nc):
        self.nc = nc
        self.stack = ExitStack()
        self.engs = {n: Eng(self, n) for n in ("pe", "act", "dve", "pool", "sp")}
        self.dctr = {}
        self.fuzzy = {}
        self.n = 0

    def sb(self, name, shape, dt):
        return self.stack.enter_context(self.nc.sbuf_tensor("s_" + name, list(shape), dt))

    def ps(self, name, shape, dt=F32):
        return self.stack.enter_context(self.nc.psum_tensor("p_" + name, list(shape), dt))

    def op(self, eng, fn, reads=(), writes=(), dma=None, cowrites=()):
        E = self.engs[eng]
        deps = {}

        def add(tok):
            s, v = tok
            k = id(s)
            if k in self.fuzzy:
                v = max(v, self.fuzzy[k].val if self.fuzzy[k].sem is s else v)
            if k not in deps or deps[k][1] < v:
                deps[k] = (s, v)

        for b in reads:
            for t in b.w:
                add(t)
        for b in writes:
            for t in b.w:
                add(t)
            for t in b.r:
                add(t)
        for b in cowrites:
            for t in b.r:
                add(t)
        if dma is not None:
            if dma not in self.dctr:
                self.dctr[dma] = Ctr(self, "d" + dma, 16)
            ctr = self.dctr[dma]
            if dma.startswith("st") or dma.startswith("xo"):
                self.fuzzy[id(ctr.sem)] = ctr
        else:
            ctr = E.ctr
        waits = []
        for k, (s, v) in deps.items():
            if eng == "pe" and dma is None and s is E.ctr.sem:
                continue
            if E.waited.get(k, 0) >= v:
                continue
            E.waited[k] = v
            waits.append((s, v))
        tok = ctr.next()
        E.ops.append((waits, fn, tok[0], ctr.step))
        for b in reads:
            b.r.append(tok)
        for b in writes:
            b.w = [tok]
            b.r = []
        for b in cowrites:
            b.w.append(tok)
        self.n += 1
        return tok

    def finish(self):
        E = self.engs["sp"]
        waits = []
        for ctr in self.dctr.values():
            for s, v in ctr.done + [(ctr.sem, ctr.val)]:
                if v > 0 and E.waited.get(id(s), 0) < v:
                    waits.append((s, v))
        E.ops.append((waits, None, None, 0))

    def emit(self):
        nc = self.nc

        def mk(E):
            def run(e):
                for waits, fn, sem, step in E.ops:
                    for ws, wv in waits:
                        e.wait_ge(ws, wv)
                    if fn is not None:
                        fn(e).then_inc(sem, step)
            return run

        with nc.Block() as block:
            block.tensor(mk(self.engs["pe"]))
            block.scalar(mk(self.engs["act"]))
            block.vector(mk(self.engs["dve"]))
            block.gpsimd(mk(self.engs["pool"]))
            block.sync(mk(self.engs["sp"]))
        self.stack.close()


def blocks_of(T, tb=TB):
    bl = [(c, min(c + tb, T)) for c in range(0, T, tb)]
    if bl[-1][1] - bl[-1][0] < tb:
        bl = [bl[-1]] + bl[:-1]
    return bl


def chunked(ap):
    return ap.rearrange("(c p) t -> p c t", p=128)


class RL:
    def __init__(self, P, T, segs, vecs_ap, nvec):
        self.P, self.T, self.segs = P, T, segs
        self.blocks = blocks_of(T)
        nb = len(self.blocks)
        nc = P.nc
        self.xn2 = [P.sb(f"xn{i}", [128, NCH, TB], F32) for i in range(2)]
        self.big = P.sb("big", [128, max(NCH * T, FCH * TB)], BF16)
        self.hb = self.big[:, 0:NCH * T].rearrange("p (c t) -> p c t", c=NCH)
        self.ab = self.big[:, 0:FCH * TB].rearrange("p (c t) -> p c t", c=FCH)
        self.wa2 = [P.sb(f"wa{i}", [128, NCH, 256], BF16) for i in range(2)]
        self.wa = [self.wa2[i // 2][:, :, (i % 2) * 128:(i % 2 + 1) * 128] for i in range(4)]
        self.wap_i = 0
        self.wb = [P.sb(f"wb{i}", [128, FCH, 128], BF16) for i in range(2)]
        self.mu = P.sb("mu", [128, TB], F32)
        self.msq = P.sb("msq", [128, TB], F32)
        self.rstd = P.sb("rstd", [128, TB], F32)
        self.sg = [P.sb(f"sg{i}", [128, TB], F32) for i in range(2)]
        self.zt = [P.sb(f"zt{i}", [128, TB], F32) for i in range(2)]
        self.tmp = [P.sb(f"tmp{i}", [128, TB], F32) for i in range(2)]
        self.xr = [P.sb(f"xr{i}", [128, TB], F32) for i in range(4)]
        self.at = [P.sb(f"at{i}", [128, TB], BF16) for i in range(2)]
        self.zb = [P.sb(f"zb{i}", [128, TB], BF16) for i in range(2)]
        self.zq = [P.sb(f"zq{i}", [128, TB], BF16) for i in range(2)]
        self.ones = P.sb("ones", [128, 128], BF16)
        self.vecs = P.sb("vecs", [128, nvec, NCH], F32)
        self.s1 = P.ps("s1", [128, TB])
        self.s2 = P.ps("s2", [128, TB])
        self.pg = [P.ps(f"pg{i}", [128, TB]) for i in range(2)]
        self.pu = [P.ps(f"pu{i}", [128, TB]) for i in range(2)]
        self.py = [P.ps(f"py{i}", [128, TB]) for i in range(2)]
        self.xs = [nc.dram_tensor(f"xs{i}", [nb, 128, NCH, TB], F32).ap() for i in range(2)]
        self.hs = nc.dram_tensor("hs", [nb, 128, NCH, TB], BF16).ap()
        self.A = nc.dram_tensor("Asp", [nb, 128, FCH, TB], BF16).ap()
        B = Buf
        BL = lambda n: [B() for _ in range(n)]
        self.b_xn2 = [BL(NCH), BL(NCH)]
        self.b_hb = B("hb")
        self.b_abc = BL(FCH)
        self.b_wa, self.b_wb = BL(4), BL(2)
        self.b_mu, self.b_msq, self.b_rstd = B(), B(), B()
        self.b_sg, self.b_zt, self.b_tmp, self.b_xr, self.b_at = BL(2), BL(2), BL(2), BL(4), BL(2)
        self.b_zb, self.b_zq = BL(2), BL(2)
        self.b_s1, self.b_s2 = B(), B()
        self.b_pg, self.b_pu, self.b_py = BL(2), BL(2), BL(2)
        self.b_ones, self.b_vecs = B(), B()
        self.b_xs = [BL(nb), BL(nb)]
        self.b_hs = BL(nb)
        self.b_A = BL(nb)
        self.W_HB = [self.b_hb] + self.b_abc
        self.wa_i = self.wb_i = self.zt_i = self.xr_i = self.at_i = self.pp_i = self.zb_i = 0
        self.xs_cur = 0
        ones, vecs = self.ones, self.vecs
        P.op("dve", lambda e: e.memset(ones[:], 1.0), writes=[self.b_ones])
        P.op("sp", lambda e: e.dma_start(out=vecs[:], in_=vecs_ap), writes=[self.b_vecs], dma="vecs")

    def vc(self, v, c):
        return self.vecs[:, v, c:c + 1]

    def derive(self, v, mul=None, add=None):
        vecs = self.vecs
        if add is not None:
            self.P.op("dve", lambda e: e.tensor_scalar_add(out=vecs[:, v, :], in0=vecs[:, v, :], scalar1=float(add)),
                      reads=[self.b_vecs], writes=[self.b_vecs])
        if mul is not None:
            self.P.op("dve", lambda e: e.tensor_scalar_mul(out=vecs[:, v, :], in0=vecs[:, v, :], scalar1=float(mul)),
                      reads=[self.b_vecs], writes=[self.b_vecs])

    def segparts(self, c0, c1):
        out = []
        for si, (s0, s1) in enumerate(self.segs):
            a, b = max(c0, s0), min(c1, s1)
            if a < b:
                out.append((si, a - c0, b - c0))
        return out

    def first_prologue(self, x_ap, x_buf, vshift, vscale1):
        P = self.P
        hb = self.hb
        for bi, (c0, c1) in enumerate(self.blocks):
            w = c1 - c0
            s = bi % 2
            xn = self.xn2[s]
            P.op("sp", lambda e, xn=xn, w=w, c0=c0, c1=c1: e.dma_start(out=xn[:, :, :w], in_=chunked(x_ap)[:, :, c0:c1]),
                 reads=[x_buf], writes=self.b_xn2[s], dma=f"xin{s}")
            for si, a, b in self.segparts(c0, c1):
                for c in range(NCH):
                    P.op("act", lambda e, xn=xn, c=c, si=si, a=a, b=b, c0=c0: e.activation(
                        out=hb[:, c, c0 + a:c0 + b], in_=xn[:, c, a:b], func=AF.Identity,
                        scale=self.vc(vscale1[si], c), bias=self.vc(vshift[si], c)),
                        reads=[self.b_vecs, self.b_xn2[s][c]], cowrites=self.W_HB)

    def load_hb(self, only=None):
        hb, hs = self.hb, self.hs
        for bi, (c0, c1) in enumerate(self.blocks):
            if only is not None and bi not in only:
                continue
            w = c1 - c0
            self.P.op("sp", lambda e, bi=bi, c0=c0, c1=c1, w=w: e.dma_start(out=hb[:, :, c0:c1], in_=hs[bi, :, :, :w]),
                      reads=[self.b_hs[bi]], cowrites=self.W_HB, dma="ldh")
            self.b_hs[bi].w = []

    def load_wa(self, W_ap, g):
        i = self.wa_i
        self.wa_i = (i + 1) % 4
        t = self.wa[i]
        self.P.op("pool", lambda e: e.dma_start(out=t, in_=W_ap[g]), writes=[self.b_wa[i]], dma=f"wa{i}")
        return i

    def load_wa_pair(self, W2_ap, g):
        p = self.wap_i
        self.wap_i ^= 1
        t = self.wa2[p]
        self.P.op("pool", lambda e: e.dma_start(out=t[:], in_=W2_ap[g]), writes=[self.b_wa[2 * p], self.b_wa[2 * p + 1]],
                  dma=f"wp{p}")
        self.wa_i = (2 * p + 2) % 4
        return 2 * p, 2 * p + 1

    def load_wb(self, W_ap, g):
        i = self.wb_i
        self.wb_i = (i + 1) % 2
        t = self.wb[i]
        self.P.op("pool", lambda e: e.dma_start(out=t[:], in_=W_ap[g]), writes=[self.b_wb[i]], dma=f"wb{i}")
        return i

    def mm(self, dst, dst_buf, wt, wbuf, kc, X, lo, hi, xbufs):
        w = hi - lo

        def f(e):
            for k in range(kc):
                ins = e.matmul(dst[:, :w], lhsT=wt[:, k, :], rhs=X[:, k, lo:hi], start=(k == 0), stop=(k == kc - 1))
            return ins
        self.P.op("pe", f, reads=[wbuf] + list(xbufs), writes=[dst_buf])

    def store(self, dst_ap, dst_buf, row0, c0, c1, tile, tbuf):
        w = c1 - c0
        self.P.op("sp", lambda e: e.dma_start(out=dst_ap[row0:row0 + 128, c0:c1], in_=tile[:, :w]),
                  reads=[tbuf], cowrites=[dst_buf], dma="st")

    def phase1(self, wgu):
        P = self.P
        hb, A = self.hb, self.A
        ld = lambda fc: self.load_wa_pair(wgu, fc)
        nxt = ld(0)
        for fc in range(FCH):
            ig, iu = nxt
            if fc + 1 < FCH:
                nxt = ld(fc + 1)
            for bi, (c0, c1) in enumerate(self.blocks):
                w = c1 - c0
                pb = self.pp_i
                self.pp_i ^= 1
                self.mm(self.pg[pb], self.b_pg[pb], self.wa[ig], self.b_wa[ig], NCH, hb, c0, c1, [self.b_hb])
                self.mm(self.pu[pb], self.b_pu[pb], self.wa[iu], self.b_wa[iu], NCH, hb, c0, c1, [self.b_hb])
                sg, pg, pu = self.sg[pb], self.pg[pb], self.pu[pb]
                ai = self.at_i
                self.at_i ^= 1
                at = self.at[ai]
                P.op("act", lambda e, sg=sg, pg=pg, w=w: e.activation(out=sg[:, :w], in_=pg[:, :w], func=AF.Silu),
                     reads=[self.b_pg[pb]], writes=[self.b_sg[pb]])
                P.op("dve", lambda e, sg=sg, pu=pu, at=at, w=w: e.tensor_tensor(out=at[:, :w], in0=sg[:, :w], in1=pu[:, :w],
                                                                               op=ALU.mult),
                     reads=[self.b_sg[pb], self.b_pu[pb]], writes=[self.b_at[ai]])
                P.op("sp", lambda e, at=at, bi=bi, fc=fc, w=w: e.dma_start(out=A[bi, :, fc, :w], in_=at[:, :w]),
                     reads=[self.b_at[ai]], cowrites=[self.b_A[bi]], dma="sta")

    def phase2(self, kind, W, res, vgate, ln, mod=None, out_ap=None, reload_hb=False):
        P = self.P
        hb, ab, A, ones = self.hb, self.ab, self.A, self.ones
        s1, s2, mu, msq, rstd = self.s1, self.s2, self.mu, self.msq, self.rstd
        vg, vb = ln
        xs_in = self.xs[self.xs_cur]
        b_xs_in = self.b_xs[self.xs_cur]
        xs_out = self.xs[self.xs_cur ^ 1]
        b_xs_out = self.b_xs[self.xs_cur ^ 1]
        hs = self.hs
        pending = []

        def ln_apply(bi, c0, c1, s, c):
            w = c1 - c0
            xn = self.xn2[s]
            bx = self.b_xn2[s][c]
            P.op("dve", lambda e: e.tensor_tensor(out=xn[:, c, :w], in0=xn[:, c, :w], in1=mu[:, :w], op=ALU.subtract),
                 reads=[self.b_mu], writes=[bx])
            P.op("dve", lambda e: e.tensor_tensor(out=xn[:, c, :w], in0=xn[:, c, :w], in1=rstd[:, :w], op=ALU.mult),
                 reads=[self.b_rstd], writes=[bx])
            P.op("act", lambda e: e.activation(out=xn[:, c, :w], in_=xn[:, c, :w], func=AF.Identity,
                                               scale=self.vc(vg, c), bias=self.vc(vb, c)),
                 reads=[self.b_vecs], writes=[bx])
            if mod is not None:
                ai = self.at_i
                self.at_i ^= 1
                at = self.at[ai]
                for si, a, b in self.segparts(c0, c1):
                    P.op("act", lambda e, si=si, a=a, b=b: e.activation(
                        out=at[:, a:b], in_=xn[:, c, a:b], func=AF.Identity,
                        scale=self.vc(mod[1][si], c), bias=self.vc(mod[0][si], c)),
                        reads=[self.b_vecs, bx], cowrites=[self.b_at[ai]])
                P.op("sp", lambda e: e.dma_start(out=hs[bi, :, c, :w], in_=at[:, :w]),
                     reads=[self.b_at[ai]], cowrites=[self.b_hs[bi]], dma="sth")
                self.b_at[ai].w = []
            if c == NCH - 1:
                if out_ap is not None:
                    P.op("sp", lambda e: e.dma_start(out=chunked(out_ap)[:, :, c0:c1], in_=xn[:, :, :w]),
                         reads=self.b_xn2[s], writes=[Buf()], dma=f"xo{s}")
                else:
                    P.op("sp", lambda e: e.dma_start(out=xs_out[bi, :, :, :w], in_=xn[:, :, :w]),
                         reads=self.b_xn2[s], writes=[b_xs_out[bi]], dma=f"xo{s}")

        seq = [(bi_, dc_) for bi_ in range(len(self.blocks)) for dc_ in range(NCH)]

        def xr_load(k):
            if k >= len(seq):
                return
            bi_, dc_ = seq[k]
            c0_, c1_ = self.blocks[bi_]
            w_ = c1_ - c0_
            xi_ = k % 4
            xr_ = self.xr[xi_]
            if res[0] == "xs":
                rv, rb = xs_in[bi_, :, dc_, :w_], b_xs_in[bi_]
            else:
                rv, rb = res[1][dc_ * 128:(dc_ + 1) * 128, c0_:c1_], res[2]
            P.op("sp", lambda e: e.dma_start(out=xr_[:, :w_], in_=rv), reads=[rb], writes=[self.b_xr[xi_]], dma=f"xr{xi_}")

        for k0 in range(3):
            xr_load(k0)
        scaled = True if res[0] == "xs" else res[3]
        kpos = 0
        for bi, (c0, c1) in enumerate(self.blocks):
            w = c1 - c0
            s = bi % 2
            xn = self.xn2[s]
            if kind == "ffn":
                P.op("sp", lambda e, bi=bi, w=w: e.dma_start(out=ab[:, :, :w], in_=A[bi, :, :, :w]),
                     reads=[self.b_A[bi]], writes=self.W_HB, dma="lda")
                self.b_A[bi].w = []
                ldw = lambda dc: self.load_wb(W, dc)
                nxt = ldw(0)
            else:
                nxtp = self.load_wa_pair(W, 0)
            stats_prev = None
            for dc in range(NCH):
                if kind == "ffn":
                    iw = nxt
                    if dc + 1 < NCH:
                        nxt = ldw(dc + 1)
                else:
                    if dc % 2 == 0:
                        curp = nxtp
                        if dc + 2 < NCH:
                            nxtp = self.load_wa_pair(W, dc // 2 + 1)
                    iw = curp[dc % 2]
                pb = dc % 2
                xi = kpos % 4
                xr = self.xr[xi]
                if kind == "ffn":
                    self.mm(self.py[pb], self.b_py[pb], self.wb[iw], self.b_wb[iw], FCH, ab, 0, w, self.b_abc)
                else:
                    self.mm(self.py[pb], self.b_py[pb], self.wa[iw], self.b_wa[iw], NCH, hb, c0, c1, [self.b_hb])
                py = self.py[pb]
                bx = self.b_xn2[s][dc]
                if scaled:
                    first = True
                    for si, a, b in self.segparts(c0, c1):
                        kw = dict(writes=[bx]) if first else dict(cowrites=[bx])
                        first = False
                        P.op("dve", lambda e, si=si, a=a, b=b, xn=xn, xr=xr, py=py, dc=dc: e.scalar_tensor_tensor(
                            out=xn[:, dc, a:b], in0=py[:, a:b], scalar=self.vc(vgate[si], dc), in1=xr[:, a:b],
                            op0=ALU.mult, op1=ALU.add),
                            reads=[self.b_py[pb], self.b_vecs, self.b_xr[xi]], **kw)
                else:
                    ti = self.zt_i
                    self.zt_i ^= 1
                    tmp = self.tmp[ti]
                    for si, a, b in self.segparts(c0, c1):
                        P.op("act", lambda e, si=si, a=a, b=b, tmp=tmp, py=py, dc=dc: e.activation(
                            out=tmp[:, a:b], in_=py[:, a:b], func=AF.Copy, scale=self.vc(vgate[si], dc)),
                            reads=[self.b_py[pb], self.b_vecs], cowrites=[self.b_tmp[ti]])
                    P.op("dve", lambda e, xn=xn, xr=xr, tmp=tmp, dc=dc, w=w: e.scalar_tensor_tensor(
                        out=xn[:, dc, :w], in0=xr[:, :w], scalar=ALPHA, in1=tmp[:, :w], op0=ALU.mult, op1=ALU.add),
                        reads=[self.b_xr[xi], self.b_tmp[ti]], writes=[bx])
                    self.b_tmp[ti].w = []
                xr_load(kpos + 3)
                kpos += 1
                zi = self.zb_i
                self.zb_i ^= 1
                zb, zq = self.zb[zi], self.zq[zi]
                P.op("act", lambda e, xn=xn, zb=zb, dc=dc, w=w: e.activation(out=zb[:, :w], in_=xn[:, dc, :w], func=AF.Copy),
                     reads=[bx], writes=[self.b_zb[zi]])
                P.op("pool" if kind == "proj" else "dve", lambda e, xn=xn, zq=zq, dc=dc, w=w: e.tensor_tensor(out=zq[:, :w], in0=xn[:, dc, :w],
                                                                               in1=xn[:, dc, :w], op=ALU.mult),
                     reads=[bx], writes=[self.b_zq[zi]])

                def stats(e, zb=zb, zq=zq, dc=dc, w=w):
                    e.matmul(s1[:, :w], lhsT=ones[:], rhs=zb[:, :w], start=(dc == 0), stop=(dc == NCH - 1))
                    return e.matmul(s2[:, :w], lhsT=ones[:], rhs=zq[:, :w], start=(dc == 0), stop=(dc == NCH - 1))
                this_stats = (stats, [self.b_zb[zi], self.b_zq[zi], self.b_ones])
                if stats_prev is not None:
                    P.op("pe", stats_prev[0], reads=stats_prev[1], cowrites=[self.b_s1, self.b_s2])
                stats_prev = this_stats
                if pending:
                    pending.pop(0)()
            P.op("pe", stats_prev[0], reads=stats_prev[1], cowrites=[self.b_s1, self.b_s2])
            while pending:
                pending.pop(0)()
            P.op("dve", lambda e, w=w: e.tensor_scalar_mul(out=mu[:, :w], in0=s1[:, :w], scalar1=1.0 / D),
                 reads=[self.b_s1], writes=[self.b_mu])
            P.op("dve", lambda e, w=w: e.tensor_tensor(out=msq[:, :w], in0=mu[:, :w], in1=mu[:, :w], op=ALU.mult),
                 reads=[self.b_mu], writes=[self.b_msq])
            P.op("dve", lambda e, w=w: e.scalar_tensor_tensor(out=rstd[:, :w], in0=s2[:, :w], scalar=1.0 / D, in1=msq[:, :w],
                                                              op0=ALU.mult, op1=ALU.subtract),
                 reads=[self.b_s2, self.b_msq], writes=[self.b_rstd])
            P.op("dve", lambda e, w=w: e.tensor_scalar_add(out=rstd[:, :w], in0=rstd[:, :w], scalar1=LN_EPS),
                 reads=[self.b_rstd], writes=[self.b_rstd])
            P.op("act", lambda e, w=w: e.activation(out=rstd[:, :w], in_=rstd[:, :w], func=AF.Sqrt),
                 reads=[self.b_rstd], writes=[self.b_rstd])
            P.op("dve", lambda e, w=w: e.reciprocal(out=rstd[:, :w], in_=rstd[:, :w]),
                 reads=[self.b_rstd], writes=[self.b_rstd])
            self.b_s1.w, self.b_s2.w = [], []
            for c in range(NCH):
                pending.append(lambda bi=bi, c0=c0, c1=c1, s=s, c=c: ln_apply(bi, c0, c1, s, c))
        nb = len(self.blocks)
        if reload_hb:
            self.load_hb(only=range(nb - 1))
        while pending:
            pending.pop(0)()
        if reload_hb:
            self.load_hb(only=[nb - 1])
        if out_ap is None:
            self.xs_cur ^= 1


def tile_w(W, gcols):
    K, N = W.shape
    return np.ascontiguousarray(W.reshape(K // 128, 128, N // gcols, gcols).transpose(2, 1, 0, 3))


def dram_in(nc, name, shape, dt=F32):
    return nc.dram_tensor(name, list(shape), dt, kind="ExternalInput").ap()


def dram_out(nc, name, shape, dt=F32):
    return nc.dram_tensor(name, list(shape), dt, kind="ExternalOutput").ap()


def ffn_w_in(nc, sfx):
    return (dram_in(nc, "wgu" + sfx, [FCH, 128, NCH, 256]), dram_in(nc, "wd" + sfx, [NCH, 128, FCH, 128]))


MODC = 2 * 9 * D // NCORE


def build_L0():
    nc = bass.Bass("TRN2", target_bir_lowering=False)
    cv_ap = dram_in(nc, "cv", [128, NCH, 3])
    w_ap = dram_in(nc, "w", [D, MODC])
    b_ap = dram_in(nc, "b", [3, MODC])
    o_ap = dram_out(nc, "mod", [3, MODC])
    P = Prog(nc)
    cv = P.sb("cv", [128, NCH, 3], F32)
    cb = P.sb("cb", [128, NCH, 3], BF16)
    bt = P.sb("bt", [3, MODC], F32)
    ot = P.sb("ot", [3, MODC], F32)
    wt = [P.sb(f"w{i}", [128, NCH, 512], BF16) for i in range(2)]
    ps = [P.ps(f"ps{i}", [128, 512]) for i in range(2)]
    b_cv, b_cb, b_bt, b_ot = Buf(), Buf(), Buf(), Buf()
    b_w, b_ps = [Buf(), Buf()], [Buf(), Buf()]
    P.op("sp", lambda e: e.dma_start(out=cv[:], in_=cv_ap), writes=[b_cv], dma="cv")
    P.op("sp", lambda e: e.dma_start(out=bt[:], in_=b_ap), writes=[b_bt], dma="bt")
    P.op("act", lambda e: e.activation(out=cb[:], in_=cv[:], func=AF.Silu), reads=[b_cv], writes=[b_cb])
    wv = w_ap.rearrange("(k p) f -> p k f", p=128)
    for t in range(MODC // 512):
        i = t % 2
        P.op("pool", lambda e, t=t, i=i: e.dma_start(out=wt[i][:], in_=wv[:, :, t * 512:(t + 1) * 512]),
             writes=[b_w[i]], dma=f"w{i}")

        def f(e, i=i):
            for k in range(NCH):
                ins = e.matmul(ps[i][0:3, :], lhsT=cb[:, k, :], rhs=wt[i][:, k, :], start=(k == 0), stop=(k == NCH - 1))
            return ins
        P.op("pe", f, reads=[b_cb, b_w[i]], writes=[b_ps[i]])
        P.op("dve", lambda e, t=t, i=i: e.tensor_tensor(out=ot[:, t * 512:(t + 1) * 512], in0=ps[i][0:3, :],
                                                        in1=bt[:, t * 512:(t + 1) * 512], op=ALU.add),
             reads=[b_ps[i], b_bt], writes=[b_ot])
    P.op("sp", lambda e: e.dma_start(out=o_ap, in_=ot[:]), reads=[b_ot], writes=[Buf()], dma="st")
    P.finish()
    P.emit()
    return nc


def simple_proj(R, W, ng, dst, b_o):
    P = R.P
    nxt = R.load_wa(W, 0)
    for g in range(ng):
        iw = nxt
        if g + 1 < ng:
            nxt = R.load_wa(W, g + 1)
        for bi, (c0, c1) in enumerate(R.blocks):
            w = c1 - c0
            pb = R.pp_i
            R.pp_i ^= 1
            R.mm(R.pg[pb], R.b_pg[pb], R.wa[iw], R.b_wa[iw], NCH, R.hb, c0, c1, [R.b_hb])
            i = R.zt_i
            R.zt_i ^= 1
            zt, pg = R.zt[i], R.pg[pb]
            P.op("act", lambda e, pg=pg, zt=zt, w=w: e.activation(out=zt[:, :w], in_=pg[:, :w], func=AF.Copy),
                 reads=[R.b_pg[pb]], writes=[R.b_zt[i]])
            R.store(dst, b_o, g * 128, c0, c1, zt, R.b_zt[i])


def build_LA(T, segs):
    nc = bass.Bass("TRN2", target_bir_lowering=False)
    nseg = len(segs)
    nvec = 5 * nseg + 2
    x_ap = dram_in(nc, "xT", [D, T])
    vecs_ap = dram_in(nc, "vecs", [128, nvec, NCH])
    wgu, wd = ffn_w_in(nc, "")
    w_gb = dram_in(nc, "w_gb", [8, 128, NCH, 128])
    w_gc = dram_in(nc, "w_gc", [8, 128, NCH, 128])
    w_xi = dram_in(nc, "w_xi", [8, 128, NCH, 128])
    w_uf = dram_in(nc, "w_uf", [8, 128, NCH, 128])
    xn1 = dram_out(nc, "xn1", [D, T])
    gb = dram_out(nc, "gb", [1024, T])
    vv = dram_out(nc, "vv", [1024, T])
    uf = dram_out(nc, "uf", [1024, T])
    P = Prog(nc)
    R = RL(P, T, segs, vecs_ap, nvec)
    V = lambda s, k: 5 * s + k
    SL = lambda k: [V(s, k) for s in range(nseg)]
    VG, VB = 5 * nseg, 5 * nseg + 1
    for s in range(nseg):
        R.derive(V(s, 1), add=1.0)
        R.derive(V(s, 2), mul=0.5)
        R.derive(V(s, 4), add=1.0)
        R.derive(V(s, 4), mul=1.0 / ALPHA)
    R.derive(VG, mul=ALPHA)
    R.derive(VB, mul=ALPHA)
    b_x = Buf()
    b_o = Buf()
    R.first_prologue(x_ap, b_x, SL(0), SL(1))
    R.phase1(wgu)
    R.phase2("ffn", wd, ("ap", x_ap, b_x, False), SL(2), ln=(VG, VB), mod=(SL(3), SL(4)), out_ap=xn1, reload_hb=True)
    simple_proj(R, w_gb, 8, gb, b_o)
    ld = lambda jj: (R.load_wa(w_gc, jj), R.load_wa(w_xi, jj))
    nxt = ld(0)
    for jj in range(8):
        ic, ix = nxt
        if jj + 1 < 8:
            nxt = ld(jj + 1)
        for bi, (c0, c1) in enumerate(R.blocks):
            w = c1 - c0
            pb = R.pp_i
            R.pp_i ^= 1
            R.mm(R.pg[pb], R.b_pg[pb], R.wa[ic], R.b_wa[ic], NCH, R.hb, c0, c1, [R.b_hb])
            R.mm(R.pu[pb], R.b_pu[pb], R.wa[ix], R.b_wa[ix], NCH, R.hb, c0, c1, [R.b_hb])
            i = R.zt_i
            R.zt_i ^= 1
            zt, sg, pg, pu = R.zt[i], R.sg[pb], R.pg[pb], R.pu[pb]
            P.op("act", lambda e, pg=pg, sg=sg, w=w: e.activation(out=sg[:, :w], in_=pg[:, :w], func=AF.Copy),
                 reads=[R.b_pg[pb]], writes=[R.b_sg[pb]])
            P.op("dve", lambda e, pu=pu, sg=sg, zt=zt, w=w: e.tensor_tensor(out=zt[:, :w], in0=pu[:, :w], in1=sg[:, :w],
                                                                           op=ALU.mult),
                 reads=[R.b_pu[pb], R.b_sg[pb]], writes=[R.b_zt[i]])
            R.store(vv, b_o, jj * 128, c0, c1, zt, R.b_zt[i])
    simple_proj(R, w_uf, 8, uf, b_o)
    P.finish()
    P.emit()
    return nc


def build_LF():
    nc = bass.Bass("TRN2", target_bir_lowering=False)
    xl = dram_in(nc, "xl", [2, SEQ, 128])
    xc = dram_in(nc, "xc", [2, 128, CTX])
    ftw_ap = dram_in(nc, "ftw", [128, 64, 256])
    fc_ap = dram_in(nc, "fc", [128, 2, 256])
    c64_ap = dram_in(nc, "c64", [64, 2, 64])
    c256_ap = dram_in(nc, "c256", [128, 4, 256])
    yl = dram_out(nc, "yl", [2, 128, SEQ])
    yc = dram_out(nc, "yc", [2, 128, CTX])
    P = Prog(nc)
    ftw = P.sb("ftw", [128, 64, 256], BF16)
    fc = P.sb("fc", [128, 2, 256], BF16)
    c64 = P.sb("c64", [64, 2, 64], BF16)
    c256 = P.sb("c256", [128, 4, 256], BF16)
    XA = P.sb("XA", [128, 64, 128], BF16)
    D1 = P.sb("D1", [128, 64, 256], BF16)
    D2 = P.sb("D2", [64, 128, 256], BF16)
    Y = P.sb("Y", [128, SEQ], F32)
    XC = P.sb("XC", [128, CTX], BF16)
    DA = P.sb("DA", [128, 2, 256], BF16)
    YC = P.sb("YC", [128, CTX], F32)
    pp = [P.ps(f"pp{i}", [128, 512]) for i in range(4)]
    b_pp = [Buf() for _ in range(4)]
    b_t, b_XA, b_D1, b_D2, b_Y, b_XC, b_DA, b_YC = (Buf() for _ in range(8))
    for t, ap, k in ((ftw, ftw_ap, "t0"), (fc, fc_ap, "t1"), (c64, c64_ap, "t2"), (c256, c256_ap, "t3")):
        P.op("pool", lambda e, t=t, ap=ap: e.dma_start(out=t[:], in_=ap), cowrites=[b_t], dma=k)
    pi = [0]

    def nextp():
        pi[0] = (pi[0] + 1) % 4
        return pi[0]
    ev = [0]

    def evac(out_ap, in_ap, rd, wr, co=False):
        ev[0] ^= 1
        kw = dict(cowrites=[wr]) if co else dict(writes=[wr])
        if ev[0]:
            P.op("act", lambda e: e.activation(out=out_ap, in_=in_ap, func=AF.Copy), reads=[rd], **kw)
        else:
            P.op("dve", lambda e: e.tensor_copy(out=out_ap, in_=in_ap), reads=[rd], **kw)

    for u in range(2):
        src = xl[u].rearrange("(a r) c -> a r c", r=64)
        P.op("pool", lambda e, src=src: e.dma_start(out=XA[:], in_=src), writes=[b_XA], dma="xa")
        P.op("pool", lambda e, u=u: e.dma_start(out=XC[:], in_=xc[u]), writes=[b_XC], dma="xc")
        for b0 in range(0, 64, 2):
            i = nextp()

            def f(e, b0=b0, i=i):
                for q in range(2):
                    ins = e.matmul(pp[i][:, q * 256:(q + 1) * 256], lhsT=XA[:, b0 + q, :], rhs=ftw[:, b0 + q, :],
                                   start=True, stop=True)
                return ins
            P.op("pe", f, reads=[b_XA, b_t], writes=[b_pp[i]])
            evac(D1[:, b0:b0 + 2, :], pp[i][:].rearrange("p (q n) -> p q n", q=2), b_pp[i], b_D1, co=True)
        for a0 in range(0, 128, 2):
            i = nextp()

            def f(e, a0=a0, i=i):
                for q in range(2):
                    e.matmul(pp[i][0:64, q * 256:(q + 1) * 256], lhsT=D1[:, :, a0 + q], rhs=fc[:, 0, :], start=True, stop=False)
                    ins = e.matmul(pp[i][0:64, q * 256:(q + 1) * 256], lhsT=D1[:, :, 128 + a0 + q], rhs=fc[:, 1, :],
                                   start=False, stop=True)
                return ins
            P.op("pe", f, reads=[b_D1, b_t], writes=[b_pp[i]])
            evac(D2[:, a0:a0 + 2, :], pp[i][0:64, :].rearrange("p (q n) -> p q n", q=2), b_pp[i], b_D2, co=True)
        Yv = Y[:].rearrange("p (b a) -> p a b", a=128)
        for a0 in range(0, 128, 8):
            i = nextp()

            def f(e, a0=a0, i=i):
                for q in range(8):
                    e.matmul(pp[i][:, q * 64:(q + 1) * 64], lhsT=D2[:, a0 + q, 0:128], rhs=c64[:, 0, :], start=True, stop=False)
                    ins = e.matmul(pp[i][:, q * 64:(q + 1) * 64], lhsT=D2[:, a0 + q, 128:256], rhs=c64[:, 1, :],
                                   start=False, stop=True)
                return ins
            P.op("pe", f, reads=[b_D2, b_t], writes=[b_pp[i]])
            evac(Yv[:, a0:a0 + 8, :], pp[i][:].rearrange("p (a b) -> p a b", a=8), b_pp[i], b_Y, co=True)
        P.op("sp", lambda e, u=u: e.dma_start(out=yl[u], in_=Y[:]), reads=[b_Y], writes=[Buf()], dma="sty")
        i = nextp()

        def f(e, i=i):
            for j in range(2):
                ins = e.matmul(pp[i][:, j * 256:(j + 1) * 256], lhsT=XC[:, j * 128:(j + 1) * 128], rhs=fc[:, 0, :],
                               start=True, stop=True)
            return ins
        P.op("pe", f, reads=[b_XC, b_t], writes=[b_pp[i]])
        evac(DA[:], pp[i][:].rearrange("p (j n) -> p j n", j=2), b_pp[i], b_DA)
        i = nextp()

        def f(e, i=i):
            n = 0
            for j in range(2):
                for ri in range(2):
                    ins = e.matmul(pp[i][:, 0:256], lhsT=DA[:, j, ri * 128:(ri + 1) * 128], rhs=c256[:, 2 * ri + j, :],
                                   start=(n == 0), stop=(n == 3))
                    n += 1
            return ins
        P.op("pe", f, reads=[b_DA, b_t], writes=[b_pp[i]])
        evac(YC[:], pp[i][:, 0:256], b_pp[i], b_YC)
        P.op("sp", lambda e, u=u: e.dma_start(out=yc[u], in_=YC[:]), reads=[b_YC], writes=[Buf()], dma="styc")
    P.finish()
    P.emit()
    return nc


def fft_tables():
    a = np.arange(128)[:, None, None]
    b = np.arange(64)[None, :, None]
    ap = np.arange(128)[None, None, :]
    th = 2 * np.pi * (a * ap / 128.0 + b * ap / 8192.0)
    ftw = np.concatenate([np.cos(th), -np.sin(th)], axis=-1) / np.sqrt(128.0)
    c = np.arange(128)[:, None]
    cp = np.arange(128)[None, :]
    th = 2 * np.pi * c * cp / 128.0
    cr, ci = np.cos(th) / np.sqrt(128.0), -np.sin(th) / np.sqrt(128.0)
    fc = np.stack([np.concatenate([cr, ci], 1), np.concatenate([-ci, cr], 1)], axis=1)
    bb = np.arange(64)[:, None]
    bp = np.arange(64)[None, :]
    th = 2 * np.pi * bb * bp / 64.0
    c64 = np.stack([np.cos(th), np.sin(th)], axis=1) / 8.0
    l = np.arange(256)[:, None]
    lp = np.arange(256)[None, :]
    th = 2 * np.pi * l * lp / 256.0
    C, S = np.cos(th) / 16.0, np.sin(th) / 16.0
    c256 = np.stack([C[0:128], C[128:256], S[0:128], S[128:256]], axis=1)
    f = lambda x: np.ascontiguousarray(x, dtype=np.float32)
    return f(ftw), f(fc), f(c64), f(c256)


NBLK = SEQ // 128


def build_LC(nblk=NBLK):
    nc = bass.Bass("TRN2", target_bir_lowering=False)
    S = nblk * 128
    NS = 4
    qt_ap = dram_in(nc, "qt", [128, 4, S])
    kt_ap = dram_in(nc, "kt", [128, S + CTX])
    v_ap = dram_in(nc, "v", [128, nblk + 2, 64])
    mask_ap = dram_in(nc, "mask", [128, 384])
    sink_ap = dram_in(nc, "sink", [128, 8])
    id_ap = dram_in(nc, "ident", [128, 128])
    o_ap = dram_out(nc, "o", [S, 512])
    P = Prog(nc)
    QT = P.sb("QT", [128, 4, S], BF16)
    KT = P.sb("KT", [128, S + CTX], BF16)
    V = P.sb("V", [128, nblk + 2, 64], BF16)
    mask = P.sb("mask", [128, 384], F32)
    sink = P.sb("sink", [128, 8], F32)
    sink8 = P.sb("sink8", [128, 8], F32)
    ident = P.sb("ident", [128, 128], BF16)
    sc = [P.sb(f"sc{i}", [128, 648], F32) for i in range(NS)]
    pb = [P.sb(f"pb{i}", [128, 648], BF16) for i in range(NS)]
    pTs = [P.sb(f"pTs{i}", [128, 5, 128], BF16) for i in range(NS)]
    sm = [P.sb(f"sm{i}", [128, 4], F32) for i in range(NS)]
    ot = [P.sb(f"ot{i}", [128, 512], F32) for i in range(2)]
    scA = [P.ps(f"scA{i}", [128, 512]) for i in range(2)]
    scB = [P.ps(f"scB{i}", [128, 512]) for i in range(2)]
    pTt = [P.ps(f"pTt{i}", [128, 8, 128], BF16) for i in range(2)]
    ops = [P.ps(f"ops{i}", [128, 512]) for i in range(2)]
    BL = lambda n: [Buf() for _ in range(n)]
    b_c, b_s8 = Buf(), Buf()
    b_sc, b_pb, b_pTs, b_sm = BL(NS), BL(NS), BL(NS), BL(NS)
    b_ot, b_scA, b_scB, b_pTt, b_ops = BL(2), BL(2), BL(2), BL(2), BL(2)
    for t, ap, k in ((QT, qt_ap, "c0"), (KT, kt_ap, "c1"), (V, v_ap, "c2"), (ident, id_ap, "c3")):
        P.op("pool", lambda e, t=t, ap=ap: e.dma_start(out=t[:], in_=ap), cowrites=[b_c], dma=k)
    for t, ap, k in ((mask, mask_ap, "c4"), (sink, sink_ap, "c5")):
        P.op("sp", lambda e, t=t, ap=ap: e.dma_start(out=t[:], in_=ap), cowrites=[b_c], dma=k)
    P.op("dve", lambda e: e.tensor_scalar_mul(out=sink8[:], in0=sink[:], scalar1=8.0), reads=[b_c], writes=[b_s8])
    units = [(i, r) for i in range(nblk) for r in range(8)]
    N = len(units)

    def geo(n):
        i, r = units[n]
        kb0, kb1 = max(i - 1, 0), min(i + 1, nblk - 1)
        nloc = kb1 - kb0 + 1
        nk = nloc * 128
        mo = (kb0 - (i - 1)) * 128
        return i, r, kb0, nloc, nk, mo, nk + 256

    def S1(n):
        i, r, kb0, nloc, nk, mo, L = geo(n)
        s, p = n % NS, n % 2
        hh, j = r // 4, r % 4
        p0, p1 = hh * 64, hh * 64 + 64
        q = QT[p0:p1, j, i * 128:(i + 1) * 128]

        def f(e):
            e.matmul(scA[p][:, 0:nk], lhsT=q, rhs=KT[p0:p1, kb0 * 128:kb0 * 128 + nk], start=True, stop=True)
            return e.matmul(scB[p][:, 0:256], lhsT=q, rhs=KT[p0:p1, S:S + CTX], start=True, stop=True)
        P.op("pe", f, reads=[b_c], writes=[b_scA[p], b_scB[p]])
        P.op("dve", lambda e: e.tensor_tensor(out=sc[s][:, 0:nk], in0=scA[p][:, 0:nk], in1=mask[:, mo:mo + nk], op=ALU.add),
             reads=[b_scA[p], b_c], writes=[b_sc[s]])
        P.op("act", lambda e: e.activation(out=sc[s][:, nk:L], in_=scB[p][:, 0:256], func=AF.Copy),
             reads=[b_scB[p]], cowrites=[b_sc[s]])
        P.op("pool", lambda e: e.tensor_copy(out=sc[s][:, L:L + 1], in_=sink8[:, r:r + 1]),
             reads=[b_s8], cowrites=[b_sc[s]])
        m = sm[s]
        P.op("dve", lambda e: e.reduce_max(out=m[:, 0:1], in_=sc[s][:, 0:L + 1], axis=AX.X),
             reads=[b_sc[s]], writes=[b_sm[s]])
        P.op("dve", lambda e: e.tensor_scalar_mul(out=m[:, 1:2], in0=m[:, 0:1], scalar1=-0.125),
             reads=[b_sm[s]], writes=[b_sm[s]])

    def S2(n):
        i, r, kb0, nloc, nk, mo, L = geo(n)
        s, p = n % NS, n % 2
        nt = nloc + 2
        m = sm[s]
        P.op("act", lambda e: e.activation(out=pb[s][:, 0:L + 1], in_=sc[s][:, 0:L + 1], func=AF.Exp,
                                           scale=0.125, bias=m[:, 1:2], accum_out=m[:, 2:3]),
             reads=[b_sc[s], b_sm[s]], writes=[b_pb[s], b_sm[s]])
        P.op("dve", lambda e: e.reciprocal(out=m[:, 3:4], in_=m[:, 2:3]), reads=[b_sm[s]], writes=[b_sm[s]])

        def f(e):
            for t in range(nt):
                ins = e.transpose(out=pTt[p][:, t, :], in_=pb[s][:, t * 128:(t + 1) * 128], identity=ident[:])
            return ins
        P.op("pe", f, reads=[b_pb[s], b_c], writes=[b_pTt[p]])
        if n % 2:
            P.op("act", lambda e: e.activation(out=pTs[s][:, 0:nt, :], in_=pTt[p][:, 0:nt, :], func=AF.Copy),
                 reads=[b_pTt[p]], writes=[b_pTs[s]])
        else:
            P.op("dve", lambda e: e.tensor_copy(out=pTs[s][:, 0:nt, :], in_=pTt[p][:, 0:nt, :]),
                 reads=[b_pTt[p]], writes=[b_pTs[s]])

    def S3(n):
        i, r, kb0, nloc, nk, mo, L = geo(n)
        s, p = n % NS, n % 2
        nt = nloc + 2
        oi = i % 2
        m = sm[s]

        def f(e):
            for t in range(nt):
                vb = kb0 + t if t < nloc else nblk + (t - nloc)
                ins = e.matmul(ops[p][:, 0:64], lhsT=pTs[s][:, t, :], rhs=V[:, vb, :], start=(t == 0), stop=(t == nt - 1))
            return ins
        P.op("pe", f, reads=[b_pTs[s], b_c], writes=[b_ops[p]])
        P.op("act", lambda e: e.activation(out=ot[oi][:, r * 64:(r + 1) * 64], in_=ops[p][:, 0:64], func=AF.Copy,
                                           scale=m[:, 3:4]),
             reads=[b_ops[p], b_sm[s]], cowrites=[b_ot[oi]])
        if r == 7:
            P.op("sp", lambda e: e.dma_start(out=o_ap[i * 128:(i + 1) * 128, :], in_=ot[oi][:]),
                 reads=[b_ot[oi]], writes=[Buf()], dma=f"so{oi}")
            b_ot[oi].w = []

    for k in range(N + 2):
        if k < N:
            S1(k)
        if 0 <= k - 1 < N:
            S2(k - 1)
        if 0 <= k - 2 < N:
            S3(k - 2)
    P.finish()
    P.emit()
    return nc


def attn_mask():
    a = np.arange(128)[:, None]
    j = np.arange(384)[None, :]
    return np.where(np.abs(j - 128 - a) <= 128, 0.0, -1e30).astype(np.float32)


def build_LB(T, segs):
    nc = bass.Bass("TRN2", target_bir_lowering=False)
    nseg = len(segs)
    nvec = 9 * nseg + 11
    xn1 = dram_in(nc, "xn1", [D, T])
    gb = dram_in(nc, "gb", [1024, T])
    vp = dram_in(nc, "vp", [1024, T])
    vv = dram_in(nc, "vv", [1024, T])
    vn = dram_in(nc, "vn", [1024, T])
    yb = dram_in(nc, "yb", [1024, T])
    vecs_ap = dram_in(nc, "vecs", [128, nvec, NCH])
    wout = dram_in(nc, "wout", [8, 128, NCH, 256])
    wgu2, wd2 = ffn_w_in(nc, "2")
    wgu3, wd3 = ffn_w_in(nc, "3")
    wqkv = dram_in(nc, "wqkv", [20, 128, NCH, 128])
    wperm = dram_in(nc, "wperm", [18, 128, NCH, 128])
    cos_ap = dram_in(nc, "cosT", [128, T])
    sin_ap = dram_in(nc, "sinT", [128, T])
    xn4 = dram_out(nc, "xn4", [D, T])
    qr = dram_out(nc, "qr", [D, T])
    kr = dram_out(nc, "kr", [256, T])
    vo = dram_out(nc, "vo", [256, T])
    P = Prog(nc)
    R = RL(P, T, segs, vecs_ap, nvec)
    V = lambda s, k: 9 * s + k
    L0 = 9 * nseg
    for s in range(nseg):
        R.derive(V(s, 3), mul=0.5)
        R.derive(V(s, 6), mul=0.5)
        for k in (2, 5, 8):
            R.derive(V(s, k), add=1.0)
            R.derive(V(s, k), mul=1.0 / ALPHA)
    for k in range(2, 8):
        R.derive(L0 + k, mul=ALPHA)
    SL = lambda k: [V(s, k) for s in range(nseg)]
    b_in = Buf()
    b_o = Buf()

    def mix_fill(bi, c0, c1, w):
        tl = [R.sg[0], R.sg[1], R.zt[0], R.zt[1], R.tmp[0]]
        tb = [R.b_sg[0], R.b_sg[1], R.b_zt[0], R.b_zt[1], R.b_tmp[0]]
        for j in range(8):
            for k, src in enumerate((gb, vp, vv, vn, yb)):
                P.op("sp", lambda e, k=k, src=src, j=j: e.dma_start(out=tl[k][:, :w], in_=src[j * 128:(j + 1) * 128, c0:c1]),
                     writes=[tb[k]], dma=f"m{k}")
            P.op("dve", lambda e, j=j: e.tensor_scalar_mul(out=tl[1][:, :w], in0=tl[1][:, :w], scalar1=R.vc(L0 + 8, j)),
                 reads=[R.b_vecs], writes=[tb[1]])
            P.op("dve", lambda e, j=j: e.scalar_tensor_tensor(out=tl[1][:, :w], in0=tl[2][:, :w], scalar=R.vc(L0 + 9, j),
                                                              in1=tl[1][:, :w], op0=ALU.mult, op1=ALU.add),
                 reads=[R.b_vecs, tb[2]], writes=[tb[1]])
            P.op("dve", lambda e, j=j: e.scalar_tensor_tensor(out=tl[1][:, :w], in0=tl[3][:, :w], scalar=R.vc(L0 + 10, j),
                                                              in1=tl[1][:, :w], op0=ALU.mult, op1=ALU.add),
                 reads=[R.b_vecs, tb[3]], writes=[tb[1]])
            P.op("dve", lambda e, j=j: e.tensor_tensor(out=R.hb[:, j, c0:c1], in0=tl[1][:, :w], in1=tl[0][:, :w], op=ALU.mult),
                 reads=[tb[0], tb[1]], cowrites=R.W_HB)
            P.op("act", lambda e, j=j: e.activation(out=R.hb[:, 8 + j, c0:c1], in_=tl[4][:, :w], func=AF.Copy),
                 reads=[tb[4]], cowrites=R.W_HB)

    for bi, (c0, c1) in enumerate(R.blocks):
        mix_fill(bi, c0, c1, c1 - c0)
    R.phase2("proj", wout, ("ap", xn1, b_in, True), SL(0), ln=(L0 + 2, L0 + 3), mod=(SL(1), SL(2)), reload_hb=True)
    R.phase1(wgu2)
    R.phase2("ffn", wd2, ("xs",), SL(3), ln=(L0 + 4, L0 + 5), mod=(SL(4), SL(5)), reload_hb=True)
    R.phase1(wgu3)
    R.phase2("ffn", wd3, ("xs",), SL(6), ln=(L0 + 6, L0 + 7), mod=(SL(7), SL(8)), out_ap=xn4, reload_hb=True)
    rt = R.xn2[0][:, :, :].rearrange("p c t -> p (c t)")
    cosT, sinT = rt[:, 0:T], rt[:, T:2 * T]
    P.op("sp", lambda e: e.dma_start(out=cosT, in_=cos_ap), writes=R.b_xn2[0], dma="r0")
    P.op("sp", lambda e: e.dma_start(out=sinT, in_=sin_ap), cowrites=R.b_xn2[0], dma="r1")
    b_rope = R.b_xn2[0][0]
    ld = lambda g: (R.load_wa(wqkv, g), R.load_wa(wperm, g) if g < 18 else None)
    nxt = ld(0)
    for g in range(20):
        iw, ip = nxt
        if g + 1 < 20:
            nxt = ld(g + 1)
        for bi, (c0, c1) in enumerate(R.blocks):
            w = c1 - c0
            pb = R.pp_i
            R.pp_i ^= 1
            R.mm(R.pg[pb], R.b_pg[pb], R.wa[iw], R.b_wa[iw], NCH, R.hb, c0, c1, [R.b_hb])
            i = R.zt_i
            R.zt_i ^= 1
            zt, tmp, sg, pg, pu = R.zt[i], R.tmp[i], R.sg[pb], R.pg[pb], R.pu[pb]
            if g < 18:
                R.mm(R.pu[pb], R.b_pu[pb], R.wa[ip], R.b_wa[ip], NCH, R.hb, c0, c1, [R.b_hb])
                P.op("dve", lambda e, sg=sg, pg=pg, w=w, c0=c0, c1=c1: e.tensor_tensor(
                    out=sg[:, :w], in0=pg[:, :w], in1=cosT[:, c0:c1], op=ALU.mult),
                    reads=[R.b_pg[pb], b_rope], writes=[R.b_sg[pb]])
                P.op("dve", lambda e, tmp=tmp, pu=pu, w=w, c0=c0, c1=c1: e.tensor_tensor(
                    out=tmp[:, :w], in0=pu[:, :w], in1=sinT[:, c0:c1], op=ALU.mult),
                    reads=[R.b_pu[pb], b_rope], writes=[R.b_tmp[i]])
                P.op("dve", lambda e, zt=zt, sg=sg, tmp=tmp, w=w: e.tensor_tensor(
                    out=zt[:, :w], in0=sg[:, :w], in1=tmp[:, :w], op=ALU.add),
                    reads=[R.b_sg[pb], R.b_tmp[i]], writes=[R.b_zt[i]])
                dst, row = (qr, g * 128) if g < 16 else (kr, (g - 16) * 128)
            else:
                P.op("act", lambda e, zt=zt, pg=pg, w=w: e.activation(out=zt[:, :w], in_=pg[:, :w], func=AF.Copy),
                     reads=[R.b_pg[pb]], writes=[R.b_zt[i]])
                dst, row = vo, (g - 18) * 128
            R.store(dst, b_o, row, c0, c1, zt, R.b_zt[i])
    P.finish()
    P.emit()
    return nc


def build_LD(T):
    nc = bass.Bass("TRN2", target_bir_lowering=False)
    segs = [(0, T)]
    nvec = 10
    xn4 = dram_in(nc, "xn4", [D, T])
    oT = dram_in(nc, "oT", [D, T])
    vecs_ap = dram_in(nc, "vecs", [128, nvec, NCH])
    wout = dram_in(nc, "wout", [8, 128, NCH, 256])
    wgu, wd = ffn_w_in(nc, "")
    out = dram_out(nc, "out", [D, T])
    P = Prog(nc)
    R = RL(P, T, segs, vecs_ap, nvec)
    R.derive(2, add=1.0)
    R.derive(2, mul=1.0 / ALPHA)
    R.derive(3, mul=0.5)
    R.derive(6, mul=ALPHA)
    R.derive(7, mul=ALPHA)
    b_in = Buf()
    for bi, (c0, c1) in enumerate(R.blocks):
        P.op("pool", lambda e, c0=c0, c1=c1: e.dma_start(out=R.hb[:, :, c0:c1], in_=chunked(oT)[:, :, c0:c1]),
             cowrites=R.W_HB, dma="oin")
    R.phase2("proj", wout, ("ap", xn4, b_in, True), [0], ln=(6, 7), mod=([1], [2]), reload_hb=True)
    R.phase1(wgu)
    R.phase2("ffn", wd, ("xs",), [3], ln=(8, 9), mod=None, out_ap=out)
    P.finish()
    P.emit()
    return nc


def lay(v):
    v = np.asarray(v, dtype=np.float32)
    if v.shape[0] < D:
        v = np.concatenate([v, np.zeros(D - v.shape[0], np.float32)])
    return v.reshape(NCH, 128).T


def rope_tables(pos):
    nf = 16
    inv = np.power(10000.0, -np.arange(nf, dtype=np.float64) / nf)
    row = (pos // 64).astype(np.float64)
    col = (pos % 64).astype(np.float64)
    d = np.arange(64)
    axis, part, f = d // 32, (d % 32) // 16, d % 16
    ang = np.where(axis[:, None] == 0, row[None, :], col[None, :]) * inv[f][:, None]
    c = np.cos(ang)
    s = np.sin(ang) * np.where(part == 0, -1.0, 1.0)[:, None]
    return np.concatenate([c, c], 0).astype(np.float32), np.concatenate([s, s], 0).astype(np.float32)


def rope_perm():
    d = np.arange(64)
    part = (d % 32) // 16
    p = np.where(part == 0, d + 16, d - 16)
    cols = np.concatenate([h * 64 + p for h in range(36)])
    return cols


_CACHE = {}


def _prog(key, fn):
    if key not in _CACHE:
        _CACHE[key] = fn()
    return _CACHE[key]


def _run(nc, ins):
    res = run_bass_kernel_spmd(nc, ins, core_ids=list(range(NCORE)))
    return res.results


def kernel(x, c, ctx, c_ctx, w_mod, b_mod, ln_g, ln_b, ffn_w_gate, ffn_w_up, ffn_w_down,
           ab_w_in, ab_conv, ab_w_out, attn_w_in, attn_sink, attn_w_out):
    f32 = lambda a: np.ascontiguousarray(np.asarray(a), dtype=np.float32)
    x, c, ctx, c_ctx = f32(x), f32(c), f32(ctx), f32(c_ctx)
    w_mod, b_mod, ln_g, ln_b = f32(w_mod), f32(b_mod), f32(ln_g), f32(ln_b)
    ffn_w_gate, ffn_w_up, ffn_w_down = f32(ffn_w_gate), f32(ffn_w_up), f32(ffn_w_down)
    ab_w_in, ab_conv, ab_w_out = f32(ab_w_in), f32(ab_conv), f32(ab_w_out)
    attn_w_in, attn_sink, attn_w_out = f32(attn_w_in), f32(attn_sink), f32(attn_w_out)
    LT, CT = SEQ // 4, CTX // 4
    T = LT + CT
    segs = [(0, LT), (LT, T)]
    cores = [(r // 4, r % 4) for r in range(NCORE)]

    cv = np.ascontiguousarray(np.stack([lay(c[0]), lay(c[1]), lay(c_ctx)], axis=-1))
    wm = np.concatenate([w_mod[0], w_mod[1]], axis=1)
    bm = b_mod.reshape(-1)
    ins = []
    for r in range(NCORE):
        sl = slice(r * MODC, (r + 1) * MODC)
        ins.append({"cv": cv, "w": np.ascontiguousarray(wm[:, sl]),
                    "b": np.ascontiguousarray(np.broadcast_to(bm[sl], (3, MODC)))})
    res = _run(_prog("L0", build_L0), ins)
    del wm
    mod = np.concatenate([res[r]["mod"] for r in range(NCORE)], axis=1).reshape(3, 2, 9, D)

    def mv(b, s, layer, k):
        return lay(mod[b if s == 0 else 2, layer, k])

    _ffw = {}

    def ffw(l, i, sfx):
        if (l, i) not in _ffw:
            _ffw[(l, i)] = (np.concatenate([tile_w(ffn_w_gate[l, i], 128), tile_w(ffn_w_up[l, i], 128)], axis=-1),
                            tile_w(ffn_w_down[l, i], 128))
        a, c_ = _ffw[(l, i)]
        return {"wgu" + sfx: a, "wd" + sfx: c_}

    w_gb, w_gc = tile_w(ab_w_in[0][:, 0:1024], 128), tile_w(ab_w_in[0][:, 1024:2048], 128)
    w_xi, w_uf = tile_w(ab_w_in[0][:, 2048:3072], 128), tile_w(ab_w_in[0][:, 3072:4096], 128)

    ins = []
    for b, q in cores:
        xT = np.concatenate([x[b, q * LT:(q + 1) * LT].T, ctx[b, q * CT:(q + 1) * CT].T], axis=1)
        vecs = [mv(b, s, 0, k) for s in range(2) for k in range(5)] + [lay(ln_g[0, 0]), lay(ln_b[0, 0])]
        ins.append({"xT": np.ascontiguousarray(xT), "vecs": np.ascontiguousarray(np.stack(vecs, axis=1)),
                    **ffw(0, 0, ""), "w_gb": w_gb, "w_gc": w_gc, "w_xi": w_xi, "w_uf": w_uf})
    resA = _run(_prog("LA", lambda: build_LA(T, segs)), ins)

    def gather(res, name, nrow):
        lat = np.empty((2, nrow, SEQ), np.float32)
        cx = np.empty((2, nrow, CTX), np.float32)
        for r, (b, q) in enumerate(cores):
            a = res[r][name]
            lat[b, :, q * LT:(q + 1) * LT] = a[:, :LT]
            cx[b, :, q * CT:(q + 1) * CT] = a[:, LT:]
        return lat, cx

    UFl, UFc = gather(resA, "uf", 1024)
    VVl, VVc = gather(resA, "vv", 1024)

    ftw, fc, c64, c256 = fft_tables()
    ins = []
    for b, q in cores:
        gs = [2 * q, 2 * q + 1]
        xl = np.stack([UFl[b, g * 128:(g + 1) * 128].T for g in gs])
        xc = np.stack([UFc[b, g * 128:(g + 1) * 128] for g in gs])
        ins.append({"xl": np.ascontiguousarray(xl), "xc": np.ascontiguousarray(xc),
                    "ftw": ftw, "fc": fc, "c64": c64, "c256": c256})
    resF = _run(_prog("LF", build_LF), ins)
    YBl = np.empty((2, 1024, SEQ), np.float32)
    YBc = np.empty((2, 1024, CTX), np.float32)
    for r, (b, q) in enumerate(cores):
        for u in range(2):
            g = 2 * q + u
            YBl[b, g * 128:(g + 1) * 128] = resF[r]["yl"][u]
            YBc[b, g * 128:(g + 1) * 128] = resF[r]["yc"][u]
    del UFl, UFc

    def shift(a, k):
        o = np.zeros_like(a)
        if k > 0:
            o[..., k:] = a[..., :-k]
        else:
            o[..., :k] = a[..., -k:]
        return o

    VPl, VPc, VNl, VNc = shift(VVl, 1), shift(VVc, 1), shift(VVl, -1), shift(VVc, -1)

    def cols(lat, cx, b, q):
        return np.ascontiguousarray(np.concatenate([lat[b][:, q * LT:(q + 1) * LT], cx[b][:, q * CT:(q + 1) * CT]], axis=1))

    perm = rope_perm()
    wperm = tile_w(np.ascontiguousarray(attn_w_in[0][:, perm]), 128)
    wqkv_t = tile_w(attn_w_in[0], 128)
    wout_t = tile_w(ab_w_out[0], 256)
    _ffw.pop((0, 0), None)
    ins = []
    for r, (b, q) in enumerate(cores):
        vecs = []
        for s in range(2):
            vecs += [mv(b, s, 0, 5), mv(b, s, 0, 6), mv(b, s, 0, 7), mv(b, s, 0, 8),
                     mv(b, s, 1, 0), mv(b, s, 1, 1), mv(b, s, 1, 2), mv(b, s, 1, 3), mv(b, s, 1, 4)]
        vecs += [lay(ln_g[0, 0]), lay(ln_b[0, 0]), lay(ln_g[0, 1]), lay(ln_b[0, 1]), lay(ln_g[0, 2]), lay(ln_b[0, 2]),
                 lay(ln_g[1, 0]), lay(ln_b[1, 0]), lay(ab_conv[0, 0]), lay(ab_conv[0, 1]), lay(ab_conv[0, 2])]
        cl, sl_ = rope_tables(np.arange(q * LT, (q + 1) * LT))
        cosT = np.concatenate([cl, np.ones((128, CT), np.float32)], axis=1)
        sinT = np.concatenate([sl_, np.zeros((128, CT), np.float32)], axis=1)
        ins.append({"xn1": resA[r]["xn1"], "gb": resA[r]["gb"], "vp": cols(VPl, VPc, b, q), "vv": resA[r]["vv"],
                    "vn": cols(VNl, VNc, b, q), "yb": cols(YBl, YBc, b, q),
                    "vecs": np.ascontiguousarray(np.stack(vecs, axis=1)), "wout": wout_t,
                    **ffw(0, 1, "2"), **ffw(1, 0, "3"), "wqkv": wqkv_t, "wperm": wperm,
                    "cosT": np.ascontiguousarray(cosT), "sinT": np.ascontiguousarray(sinT)})
    resB = _run(_prog("LB", lambda: build_LB(T, segs)), ins)
    del resA, VPl, VNl, YBl, VVl
    Ql, _ = gather(resB, "qr", D)
    Kl, Kc = gather(resB, "kr", 256)
    Vl, Vc = gather(resB, "vo", 256)

    mask = attn_mask()
    ident = np.eye(128, dtype=np.float32)
    ins = []
    for r in range(NCORE):
        b, g = r // 4, r % 4
        qt = Ql[b, g * 512:(g + 1) * 512].reshape(2, 4, 64, SEQ).transpose(0, 2, 1, 3).reshape(128, 4, SEQ)
        k1 = np.concatenate([Kl[b, g * 64:(g + 1) * 64], Kc[b, g * 64:(g + 1) * 64]], axis=1)
        vl = Vl[b, g * 64:(g + 1) * 64].T.reshape(NBLK, 128, 64)
        vc_ = Vc[b, g * 64:(g + 1) * 64].T.reshape(2, 128, 64)
        vall = np.concatenate([vl, vc_], axis=0).transpose(1, 0, 2)
        ins.append({"qt": np.ascontiguousarray(qt), "kt": np.ascontiguousarray(np.concatenate([k1, k1], axis=0)),
                    "v": np.ascontiguousarray(vall), "mask": mask,
                    "sink": np.ascontiguousarray(np.broadcast_to(attn_sink[0, g * 8:(g + 1) * 8], (128, 8))),
                    "ident": ident})
    resC = _run(_prog("LC", build_LC), ins)
    del Ql
    O = np.empty((2, D, SEQ), np.float32)
    for r in range(NCORE):
        b, g = r // 4, r % 4
        O[b, g * 512:(g + 1) * 512] = resC[r]["o"].T
    del resC

    _ffw.clear()
    awout_t = tile_w(attn_w_out[0], 256)
    ins = []
    for r, (b, q) in enumerate(cores):
        vecs = [mv(b, 0, 1, 5), mv(b, 0, 1, 6), mv(b, 0, 1, 7), mv(b, 0, 1, 8),
                lay(ln_g[1, 0]), lay(ln_b[1, 0]), lay(ln_g[1, 1]), lay(ln_b[1, 1]), lay(ln_g[1, 2]), lay(ln_b[1, 2])]
        ins.append({"xn4": np.ascontiguousarray(resB[r]["xn4"][:, :LT]), "oT": np.ascontiguousarray(O[b][:, q * LT:(q + 1) * LT]),
                    "vecs": np.ascontiguousarray(np.stack(vecs, axis=1)), "wout": awout_t, **ffw(1, 1, "")})
    resD = _run(_prog("LD", lambda: build_LD(LT)), ins)
    out = np.empty((2, SEQ, D), np.float32)
    for r, (b, q) in enumerate(cores):
        out[b, q * LT:(q + 1) * LT] = resD[r]["out"].T
    return out
```

```python
import numpy as np
from contextlib import ExitStack
import concourse.bass as bass
import concourse.mybir as mybir
from concourse.bass_utils import run_bass_kernel_spmd

F32 = mybir.dt.float32
BF16 = mybir.dt.bfloat16
AF = mybir.ActivationFunctionType
ALU = mybir.AluOpType
AX = mybir.AxisListType

D = 2048
DFF = 5632
NCH = 16
FCH = 44
SEQ = 8192
CTX = 256
NCORE = 8
ALPHA = 4.0 ** 0.25
LN_EPS = 1e-5
TB = 512


class Buf:
    __slots__ = ("w", "r", "name")

    def __init__(self, name=""):
        self.w = []
        self.r = []
        self.name = name


class Ctr:
    LIMIT = 30000

    def __init__(self, P, name, step):
        self.P, self.name, self.step = P, name, step
        self.k = 0
        self.done = []
        self._new()

    def _new(self):
        self.sem = self.P.stack.enter_context(self.P.nc.semaphore(f"{self.name}_{self.k}"))
        self.k += 1
        self.val = 0

    def next(self):
        if self.val + self.step > self.LIMIT:
            self.done.append((self.sem, self.val))
            self._new()
        self.val += self.step
        return (self.sem, self.val)


class Eng:
    def __init__(self, P, name):
        self.name = name
        self.ops = []
        self.waited = {}
        self.ctr = Ctr(P, "e" + name, 1)


class Prog:
    def __init__(self, nc):
        self.nc = nc
        self.stack = ExitStack()
        self.engs = {n: Eng(self, n) for n in ("pe", "act", "dve", "pool", "sp")}
        self.dctr = {}
        self.fuzzy = {}
        self.n = 0

    def sb(self, name, shape, dt):
        return self.stack.enter_context(self.nc.sbuf_tensor("s_" + name, list(shape), dt))

    def ps(self, name, shape, dt=F32):
        return self.stack.enter_context(self.nc.psum_tensor("p_" + name, list(shape), dt))

    def op(self, eng, fn, reads=(), writes=(), dma=None, cowrites=()):
        E = self.engs[eng]
        deps = {}

        def add(tok):
            s, v = tok
            k = id(s)
            if k in self.fuzzy:
                v = max(v, self.fuzzy[k].val if self.fuzzy[k].sem is s else v)
            if k not in deps or deps[k][1] < v:
                deps[k] = (s, v)

        for b in reads:
            for t in b.w:
                add(t)
        for b in writes:
            for t in b.w:
                add(t)
            for t in b.r:
                add(t)
        for b in cowrites:
            for t in b.r:
                add(t)
        if dma is not None:
            if dma not in self.dctr:
                self.dctr[dma] = Ctr(self, "d" + dma, 16)
            ctr = self.dctr[dma]
            if dma.startswith("st") or dma.startswith("xo"):
                self.fuzzy[id(ctr.sem)] = ctr
        else:
            ctr = E.ctr
        waits = []
        for k, (s, v) in deps.items():
            if eng == "pe" and dma is None and s is E.ctr.sem:
                continue
            if E.waited.get(k, 0) >= v:
                continue
            E.waited[k] = v
            waits.append((s, v))
        tok = ctr.next()
        E.ops.append((waits, fn, tok[0], ctr.step))
        for b in reads:
            b.r.append(tok)
        for b in writes:
            b.w = [tok]
            b.r = []
        for b in cowrites:
            b.w.append(tok)
        self.n += 1
        return tok

    def finish(self):
        E = self.engs["sp"]
        waits = []
        for ctr in self.dctr.values():
            for s, v in ctr.done + [(ctr.sem, ctr.val)]:
                if v > 0 and E.waited.get(id(s), 0) < v:
                    waits.append((s, v))
        E.ops.append((waits, None, None, 0))

    def emit(self):
        nc = self.nc

        def mk(E):
            def run(e):
                for waits, fn, sem, step in E.ops:
                    for ws, wv in waits:
                        e.wait_ge(ws, wv)
                    if fn is not None:
                        fn(e).then_inc(sem, step)
            return run

        with nc.Block() as block:
            block.tensor(mk(self.engs["pe"]))
            block.scalar(mk(self.engs["act"]))
            block.vector(mk(self.engs["dve"]))
            block.gpsimd(mk(self.engs["pool"]))
            block.sync(mk(self.engs["sp"]))
        self.stack.close()


def blocks_of(T, tb=TB):
    bl = [(c, min(c + tb, T)) for c in range(0, T, tb)]
    if bl[-1][1] - bl[-1][0] < tb:
        bl = [bl[-1]] + bl[:-1]
    return bl


def chunked(ap):
    return ap.rearrange("(c p) t -> p c t", p=128)


class RL:
    def __init__(self, P, T, segs, vecs_ap, nvec):
        self.P, self.T, self.segs = P, T, segs
        self.blocks = blocks_of(T)
        nb = len(self.blocks)
        nc = P.nc
        self.xn2 = [P.sb(f"xn{i}", [128, NCH, TB], F32) for i in range(2)]
        self.big = P.sb("big", [128, max(NCH * T, FCH * TB)], BF16)
        self.hb = self.big[:, 0:NCH * T].rearrange("p (c t) -> p c t", c=NCH)
        self.ab = self.big[:, 0:FCH * TB].rearrange("p (c t) -> p c t", c=FCH)
        self.wa2 = [P.sb(f"wa{i}", [128, NCH, 256], BF16) for i in range(2)]
        self.wa = [self.wa2[i // 2][:, :, (i % 2) * 128:(i % 2 + 1) * 128] for i in range(4)]
        self.wap_i = 0
        self.wb = [P.sb(f"wb{i}", [128, FCH, 128], BF16) for i in range(2)]
        self.mu = P.sb("mu", [128, TB], F32)
        self.msq = P.sb("msq", [128, TB], F32)
        self.rstd = P.sb("rstd", [128, TB], F32)
        self.sg = [P.sb(f"sg{i}", [128, TB], F32) for i in range(2)]
        self.zt = [P.sb(f"zt{i}", [128, TB], F32) for i in range(2)]
        self.tmp = [P.sb(f"tmp{i}", [128, TB], F32) for i in range(2)]
        self.xr = [P.sb(f"xr{i}", [128, TB], F32) for i in range(4)]
        self.at = [P.sb(f"at{i}", [128, TB], BF16) for i in range(2)]
        self.zb = [P.sb(f"zb{i}", [128, TB], BF16) for i in range(2)]
        self.zq = [P.sb(f"zq{i}", [128, TB], BF16) for i in range(2)]
        self.ones = P.sb("ones", [128, 128], BF16)
        self.vecs = P.sb("vecs", [128, nvec, NCH], F32)
        self.s1 = P.ps("s1", [128, TB])
        self.s2 = P.ps("s2", [128, TB])
        self.pg = [P.ps(f"pg{i}", [128, TB]) for i in range(2)]
        self.pu = [P.ps(f"pu{i}", [128, TB]) for i in range(2)]
        self.py = [P.ps(f"py{i}", [128, TB]) for i in range(2)]
        self.xs = [nc.dram_tensor(f"xs{i}", [nb, 128, NCH, TB], F32).ap() for i in range(2)]
        self.hs = nc.dram_tensor("hs", [nb, 128, NCH, TB], BF16).ap()
        self.A = nc.dram_tensor("Asp", [nb, 128, FCH, TB], BF16).ap()
        B = Buf
        BL = lambda n: [B() for _ in range(n)]
        self.b_xn2 = [BL(NCH), BL(NCH)]
        self.b_hbk = BL(nb)
        self.b_abc = BL(FCH)
        self.b_wa, self.b_wb = BL(4), BL(2)
        self.b_mu, self.b_msq, self.b_rstd = B(), B(), B()
        self.b_sg, self.b_zt, self.b_tmp, self.b_xr, self.b_at = BL(2), BL(2), BL(2), BL(4), BL(2)
        self.b_zb, self.b_zq = BL(2), BL(2)
        self.b_s1, self.b_s2 = B(), B()
        self.b_pg, self.b_pu, self.b_py = BL(2), BL(2), BL(2)
        self.b_ones, self.b_vecs = B(), B()
        self.b_xs = [BL(nb), BL(nb)]
        self.b_hs = BL(nb)
        self.b_A = BL(nb)
        self.W_HB = self.b_hbk + self.b_abc
        self.wa_i = self.wb_i = self.zt_i = self.xr_i = self.at_i = self.pp_i = self.zb_i = 0
        self.xs_cur = 0
        ones, vecs = self.ones, self.vecs
        P.op("dve", lambda e: e.memset(ones[:], 1.0), writes=[self.b_ones])
        P.op("sp", lambda e: e.dma_start(out=vecs[:], in_=vecs_ap), writes=[self.b_vecs], dma="vecs")

    def vc(self, v, c):
        return self.vecs[:, v, c:c + 1]

    def whb(self, bi):
        return [self.b_hbk[bi]] + self.b_abc

    def derive(self, v, mul=None, add=None):
        vecs = self.vecs
        if add is not None:
            self.P.op("dve", lambda e: e.tensor_scalar_add(out=vecs[:, v, :], in0=vecs[:, v, :], scalar1=float(add)),
                      reads=[self.b_vecs], writes=[self.b_vecs])
        if mul is not None:
            self.P.op("dve", lambda e: e.tensor_scalar_mul(out=vecs[:, v, :], in0=vecs[:, v, :], scalar1=float(mul)),
                      reads=[self.b_vecs], writes=[self.b_vecs])

    def segparts(self, c0, c1):
        out = []
        for si, (s0, s1) in enumerate(self.segs):
            a, b = max(c0, s0), min(c1, s1)
            if a < b:
                out.append((si, a - c0, b - c0))
        return out

    def first_prologue(self, x_ap, x_buf, vshift, vscale1):
        P = self.P
        hb = self.hb
        for bi, (c0, c1) in enumerate(self.blocks):
            w = c1 - c0
            s = bi % 2
            xn = self.xn2[s]
            P.op("sp", lambda e, xn=xn, w=w, c0=c0, c1=c1: e.dma_start(out=xn[:, :, :w], in_=chunked(x_ap)[:, :, c0:c1]),
                 reads=[x_buf], writes=self.b_xn2[s], dma=f"xin{s}")
            for si, a, b in self.segparts(c0, c1):
                for c in range(NCH):
                    P.op("act", lambda e, xn=xn, c=c, si=si, a=a, b=b, c0=c0: e.activation(
                        out=hb[:, c, c0 + a:c0 + b], in_=xn[:, c, a:b], func=AF.Identity,
                        scale=self.vc(vscale1[si], c), bias=self.vc(vshift[si], c)),
                        reads=[self.b_vecs, self.b_xn2[s][c]], cowrites=self.whb(bi))

    def load_hb(self, only=None):
        hb, hs = self.hb, self.hs
        for bi, (c0, c1) in enumerate(self.blocks):
            if only is not None and bi not in only:
                continue
            w = c1 - c0
            self.P.op("sp", lambda e, bi=bi, c0=c0, c1=c1, w=w: e.dma_start(out=hb[:, :, c0:c1], in_=hs[bi, :, :, :w]),
                      reads=[self.b_hs[bi]], cowrites=self.whb(bi), dma=f"ldh{bi % 2}")
            self.b_hs[bi].w = []

    def load_wa(self, W_ap, g):
        i = self.wa_i
        self.wa_i = (i + 1) % 4
        t = self.wa[i]
        self.P.op("pool", lambda e: e.dma_start(out=t, in_=W_ap[g]), writes=[self.b_wa[i]], dma=f"wa{i}")
        return i

    def load_wa_pair(self, W2_ap, g):
        p = self.wap_i
        self.wap_i ^= 1
        t = self.wa2[p]
        self.P.op("pool", lambda e: e.dma_start(out=t[:], in_=W2_ap[g]), writes=[self.b_wa[2 * p], self.b_wa[2 * p + 1]],
                  dma=f"wp{p}")
        self.wa_i = (2 * p + 2) % 4
        return 2 * p, 2 * p + 1

    def load_wb(self, W_ap, g):
        i = self.wb_i
        self.wb_i = (i + 1) % 2
        t = self.wb[i]
        self.P.op("pool", lambda e: e.dma_start(out=t[:], in_=W_ap[g]), writes=[self.b_wb[i]], dma=f"wb{i}")
        return i

    def mm(self, dst, dst_buf, wt, wbuf, kc, X, lo, hi, xbufs):
        w = hi - lo

        def f(e):
            for k in range(kc):
                ins = e.matmul(dst[:, :w], lhsT=wt[:, k, :], rhs=X[:, k, lo:hi], start=(k == 0), stop=(k == kc - 1))
            return ins
        self.P.op("pe", f, reads=[wbuf] + list(xbufs), writes=[dst_buf])

    def store(self, dst_ap, dst_buf, row0, c0, c1, tile, tbuf):
        w = c1 - c0
        self.P.op("sp", lambda e: e.dma_start(out=dst_ap[row0:row0 + 128, c0:c1], in_=tile[:, :w]),
                  reads=[tbuf], cowrites=[dst_buf], dma="st")

    def phase1(self, wgu):
        P = self.P
        hb, A = self.hb, self.A
        ld = lambda fc: self.load_wa_pair(wgu, fc)
        nxt = ld(0)
        for fc in range(FCH):
            ig, iu = nxt
            if fc + 1 < FCH:
                nxt = ld(fc + 1)
            for bi, (c0, c1) in enumerate(self.blocks):
                w = c1 - c0
                pb = self.pp_i
                self.pp_i ^= 1
                self.mm(self.pg[pb], self.b_pg[pb], self.wa[ig], self.b_wa[ig], NCH, hb, c0, c1, [self.b_hbk[bi]])
                self.mm(self.pu[pb], self.b_pu[pb], self.wa[iu], self.b_wa[iu], NCH, hb, c0, c1, [self.b_hbk[bi]])
                sg, pg, pu = self.sg[pb], self.pg[pb], self.pu[pb]
                ai = self.at_i
                self.at_i ^= 1
                at = self.at[ai]
                P.op("act", lambda e, sg=sg, pg=pg, w=w: e.activation(out=sg[:, :w], in_=pg[:, :w], func=AF.Silu),
                     reads=[self.b_pg[pb]], writes=[self.b_sg[pb]])
                P.op("dve", lambda e, sg=sg, pu=pu, at=at, w=w: e.tensor_tensor(out=at[:, :w], in0=sg[:, :w], in1=pu[:, :w],
                                                                               op=ALU.mult),
                     reads=[self.b_sg[pb], self.b_pu[pb]], writes=[self.b_at[ai]])
                P.op("sp", lambda e, at=at, bi=bi, fc=fc, w=w: e.dma_start(out=A[bi, :, fc, :w], in_=at[:, :w]),
                     reads=[self.b_at[ai]], cowrites=[self.b_A[bi]], dma="sta")

    def phase2(self, kind, W, res, vgate, ln, mod=None, out_ap=None, reload_hb=False, extra=None):
        P = self.P
        hb, ab, A, ones = self.hb, self.ab, self.A, self.ones
        s1, s2, mu, msq, rstd = self.s1, self.s2, self.mu, self.msq, self.rstd
        vg, vb = ln
        xs_in = self.xs[self.xs_cur]
        b_xs_in = self.b_xs[self.xs_cur]
        xs_out = self.xs[self.xs_cur ^ 1]
        b_xs_out = self.b_xs[self.xs_cur ^ 1]
        hs = self.hs
        pending = []

        def ln_apply(bi, c0, c1, s, c):
            w = c1 - c0
            xn = self.xn2[s]
            bx = self.b_xn2[s][c]
            P.op("dve", lambda e: e.tensor_tensor(out=xn[:, c, :w], in0=xn[:, c, :w], in1=mu[:, :w], op=ALU.subtract),
                 reads=[self.b_mu], writes=[bx])
            P.op("dve", lambda e: e.tensor_tensor(out=xn[:, c, :w], in0=xn[:, c, :w], in1=rstd[:, :w], op=ALU.mult),
                 reads=[self.b_rstd], writes=[bx])
            P.op("act", lambda e: e.activation(out=xn[:, c, :w], in_=xn[:, c, :w], func=AF.Identity,
                                               scale=self.vc(vg, c), bias=self.vc(vb, c)),
                 reads=[self.b_vecs], writes=[bx])
            if mod is not None:
                ai = self.at_i
                self.at_i ^= 1
                at = self.at[ai]
                for si, a, b in self.segparts(c0, c1):
                    P.op("act", lambda e, si=si, a=a, b=b: e.activation(
                        out=at[:, a:b], in_=xn[:, c, a:b], func=AF.Identity,
                        scale=self.vc(mod[1][si], c), bias=self.vc(mod[0][si], c)),
                        reads=[self.b_vecs, bx], cowrites=[self.b_at[ai]])
                P.op("sp", lambda e: e.dma_start(out=hs[bi, :, c, :w], in_=at[:, :w]),
                     reads=[self.b_at[ai]], cowrites=[self.b_hs[bi]], dma="sth")
                self.b_at[ai].w = []
            if c == NCH - 1:
                if out_ap is not None:
                    P.op("sp", lambda e: e.dma_start(out=chunked(out_ap)[:, :, c0:c1], in_=xn[:, :, :w]),
                         reads=self.b_xn2[s], writes=[Buf()], dma=f"xo{s}")
                else:
                    P.op("sp", lambda e: e.dma_start(out=xs_out[bi, :, :, :w], in_=xn[:, :, :w]),
                         reads=self.b_xn2[s], writes=[b_xs_out[bi]], dma=f"xo{s}")

        seq = [(bi_, dc_) for bi_ in range(len(self.blocks)) for dc_ in range(NCH)]

        def xr_load(k):
            if k >= len(seq):
                return
            bi_, dc_ = seq[k]
            c0_, c1_ = self.blocks[bi_]
            w_ = c1_ - c0_
            xi_ = k % 4
            xr_ = self.xr[xi_]
            if res[0] == "xs":
                rv, rb = xs_in[bi_, :, dc_, :w_], b_xs_in[bi_]
            else:
                rv, rb = res[1][dc_ * 128:(dc_ + 1) * 128, c0_:c1_], res[2]
            P.op("sp", lambda e: e.dma_start(out=xr_[:, :w_], in_=rv), reads=[rb], writes=[self.b_xr[xi_]], dma=f"xr{xi_}")

        for k0 in range(3):
            xr_load(k0)
        scaled = True if res[0] == "xs" else res[3]
        kpos = 0
        for bi, (c0, c1) in enumerate(self.blocks):
            w = c1 - c0
            s = bi % 2
            xn = self.xn2[s]
            if kind == "ffn":
                for q in range(4):
                    k0, k1 = q * 11, (q + 1) * 11
                    wr = self.b_abc[k0:k1] + (self.b_hbk if q == 0 else [])
                    P.op("sp", lambda e, bi=bi, w=w, k0=k0, k1=k1: e.dma_start(out=ab[:, k0:k1, :w], in_=A[bi, :, k0:k1, :w]),
                         reads=[self.b_A[bi]], writes=wr, dma=f"lda{q}")
                self.b_A[bi].w = []
                ldw = lambda dc: self.load_wb(W, dc)
                nxt = ldw(0)
            else:
                nxtp = self.load_wa_pair(W, 0)
            stats_prev = None
            for dc in range(NCH):
                if kind == "ffn":
                    iw = nxt
                    if dc + 1 < NCH:
                        nxt = ldw(dc + 1)
                else:
                    if dc % 2 == 0:
                        curp = nxtp
                        if dc + 2 < NCH:
                            nxtp = self.load_wa_pair(W, dc // 2 + 1)
                    iw = curp[dc % 2]
                pb = dc % 2
                xi = kpos % 4
                xr = self.xr[xi]
                if kind == "ffn" and dc == 0:
                    for q in range(4):
                        def f(e, q=q, iw=iw, pb=pb, w=w):
                            for k in range(q * 11, (q + 1) * 11):
                                ins = e.matmul(self.py[pb][:, :w], lhsT=self.wb[iw][:, k, :], rhs=ab[:, k, 0:w],
                                               start=(k == 0), stop=(k == FCH - 1))
                            return ins
                        P.op("pe", f, reads=[self.b_wb[iw]] + self.b_abc[q * 11:(q + 1) * 11], writes=[self.b_py[pb]])
                elif kind == "ffn":
                    self.mm(self.py[pb], self.b_py[pb], self.wb[iw], self.b_wb[iw], FCH, ab, 0, w, self.b_abc)
                else:
                    self.mm(self.py[pb], self.b_py[pb], self.wa[iw], self.b_wa[iw], NCH, hb, c0, c1, [self.b_hbk[bi]])
                py = self.py[pb]
                bx = self.b_xn2[s][dc]
                if scaled:
                    first = True
                    for si, a, b in self.segparts(c0, c1):
                        kw = dict(writes=[bx]) if first else dict(cowrites=[bx])
                        first = False
                        P.op("dve", lambda e, si=si, a=a, b=b, xn=xn, xr=xr, py=py, dc=dc: e.scalar_tensor_tensor(
                            out=xn[:, dc, a:b], in0=py[:, a:b], scalar=self.vc(vgate[si], dc), in1=xr[:, a:b],
                            op0=ALU.mult, op1=ALU.add),
                            reads=[self.b_py[pb], self.b_vecs, self.b_xr[xi]], **kw)
                else:
                    ti = self.zt_i
                    self.zt_i ^= 1
                    tmp = self.tmp[ti]
                    for si, a, b in self.segparts(c0, c1):
                        P.op("act", lambda e, si=si, a=a, b=b, tmp=tmp, py=py, dc=dc: e.activation(
                            out=tmp[:, a:b], in_=py[:, a:b], func=AF.Copy, scale=self.vc(vgate[si], dc)),
                            reads=[self.b_py[pb], self.b_vecs], cowrites=[self.b_tmp[ti]])
                    P.op("dve", lambda e, xn=xn, xr=xr, tmp=tmp, dc=dc, w=w: e.scalar_tensor_tensor(
                        out=xn[:, dc, :w], in0=xr[:, :w], scalar=ALPHA, in1=tmp[:, :w], op0=ALU.mult, op1=ALU.add),
                        reads=[self.b_xr[xi], self.b_tmp[ti]], writes=[bx])
                    self.b_tmp[ti].w = []
                xr_load(kpos + 3)
                kpos += 1
                zi = self.zb_i
                self.zb_i ^= 1
                zb, zq = self.zb[zi], self.zq[zi]
                P.op("act", lambda e, xn=xn, zb=zb, dc=dc, w=w: e.activation(out=zb[:, :w], in_=xn[:, dc, :w], func=AF.Copy),
                     reads=[bx], writes=[self.b_zb[zi]])
                P.op("dve", lambda e, xn=xn, zq=zq, dc=dc, w=w: e.tensor_tensor(out=zq[:, :w], in0=xn[:, dc, :w],
                                                                               in1=xn[:, dc, :w], op=ALU.mult),
                     reads=[bx], writes=[self.b_zq[zi]])

                def stats(e, zb=zb, zq=zq, dc=dc, w=w):
                    e.matmul(s1[:, :w], lhsT=ones[:], rhs=zb[:, :w], start=(dc == 0), stop=(dc == NCH - 1))
                    return e.matmul(s2[:, :w], lhsT=ones[:], rhs=zq[:, :w], start=(dc == 0), stop=(dc == NCH - 1))
                this_stats = (stats, [self.b_zb[zi], self.b_zq[zi], self.b_ones])
                if stats_prev is not None:
                    P.op("pe", stats_prev[0], reads=stats_prev[1], cowrites=[self.b_s1, self.b_s2])
                stats_prev = this_stats
                if pending:
                    pending.pop(0)()
                if extra and extra.get(bi):
                    ex = extra[bi]
                    for _ in range(-(-len(ex) // (NCH - dc))):
                        ex.pop(0)()
            P.op("pe", stats_prev[0], reads=stats_prev[1], cowrites=[self.b_s1, self.b_s2])
            while pending:
                pending.pop(0)()
            P.op("dve", lambda e, w=w: e.tensor_scalar_mul(out=mu[:, :w], in0=s1[:, :w], scalar1=1.0 / D),
                 reads=[self.b_s1], writes=[self.b_mu])
            P.op("dve", lambda e, w=w: e.tensor_tensor(out=msq[:, :w], in0=mu[:, :w], in1=mu[:, :w], op=ALU.mult),
                 reads=[self.b_mu], writes=[self.b_msq])
            P.op("dve", lambda e, w=w: e.scalar_tensor_tensor(out=rstd[:, :w], in0=s2[:, :w], scalar=1.0 / D, in1=msq[:, :w],
                                                              op0=ALU.mult, op1=ALU.subtract),
                 reads=[self.b_s2, self.b_msq], writes=[self.b_rstd])
            P.op("dve", lambda e, w=w: e.tensor_scalar_add(out=rstd[:, :w], in0=rstd[:, :w], scalar1=LN_EPS),
                 reads=[self.b_rstd], writes=[self.b_rstd])
            P.op("act", lambda e, w=w: e.activation(out=rstd[:, :w], in_=rstd[:, :w], func=AF.Sqrt),
                 reads=[self.b_rstd], writes=[self.b_rstd])
            P.op("dve", lambda e, w=w: e.reciprocal(out=rstd[:, :w], in_=rstd[:, :w]),
                 reads=[self.b_rstd], writes=[self.b_rstd])
            self.b_s1.w, self.b_s2.w = [], []
            for c in range(NCH):
                pending.append(lambda bi=bi, c0=c0, c1=c1, s=s, c=c: ln_apply(bi, c0, c1, s, c))
        nb = len(self.blocks)
        if reload_hb:
            self.load_hb(only=range(nb - 1))
        while pending:
            pending.pop(0)()
        if reload_hb:
            self.load_hb(only=[nb - 1])
        if out_ap is None:
            self.xs_cur ^= 1


def tile_w(W, gcols):
    K, N = W.shape
    return np.ascontiguousarray(W.reshape(K // 128, 128, N // gcols, gcols).transpose(2, 1, 0, 3))


def dram_in(nc, name, shape, dt=F32):
    return nc.dram_tensor(name, list(shape), dt, kind="ExternalInput").ap()


def dram_out(nc, name, shape, dt=F32):
    return nc.dram_tensor(name, list(shape), dt, kind="ExternalOutput").ap()


def ffn_w_in(nc, sfx):
    return (dram_in(nc, "wgu" + sfx, [FCH, 128, NCH, 256]), dram_in(nc, "wd" + sfx, [NCH, 128, FCH, 128]))


MODC = 2 * 9 * D // NCORE


def build_L0():
    nc = bass.Bass("TRN2", target_bir_lowering=False)
    cv_ap = dram_in(nc, "cv", [128, NCH, 3])
    w_ap = dram_in(nc, "w", [D, MODC])
    b_ap = dram_in(nc, "b", [3, MODC])
    o_ap = dram_out(nc, "mod", [3, MODC])
    P = Prog(nc)
    cv = P.sb("cv", [128, NCH, 3], F32)
    cb = P.sb("cb", [128, NCH, 3], BF16)
    bt = P.sb("bt", [3, MODC], F32)
    ot = P.sb("ot", [3, MODC], F32)
    wt = [P.sb(f"w{i}", [128, NCH, 512], BF16) for i in range(2)]
    ps = [P.ps(f"ps{i}", [128, 512]) for i in range(2)]
    b_cv, b_cb, b_bt, b_ot = Buf(), Buf(), Buf(), Buf()
    b_w, b_ps = [Buf(), Buf()], [Buf(), Buf()]
    P.op("sp", lambda e: e.dma_start(out=cv[:], in_=cv_ap), writes=[b_cv], dma="cv")
    P.op("sp", lambda e: e.dma_start(out=bt[:], in_=b_ap), writes=[b_bt], dma="bt")
    P.op("act", lambda e: e.activation(out=cb[:], in_=cv[:], func=AF.Silu), reads=[b_cv], writes=[b_cb])
    wv = w_ap.rearrange("(k p) f -> p k f", p=128)
    for t in range(MODC // 512):
        i = t % 2
        P.op("pool", lambda e, t=t, i=i: e.dma_start(out=wt[i][:], in_=wv[:, :, t * 512:(t + 1) * 512]),
             writes=[b_w[i]], dma=f"w{i}")

        def f(e, i=i):
            for k in range(NCH):
                ins = e.matmul(ps[i][0:3, :], lhsT=cb[:, k, :], rhs=wt[i][:, k, :], start=(k == 0), stop=(k == NCH - 1))
            return ins
        P.op("pe", f, reads=[b_cb, b_w[i]], writes=[b_ps[i]])
        P.op("dve", lambda e, t=t, i=i: e.tensor_tensor(out=ot[:, t * 512:(t + 1) * 512], in0=ps[i][0:3, :],
                                                        in1=bt[:, t * 512:(t + 1) * 512], op=ALU.add),
             reads=[b_ps[i], b_bt], writes=[b_ot])
    P.op("sp", lambda e: e.dma_start(out=o_ap, in_=ot[:]), reads=[b_ot], writes=[Buf()], dma="st")
    P.finish()
    P.emit()
    return nc


def simple_proj(R, W, ng, dst, b_o):
    P = R.P
    nxt = R.load_wa(W, 0)
    for g in range(ng):
        iw = nxt
        if g + 1 < ng:
            nxt = R.load_wa(W, g + 1)
        for bi, (c0, c1) in enumerate(R.blocks):
            w = c1 - c0
            pb = R.pp_i
            R.pp_i ^= 1
            R.mm(R.pg[pb], R.b_pg[pb], R.wa[iw], R.b_wa[iw], NCH, R.hb, c0, c1, [R.b_hbk[bi]])
            i = R.zt_i
            R.zt_i ^= 1
            zt, pg = R.zt[i], R.pg[pb]
            P.op("act", lambda e, pg=pg, zt=zt, w=w: e.activation(out=zt[:, :w], in_=pg[:, :w], func=AF.Copy),
                 reads=[R.b_pg[pb]], writes=[R.b_zt[i]])
            R.store(dst, b_o, g * 128, c0, c1, zt, R.b_zt[i])


def build_LA(T, segs):
    nc = bass.Bass("TRN2", target_bir_lowering=False)
    nseg = len(segs)
    nvec = 5 * nseg + 2
    x_ap = dram_in(nc, "xT", [D, T])
    vecs_ap = dram_in(nc, "vecs", [128, nvec, NCH])
    wgu, wd = ffn_w_in(nc, "")
    w_gb = dram_in(nc, "w_gb", [8, 128, NCH, 128])
    w_gc = dram_in(nc, "w_gc", [8, 128, NCH, 128])
    w_xi = dram_in(nc, "w_xi", [8, 128, NCH, 128])
    w_uf = dram_in(nc, "w_uf", [8, 128, NCH, 128])
    xn1 = dram_out(nc, "xn1", [D, T])
    gb = dram_out(nc, "gb", [1024, T])
    vv = dram_out(nc, "vv", [1024, T])
    uf = dram_out(nc, "uf", [1024, T])
    P = Prog(nc)
    R = RL(P, T, segs, vecs_ap, nvec)
    V = lambda s, k: 5 * s + k
    SL = lambda k: [V(s, k) for s in range(nseg)]
    VG, VB = 5 * nseg, 5 * nseg + 1
    for s in range(nseg):
        R.derive(V(s, 1), add=1.0)
        R.derive(V(s, 2), mul=0.5)
        R.derive(V(s, 4), add=1.0)
        R.derive(V(s, 4), mul=1.0 / ALPHA)
    R.derive(VG, mul=ALPHA)
    R.derive(VB, mul=ALPHA)
    b_x = Buf()
    b_o = Buf()
    R.first_prologue(x_ap, b_x, SL(0), SL(1))
    R.phase1(wgu)
    R.phase2("ffn", wd, ("ap", x_ap, b_x, False), SL(2), ln=(VG, VB), mod=(SL(3), SL(4)), out_ap=xn1, reload_hb=True)
    simple_proj(R, w_gb, 8, gb, b_o)
    ld = lambda jj: (R.load_wa(w_gc, jj), R.load_wa(w_xi, jj))
    nxt = ld(0)
    for jj in range(8):
        ic, ix = nxt
        if jj + 1 < 8:
            nxt = ld(jj + 1)
        for bi, (c0, c1) in enumerate(R.blocks):
            w = c1 - c0
            pb = R.pp_i
            R.pp_i ^= 1
            R.mm(R.pg[pb], R.b_pg[pb], R.wa[ic], R.b_wa[ic], NCH, R.hb, c0, c1, [R.b_hbk[bi]])
            R.mm(R.pu[pb], R.b_pu[pb], R.wa[ix], R.b_wa[ix], NCH, R.hb, c0, c1, [R.b_hbk[bi]])
            i = R.zt_i
            R.zt_i ^= 1
            zt, sg, pg, pu = R.zt[i], R.sg[pb], R.pg[pb], R.pu[pb]
            P.op("act", lambda e, pg=pg, sg=sg, w=w: e.activation(out=sg[:, :w], in_=pg[:, :w], func=AF.Copy),
                 reads=[R.b_pg[pb]], writes=[R.b_sg[pb]])
            P.op("dve", lambda e, pu=pu, sg=sg, zt=zt, w=w: e.tensor_tensor(out=zt[:, :w], in0=pu[:, :w], in1=sg[:, :w],
                                                                           op=ALU.mult),
                 reads=[R.b_pu[pb], R.b_sg[pb]], writes=[R.b_zt[i]])
            R.store(vv, b_o, jj * 128, c0, c1, zt, R.b_zt[i])
    simple_proj(R, w_uf, 8, uf, b_o)
    P.finish()
    P.emit()
    return nc


def build_LF():
    nc = bass.Bass("TRN2", target_bir_lowering=False)
    xl = dram_in(nc, "xl", [2, SEQ, 128])
    xc = dram_in(nc, "xc", [2, 128, CTX])
    ftw_ap = dram_in(nc, "ftw", [128, 64, 256])
    fc_ap = dram_in(nc, "fc", [128, 2, 256])
    c64_ap = dram_in(nc, "c64", [64, 2, 64])
    c256_ap = dram_in(nc, "c256", [128, 4, 256])
    yl = dram_out(nc, "yl", [2, 128, SEQ])
    yc = dram_out(nc, "yc", [2, 128, CTX])
    P = Prog(nc)
    ftw = P.sb("ftw", [128, 64, 256], BF16)
    fc = P.sb("fc", [128, 2, 256], BF16)
    c64 = P.sb("c64", [64, 2, 64], BF16)
    c256 = P.sb("c256", [128, 4, 256], BF16)
    XA = P.sb("XA", [128, 64, 128], BF16)
    D1 = P.sb("D1", [128, 64, 256], BF16)
    D2 = P.sb("D2", [64, 128, 256], BF16)
    Y = P.sb("Y", [128, SEQ], F32)
    XC = P.sb("XC", [128, CTX], BF16)
    DA = P.sb("DA", [128, 2, 256], BF16)
    YC = P.sb("YC", [128, CTX], F32)
    pp = [P.ps(f"pp{i}", [128, 512]) for i in range(4)]
    b_pp = [Buf() for _ in range(4)]
    b_t, b_XA, b_D1, b_D2, b_Y, b_XC, b_DA, b_YC = (Buf() for _ in range(8))
    for t, ap, k in ((ftw, ftw_ap, "t0"), (fc, fc_ap, "t1"), (c64, c64_ap, "t2"), (c256, c256_ap, "t3")):
        P.op("pool", lambda e, t=t, ap=ap: e.dma_start(out=t[:], in_=ap), cowrites=[b_t], dma=k)
    pi = [0]

    def nextp():
        pi[0] = (pi[0] + 1) % 4
        return pi[0]
    ev = [0]

    def evac(out_ap, in_ap, rd, wr, co=False):
        ev[0] ^= 1
        kw = dict(cowrites=[wr]) if co else dict(writes=[wr])
        if ev[0]:
            P.op("act", lambda e: e.activation(out=out_ap, in_=in_ap, func=AF.Copy), reads=[rd], **kw)
        else:
            P.op("dve", lambda e: e.tensor_copy(out=out_ap, in_=in_ap), reads=[rd], **kw)

    for u in range(2):
        src = xl[u].rearrange("(a r) c -> a r c", r=64)
        P.op("pool", lambda e, src=src: e.dma_start(out=XA[:], in_=src), writes=[b_XA], dma="xa")
        P.op("pool", lambda e, u=u: e.dma_start(out=XC[:], in_=xc[u]), writes=[b_XC], dma="xc")
        for b0 in range(0, 64, 2):
            i = nextp()

            def f(e, b0=b0, i=i):
                for q in range(2):
                    ins = e.matmul(pp[i][:, q * 256:(q + 1) * 256], lhsT=XA[:, b0 + q, :], rhs=ftw[:, b0 + q, :],
                                   start=True, stop=True)
                return ins
            P.op("pe", f, reads=[b_XA, b_t], writes=[b_pp[i]])
            evac(D1[:, b0:b0 + 2, :], pp[i][:].rearrange("p (q n) -> p q n", q=2), b_pp[i], b_D1, co=True)
        for a0 in range(0, 128, 2):
            i = nextp()

            def f(e, a0=a0, i=i):
                for q in range(2):
                    e.matmul(pp[i][0:64, q * 256:(q + 1) * 256], lhsT=D1[:, :, a0 + q], rhs=fc[:, 0, :], start=True, stop=False)
                    ins = e.matmul(pp[i][0:64, q * 256:(q + 1) * 256], lhsT=D1[:, :, 128 + a0 + q], rhs=fc[:, 1, :],
                                   start=False, stop=True)
                return ins
            P.op("pe", f, reads=[b_D1, b_t], writes=[b_pp[i]])
            evac(D2[:, a0:a0 + 2, :], pp[i][0:64, :].rearrange("p (q n) -> p q n", q=2), b_pp[i], b_D2, co=True)
        Yv = Y[:].rearrange("p (b a) -> p a b", a=128)
        for a0 in range(0, 128, 8):
            i = nextp()

            def f(e, a0=a0, i=i):
                for q in range(8):
                    e.matmul(pp[i][:, q * 64:(q + 1) * 64], lhsT=D2[:, a0 + q, 0:128], rhs=c64[:, 0, :], start=True, stop=False)
                    ins = e.matmul(pp[i][:, q * 64:(q + 1) * 64], lhsT=D2[:, a0 + q, 128:256], rhs=c64[:, 1, :],
                                   start=False, stop=True)
                return ins
            P.op("pe", f, reads=[b_D2, b_t], writes=[b_pp[i]])
            evac(Yv[:, a0:a0 + 8, :], pp[i][:].rearrange("p (a b) -> p a b", a=8), b_pp[i], b_Y, co=True)
        P.op("sp", lambda e, u=u: e.dma_start(out=yl[u], in_=Y[:]), reads=[b_Y], writes=[Buf()], dma="sty")
        i = nextp()

        def f(e, i=i):
            for j in range(2):
                ins = e.matmul(pp[i][:, j * 256:(j + 1) * 256], lhsT=XC[:, j * 128:(j + 1) * 128], rhs=fc[:, 0, :],
                               start=True, stop=True)
            return ins
        P.op("pe", f, reads=[b_XC, b_t], writes=[b_pp[i]])
        evac(DA[:], pp[i][:].rearrange("p (j n) -> p j n", j=2), b_pp[i], b_DA)
        i = nextp()

        def f(e, i=i):
            n = 0
            for j in range(2):
                for ri in range(2):
                    ins = e.matmul(pp[i][:, 0:256], lhsT=DA[:, j, ri * 128:(ri + 1) * 128], rhs=c256[:, 2 * ri + j, :],
                                   start=(n == 0), stop=(n == 3))
                    n += 1
            return ins
        P.op("pe", f, reads=[b_DA, b_t], writes=[b_pp[i]])
        evac(YC[:], pp[i][:, 0:256], b_pp[i], b_YC)
        P.op("sp", lambda e, u=u: e.dma_start(out=yc[u], in_=YC[:]), reads=[b_YC], writes=[Buf()], dma="styc")
    P.finish()
    P.emit()
    return nc


def fft_tables():
    a = np.arange(128)[:, None, None]
    b = np.arange(64)[None, :, None]
    ap = np.arange(128)[None, None, :]
    th = 2 * np.pi * (a * ap / 128.0 + b * ap / 8192.0)
    ftw = np.concatenate([np.cos(th), -np.sin(th)], axis=-1) / np.sqrt(128.0)
    c = np.arange(128)[:, None]
    cp = np.arange(128)[None, :]
    th = 2 * np.pi * c * cp / 128.0
    cr, ci = np.cos(th) / np.sqrt(128.0), -np.sin(th) / np.sqrt(128.0)
    fc = np.stack([np.concatenate([cr, ci], 1), np.concatenate([-ci, cr], 1)], axis=1)
    bb = np.arange(64)[:, None]
    bp = np.arange(64)[None, :]
    th = 2 * np.pi * bb * bp / 64.0
    c64 = np.stack([np.cos(th), np.sin(th)], axis=1) / 8.0
    l = np.arange(256)[:, None]
    lp = np.arange(256)[None, :]
    th = 2 * np.pi * l * lp / 256.0
    C, S = np.cos(th) / 16.0, np.sin(th) / 16.0
    c256 = np.stack([C[0:128], C[128:256], S[0:128], S[128:256]], axis=1)
    f = lambda x: np.ascontiguousarray(x, dtype=np.float32)
    return f(ftw), f(fc), f(c64), f(c256)


NBLK = SEQ // 128


def build_LC(nblk=NBLK):
    nc = bass.Bass("TRN2", target_bir_lowering=False)
    S = nblk * 128
    NS = 4
    qt_ap = dram_in(nc, "qt", [128, 4, S])
    kt_ap = dram_in(nc, "kt", [128, S + CTX])
    v_ap = dram_in(nc, "v", [128, nblk + 2, 64])
    mask_ap = dram_in(nc, "mask", [128, 384])
    sink_ap = dram_in(nc, "sink", [128, 8])
    id_ap = dram_in(nc, "ident", [128, 128])
    o_ap = dram_out(nc, "o", [S, 512])
    P = Prog(nc)
    QT = P.sb("QT", [128, 4, S], BF16)
    KT = P.sb("KT", [128, S + CTX], BF16)
    V = P.sb("V", [128, nblk + 2, 64], BF16)
    mask = P.sb("mask", [128, 384], F32)
    sink = P.sb("sink", [128, 8], F32)
    sink8 = P.sb("sink8", [128, 8], F32)
    ident = P.sb("ident", [128, 128], BF16)
    sc = [P.sb(f"sc{i}", [128, 648], F32) for i in range(NS)]
    pb = [P.sb(f"pb{i}", [128, 648], BF16) for i in range(NS)]
    pTs = [P.sb(f"pTs{i}", [128, 5, 128], BF16) for i in range(NS)]
    sm = [P.sb(f"sm{i}", [128, 4], F32) for i in range(NS)]
    ot = [P.sb(f"ot{i}", [128, 512], F32) for i in range(2)]
    scA = [P.ps(f"scA{i}", [128, 512]) for i in range(2)]
    scB = [P.ps(f"scB{i}", [128, 512]) for i in range(2)]
    pTt = [P.ps(f"pTt{i}", [128, 8, 128], BF16) for i in range(2)]
    ops = [P.ps(f"ops{i}", [128, 512]) for i in range(2)]
    BL = lambda n: [Buf() for _ in range(n)]
    b_c, b_s8 = Buf(), Buf()
    b_sc, b_pb, b_pTs, b_sm = BL(NS), BL(NS), BL(NS), BL(NS)
    b_ot, b_scA, b_scB, b_pTt, b_ops = BL(2), BL(2), BL(2), BL(2), BL(2)
    for t, ap, k in ((QT, qt_ap, "c0"), (KT, kt_ap, "c1"), (V, v_ap, "c2"), (ident, id_ap, "c3")):
        P.op("pool", lambda e, t=t, ap=ap: e.dma_start(out=t[:], in_=ap), cowrites=[b_c], dma=k)
    for t, ap, k in ((mask, mask_ap, "c4"), (sink, sink_ap, "c5")):
        P.op("sp", lambda e, t=t, ap=ap: e.dma_start(out=t[:], in_=ap), cowrites=[b_c], dma=k)
    P.op("dve", lambda e: e.tensor_scalar_mul(out=sink8[:], in0=sink[:], scalar1=8.0), reads=[b_c], writes=[b_s8])
    units = [(i, r) for i in range(nblk) for r in range(8)]
    N = len(units)

    def geo(n):
        i, r = units[n]
        kb0, kb1 = max(i - 1, 0), min(i + 1, nblk - 1)
        nloc = kb1 - kb0 + 1
        nk = nloc * 128
        mo = (kb0 - (i - 1)) * 128
        return i, r, kb0, nloc, nk, mo, nk + 256

    def S1(n):
        i, r, kb0, nloc, nk, mo, L = geo(n)
        s, p = n % NS, n % 2
        hh, j = r // 4, r % 4
        p0, p1 = hh * 64, hh * 64 + 64
        q = QT[p0:p1, j, i * 128:(i + 1) * 128]

        def f(e):
            e.matmul(scA[p][:, 0:nk], lhsT=q, rhs=KT[p0:p1, kb0 * 128:kb0 * 128 + nk], start=True, stop=True)
            return e.matmul(scB[p][:, 0:256], lhsT=q, rhs=KT[p0:p1, S:S + CTX], start=True, stop=True)
        P.op("pe", f, reads=[b_c], writes=[b_scA[p], b_scB[p]])
        P.op("dve", lambda e: e.tensor_tensor(out=sc[s][:, 0:nk], in0=scA[p][:, 0:nk], in1=mask[:, mo:mo + nk], op=ALU.add),
             reads=[b_scA[p], b_c], writes=[b_sc[s]])
        P.op("act", lambda e: e.activation(out=sc[s][:, nk:L], in_=scB[p][:, 0:256], func=AF.Copy),
             reads=[b_scB[p]], cowrites=[b_sc[s]])
        P.op("pool", lambda e: e.tensor_copy(out=sc[s][:, L:L + 1], in_=sink8[:, r:r + 1]),
             reads=[b_s8], cowrites=[b_sc[s]])
        m = sm[s]
        P.op("dve", lambda e: e.reduce_max(out=m[:, 0:1], in_=sc[s][:, 0:L + 1], axis=AX.X),
             reads=[b_sc[s]], writes=[b_sm[s]])
        P.op("dve", lambda e: e.tensor_scalar_mul(out=m[:, 1:2], in0=m[:, 0:1], scalar1=-0.125),
             reads=[b_sm[s]], writes=[b_sm[s]])

    def S2(n):
        i, r, kb0, nloc, nk, mo, L = geo(n)
        s, p = n % NS, n % 2
        nt = nloc + 2
        m = sm[s]
        P.op("act", lambda e: e.activation(out=pb[s][:, 0:L + 1], in_=sc[s][:, 0:L + 1], func=AF.Exp,
                                           scale=0.125, bias=m[:, 1:2], accum_out=m[:, 2:3]),
             reads=[b_sc[s], b_sm[s]], writes=[b_pb[s], b_sm[s]])
        P.op("dve", lambda e: e.reciprocal(out=m[:, 3:4], in_=m[:, 2:3]), reads=[b_sm[s]], writes=[b_sm[s]])

        def f(e):
            for t in range(nt):
                ins = e.transpose(out=pTt[p][:, t, :], in_=pb[s][:, t * 128:(t + 1) * 128], identity=ident[:])
            return ins
        P.op("pe", f, reads=[b_pb[s], b_c], writes=[b_pTt[p]])
        if n % 2:
            P.op("act", lambda e: e.activation(out=pTs[s][:, 0:nt, :], in_=pTt[p][:, 0:nt, :], func=AF.Copy),
                 reads=[b_pTt[p]], writes=[b_pTs[s]])
        else:
            P.op("dve", lambda e: e.tensor_copy(out=pTs[s][:, 0:nt, :], in_=pTt[p][:, 0:nt, :]),
                 reads=[b_pTt[p]], writes=[b_pTs[s]])

    def S3(n):
        i, r, kb0, nloc, nk, mo, L = geo(n)
        s, p = n % NS, n % 2
        nt = nloc + 2
        oi = i % 2
        m = sm[s]

        def f(e):
            for t in range(nt):
                vb = kb0 + t if t < nloc else nblk + (t - nloc)
                ins = e.matmul(ops[p][:, 0:64], lhsT=pTs[s][:, t, :], rhs=V[:, vb, :], start=(t == 0), stop=(t == nt - 1))
            return ins
        P.op("pe", f, reads=[b_pTs[s], b_c], writes=[b_ops[p]])
        P.op("act", lambda e: e.activation(out=ot[oi][:, r * 64:(r + 1) * 64], in_=ops[p][:, 0:64], func=AF.Copy,
                                           scale=m[:, 3:4]),
             reads=[b_ops[p], b_sm[s]], cowrites=[b_ot[oi]])
        if r == 7:
            P.op("sp", lambda e: e.dma_start(out=o_ap[i * 128:(i + 1) * 128, :], in_=ot[oi][:]),
                 reads=[b_ot[oi]], writes=[Buf()], dma=f"so{oi}")
            b_ot[oi].w = []

    for k in range(N + 2):
        if k < N:
            S1(k)
        if 0 <= k - 1 < N:
            S2(k - 1)
        if 0 <= k - 2 < N:
            S3(k - 2)
    P.finish()
    P.emit()
    return nc


def attn_mask():
    a = np.arange(128)[:, None]
    j = np.arange(384)[None, :]
    return np.where(np.abs(j - 128 - a) <= 128, 0.0, -1e30).astype(np.float32)


def build_LB(T, segs):
    nc = bass.Bass("TRN2", target_bir_lowering=False)
    nseg = len(segs)
    nvec = 9 * nseg + 11
    xn1 = dram_in(nc, "xn1", [D, T])
    gb = dram_in(nc, "gb", [1024, T])
    vp = dram_in(nc, "vp", [1024, T])
    vv = dram_in(nc, "vv", [1024, T])
    vn = dram_in(nc, "vn", [1024, T])
    yb = dram_in(nc, "yb", [1024, T])
    vecs_ap = dram_in(nc, "vecs", [128, nvec, NCH])
    wout = dram_in(nc, "wout", [8, 128, NCH, 256])
    wgu2, wd2 = ffn_w_in(nc, "2")
    wgu3, wd3 = ffn_w_in(nc, "3")
    wqkv = dram_in(nc, "wqkv", [20, 128, NCH, 128])
    wperm = dram_in(nc, "wperm", [18, 128, NCH, 128])
    cos_ap = dram_in(nc, "cosT", [128, T])
    sin_ap = dram_in(nc, "sinT", [128, T])
    xn4 = dram_out(nc, "xn4", [D, T])
    qr = dram_out(nc, "qr", [D, T])
    kr = dram_out(nc, "kr", [256, T])
    vo = dram_out(nc, "vo", [256, T])
    P = Prog(nc)
    R = RL(P, T, segs, vecs_ap, nvec)
    V = lambda s, k: 9 * s + k
    L0 = 9 * nseg
    for s in range(nseg):
        R.derive(V(s, 3), mul=0.5)
        R.derive(V(s, 6), mul=0.5)
        for k in (2, 5, 8):
            R.derive(V(s, k), add=1.0)
            R.derive(V(s, k), mul=1.0 / ALPHA)
    for k in range(2, 8):
        R.derive(L0 + k, mul=ALPHA)
    SL = lambda k: [V(s, k) for s in range(nseg)]
    b_in = Buf()
    b_o = Buf()

    def mix_fill(bi, c0, c1, w, j):
        tl = [R.sg[0], R.sg[1], R.zt[0], R.zt[1], R.tmp[0]]
        tb = [R.b_sg[0], R.b_sg[1], R.b_zt[0], R.b_zt[1], R.b_tmp[0]]
        if True:
            for k, src in enumerate((gb, vp, vv, vn, yb)):
                P.op("sp", lambda e, k=k, src=src, j=j: e.dma_start(out=tl[k][:, :w], in_=src[j * 128:(j + 1) * 128, c0:c1]),
                     writes=[tb[k]], dma=f"m{k}")
            P.op("dve", lambda e, j=j: e.tensor_scalar_mul(out=tl[1][:, :w], in0=tl[1][:, :w], scalar1=R.vc(L0 + 8, j)),
                 reads=[R.b_vecs], writes=[tb[1]])
            P.op("dve", lambda e, j=j: e.scalar_tensor_tensor(out=tl[1][:, :w], in0=tl[2][:, :w], scalar=R.vc(L0 + 9, j),
                                                              in1=tl[1][:, :w], op0=ALU.mult, op1=ALU.add),
                 reads=[R.b_vecs, tb[2]], writes=[tb[1]])
            P.op("dve", lambda e, j=j: e.scalar_tensor_tensor(out=tl[1][:, :w], in0=tl[3][:, :w], scalar=R.vc(L0 + 10, j),
                                                              in1=tl[1][:, :w], op0=ALU.mult, op1=ALU.add),
                 reads=[R.b_vecs, tb[3]], writes=[tb[1]])
            P.op("dve", lambda e, j=j: e.tensor_tensor(out=R.hb[:, j, c0:c1], in0=tl[1][:, :w], in1=tl[0][:, :w], op=ALU.mult),
                 reads=[tb[0], tb[1]], cowrites=R.whb(bi))
            P.op("act", lambda e, j=j: e.activation(out=R.hb[:, 8 + j, c0:c1], in_=tl[4][:, :w], func=AF.Copy),
                 reads=[tb[4]], cowrites=R.whb(bi))

    extra = {}
    for bi, (c0, c1) in enumerate(R.blocks):
        fl = [lambda bi=bi, c0=c0, c1=c1, j=j: mix_fill(bi, c0, c1, c1 - c0, j) for j in range(8)]
        if bi == 0:
            for f in fl:
                f()
        else:
            extra[bi - 1] = fl
    R.phase2("proj", wout, ("ap", xn1, b_in, True), SL(0), ln=(L0 + 2, L0 + 3), mod=(SL(1), SL(2)), reload_hb=True,
             extra=extra)
    R.phase1(wgu2)
    R.phase2("ffn", wd2, ("xs",), SL(3), ln=(L0 + 4, L0 + 5), mod=(SL(4), SL(5)), reload_hb=True)
    R.phase1(wgu3)
    R.phase2("ffn", wd3, ("xs",), SL(6), ln=(L0 + 6, L0 + 7), mod=(SL(7), SL(8)), out_ap=xn4, reload_hb=True)
    rt = R.xn2[0][:, :, :].rearrange("p c t -> p (c t)")
    cosT, sinT = rt[:, 0:T], rt[:, T:2 * T]
    P.op("sp", lambda e: e.dma_start(out=cosT, in_=cos_ap), writes=R.b_xn2[0], dma="r0")
    P.op("sp", lambda e: e.dma_start(out=sinT, in_=sin_ap), cowrites=R.b_xn2[0], dma="r1")
    b_rope = R.b_xn2[0][0]
    ld = lambda g: (R.load_wa(wqkv, g), R.load_wa(wperm, g) if g < 18 else None)
    nxt = ld(0)
    for g in range(20):
        iw, ip = nxt
        if g + 1 < 20:
            nxt = ld(g + 1)
        for bi, (c0, c1) in enumerate(R.blocks):
            w = c1 - c0
            pb = R.pp_i
            R.pp_i ^= 1
            R.mm(R.pg[pb], R.b_pg[pb], R.wa[iw], R.b_wa[iw], NCH, R.hb, c0, c1, [R.b_hbk[bi]])
            i = R.zt_i
            R.zt_i ^= 1
            zt, tmp, sg, pg, pu = R.zt[i], R.tmp[i], R.sg[pb], R.pg[pb], R.pu[pb]
            if g < 18:
                R.mm(R.pu[pb], R.b_pu[pb], R.wa[ip], R.b_wa[ip], NCH, R.hb, c0, c1, [R.b_hbk[bi]])
                P.op("dve", lambda e, sg=sg, pg=pg, w=w, c0=c0, c1=c1: e.tensor_tensor(
                    out=sg[:, :w], in0=pg[:, :w], in1=cosT[:, c0:c1], op=ALU.mult),
                    reads=[R.b_pg[pb], b_rope], writes=[R.b_sg[pb]])
                P.op("dve", lambda e, tmp=tmp, pu=pu, w=w, c0=c0, c1=c1: e.tensor_tensor(
                    out=tmp[:, :w], in0=pu[:, :w], in1=sinT[:, c0:c1], op=ALU.mult),
                    reads=[R.b_pu[pb], b_rope], writes=[R.b_tmp[i]])
                P.op("dve", lambda e, zt=zt, sg=sg, tmp=tmp, w=w: e.tensor_tensor(
                    out=zt[:, :w], in0=sg[:, :w], in1=tmp[:, :w], op=ALU.add),
                    reads=[R.b_sg[pb], R.b_tmp[i]], writes=[R.b_zt[i]])
                dst, row = (qr, g * 128) if g < 16 else (kr, (g - 16) * 128)
            else:
                P.op("act", lambda e, zt=zt, pg=pg, w=w: e.activation(out=zt[:, :w], in_=pg[:, :w], func=AF.Copy),
                     reads=[R.b_pg[pb]], writes=[R.b_zt[i]])
                dst, row = vo, (g - 18) * 128
            R.store(dst, b_o, row, c0, c1, zt, R.b_zt[i])
    P.finish()
    P.emit()
    return nc


def build_LD(T):
    nc = bass.Bass("TRN2", target_bir_lowering=False)
    segs = [(0, T)]
    nvec = 10
    xn4 = dram_in(nc, "xn4", [D, T])
    oT = dram_in(nc, "oT", [D, T])
    vecs_ap = dram_in(nc, "vecs", [128, nvec, NCH])
    wout = dram_in(nc, "wout", [8, 128, NCH, 256])
    wgu, wd = ffn_w_in(nc, "")
    out = dram_out(nc, "out", [D, T])
    P = Prog(nc)
    R = RL(P, T, segs, vecs_ap, nvec)
    R.derive(2, add=1.0)
    R.derive(2, mul=1.0 / ALPHA)
    R.derive(3, mul=0.5)
    R.derive(6, mul=ALPHA)
    R.derive(7, mul=ALPHA)
    b_in = Buf()
    for bi, (c0, c1) in enumerate(R.blocks):
        P.op("pool", lambda e, c0=c0, c1=c1: e.dma_start(out=R.hb[:, :, c0:c1], in_=chunked(oT)[:, :, c0:c1]),
             cowrites=R.whb(bi), dma="oin")
    R.phase2("proj", wout, ("ap", xn4, b_in, True), [0], ln=(6, 7), mod=([1], [2]), reload_hb=True)
    R.phase1(wgu)
    R.phase2("ffn", wd, ("xs",), [3], ln=(8, 9), mod=None, out_ap=out)
    P.finish()
    P.emit()
    return nc


def lay(v):
    v = np.asarray(v, dtype=np.float32)
    if v.shape[0] < D:
        v = np.concatenate([v, np.zeros(D - v.shape[0], np.float32)])
    return v.reshape(NCH, 128).T


def rope_tables(pos):
    nf = 16
    inv = np.power(10000.0, -np.arange(nf, dtype=np.float64) / nf)
    row = (pos // 64).astype(np.float64)
    col = (pos % 64).astype(np.float64)
    d = np.arange(64)
    axis, part, f = d // 32, (d % 32) // 16, d % 16
    ang = np.where(axis[:, None] == 0, row[None, :], col[None, :]) * inv[f][:, None]
    c = np.cos(ang)
    s = np.sin(ang) * np.where(part == 0, -1.0, 1.0)[:, None]
    return np.concatenate([c, c], 0).astype(np.float32), np.concatenate([s, s], 0).astype(np.float32)


def rope_perm():
    d = np.arange(64)
    part = (d % 32) // 16
    p = np.where(part == 0, d + 16, d - 16)
    cols = np.concatenate([h * 64 + p for h in range(36)])
    return cols


_CACHE = {}


def _prog(key, fn):
    if key not in _CACHE:
        _CACHE[key] = fn()
    return _CACHE[key]


def _run(nc, ins):
    res = run_bass_kernel_spmd(nc, ins, core_ids=list(range(NCORE)))
    return res.results


def kernel(x, c, ctx, c_ctx, w_mod, b_mod, ln_g, ln_b, ffn_w_gate, ffn_w_up, ffn_w_down,
           ab_w_in, ab_conv, ab_w_out, attn_w_in, attn_sink, attn_w_out):
    f32 = lambda a: np.ascontiguousarray(np.asarray(a), dtype=np.float32)
    x, c, ctx, c_ctx = f32(x), f32(c), f32(ctx), f32(c_ctx)
    w_mod, b_mod, ln_g, ln_b = f32(w_mod), f32(b_mod), f32(ln_g), f32(ln_b)
    ffn_w_gate, ffn_w_up, ffn_w_down = f32(ffn_w_gate), f32(ffn_w_up), f32(ffn_w_down)
    ab_w_in, ab_conv, ab_w_out = f32(ab_w_in), f32(ab_conv), f32(ab_w_out)
    attn_w_in, attn_sink, attn_w_out = f32(attn_w_in), f32(attn_sink), f32(attn_w_out)
    LT, CT = SEQ // 4, CTX // 4
    T = LT + CT
    segs = [(0, LT), (LT, T)]
    cores = [(r // 4, r % 4) for r in range(NCORE)]

    cv = np.ascontiguousarray(np.stack([lay(c[0]), lay(c[1]), lay(c_ctx)], axis=-1))
    wm = np.concatenate([w_mod[0], w_mod[1]], axis=1)
    bm = b_mod.reshape(-1)
    ins = []
    for r in range(NCORE):
        sl = slice(r * MODC, (r + 1) * MODC)
        ins.append({"cv": cv, "w": np.ascontiguousarray(wm[:, sl]),
                    "b": np.ascontiguousarray(np.broadcast_to(bm[sl], (3, MODC)))})
    res = _run(_prog("L0", build_L0), ins)
    del wm
    mod = np.concatenate([res[r]["mod"] for r in range(NCORE)], axis=1).reshape(3, 2, 9, D)

    def mv(b, s, layer, k):
        return lay(mod[b if s == 0 else 2, layer, k])

    _ffw = {}

    def ffw(l, i, sfx):
        if (l, i) not in _ffw:
            _ffw[(l, i)] = (np.concatenate([tile_w(ffn_w_gate[l, i], 128), tile_w(ffn_w_up[l, i], 128)], axis=-1),
                            tile_w(ffn_w_down[l, i], 128))
        a, c_ = _ffw[(l, i)]
        return {"wgu" + sfx: a, "wd" + sfx: c_}

    w_gb, w_gc = tile_w(ab_w_in[0][:, 0:1024], 128), tile_w(ab_w_in[0][:, 1024:2048], 128)
    w_xi, w_uf = tile_w(ab_w_in[0][:, 2048:3072], 128), tile_w(ab_w_in[0][:, 3072:4096], 128)

    ins = []
    for b, q in cores:
        xT = np.concatenate([x[b, q * LT:(q + 1) * LT].T, ctx[b, q * CT:(q + 1) * CT].T], axis=1)
        vecs = [mv(b, s, 0, k) for s in range(2) for k in range(5)] + [lay(ln_g[0, 0]), lay(ln_b[0, 0])]
        ins.append({"xT": np.ascontiguousarray(xT), "vecs": np.ascontiguousarray(np.stack(vecs, axis=1)),
                    **ffw(0, 0, ""), "w_gb": w_gb, "w_gc": w_gc, "w_xi": w_xi, "w_uf": w_uf})
    resA = _run(_prog("LA", lambda: build_LA(T, segs)), ins)

    def gather(res, name, nrow):
        lat = np.empty((2, nrow, SEQ), np.float32)
        cx = np.empty((2, nrow, CTX), np.float32)
        for r, (b, q) in enumerate(cores):
            a = res[r][name]
            lat[b, :, q * LT:(q + 1) * LT] = a[:, :LT]
            cx[b, :, q * CT:(q + 1) * CT] = a[:, LT:]
        return lat, cx

    UFl, UFc = gather(resA, "uf", 1024)
    VVl, VVc = gather(resA, "vv", 1024)

    ftw, fc, c64, c256 = fft_tables()
    ins = []
    for b, q in cores:
        gs = [2 * q, 2 * q + 1]
        xl = np.stack([UFl[b, g * 128:(g + 1) * 128].T for g in gs])
        xc = np.stack([UFc[b, g * 128:(g + 1) * 128] for g in gs])
        ins.append({"xl": np.ascontiguousarray(xl), "xc": np.ascontiguousarray(xc),
                    "ftw": ftw, "fc": fc, "c64": c64, "c256": c256})
    resF = _run(_prog("LF", build_LF), ins)
    YBl = np.empty((2, 1024, SEQ), np.float32)
    YBc = np.empty((2, 1024, CTX), np.float32)
    for r, (b, q) in enumerate(cores):
        for u in range(2):
            g = 2 * q + u
            YBl[b, g * 128:(g + 1) * 128] = resF[r]["yl"][u]
            YBc[b, g * 128:(g + 1) * 128] = resF[r]["yc"][u]
    del UFl, UFc

    def shift(a, k):
        o = np.zeros_like(a)
        if k > 0:
            o[..., k:] = a[..., :-k]
        else:
            o[..., :k] = a[..., -k:]
        return o

    VPl, VPc, VNl, VNc = shift(VVl, 1), shift(VVc, 1), shift(VVl, -1), shift(VVc, -1)

    def cols(lat, cx, b, q):
        return np.ascontiguousarray(np.concatenate([lat[b][:, q * LT:(q + 1) * LT], cx[b][:, q * CT:(q + 1) * CT]], axis=1))

    perm = rope_perm()
    wperm = tile_w(np.ascontiguousarray(attn_w_in[0][:, perm]), 128)
    wqkv_t = tile_w(attn_w_in[0], 128)
    wout_t = tile_w(ab_w_out[0], 256)
    _ffw.pop((0, 0), None)
    ins = []
    for r, (b, q) in enumerate(cores):
        vecs = []
        for s in range(2):
            vecs += [mv(b, s, 0, 5), mv(b, s, 0, 6), mv(b, s, 0, 7), mv(b, s, 0, 8),
                     mv(b, s, 1, 0), mv(b, s, 1, 1), mv(b, s, 1, 2), mv(b, s, 1, 3), mv(b, s, 1, 4)]
        vecs += [lay(ln_g[0, 0]), lay(ln_b[0, 0]), lay(ln_g[0, 1]), lay(ln_b[0, 1]), lay(ln_g[0, 2]), lay(ln_b[0, 2]),
                 lay(ln_g[1, 0]), lay(ln_b[1, 0]), lay(ab_conv[0, 0]), lay(ab_conv[0, 1]), lay(ab_conv[0, 2])]
        cl, sl_ = rope_tables(np.arange(q * LT, (q + 1) * LT))
        cosT = np.concatenate([cl, np.ones((128, CT), np.float32)], axis=1)
        sinT = np.concatenate([sl_, np.zeros((128, CT), np.float32)], axis=1)
        ins.append({"xn1": resA[r]["xn1"], "gb": resA[r]["gb"], "vp": cols(VPl, VPc, b, q), "vv": resA[r]["vv"],
                    "vn": cols(VNl, VNc, b, q), "yb": cols(YBl, YBc, b, q),
                    "vecs": np.ascontiguousarray(np.stack(vecs, axis=1)), "wout": wout_t,
                    **ffw(0, 1, "2"), **ffw(1, 0, "3"), "wqkv": wqkv_t, "wperm": wperm,
                    "cosT": np.ascontiguousarray(cosT), "sinT": np.ascontiguousarray(sinT)})
    resB = _run(_prog("LB", lambda: build_LB(T, segs)), ins)
    del resA, VPl, VNl, YBl, VVl
    Ql, _ = gather(resB, "qr", D)
    Kl, Kc = gather(resB, "kr", 256)
    Vl, Vc = gather(resB, "vo", 256)

    mask = attn_mask()
    ident = np.eye(128, dtype=np.float32)
    ins = []
    for r in range(NCORE):
        b, g = r // 4, r % 4
        qt = Ql[b, g * 512:(g + 1) * 512].reshape(2, 4, 64, SEQ).transpose(0, 2, 1, 3).reshape(128, 4, SEQ)
        k1 = np.concatenate([Kl[b, g * 64:(g + 1) * 64], Kc[b, g * 64:(g + 1) * 64]], axis=1)
        vl = Vl[b, g * 64:(g + 1) * 64].T.reshape(NBLK, 128, 64)
        vc_ = Vc[b, g * 64:(g + 1) * 64].T.reshape(2, 128, 64)
        vall = np.concatenate([vl, vc_], axis=0).transpose(1, 0, 2)
        ins.append({"qt": np.ascontiguousarray(qt), "kt": np.ascontiguousarray(np.concatenate([k1, k1], axis=0)),
                    "v": np.ascontiguousarray(vall), "mask": mask,
                    "sink": np.ascontiguousarray(np.broadcast_to(attn_sink[0, g * 8:(g + 1) * 8], (128, 8))),
                    "ident": ident})
    resC = _run(_prog("LC", build_LC), ins)
    del Ql
    O = np.empty((2, D, SEQ), np.float32)
    for r in range(NCORE):
        b, g = r // 4, r % 4
        O[b, g * 512:(g + 1) * 512] = resC[r]["o"].T
    del resC

    _ffw.clear()
    awout_t = tile_w(attn_w_out[0], 256)
    ins = []
    for r, (b, q) in enumerate(cores):
        vecs = [mv(b, 0, 1, 5), mv(b, 0, 1, 6), mv(b, 0, 1, 7), mv(b, 0, 1, 8),
                lay(ln_g[1, 0]), lay(ln_b[1, 0]), lay(ln_g[1, 1]), lay(ln_b[1, 1]), lay(ln_g[1, 2]), lay(ln_b[1, 2])]
        ins.append({"xn4": np.ascontiguousarray(resB[r]["xn4"][:, :LT]), "oT": np.ascontiguousarray(O[b][:, q * LT:(q + 1) * LT]),
                    "vecs": np.ascontiguousarray(np.stack(vecs, axis=1)), "wout": awout_t, **ffw(1, 1, "")})
    resD = _run(_prog("LD", lambda: build_LD(LT)), ins)
    out = np.empty((2, SEQ, D), np.float32)
    for r, (b, q) in enumerate(cores):
        out[b, q * LT:(q + 1) * LT] = resD[r]["out"].T
    return out
```

```python
import numpy as np
from contextlib import ExitStack
import concourse.bass as bass
import concourse.mybir as mybir
from concourse.bass_utils import run_bass_kernel_spmd

F32 = mybir.dt.float32
BF16 = mybir.dt.bfloat16
AF = mybir.ActivationFunctionType
ALU = mybir.AluOpType
AX = mybir.AxisListType

D = 2048
DFF = 5632
NCH = 16
FCH = 44
SEQ = 8192
CTX = 256
NCORE = 8
ALPHA = 4.0 ** 0.25
LN_EPS = 1e-5
TB = 512


class Buf:
    __slots__ = ("w", "r", "name")

    def __init__(self, name=""):
        self.w = []
        self.r = []
        self.name = name


class Ctr:
    LIMIT = 30000

    def __init__(self, P, name, step):
        self.P, self.name, self.step = P, name, step
        self.k = 0
        self.done = []
        self._new()

    def _new(self):
        self.sem = self.P.stack.enter_context(self.P.nc.semaphore(f"{self.name}_{self.k}"))
        self.k += 1
        self.val = 0

    def next(self):
        if self.val + self.step > self.LIMIT:
            self.done.append((self.sem, self.val))
            self._new()
        self.val += self.step
        return (self.sem, self.val)


class Eng:
    def __init__(self, P, name):
        self.name = name
        self.ops = []
        self.waited = {}
        self.ctr = Ctr(P, "e" + name, 1)


class Prog:
    def __init__(self, nc):
        self.nc = nc
        self.stack = ExitStack()
        self.engs = {n: Eng(self, n) for n in ("pe", "act", "dve", "pool", "sp")}
        self.dctr = {}
        self.fuzzy = {}
        self.n = 0

    def sb(self, name, shape, dt):
        return self.stack.enter_context(self.nc.sbuf_tensor("s_" + name, list(shape), dt))

    def ps(self, name, shape, dt=F32):
        return self.stack.enter_context(self.nc.psum_tensor("p_" + name, list(shape), dt))

    def op(self, eng, fn, reads=(), writes=(), dma=None, cowrites=()):
        E = self.engs[eng]
        deps = {}

        def add(tok):
            s, v = tok
            k = id(s)
            if k in self.fuzzy:
                v = max(v, self.fuzzy[k].val if self.fuzzy[k].sem is s else v)
            if k not in deps or deps[k][1] < v:
                deps[k] = (s, v)

        for b in reads:
            for t in b.w:
                add(t)
        for b in writes:
            for t in b.w:
                add(t)
            for t in b.r:
                add(t)
        for b in cowrites:
            for t in b.r:
                add(t)
        if dma is not None:
            if dma not in self.dctr:
                self.dctr[dma] = Ctr(self, "d" + dma, 16)
            ctr = self.dctr[dma]
            if dma.startswith("st") or dma.startswith("xo"):
                self.fuzzy[id(ctr.sem)] = ctr
        else:
            ctr = E.ctr
        waits = []
        for k, (s, v) in deps.items():
            if eng == "pe" and dma is None and s is E.ctr.sem:
                continue
            if E.waited.get(k, 0) >= v:
                continue
            E.waited[k] = v
            waits.append((s, v))
        tok = ctr.next()
        E.ops.append((waits, fn, tok[0], ctr.step))
        for b in reads:
            b.r.append(tok)
        for b in writes:
            b.w = [tok]
            b.r = []
        for b in cowrites:
            b.w.append(tok)
        self.n += 1
        return tok

    def finish(self):
        E = self.engs["sp"]
        waits = []
        for ctr in self.dctr.values():
            for s, v in ctr.done + [(ctr.sem, ctr.val)]:
                if v > 0 and E.waited.get(id(s), 0) < v:
                    waits.append((s, v))
        E.ops.append((waits, None, None, 0))

    def emit(self):
        nc = self.nc

        def mk(E):
            def run(e):
                for waits, fn, sem, step in E.ops:
                    for ws, wv in waits:
                        e.wait_ge(ws, wv)
                    if fn is not None:
                        fn(e).then_inc(sem, step)
            return run

        with nc.Block() as block:
            block.tensor(mk(self.engs["pe"]))
            block.scalar(mk(self.engs["act"]))
            block.vector(mk(self.engs["dve"]))
            block.gpsimd(mk(self.engs["pool"]))
            block.sync(mk(self.engs["sp"]))
        self.stack.close()


def blocks_of(T, tb=TB):
    bl = [(c, min(c + tb, T)) for c in range(0, T, tb)]
    if bl[-1][1] - bl[-1][0] < tb:
        bl = [bl[-1]] + bl[:-1]
    return bl


def chunked(ap):
    return ap.rearrange("(c p) t -> p c t", p=128)


class RL:
    def __init__(self, P, T, segs, vecs_ap, nvec):
        self.P, self.T, self.segs = P, T, segs
        self.blocks = blocks_of(T)
        nb = len(self.blocks)
        nc = P.nc
        self.xn2 = [P.sb(f"xn{i}", [128, NCH, TB], F32) for i in range(2)]
        self.big = P.sb("big", [128, max(NCH * T, FCH * TB)], BF16)
        self.hb = self.big[:, 0:NCH * T].rearrange("p (c t) -> p c t", c=NCH)
        self.ab = self.big[:, 0:FCH * TB].rearrange("p (c t) -> p c t", c=FCH)
        self.wa2 = [P.sb(f"wa{i}", [128, NCH, 256], BF16) for i in range(2)]
        self.wa = [self.wa2[i // 2][:, :, (i % 2) * 128:(i % 2 + 1) * 128] for i in range(4)]
        self.wap_i = 0
        self.wb = [P.sb(f"wb{i}", [128, FCH, 128], BF16) for i in range(2)]
        self.mu = P.sb("mu", [128, TB], F32)
        self.msq = P.sb("msq", [128, TB], F32)
        self.rstd = P.sb("rstd", [128, TB], F32)
        self.sg = [P.sb(f"sg{i}", [128, TB], F32) for i in range(2)]
        self.zt = [P.sb(f"zt{i}", [128, TB], F32) for i in range(2)]
        self.tmp = [P.sb(f"tmp{i}", [128, TB], F32) for i in range(2)]
        self.xr = [P.sb(f"xr{i}", [128, TB], F32) for i in range(4)]
        self.at = [P.sb(f"at{i}", [128, TB], BF16) for i in range(2)]
        self.zb = [P.sb(f"zb{i}", [128, TB], BF16) for i in range(2)]
        self.zq = [P.sb(f"zq{i}", [128, TB], BF16) for i in range(2)]
        self.ones = P.sb("ones", [128, 128], BF16)
        self.vecs = P.sb("vecs", [128, nvec, NCH], F32)
        self.s1 = P.ps("s1", [128, TB])
        self.s2 = P.ps("s2", [128, TB])
        self.pg = [P.ps(f"pg{i}", [128, TB]) for i in range(2)]
        self.pu = [P.ps(f"pu{i}", [128, TB]) for i in range(2)]
        self.py = [P.ps(f"py{i}", [128, TB]) for i in range(2)]
        self.xs = [nc.dram_tensor(f"xs{i}", [nb, 128, NCH, TB], F32).ap() for i in range(2)]
        self.hs = nc.dram_tensor("hs", [nb, 128, NCH, TB], BF16).ap()
        self.A = nc.dram_tensor("Asp", [nb, 128, FCH, TB], BF16).ap()
        B = Buf
        BL = lambda n: [B() for _ in range(n)]
        self.b_xn2 = [BL(NCH), BL(NCH)]
        self.b_hbk = BL(nb)
        self.b_abc = BL(FCH)
        self.b_wa, self.b_wb = BL(4), BL(2)
        self.b_mu, self.b_msq, self.b_rstd = B(), B(), B()
        self.b_sg, self.b_zt, self.b_tmp, self.b_xr, self.b_at = BL(2), BL(2), BL(2), BL(4), BL(2)
        self.b_zb, self.b_zq = BL(2), BL(2)
        self.b_s1, self.b_s2 = B(), B()
        self.b_pg, self.b_pu, self.b_py = BL(2), BL(2), BL(2)
        self.b_ones, self.b_vecs = B(), B()
        self.b_xs = [BL(nb), BL(nb)]
        self.b_hs = BL(nb)
        self.b_A = BL(nb)
        self.W_HB = self.b_hbk + self.b_abc
        self.wa_i = self.wb_i = self.zt_i = self.xr_i = self.at_i = self.pp_i = self.zb_i = 0
        self.xs_cur = 0
        ones, vecs = self.ones, self.vecs
        P.op("dve", lambda e: e.memset(ones[:], 1.0), writes=[self.b_ones])
        P.op("sp", lambda e: e.dma_start(out=vecs[:], in_=vecs_ap), writes=[self.b_vecs], dma="vecs")

    def vc(self, v, c):
        return self.vecs[:, v, c:c + 1]

    def whb(self, bi):
        return [self.b_hbk[bi]] + self.b_abc

    def derive(self, v, mul=None, add=None):
        vecs = self.vecs
        if add is not None:
            self.P.op("dve", lambda e: e.tensor_scalar_add(out=vecs[:, v, :], in0=vecs[:, v, :], scalar1=float(add)),
                      reads=[self.b_vecs], writes=[self.b_vecs])
        if mul is not None:
            self.P.op("dve", lambda e: e.tensor_scalar_mul(out=vecs[:, v, :], in0=vecs[:, v, :], scalar1=float(mul)),
                      reads=[self.b_vecs], writes=[self.b_vecs])

    def segparts(self, c0, c1):
        out = []
        for si, (s0, s1) in enumerate(self.segs):
            a, b = max(c0, s0), min(c1, s1)
            if a < b:
                out.append((si, a - c0, b - c0))
        return out

    def first_prologue(self, x_ap, x_buf, vshift, vscale1):
        P = self.P
        hb = self.hb
        for bi, (c0, c1) in enumerate(self.blocks):
            w = c1 - c0
            s = bi % 2
            xn = self.xn2[s]
            P.op("sp", lambda e, xn=xn, w=w, c0=c0, c1=c1: e.dma_start(out=xn[:, :, :w], in_=chunked(x_ap)[:, :, c0:c1]),
                 reads=[x_buf], writes=self.b_xn2[s], dma=f"xin{s}")
            for si, a, b in self.segparts(c0, c1):
                for c in range(NCH):
                    P.op("act", lambda e, xn=xn, c=c, si=si, a=a, b=b, c0=c0: e.activation(
                        out=hb[:, c, c0 + a:c0 + b], in_=xn[:, c, a:b], func=AF.Identity,
                        scale=self.vc(vscale1[si], c), bias=self.vc(vshift[si], c)),
                        reads=[self.b_vecs, self.b_xn2[s][c]], cowrites=self.whb(bi))

    def load_hb(self, only=None):
        hb, hs = self.hb, self.hs
        for bi, (c0, c1) in enumerate(self.blocks):
            if only is not None and bi not in only:
                continue
            w = c1 - c0
            self.P.op("sp", lambda e, bi=bi, c0=c0, c1=c1, w=w: e.dma_start(out=hb[:, :, c0:c1], in_=hs[bi, :, :, :w]),
                      reads=[self.b_hs[bi]], cowrites=self.whb(bi), dma=f"ldh{bi % 2}")
            self.b_hs[bi].w = []

    def load_wa(self, W_ap, g):
        i = self.wa_i
        self.wa_i = (i + 1) % 4
        t = self.wa[i]
        self.P.op("pool", lambda e: e.dma_start(out=t, in_=W_ap[g]), writes=[self.b_wa[i]], dma=f"wa{i}")
        return i

    def load_wa_pair(self, W2_ap, g):
        p = self.wap_i
        self.wap_i ^= 1
        t = self.wa2[p]
        self.P.op("pool", lambda e: e.dma_start(out=t[:], in_=W2_ap[g]), writes=[self.b_wa[2 * p], self.b_wa[2 * p + 1]],
                  dma=f"wp{p}")
        self.wa_i = (2 * p + 2) % 4
        return 2 * p, 2 * p + 1

    def load_wb(self, W_ap, g):
        i = self.wb_i
        self.wb_i = (i + 1) % 2
        t = self.wb[i]
        self.P.op("pool", lambda e: e.dma_start(out=t[:], in_=W_ap[g]), writes=[self.b_wb[i]], dma=f"wb{i}")
        return i

    def mm(self, dst, dst_buf, wt, wbuf, kc, X, lo, hi, xbufs):
        w = hi - lo

        def f(e):
            for k in range(kc):
                ins = e.matmul(dst[:, :w], lhsT=wt[:, k, :], rhs=X[:, k, lo:hi], start=(k == 0), stop=(k == kc - 1))
            return ins
        self.P.op("pe", f, reads=[wbuf] + list(xbufs), writes=[dst_buf])

    def store(self, dst_ap, dst_buf, row0, c0, c1, tile, tbuf):
        w = c1 - c0
        self.P.op("sp", lambda e: e.dma_start(out=dst_ap[row0:row0 + 128, c0:c1], in_=tile[:, :w]),
                  reads=[tbuf], cowrites=[dst_buf], dma="st")

    def phase1(self, wgu):
        P = self.P
        hb, A = self.hb, self.A
        ld = lambda fc: self.load_wa_pair(wgu, fc)
        nxt = ld(0)
        for fc in range(FCH):
            ig, iu = nxt
            if fc + 1 < FCH:
                nxt = ld(fc + 1)
            for bi, (c0, c1) in enumerate(self.blocks):
                w = c1 - c0
                pb = self.pp_i
                self.pp_i ^= 1
                self.mm(self.pg[pb], self.b_pg[pb], self.wa[ig], self.b_wa[ig], NCH, hb, c0, c1, [self.b_hbk[bi]])
                self.mm(self.pu[pb], self.b_pu[pb], self.wa[iu], self.b_wa[iu], NCH, hb, c0, c1, [self.b_hbk[bi]])
                sg, pg, pu = self.sg[pb], self.pg[pb], self.pu[pb]
                ai = self.at_i
                self.at_i ^= 1
                at = self.at[ai]
                P.op("act", lambda e, sg=sg, pg=pg, w=w: e.activation(out=sg[:, :w], in_=pg[:, :w], func=AF.Silu),
                     reads=[self.b_pg[pb]], writes=[self.b_sg[pb]])
                P.op("dve", lambda e, sg=sg, pu=pu, at=at, w=w: e.tensor_tensor(out=at[:, :w], in0=sg[:, :w], in1=pu[:, :w],
                                                                               op=ALU.mult),
                     reads=[self.b_sg[pb], self.b_pu[pb]], writes=[self.b_at[ai]])
                P.op("sp", lambda e, at=at, bi=bi, fc=fc, w=w: e.dma_start(out=A[bi, :, fc, :w], in_=at[:, :w]),
                     reads=[self.b_at[ai]], cowrites=[self.b_A[bi]], dma="sta")

    def phase2(self, kind, W, res, vgate, ln, mod=None, out_ap=None, reload_hb=False, extra=None):
        P = self.P
        hb, ab, A, ones = self.hb, self.ab, self.A, self.ones
        s1, s2, mu, msq, rstd = self.s1, self.s2, self.mu, self.msq, self.rstd
        vg, vb = ln
        xs_in = self.xs[self.xs_cur]
        b_xs_in = self.b_xs[self.xs_cur]
        xs_out = self.xs[self.xs_cur ^ 1]
        b_xs_out = self.b_xs[self.xs_cur ^ 1]
        hs = self.hs
        pending = []

        def ln_apply(bi, c0, c1, s, c):
            w = c1 - c0
            xn = self.xn2[s]
            bx = self.b_xn2[s][c]
            P.op("dve", lambda e: e.tensor_tensor(out=xn[:, c, :w], in0=xn[:, c, :w], in1=mu[:, :w], op=ALU.subtract),
                 reads=[self.b_mu], writes=[bx])
            P.op("dve", lambda e: e.tensor_tensor(out=xn[:, c, :w], in0=xn[:, c, :w], in1=rstd[:, :w], op=ALU.mult),
                 reads=[self.b_rstd], writes=[bx])
            P.op("act", lambda e: e.activation(out=xn[:, c, :w], in_=xn[:, c, :w], func=AF.Identity,
                                               scale=self.vc(vg, c), bias=self.vc(vb, c)),
                 reads=[self.b_vecs], writes=[bx])
            if mod is not None:
                ai = self.at_i
                self.at_i ^= 1
                at = self.at[ai]
                for si, a, b in self.segparts(c0, c1):
                    P.op("act", lambda e, si=si, a=a, b=b: e.activation(
                        out=at[:, a:b], in_=xn[:, c, a:b], func=AF.Identity,
                        scale=self.vc(mod[1][si], c), bias=self.vc(mod[0][si], c)),
                        reads=[self.b_vecs, bx], cowrites=[self.b_at[ai]])
                P.op("sp", lambda e: e.dma_start(out=hs[bi, :, c, :w], in_=at[:, :w]),
                     reads=[self.b_at[ai]], cowrites=[self.b_hs[bi]], dma="sth")
                self.b_at[ai].w = []
            if c == NCH - 1:
                if out_ap is not None:
                    P.op("sp", lambda e: e.dma_start(out=chunked(out_ap)[:, :, c0:c1], in_=xn[:, :, :w]),
                         reads=self.b_xn2[s], writes=[Buf()], dma=f"xo{s}")
                else:
                    P.op("sp", lambda e: e.dma_start(out=xs_out[bi, :, :, :w], in_=xn[:, :, :w]),
                         reads=self.b_xn2[s], writes=[b_xs_out[bi]], dma=f"xo{s}")

        seq = [(bi_, dc_) for bi_ in range(len(self.blocks)) for dc_ in range(NCH)]

        def xr_load(k):
            if k >= len(seq):
                return
            bi_, dc_ = seq[k]
            c0_, c1_ = self.blocks[bi_]
            w_ = c1_ - c0_
            xi_ = k % 4
            xr_ = self.xr[xi_]
            if res[0] == "xs":
                rv, rb = xs_in[bi_, :, dc_, :w_], b_xs_in[bi_]
            else:
                rv, rb = res[1][dc_ * 128:(dc_ + 1) * 128, c0_:c1_], res[2]
            P.op("sp", lambda e: e.dma_start(out=xr_[:, :w_], in_=rv), reads=[rb], writes=[self.b_xr[xi_]], dma=f"xr{xi_}")

        for k0 in range(3):
            xr_load(k0)
        scaled = True if res[0] == "xs" else res[3]
        kpos = 0
        for bi, (c0, c1) in enumerate(self.blocks):
            w = c1 - c0
            s = bi % 2
            xn = self.xn2[s]
            if kind == "ffn":
                for q in range(4):
                    k0, k1 = q * 11, (q + 1) * 11
                    wr = self.b_abc[k0:k1] + (self.b_hbk if q == 0 else [])
                    P.op("sp", lambda e, bi=bi, w=w, k0=k0, k1=k1: e.dma_start(out=ab[:, k0:k1, :w], in_=A[bi, :, k0:k1, :w]),
                         reads=[self.b_A[bi]], writes=wr, dma=f"lda{q}")
                self.b_A[bi].w = []
                ldw = lambda dc: self.load_wb(W, dc)
                nxt = ldw(0)
            else:
                nxtp = self.load_wa_pair(W, 0)
            stats_prev = None
            for dc in range(NCH):
                if kind == "ffn":
                    iw = nxt
                    if dc + 1 < NCH:
                        nxt = ldw(dc + 1)
                else:
                    if dc % 2 == 0:
                        curp = nxtp
                        if dc + 2 < NCH:
                            nxtp = self.load_wa_pair(W, dc // 2 + 1)
                    iw = curp[dc % 2]
                pb = dc % 2
                xi = kpos % 4
                xr = self.xr[xi]
                if kind == "ffn" and dc == 0:
                    for q in range(4):
                        def f(e, q=q, iw=iw, pb=pb, w=w):
                            for k in range(q * 11, (q + 1) * 11):
                                ins = e.matmul(self.py[pb][:, :w], lhsT=self.wb[iw][:, k, :], rhs=ab[:, k, 0:w],
                                               start=(k == 0), stop=(k == FCH - 1))
                            return ins
                        P.op("pe", f, reads=[self.b_wb[iw]] + self.b_abc[q * 11:(q + 1) * 11], writes=[self.b_py[pb]])
                elif kind == "ffn":
                    self.mm(self.py[pb], self.b_py[pb], self.wb[iw], self.b_wb[iw], FCH, ab, 0, w, self.b_abc)
                else:
                    self.mm(self.py[pb], self.b_py[pb], self.wa[iw], self.b_wa[iw], NCH, hb, c0, c1, [self.b_hbk[bi]])
                py = self.py[pb]
                bx = self.b_xn2[s][dc]
                if scaled:
                    first = True
                    for si, a, b in self.segparts(c0, c1):
                        kw = dict(writes=[bx]) if first else dict(cowrites=[bx])
                        first = False
                        P.op("dve", lambda e, si=si, a=a, b=b, xn=xn, xr=xr, py=py, dc=dc: e.scalar_tensor_tensor(
                            out=xn[:, dc, a:b], in0=py[:, a:b], scalar=self.vc(vgate[si], dc), in1=xr[:, a:b],
                            op0=ALU.mult, op1=ALU.add),
                            reads=[self.b_py[pb], self.b_vecs, self.b_xr[xi]], **kw)
                else:
                    ti = self.zt_i
                    self.zt_i ^= 1
                    tmp = self.tmp[ti]
                    for si, a, b in self.segparts(c0, c1):
                        P.op("act", lambda e, si=si, a=a, b=b, tmp=tmp, py=py, dc=dc: e.activation(
                            out=tmp[:, a:b], in_=py[:, a:b], func=AF.Copy, scale=self.vc(vgate[si], dc)),
                            reads=[self.b_py[pb], self.b_vecs], cowrites=[self.b_tmp[ti]])
                    P.op("dve", lambda e, xn=xn, xr=xr, tmp=tmp, dc=dc, w=w: e.scalar_tensor_tensor(
                        out=xn[:, dc, :w], in0=xr[:, :w], scalar=ALPHA, in1=tmp[:, :w], op0=ALU.mult, op1=ALU.add),
                        reads=[self.b_xr[xi], self.b_tmp[ti]], writes=[bx])
                    self.b_tmp[ti].w = []
                xr_load(kpos + 3)
                kpos += 1
                zi = self.zb_i
                self.zb_i ^= 1
                zb, zq = self.zb[zi], self.zq[zi]
                P.op("act", lambda e, xn=xn, zb=zb, dc=dc, w=w: e.activation(out=zb[:, :w], in_=xn[:, dc, :w], func=AF.Copy),
                     reads=[bx], writes=[self.b_zb[zi]])
                P.op("dve", lambda e, xn=xn, zq=zq, dc=dc, w=w: e.tensor_tensor(out=zq[:, :w], in0=xn[:, dc, :w],
                                                                               in1=xn[:, dc, :w], op=ALU.mult),
                     reads=[bx], writes=[self.b_zq[zi]])

                def stats(e, zb=zb, zq=zq, dc=dc, w=w):
                    e.matmul(s1[:, :w], lhsT=ones[:], rhs=zb[:, :w], start=(dc == 0), stop=(dc == NCH - 1))
                    return e.matmul(s2[:, :w], lhsT=ones[:], rhs=zq[:, :w], start=(dc == 0), stop=(dc == NCH - 1))
                this_stats = (stats, [self.b_zb[zi], self.b_zq[zi], self.b_ones])
                if stats_prev is not None:
                    P.op("pe", stats_prev[0], reads=stats_prev[1], cowrites=[self.b_s1, self.b_s2])
                stats_prev = this_stats
                if pending:
                    pending.pop(0)()
                if extra and extra.get(bi):
                    ex = extra[bi]
                    for _ in range(-(-len(ex) // (NCH - dc))):
                        ex.pop(0)()
            P.op("pe", stats_prev[0], reads=stats_prev[1], cowrites=[self.b_s1, self.b_s2])
            while pending:
                pending.pop(0)()
            P.op("dve", lambda e, w=w: e.tensor_scalar_mul(out=mu[:, :w], in0=s1[:, :w], scalar1=1.0 / D),
                 reads=[self.b_s1], writes=[self.b_mu])
            P.op("dve", lambda e, w=w: e.tensor_tensor(out=msq[:, :w], in0=mu[:, :w], in1=mu[:, :w], op=ALU.mult),
                 reads=[self.b_mu], writes=[self.b_msq])
            P.op("dve", lambda e, w=w: e.scalar_tensor_tensor(out=rstd[:, :w], in0=s2[:, :w], scalar=1.0 / D, in1=msq[:, :w],
                                                              op0=ALU.mult, op1=ALU.subtract),
                 reads=[self.b_s2, self.b_msq], writes=[self.b_rstd])
            P.op("dve", lambda e, w=w: e.tensor_scalar_add(out=rstd[:, :w], in0=rstd[:, :w], scalar1=LN_EPS),
                 reads=[self.b_rstd], writes=[self.b_rstd])
            P.op("act", lambda e, w=w: e.activation(out=rstd[:, :w], in_=rstd[:, :w], func=AF.Sqrt),
                 reads=[self.b_rstd], writes=[self.b_rstd])
            P.op("dve", lambda e, w=w: e.reciprocal(out=rstd[:, :w], in_=rstd[:, :w]),
                 reads=[self.b_rstd], writes=[self.b_rstd])
            self.b_s1.w, self.b_s2.w = [], []
            for c in range(NCH):
                pending.append(lambda bi=bi, c0=c0, c1=c1, s=s, c=c: ln_apply(bi, c0, c1, s, c))
        nb = len(self.blocks)
        if reload_hb:
            self.load_hb(only=range(nb - 1))
        while pending:
            pending.pop(0)()
        if reload_hb:
            self.load_hb(only=[nb - 1])
        if out_ap is None:
            self.xs_cur ^= 1


def tile_w(W, gcols):
    K, N = W.shape
    return np.ascontiguousarray(W.reshape(K // 128, 128, N // gcols, gcols).transpose(2, 1, 0, 3))


def dram_in(nc, name, shape, dt=F32):
    return nc.dram_tensor(name, list(shape), dt, kind="ExternalInput").ap()


def dram_out(nc, name, shape, dt=F32):
    return nc.dram_tensor(name, list(shape), dt, kind="ExternalOutput").ap()


def ffn_w_in(nc, sfx):
    return (dram_in(nc, "wgu" + sfx, [FCH, 128, NCH, 256]), dram_in(nc, "wd" + sfx, [NCH, 128, FCH, 128]))


MODC = 2 * 9 * D // NCORE


def build_L0():
    nc = bass.Bass("TRN2", target_bir_lowering=False)
    cv_ap = dram_in(nc, "cv", [128, NCH, 3])
    w_ap = dram_in(nc, "w", [D, MODC])
    b_ap = dram_in(nc, "b", [3, MODC])
    o_ap = dram_out(nc, "mod", [3, MODC])
    P = Prog(nc)
    cv = P.sb("cv", [128, NCH, 3], F32)
    cb = P.sb("cb", [128, NCH, 3], BF16)
    bt = P.sb("bt", [3, MODC], F32)
    ot = P.sb("ot", [3, MODC], F32)
    wt = [P.sb(f"w{i}", [128, NCH, 512], BF16) for i in range(2)]
    ps = [P.ps(f"ps{i}", [128, 512]) for i in range(2)]
    b_cv, b_cb, b_bt, b_ot = Buf(), Buf(), Buf(), Buf()
    b_w, b_ps = [Buf(), Buf()], [Buf(), Buf()]
    P.op("sp", lambda e: e.dma_start(out=cv[:], in_=cv_ap), writes=[b_cv], dma="cv")
    P.op("sp", lambda e: e.dma_start(out=bt[:], in_=b_ap), writes=[b_bt], dma="bt")
    P.op("act", lambda e: e.activation(out=cb[:], in_=cv[:], func=AF.Silu), reads=[b_cv], writes=[b_cb])
    wv = w_ap.rearrange("(k p) f -> p k f", p=128)
    for t in range(MODC // 512):
        i = t % 2
        P.op("pool", lambda e, t=t, i=i: e.dma_start(out=wt[i][:], in_=wv[:, :, t * 512:(t + 1) * 512]),
             writes=[b_w[i]], dma=f"w{i}")

        def f(e, i=i):
            for k in range(NCH):
                ins = e.matmul(ps[i][0:3, :], lhsT=cb[:, k, :], rhs=wt[i][:, k, :], start=(k == 0), stop=(k == NCH - 1))
            return ins
        P.op("pe", f, reads=[b_cb, b_w[i]], writes=[b_ps[i]])
        P.op("dve", lambda e, t=t, i=i: e.tensor_tensor(out=ot[:, t * 512:(t + 1) * 512], in0=ps[i][0:3, :],
                                                        in1=bt[:, t * 512:(t + 1) * 512], op=ALU.add),
             reads=[b_ps[i], b_bt], writes=[b_ot])
    P.op("sp", lambda e: e.dma_start(out=o_ap, in_=ot[:]), reads=[b_ot], writes=[Buf()], dma="st")
    P.finish()
    P.emit()
    return nc


def simple_proj(R, W, ng, dst, b_o):
    P = R.P
    nxt = R.load_wa(W, 0)
    for g in range(ng):
        iw = nxt
        if g + 1 < ng:
            nxt = R.load_wa(W, g + 1)
        for bi, (c0, c1) in enumerate(R.blocks):
            w = c1 - c0
            pb = R.pp_i
            R.pp_i ^= 1
            R.mm(R.pg[pb], R.b_pg[pb], R.wa[iw], R.b_wa[iw], NCH, R.hb, c0, c1, [R.b_hbk[bi]])
            i = R.zt_i
            R.zt_i ^= 1
            zt, pg = R.zt[i], R.pg[pb]
            P.op("act", lambda e, pg=pg, zt=zt, w=w: e.activation(out=zt[:, :w], in_=pg[:, :w], func=AF.Copy),
                 reads=[R.b_pg[pb]], writes=[R.b_zt[i]])
            R.store(dst, b_o, g * 128, c0, c1, zt, R.b_zt[i])


def build_LA(T, segs):
    nc = bass.Bass("TRN2", target_bir_lowering=False)
    nseg = len(segs)
    nvec = 5 * nseg + 2
    x_ap = dram_in(nc, "xT", [D, T])
    vecs_ap = dram_in(nc, "vecs", [128, nvec, NCH])
    wgu, wd = ffn_w_in(nc, "")
    w_gb = dram_in(nc, "w_gb", [8, 128, NCH, 128])
    w_gc = dram_in(nc, "w_gc", [8, 128, NCH, 128])
    w_xi = dram_in(nc, "w_xi", [8, 128, NCH, 128])
    w_uf = dram_in(nc, "w_uf", [8, 128, NCH, 128])
    xn1 = dram_out(nc, "xn1", [D, T])
    gb = dram_out(nc, "gb", [1024, T])
    vv = dram_out(nc, "vv", [1024, T])
    uf = dram_out(nc, "uf", [1024, T])
    P = Prog(nc)
    R = RL(P, T, segs, vecs_ap, nvec)
    V = lambda s, k: 5 * s + k
    SL = lambda k: [V(s, k) for s in range(nseg)]
    VG, VB = 5 * nseg, 5 * nseg + 1
    for s in range(nseg):
        R.derive(V(s, 1), add=1.0)
        R.derive(V(s, 2), mul=0.5)
        R.derive(V(s, 4), add=1.0)
        R.derive(V(s, 4), mul=1.0 / ALPHA)
    R.derive(VG, mul=ALPHA)
    R.derive(VB, mul=ALPHA)
    b_x = Buf()
    b_o = Buf()
    R.first_prologue(x_ap, b_x, SL(0), SL(1))
    R.phase1(wgu)
    R.phase2("ffn", wd, ("ap", x_ap, b_x, False), SL(2), ln=(VG, VB), mod=(SL(3), SL(4)), out_ap=xn1, reload_hb=True)
    simple_proj(R, w_gb, 8, gb, b_o)
    ld = lambda jj: (R.load_wa(w_gc, jj), R.load_wa(w_xi, jj))
    nxt = ld(0)
    for jj in range(8):
        ic, ix = nxt
        if jj + 1 < 8:
            nxt = ld(jj + 1)
        for bi, (c0, c1) in enumerate(R.blocks):
            w = c1 - c0
            pb = R.pp_i
            R.pp_i ^= 1
            R.mm(R.pg[pb], R.b_pg[pb], R.wa[ic], R.b_wa[ic], NCH, R.hb, c0, c1, [R.b_hbk[bi]])
            R.mm(R.pu[pb], R.b_pu[pb], R.wa[ix], R.b_wa[ix], NCH, R.hb, c0, c1, [R.b_hbk[bi]])
            i = R.zt_i
            R.zt_i ^= 1
            zt, sg, pg, pu = R.zt[i], R.sg[pb], R.pg[pb], R.pu[pb]
            P.op("act", lambda e, pg=pg, sg=sg, w=w: e.activation(out=sg[:, :w], in_=pg[:, :w], func=AF.Copy),
                 reads=[R.b_pg[pb]], writes=[R.b_sg[pb]])
            P.op("dve", lambda e, pu=pu, sg=sg, zt=zt, w=w: e.tensor_tensor(out=zt[:, :w], in0=pu[:, :w], in1=sg[:, :w],
                                                                           op=ALU.mult),
                 reads=[R.b_pu[pb], R.b_sg[pb]], writes=[R.b_zt[i]])
            R.store(vv, b_o, jj * 128, c0, c1, zt, R.b_zt[i])
    simple_proj(R, w_uf, 8, uf, b_o)
    P.finish()
    P.emit()
    return nc


def build_LF():
    nc = bass.Bass("TRN2", target_bir_lowering=False)
    xl = dram_in(nc, "xl", [2, SEQ, 128])
    xc = dram_in(nc, "xc", [2, 128, CTX])
    ftw_ap = dram_in(nc, "ftw", [128, 64, 256])
    fc_ap = dram_in(nc, "fc", [128, 2, 256])
    c64_ap = dram_in(nc, "c64", [64, 2, 64])
    c256_ap = dram_in(nc, "c256", [128, 4, 256])
    yl = dram_out(nc, "yl", [2, 128, SEQ])
    yc = dram_out(nc, "yc", [2, 128, CTX])
    P = Prog(nc)
    ftw = P.sb("ftw", [128, 64, 256], BF16)
    fc = P.sb("fc", [128, 2, 256], BF16)
    c64 = P.sb("c64", [64, 2, 64], BF16)
    c256 = P.sb("c256", [128, 4, 256], BF16)
    XA = P.sb("XA", [128, 64, 128], BF16)
    D1 = P.sb("D1", [128, 64, 256], BF16)
    D2 = P.sb("D2", [64, 128, 256], BF16)
    Y = P.sb("Y", [128, SEQ], F32)
    XC = P.sb("XC", [128, CTX], BF16)
    DA = P.sb("DA", [128, 2, 256], BF16)
    YC = P.sb("YC", [128, CTX], F32)
    pp = [P.ps(f"pp{i}", [128, 512]) for i in range(4)]
    b_pp = [Buf() for _ in range(4)]
    b_t, b_XA, b_D1, b_D2, b_Y, b_XC, b_DA, b_YC = (Buf() for _ in range(8))
    for t, ap, k in ((ftw, ftw_ap, "t0"), (fc, fc_ap, "t1"), (c64, c64_ap, "t2"), (c256, c256_ap, "t3")):
        P.op("pool", lambda e, t=t, ap=ap: e.dma_start(out=t[:], in_=ap), cowrites=[b_t], dma=k)
    pi = [0]

    def nextp():
        pi[0] = (pi[0] + 1) % 4
        return pi[0]
    ev = [0]

    def evac(out_ap, in_ap, rd, wr, co=False):
        ev[0] ^= 1
        kw = dict(cowrites=[wr]) if co else dict(writes=[wr])
        if ev[0]:
            P.op("act", lambda e: e.activation(out=out_ap, in_=in_ap, func=AF.Copy), reads=[rd], **kw)
        else:
            P.op("dve", lambda e: e.tensor_copy(out=out_ap, in_=in_ap), reads=[rd], **kw)

    for u in range(2):
        src = xl[u].rearrange("(a r) c -> a r c", r=64)
        P.op("pool", lambda e, src=src: e.dma_start(out=XA[:], in_=src), writes=[b_XA], dma="xa")
        P.op("pool", lambda e, u=u: e.dma_start(out=XC[:], in_=xc[u]), writes=[b_XC], dma="xc")
        for b0 in range(0, 64, 2):
            i = nextp()

            def f(e, b0=b0, i=i):
                for q in range(2):
                    ins = e.matmul(pp[i][:, q * 256:(q + 1) * 256], lhsT=XA[:, b0 + q, :], rhs=ftw[:, b0 + q, :],
                                   start=True, stop=True)
                return ins
            P.op("pe", f, reads=[b_XA, b_t], writes=[b_pp[i]])
            evac(D1[:, b0:b0 + 2, :], pp[i][:].rearrange("p (q n) -> p q n", q=2), b_pp[i], b_D1, co=True)
        for a0 in range(0, 128, 2):
            i = nextp()

            def f(e, a0=a0, i=i):
                for q in range(2):
                    e.matmul(pp[i][0:64, q * 256:(q + 1) * 256], lhsT=D1[:, :, a0 + q], rhs=fc[:, 0, :], start=True, stop=False)
                    ins = e.matmul(pp[i][0:64, q * 256:(q + 1) * 256], lhsT=D1[:, :, 128 + a0 + q], rhs=fc[:, 1, :],
                                   start=False, stop=True)
                return ins
            P.op("pe", f, reads=[b_D1, b_t], writes=[b_pp[i]])
            evac(D2[:, a0:a0 + 2, :], pp[i][0:64, :].rearrange("p (q n) -> p q n", q=2), b_pp[i], b_D2, co=True)
        Yv = Y[:].rearrange("p (b a) -> p a b", a=128)
        for a0 in range(0, 128, 8):
            i = nextp()

            def f(e, a0=a0, i=i):
                for q in range(8):
                    e.matmul(pp[i][:, q * 64:(q + 1) * 64], lhsT=D2[:, a0 + q, 0:128], rhs=c64[:, 0, :], start=True, stop=False)
                    ins = e.matmul(pp[i][:, q * 64:(q + 1) * 64], lhsT=D2[:, a0 + q, 128:256], rhs=c64[:, 1, :],
                                   start=False, stop=True)
                return ins
            P.op("pe", f, reads=[b_D2, b_t], writes=[b_pp[i]])
            evac(Yv[:, a0:a0 + 8, :], pp[i][:].rearrange("p (a b) -> p a b", a=8), b_pp[i], b_Y, co=True)
        P.op("sp", lambda e, u=u: e.dma_start(out=yl[u], in_=Y[:]), reads=[b_Y], writes=[Buf()], dma="sty")
        i = nextp()

        def f(e, i=i):
            for j in range(2):
                ins = e.matmul(pp[i][:, j * 256:(j + 1) * 256], lhsT=XC[:, j * 128:(j + 1) * 128], rhs=fc[:, 0, :],
                               start=True, stop=True)
            return ins
        P.op("pe", f, reads=[b_XC, b_t], writes=[b_pp[i]])
        evac(DA[:], pp[i][:].rearrange("p (j n) -> p j n", j=2), b_pp[i], b_DA)
        i = nextp()

        def f(e, i=i):
            n = 0
            for j in range(2):
                for ri in range(2):
                    ins = e.matmul(pp[i][:, 0:256], lhsT=DA[:, j, ri * 128:(ri + 1) * 128], rhs=c256[:, 2 * ri + j, :],
                                   start=(n == 0), stop=(n == 3))
                    n += 1
            return ins
        P.op("pe", f, reads=[b_DA, b_t], writes=[b_pp[i]])
        evac(YC[:], pp[i][:, 0:256], b_pp[i], b_YC)
        P.op("sp", lambda e, u=u: e.dma_start(out=yc[u], in_=YC[:]), reads=[b_YC], writes=[Buf()], dma="styc")
    P.finish()
    P.emit()
    return nc


def fft_tables():
    a = np.arange(128)[:, None, None]
    b = np.arange(64)[None, :, None]
    ap = np.arange(128)[None, None, :]
    th = 2 * np.pi * (a * ap / 128.0 + b * ap / 8192.0)
    ftw = np.concatenate([np.cos(th), -np.sin(th)], axis=-1) / np.sqrt(128.0)
    c = np.arange(128)[:, None]
    cp = np.arange(128)[None, :]
    th = 2 * np.pi * c * cp / 128.0
    cr, ci = np.cos(th) / np.sqrt(128.0), -np.sin(th) / np.sqrt(128.0)
    fc = np.stack([np.concatenate([cr, ci], 1), np.concatenate([-ci, cr], 1)], axis=1)
    bb = np.arange(64)[:, None]
    bp = np.arange(64)[None, :]
    th = 2 * np.pi * bb * bp / 64.0
    c64 = np.stack([np.cos(th), np.sin(th)], axis=1) / 8.0
    l = np.arange(256)[:, None]
    lp = np.arange(256)[None, :]
    th = 2 * np.pi * l * lp / 256.0
    C, S = np.cos(th) / 16.0, np.sin(th) / 16.0
    c256 = np.stack([C[0:128], C[128:256], S[0:128], S[128:256]], axis=1)
    f = lambda x: np.ascontiguousarray(x, dtype=np.float32)
    return f(ftw), f(fc), f(c64), f(c256)


NBLK = SEQ // 128


def build_LC(nblk=NBLK):
    nc = bass.Bass("TRN2", target_bir_lowering=False)
    S = nblk * 128
    NS = 6
    qt_ap = dram_in(nc, "qt", [128, 4, S])
    kt_ap = dram_in(nc, "kt", [128, S + CTX])
    v_ap = dram_in(nc, "v", [128, nblk + 2, 64])
    mask_ap = dram_in(nc, "mask", [128, 384])
    sink_ap = dram_in(nc, "sink", [128, 8])
    id_ap = dram_in(nc, "ident", [128, 128])
    o_ap = dram_out(nc, "o", [S, 512])
    P = Prog(nc)
    QT = P.sb("QT", [128, 4, S], BF16)
    KT = P.sb("KT", [128, S + CTX], BF16)
    V = P.sb("V", [128, nblk + 2, 64], BF16)
    mask = P.sb("mask", [128, 384], F32)
    sink = P.sb("sink", [128, 8], F32)
    sink8 = P.sb("sink8", [128, 8], F32)
    ident = P.sb("ident", [128, 128], BF16)
    sc = [P.sb(f"sc{i}", [128, 648], F32) for i in range(NS)]
    pb = [P.sb(f"pb{i}", [128, 648], BF16) for i in range(NS)]
    pTs = [P.sb(f"pTs{i}", [128, 5, 128], BF16) for i in range(NS)]
    sm = [P.sb(f"sm{i}", [128, 4], F32) for i in range(NS)]
    ot = [P.sb(f"ot{i}", [128, 512], F32) for i in range(2)]
    scA = [P.ps(f"scA{i}", [128, 512]) for i in range(2)]
    scB = [P.ps(f"scB{i}", [128, 512]) for i in range(2)]
    pTt = [P.ps(f"pTt{i}", [128, 8, 128], BF16) for i in range(2)]
    ops = [P.ps(f"ops{i}", [128, 512]) for i in range(2)]
    BL = lambda n: [Buf() for _ in range(n)]
    b_c, b_s8 = Buf(), Buf()
    b_sc, b_pb, b_pTs, b_sm = BL(NS), BL(NS), BL(NS), BL(NS)
    b_ot, b_scA, b_scB, b_pTt, b_ops = BL(2), BL(2), BL(2), BL(2), BL(2)
    for t, ap, k in ((QT, qt_ap, "c0"), (KT, kt_ap, "c1"), (V, v_ap, "c2"), (ident, id_ap, "c3")):
        P.op("pool", lambda e, t=t, ap=ap: e.dma_start(out=t[:], in_=ap), cowrites=[b_c], dma=k)
    for t, ap, k in ((mask, mask_ap, "c4"), (sink, sink_ap, "c5")):
        P.op("sp", lambda e, t=t, ap=ap: e.dma_start(out=t[:], in_=ap), cowrites=[b_c], dma=k)
    P.op("dve", lambda e: e.tensor_scalar_mul(out=sink8[:], in0=sink[:], scalar1=8.0), reads=[b_c], writes=[b_s8])
    units = [(i, r) for i in range(nblk) for r in range(8)]
    N = len(units)

    def geo(n):
        i, r = units[n]
        kb0, kb1 = max(i - 1, 0), min(i + 1, nblk - 1)
        nloc = kb1 - kb0 + 1
        nk = nloc * 128
        mo = (kb0 - (i - 1)) * 128
        return i, r, kb0, nloc, nk, mo, nk + 256

    def S1(n):
        i, r, kb0, nloc, nk, mo, L = geo(n)
        s, p = n % NS, n % 2
        hh, j = r // 4, r % 4
        p0, p1 = hh * 64, hh * 64 + 64
        q = QT[p0:p1, j, i * 128:(i + 1) * 128]

        def f(e):
            e.matmul(scA[p][:, 0:nk], lhsT=q, rhs=KT[p0:p1, kb0 * 128:kb0 * 128 + nk], start=True, stop=True)
            return e.matmul(scB[p][:, 0:256], lhsT=q, rhs=KT[p0:p1, S:S + CTX], start=True, stop=True)
        P.op("pe", f, reads=[b_c], writes=[b_scA[p], b_scB[p]])
        P.op("dve", lambda e: e.tensor_tensor(out=sc[s][:, 0:nk], in0=scA[p][:, 0:nk], in1=mask[:, mo:mo + nk], op=ALU.add),
             reads=[b_scA[p], b_c], writes=[b_sc[s]])
        P.op("act", lambda e: e.activation(out=sc[s][:, nk:L], in_=scB[p][:, 0:256], func=AF.Copy),
             reads=[b_scB[p]], cowrites=[b_sc[s]])
        P.op("pool", lambda e: e.tensor_copy(out=sc[s][:, L:L + 1], in_=sink8[:, r:r + 1]),
             reads=[b_s8], cowrites=[b_sc[s]])
        m = sm[s]
        P.op("dve", lambda e: e.reduce_max(out=m[:, 0:1], in_=sc[s][:, 0:L + 1], axis=AX.X),
             reads=[b_sc[s]], writes=[b_sm[s]])
        P.op("dve", lambda e: e.tensor_scalar_mul(out=m[:, 1:2], in0=m[:, 0:1], scalar1=-0.125),
             reads=[b_sm[s]], writes=[b_sm[s]])

    def S2(n):
        i, r, kb0, nloc, nk, mo, L = geo(n)
        s, p = n % NS, n % 2
        nt = nloc + 2
        m = sm[s]
        P.op("act", lambda e: e.activation(out=pb[s][:, 0:L + 1], in_=sc[s][:, 0:L + 1], func=AF.Exp,
                                           scale=0.125, bias=m[:, 1:2], accum_out=m[:, 2:3]),
             reads=[b_sc[s], b_sm[s]], writes=[b_pb[s], b_sm[s]])
        P.op("dve", lambda e: e.reciprocal(out=m[:, 3:4], in_=m[:, 2:3]), reads=[b_sm[s]], writes=[b_sm[s]])

    def S2b(n):
        i, r, kb0, nloc, nk, mo, L = geo(n)
        s, p = n % NS, n % 2
        nt = nloc + 2

        def f(e):
            for t in range(nt):
                ins = e.transpose(out=pTt[p][:, t, :], in_=pb[s][:, t * 128:(t + 1) * 128], identity=ident[:])
            return ins
        P.op("pe", f, reads=[b_pb[s], b_c], writes=[b_pTt[p]])
        if n % 2:
            P.op("act", lambda e: e.activation(out=pTs[s][:, 0:nt, :], in_=pTt[p][:, 0:nt, :], func=AF.Copy),
                 reads=[b_pTt[p]], writes=[b_pTs[s]])
        else:
            P.op("dve", lambda e: e.tensor_copy(out=pTs[s][:, 0:nt, :], in_=pTt[p][:, 0:nt, :]),
                 reads=[b_pTt[p]], writes=[b_pTs[s]])

    def S3(n):
        i, r, kb0, nloc, nk, mo, L = geo(n)
        s, p = n % NS, n % 2
        nt = nloc + 2
        oi = i % 2
        m = sm[s]

        def f(e):
            for t in range(nt):
                vb = kb0 + t if t < nloc else nblk + (t - nloc)
                ins = e.matmul(ops[p][:, 0:64], lhsT=pTs[s][:, t, :], rhs=V[:, vb, :], start=(t == 0), stop=(t == nt - 1))
            return ins
        P.op("pe", f, reads=[b_pTs[s], b_c], writes=[b_ops[p]])
        P.op("act", lambda e: e.activation(out=ot[oi][:, r * 64:(r + 1) * 64], in_=ops[p][:, 0:64], func=AF.Copy,
                                           scale=m[:, 3:4]),
             reads=[b_ops[p], b_sm[s]], cowrites=[b_ot[oi]])
        if r == 7:
            P.op("sp", lambda e: e.dma_start(out=o_ap[i * 128:(i + 1) * 128, :], in_=ot[oi][:]),
                 reads=[b_ot[oi]], writes=[Buf()], dma=f"so{oi}")
            b_ot[oi].w = []

    for k in range(N + 3):
        if k < N:
            S1(k)
        if 0 <= k - 1 < N:
            S2(k - 1)
        if 0 <= k - 2 < N:
            S2b(k - 2)
        if 0 <= k - 3 < N:
            S3(k - 3)
    P.finish()
    P.emit()
    return nc


def attn_mask():
    a = np.arange(128)[:, None]
    j = np.arange(384)[None, :]
    return np.where(np.abs(j - 128 - a) <= 128, 0.0, -1e30).astype(np.float32)


def build_LB(T, segs):
    nc = bass.Bass("TRN2", target_bir_lowering=False)
    nseg = len(segs)
    nvec = 9 * nseg + 11
    xn1 = dram_in(nc, "xn1", [D, T])
    gb = dram_in(nc, "gb", [1024, T])
    vp = dram_in(nc, "vp", [1024, T])
    vv = dram_in(nc, "vv", [1024, T])
    vn = dram_in(nc, "vn", [1024, T])
    yb = dram_in(nc, "yb", [1024, T])
    vecs_ap = dram_in(nc, "vecs", [128, nvec, NCH])
    wout = dram_in(nc, "wout", [8, 128, NCH, 256])
    wgu2, wd2 = ffn_w_in(nc, "2")
    wgu3, wd3 = ffn_w_in(nc, "3")
    wqkv = dram_in(nc, "wqkv", [20, 128, NCH, 128])
    wperm = dram_in(nc, "wperm", [18, 128, NCH, 128])
    cos_ap = dram_in(nc, "cosT", [128, T])
    sin_ap = dram_in(nc, "sinT", [128, T])
    xn4 = dram_out(nc, "xn4", [D, T])
    qr = dram_out(nc, "qr", [D, T])
    kr = dram_out(nc, "kr", [256, T])
    vo = dram_out(nc, "vo", [256, T])
    P = Prog(nc)
    R = RL(P, T, segs, vecs_ap, nvec)
    V = lambda s, k: 9 * s + k
    L0 = 9 * nseg
    for s in range(nseg):
        R.derive(V(s, 3), mul=0.5)
        R.derive(V(s, 6), mul=0.5)
        for k in (2, 5, 8):
            R.derive(V(s, k), add=1.0)
            R.derive(V(s, k), mul=1.0 / ALPHA)
    for k in range(2, 8):
        R.derive(L0 + k, mul=ALPHA)
    SL = lambda k: [V(s, k) for s in range(nseg)]
    b_in = Buf()
    b_o = Buf()

    def mix_fill(bi, c0, c1, w, j):
        tl = [R.sg[0], R.sg[1], R.zt[0], R.zt[1], R.tmp[0]]
        tb = [R.b_sg[0], R.b_sg[1], R.b_zt[0], R.b_zt[1], R.b_tmp[0]]
        if True:
            for k, src in enumerate((gb, vp, vv, vn, yb)):
                P.op("sp", lambda e, k=k, src=src, j=j: e.dma_start(out=tl[k][:, :w], in_=src[j * 128:(j + 1) * 128, c0:c1]),
                     writes=[tb[k]], dma=f"m{k}")
            P.op("dve", lambda e, j=j: e.tensor_scalar_mul(out=tl[1][:, :w], in0=tl[1][:, :w], scalar1=R.vc(L0 + 8, j)),
                 reads=[R.b_vecs], writes=[tb[1]])
            P.op("dve", lambda e, j=j: e.scalar_tensor_tensor(out=tl[1][:, :w], in0=tl[2][:, :w], scalar=R.vc(L0 + 9, j),
                                                              in1=tl[1][:, :w], op0=ALU.mult, op1=ALU.add),
                 reads=[R.b_vecs, tb[2]], writes=[tb[1]])
            P.op("dve", lambda e, j=j: e.scalar_tensor_tensor(out=tl[1][:, :w], in0=tl[3][:, :w], scalar=R.vc(L0 + 10, j),
                                                              in1=tl[1][:, :w], op0=ALU.mult, op1=ALU.add),
                 reads=[R.b_vecs, tb[3]], writes=[tb[1]])
            P.op("dve", lambda e, j=j: e.tensor_tensor(out=R.hb[:, j, c0:c1], in0=tl[1][:, :w], in1=tl[0][:, :w], op=ALU.mult),
                 reads=[tb[0], tb[1]], cowrites=R.whb(bi))
            P.op("act", lambda e, j=j: e.activation(out=R.hb[:, 8 + j, c0:c1], in_=tl[4][:, :w], func=AF.Copy),
                 reads=[tb[4]], cowrites=R.whb(bi))

    extra = {}
    for bi, (c0, c1) in enumerate(R.blocks):
        fl = [lambda bi=bi, c0=c0, c1=c1, j=j: mix_fill(bi, c0, c1, c1 - c0, j) for j in range(8)]
        if bi == 0:
            for f in fl:
                f()
        else:
            extra[bi - 1] = fl
    R.phase2("proj", wout, ("ap", xn1, b_in, True), SL(0), ln=(L0 + 2, L0 + 3), mod=(SL(1), SL(2)), reload_hb=True,
             extra=extra)
    R.phase1(wgu2)
    R.phase2("ffn", wd2, ("xs",), SL(3), ln=(L0 + 4, L0 + 5), mod=(SL(4), SL(5)), reload_hb=True)
    R.phase1(wgu3)
    R.phase2("ffn", wd3, ("xs",), SL(6), ln=(L0 + 6, L0 + 7), mod=(SL(7), SL(8)), out_ap=xn4, reload_hb=True)
    rt = R.xn2[0][:, :, :].rearrange("p c t -> p (c t)")
    cosT, sinT = rt[:, 0:T], rt[:, T:2 * T]
    P.op("sp", lambda e: e.dma_start(out=cosT, in_=cos_ap), writes=R.b_xn2[0], dma="r0")
    P.op("sp", lambda e: e.dma_start(out=sinT, in_=sin_ap), cowrites=R.b_xn2[0], dma="r1")
    b_rope = R.b_xn2[0][0]
    ld = lambda g: (R.load_wa(wqkv, g), R.load_wa(wperm, g) if g < 18 else None)
    nxt = ld(0)
    for g in range(20):
        iw, ip = nxt
        if g + 1 < 20:
            nxt = ld(g + 1)
        for bi, (c0, c1) in enumerate(R.blocks):
            w = c1 - c0
            pb = R.pp_i
            R.pp_i ^= 1
            R.mm(R.pg[pb], R.b_pg[pb], R.wa[iw], R.b_wa[iw], NCH, R.hb, c0, c1, [R.b_hbk[bi]])
            i = R.zt_i
            R.zt_i ^= 1
            zt, tmp, sg, pg, pu = R.zt[i], R.tmp[i], R.sg[pb], R.pg[pb], R.pu[pb]
            if g < 18:
                R.mm(R.pu[pb], R.b_pu[pb], R.wa[ip], R.b_wa[ip], NCH, R.hb, c0, c1, [R.b_hbk[bi]])
                P.op("dve", lambda e, sg=sg, pg=pg, w=w, c0=c0, c1=c1: e.tensor_tensor(
                    out=sg[:, :w], in0=pg[:, :w], in1=cosT[:, c0:c1], op=ALU.mult),
                    reads=[R.b_pg[pb], b_rope], writes=[R.b_sg[pb]])
                P.op("dve", lambda e, tmp=tmp, pu=pu, w=w, c0=c0, c1=c1: e.tensor_tensor(
                    out=tmp[:, :w], in0=pu[:, :w], in1=sinT[:, c0:c1], op=ALU.mult),
                    reads=[R.b_pu[pb], b_rope], writes=[R.b_tmp[i]])
                P.op("dve", lambda e, zt=zt, sg=sg, tmp=tmp, w=w: e.tensor_tensor(
                    out=zt[:, :w], in0=sg[:, :w], in1=tmp[:, :w], op=ALU.add),
                    reads=[R.b_sg[pb], R.b_tmp[i]], writes=[R.b_zt[i]])
                dst, row = (qr, g * 128) if g < 16 else (kr, (g - 16) * 128)
            else:
                P.op("act", lambda e, zt=zt, pg=pg, w=w: e.activation(out=zt[:, :w], in_=pg[:, :w], func=AF.Copy),
                     reads=[R.b_pg[pb]], writes=[R.b_zt[i]])
                dst, row = vo, (g - 18) * 128
            R.store(dst, b_o, row, c0, c1, zt, R.b_zt[i])
    P.finish()
    P.emit()
    return nc


def build_LD(T):
    nc = bass.Bass("TRN2", target_bir_lowering=False)
    segs = [(0, T)]
    nvec = 10
    xn4 = dram_in(nc, "xn4", [D, T])
    oT = dram_in(nc, "oT", [D, T])
    vecs_ap = dram_in(nc, "vecs", [128, nvec, NCH])
    wout = dram_in(nc, "wout", [8, 128, NCH, 256])
    wgu, wd = ffn_w_in(nc, "")
    out = dram_out(nc, "out", [D, T])
    P = Prog(nc)
    R = RL(P, T, segs, vecs_ap, nvec)
    R.derive(2, add=1.0)
    R.derive(2, mul=1.0 / ALPHA)
    R.derive(3, mul=0.5)
    R.derive(6, mul=ALPHA)
    R.derive(7, mul=ALPHA)
    b_in = Buf()
    for bi, (c0, c1) in enumerate(R.blocks):
        P.op("pool", lambda e, c0=c0, c1=c1: e.dma_start(out=R.hb[:, :, c0:c1], in_=chunked(oT)[:, :, c0:c1]),
             cowrites=R.whb(bi), dma="oin")
    R.phase2("proj", wout, ("ap", xn4, b_in, True), [0], ln=(6, 7), mod=([1], [2]), reload_hb=True)
    R.phase1(wgu)
    R.phase2("ffn", wd, ("xs",), [3], ln=(8, 9), mod=None, out_ap=out)
    P.finish()
    P.emit()
    return nc


def lay(v):
    v = np.asarray(v, dtype=np.float32)
    if v.shape[0] < D:
        v = np.concatenate([v, np.zeros(D - v.shape[0], np.float32)])
    return v.reshape(NCH, 128).T


def rope_tables(pos):
    nf = 16
    inv = np.power(10000.0, -np.arange(nf, dtype=np.float64) / nf)
    row = (pos // 64).astype(np.float64)
    col = (pos % 64).astype(np.float64)
    d = np.arange(64)
    axis, part, f = d // 32, (d % 32) // 16, d % 16
    ang = np.where(axis[:, None] == 0, row[None, :], col[None, :]) * inv[f][:, None]
    c = np.cos(ang)
    s = np.sin(ang) * np.where(part == 0, -1.0, 1.0)[:, None]
    return np.concatenate([c, c], 0).astype(np.float32), np.concatenate([s, s], 0).astype(np.float32)


def rope_perm():
    d = np.arange(64)
    part = (d % 32) // 16
    p = np.where(part == 0, d + 16, d - 16)
    cols = np.concatenate([h * 64 + p for h in range(36)])
    return cols


_CACHE = {}


def _prog(key, fn):
    if key not in _CACHE:
        _CACHE[key] = fn()
    return _CACHE[key]


def _run(nc, ins):
    res = run_bass_kernel_spmd(nc, ins, core_ids=list(range(NCORE)))
    return res.results


def kernel(x, c, ctx, c_ctx, w_mod, b_mod, ln_g, ln_b, ffn_w_gate, ffn_w_up, ffn_w_down,
           ab_w_in, ab_conv, ab_w_out, attn_w_in, attn_sink, attn_w_out):
    f32 = lambda a: np.ascontiguousarray(np.asarray(a), dtype=np.float32)
    x, c, ctx, c_ctx = f32(x), f32(c), f32(ctx), f32(c_ctx)
    w_mod, b_mod, ln_g, ln_b = f32(w_mod), f32(b_mod), f32(ln_g), f32(ln_b)
    ffn_w_gate, ffn_w_up, ffn_w_down = f32(ffn_w_gate), f32(ffn_w_up), f32(ffn_w_down)
    ab_w_in, ab_conv, ab_w_out = f32(ab_w_in), f32(ab_conv), f32(ab_w_out)
    attn_w_in, attn_sink, attn_w_out = f32(attn_w_in), f32(attn_sink), f32(attn_w_out)
    LT, CT = SEQ // 4, CTX // 4
    T = LT + CT
    segs = [(0, LT), (LT, T)]
    cores = [(r // 4, r % 4) for r in range(NCORE)]

    cv = np.ascontiguousarray(np.stack([lay(c[0]), lay(c[1]), lay(c_ctx)], axis=-1))
    wm = np.concatenate([w_mod[0], w_mod[1]], axis=1)
    bm = b_mod.reshape(-1)
    ins = []
    for r in range(NCORE):
        sl = slice(r * MODC, (r + 1) * MODC)
        ins.append({"cv": cv, "w": np.ascontiguousarray(wm[:, sl]),
                    "b": np.ascontiguousarray(np.broadcast_to(bm[sl], (3, MODC)))})
    res = _run(_prog("L0", build_L0), ins)
    del wm
    mod = np.concatenate([res[r]["mod"] for r in range(NCORE)], axis=1).reshape(3, 2, 9, D)

    def mv(b, s, layer, k):
        return lay(mod[b if s == 0 else 2, layer, k])

    _ffw = {}

    def ffw(l, i, sfx):
        if (l, i) not in _ffw:
            _ffw[(l, i)] = (np.concatenate([tile_w(ffn_w_gate[l, i], 128), tile_w(ffn_w_up[l, i], 128)], axis=-1),
                            tile_w(ffn_w_down[l, i], 128))
        a, c_ = _ffw[(l, i)]
        return {"wgu" + sfx: a, "wd" + sfx: c_}

    w_gb, w_gc = tile_w(ab_w_in[0][:, 0:1024], 128), tile_w(ab_w_in[0][:, 1024:2048], 128)
    w_xi, w_uf = tile_w(ab_w_in[0][:, 2048:3072], 128), tile_w(ab_w_in[0][:, 3072:4096], 128)

    ins = []
    for b, q in cores:
        xT = np.concatenate([x[b, q * LT:(q + 1) * LT].T, ctx[b, q * CT:(q + 1) * CT].T], axis=1)
        vecs = [mv(b, s, 0, k) for s in range(2) for k in range(5)] + [lay(ln_g[0, 0]), lay(ln_b[0, 0])]
        ins.append({"xT": np.ascontiguousarray(xT), "vecs": np.ascontiguousarray(np.stack(vecs, axis=1)),
                    **ffw(0, 0, ""), "w_gb": w_gb, "w_gc": w_gc, "w_xi": w_xi, "w_uf": w_uf})
    resA = _run(_prog("LA", lambda: build_LA(T, segs)), ins)

    def gather(res, name, nrow):
        lat = np.empty((2, nrow, SEQ), np.float32)
        cx = np.empty((2, nrow, CTX), np.float32)
        for r, (b, q) in enumerate(cores):
            a = res[r][name]
            lat[b, :, q * LT:(q + 1) * LT] = a[:, :LT]
            cx[b, :, q * CT:(q + 1) * CT] = a[:, LT:]
        return lat, cx

    UFl, UFc = gather(resA, "uf", 1024)
    VVl, VVc = gather(resA, "vv", 1024)

    ftw, fc, c64, c256 = fft_tables()
    ins = []
    for b, q in cores:
        gs = [2 * q, 2 * q + 1]
        xl = np.stack([UFl[b, g * 128:(g + 1) * 128].T for g in gs])
        xc = np.stack([UFc[b, g * 128:(g + 1) * 128] for g in gs])
        ins.append({"xl": np.ascontiguousarray(xl), "xc": np.ascontiguousarray(xc),
                    "ftw": ftw, "fc": fc, "c64": c64, "c256": c256})
    resF = _run(_prog("LF", build_LF), ins)
    YBl = np.empty((2, 1024, SEQ), np.float32)
    YBc = np.empty((2, 1024, CTX), np.float32)
    for r, (b, q) in enumerate(cores):
        for u in range(2):
            g = 2 * q + u
            YBl[b, g * 128:(g + 1) * 128] = resF[r]["yl"][u]
            YBc[b, g * 128:(g + 1) * 128] = resF[r]["yc"][u]
    del UFl, UFc

    def shift(a, k):
        o = np.zeros_like(a)
        if k > 0:
            o[..., k:] = a[..., :-k]
        else:
            o[..., :k] = a[..., -k:]
        return o

    VPl, VPc, VNl, VNc = shift(VVl, 1), shift(VVc, 1), shift(VVl, -1), shift(VVc, -1)

    def cols(lat, cx, b, q):
        return np.ascontiguousarray(np.concatenate([lat[b][:, q * LT:(q + 1) * LT], cx[b][:, q * CT:(q + 1) * CT]], axis=1))

    perm = rope_perm()
    wperm = tile_w(np.ascontiguousarray(attn_w_in[0][:, perm]), 128)
    wqkv_t = tile_w(attn_w_in[0], 128)
    wout_t = tile_w(ab_w_out[0], 256)
    _ffw.pop((0, 0), None)
    ins = []
    for r, (b, q) in enumerate(cores):
        vecs = []
        for s in range(2):
            vecs += [mv(b, s, 0, 5), mv(b, s, 0, 6), mv(b, s, 0, 7), mv(b, s, 0, 8),
                     mv(b, s, 1, 0), mv(b, s, 1, 1), mv(b, s, 1, 2), mv(b, s, 1, 3), mv(b, s, 1, 4)]
        vecs += [lay(ln_g[0, 0]), lay(ln_b[0, 0]), lay(ln_g[0, 1]), lay(ln_b[0, 1]), lay(ln_g[0, 2]), lay(ln_b[0, 2]),
                 lay(ln_g[1, 0]), lay(ln_b[1, 0]), lay(ab_conv[0, 0]), lay(ab_conv[0, 1]), lay(ab_conv[0, 2])]
        cl, sl_ = rope_tables(np.arange(q * LT, (q + 1) * LT))
        cosT = np.concatenate([cl, np.ones((128, CT), np.float32)], axis=1)
        sinT = np.concatenate([sl_, np.zeros((128, CT), np.float32)], axis=1)
        ins.append({"xn1": resA[r]["xn1"], "gb": resA[r]["gb"], "vp": cols(VPl, VPc, b, q), "vv": resA[r]["vv"],
                    "vn": cols(VNl, VNc, b, q), "yb": cols(YBl, YBc, b, q),
                    "vecs": np.ascontiguousarray(np.stack(vecs, axis=1)), "wout": wout_t,
                    **ffw(0, 1, "2"), **ffw(1, 0, "3"), "wqkv": wqkv_t, "wperm": wperm,
                    "cosT": np.ascontiguousarray(cosT), "sinT": np.ascontiguousarray(sinT)})
    resB = _run(_prog("LB", lambda: build_LB(T, segs)), ins)
    del resA, VPl, VNl, YBl, VVl
    Ql, _ = gather(resB, "qr", D)
    Kl, Kc = gather(resB, "kr", 256)
    Vl, Vc = gather(resB, "vo", 256)

    mask = attn_mask()
    ident = np.eye(128, dtype=np.float32)
    ins = []
    for r in range(NCORE):
        b, g = r // 4, r % 4
        qt = Ql[b, g * 512:(g + 1) * 512].reshape(2, 4, 64, SEQ).transpose(0, 2, 1, 3).reshape(128, 4, SEQ)
        k1 = np.concatenate([Kl[b, g * 64:(g + 1) * 64], Kc[b, g * 64:(g + 1) * 64]], axis=1)
        vl = Vl[b, g * 64:(g + 1) * 64].T.reshape(NBLK, 128, 64)
        vc_ = Vc[b, g * 64:(g + 1) * 64].T.reshape(2, 128, 64)
        vall = np.concatenate([vl, vc_], axis=0).transpose(1, 0, 2)
        ins.append({"qt": np.ascontiguousarray(qt), "kt": np.ascontiguousarray(np.concatenate([k1, k1], axis=0)),
                    "v": np.ascontiguousarray(vall), "mask": mask,
                    "sink": np.ascontiguousarray(np.broadcast_to(attn_sink[0, g * 8:(g + 1) * 8], (128, 8))),
                    "ident": ident})
    resC = _run(_prog("LC", build_LC), ins)
    del Ql
    O = np.empty((2, D, SEQ), np.float32)
    for r in range(NCORE):
        b, g = r // 4, r % 4
        O[b, g * 512:(g + 1) * 512] = resC[r]["o"].T
    del resC

    _ffw.clear()
    awout_t = tile_w(attn_w_out[0], 256)
    ins = []
    for r, (b, q) in enumerate(cores):
        vecs = [mv(b, 0, 1, 5), mv(b, 0, 1, 6), mv(b, 0, 1, 7), mv(b, 0, 1, 8),
                lay(ln_g[1, 0]), lay(ln_b[1, 0]), lay(ln_g[1, 1]), lay(ln_b[1, 1]), lay(ln_g[1, 2]), lay(ln_b[1, 2])]
        ins.append({"xn4": np.ascontiguousarray(resB[r]["xn4"][:, :LT]), "oT": np.ascontiguousarray(O[b][:, q * LT:(q + 1) * LT]),
                    "vecs": np.ascontiguousarray(np.stack(vecs, axis=1)), "wout": awout_t, **ffw(1, 1, "")})
    resD = _run(_prog("LD", lambda: build_LD(LT)), ins)
    out = np.empty((2, SEQ, D), np.float32)
    for r, (b, q) in enumerate(cores):
        out[b, q * LT:(q + 1) * LT] = resD[r]["out"].T
    return out
```

```python
import numpy as np
from contextlib import ExitStack
import concourse.bass as bass
import concourse.mybir as mybir
from concourse.bass_utils import run_bass_kernel_spmd

F32 = mybir.dt.float32
BF16 = mybir.dt.bfloat16
AF = mybir.ActivationFunctionType
ALU = mybir.AluOpType
AX = mybir.AxisListType

D = 2048
DFF = 5632
NCH = 16
FCH = 44
SEQ = 8192
CTX = 256
NCORE = 8
ALPHA = 4.0 ** 0.25
LN_EPS = 1e-5
TB = 512


class Buf:
    __slots__ = ("w", "r", "name")

    def __init__(self, name=""):
        self.w = []
        self.r = []
        self.name = name


class Ctr:
    LIMIT = 30000

    def __init__(self, P, name, step):
        self.P, self.name, self.step = P, name, step
        self.k = 0
        self.done = []
        self._new()

    def _new(self):
        self.sem = self.P.stack.enter_context(self.P.nc.semaphore(f"{self.name}_{self.k}"))
        self.k += 1
        self.val = 0

    def next(self):
        if self.val + self.step > self.LIMIT:
            self.done.append((self.sem, self.val))
            self._new()
        self.val += self.step
        return (self.sem, self.val)


class Eng:
    def __init__(self, P, name):
        self.name = name
        self.ops = []
        self.waited = {}
        self.ctr = Ctr(P, "e" + name, 1)


class Prog:
    def __init__(self, nc):
        self.nc = nc
        self.stack = ExitStack()
        self.engs = {n: Eng(self, n) for n in ("pe", "act", "dve", "pool", "sp")}
        self.dctr = {}
        self.fuzzy = {}
        self.n = 0

    def sb(self, name, shape, dt):
        return self.stack.enter_context(self.nc.sbuf_tensor("s_" + name, list(shape), dt))

    def ps(self, name, shape, dt=F32):
        return self.stack.enter_context(self.nc.psum_tensor("p_" + name, list(shape), dt))

    def op(self, eng, fn, reads=(), writes=(), dma=None, cowrites=()):
        E = self.engs[eng]
        deps = {}

        def add(tok):
            s, v = tok
            k = id(s)
            if k in self.fuzzy:
                v = max(v, self.fuzzy[k].val if self.fuzzy[k].sem is s else v)
            if k not in deps or deps[k][1] < v:
                deps[k] = (s, v)

        for b in reads:
            for t in b.w:
                add(t)
        for b in writes:
            for t in b.w:
                add(t)
            for t in b.r:
                add(t)
        for b in cowrites:
            for t in b.r:
                add(t)
        if dma is not None:
            if dma not in self.dctr:
                self.dctr[dma] = Ctr(self, "d" + dma, 16)
            ctr = self.dctr[dma]
            if dma.startswith("st") or dma.startswith("xo"):
                self.fuzzy[id(ctr.sem)] = ctr
        else:
            ctr = E.ctr
        waits = []
        for k, (s, v) in deps.items():
            if eng == "pe" and dma is None and s is E.ctr.sem:
                continue
            if E.waited.get(k, 0) >= v:
                continue
            E.waited[k] = v
            waits.append((s, v))
        tok = ctr.next()
        E.ops.append((waits, fn, tok[0], ctr.step))
        for b in reads:
            b.r.append(tok)
        for b in writes:
            b.w = [tok]
            b.r = []
        for b in cowrites:
            b.w.append(tok)
        self.n += 1
        return tok

    def finish(self):
        E = self.engs["sp"]
        waits = []
        for ctr in self.dctr.values():
            for s, v in ctr.done + [(ctr.sem, ctr.val)]:
                if v > 0 and E.waited.get(id(s), 0) < v:
                    waits.append((s, v))
        E.ops.append((waits, None, None, 0))

    def emit(self):
        nc = self.nc

        def mk(E):
            def run(e):
                for waits, fn, sem, step in E.ops:
                    for ws, wv in waits:
                        e.wait_ge(ws, wv)
                    if fn is not None:
                        fn(e).then_inc(sem, step)
            return run

        with nc.Block() as block:
            block.tensor(mk(self.engs["pe"]))
            block.scalar(mk(self.engs["act"]))
            block.vector(mk(self.engs["dve"]))
            block.gpsimd(mk(self.engs["pool"]))
            block.sync(mk(self.engs["sp"]))
        self.stack.close()


def blocks_of(T, tb=TB):
    bl = [(c, min(c + tb, T)) for c in range(0, T, tb)]
    if bl[-1][1] - bl[-1][0] < tb:
        bl = [bl[-1]] + bl[:-1]
    return bl


def chunked(ap):
    return ap.rearrange("(c p) t -> p c t", p=128)


class RL:
    def __init__(self, P, T, segs, vecs_ap, nvec):
        self.P, self.T, self.segs = P, T, segs
        self.blocks = blocks_of(T)
        nb = len(self.blocks)
        nc = P.nc
        self.xn2 = [P.sb(f"xn{i}", [128, NCH, TB], F32) for i in range(2)]
        self.big = P.sb("big", [128, max(NCH * T, FCH * TB)], BF16)
        self.hb = self.big[:, 0:NCH * T].rearrange("p (c t) -> p c t", c=NCH)
        self.ab = self.big[:, 0:FCH * TB].rearrange("p (c t) -> p c t", c=FCH)
        self.wa2 = [P.sb(f"wa{i}", [128, NCH, 256], BF16) for i in range(2)]
        self.wa = [self.wa2[i // 2][:, :, (i % 2) * 128:(i % 2 + 1) * 128] for i in range(4)]
        self.wap_i = 0
        self.wb = [P.sb(f"wb{i}", [128, FCH, 128], BF16) for i in range(2)]
        self.mu = P.sb("mu", [128, TB], F32)
        self.msq = P.sb("msq", [128, TB], F32)
        self.rstd = P.sb("rstd", [128, TB], F32)
        self.sg = [P.sb(f"sg{i}", [128, TB], F32) for i in range(2)]
        self.zt = [P.sb(f"zt{i}", [128, TB], F32) for i in range(2)]
        self.tmp = [P.sb(f"tmp{i}", [128, TB], F32) for i in range(2)]
        self.xr = [P.sb(f"xr{i}", [128, TB], F32) for i in range(4)]
        self.at = [P.sb(f"at{i}", [128, TB], BF16) for i in range(2)]
        self.zb = [P.sb(f"zb{i}", [128, TB], BF16) for i in range(2)]
        self.zq = [P.sb(f"zq{i}", [128, TB], BF16) for i in range(2)]
        self.ones = P.sb("ones", [128, 128], BF16)
        self.vecs = P.sb("vecs", [128, nvec, NCH], F32)
        self.s1 = P.ps("s1", [128, TB])
        self.s2 = P.ps("s2", [128, TB])
        self.pg = [P.ps(f"pg{i}", [128, TB]) for i in range(2)]
        self.pu = [P.ps(f"pu{i}", [128, TB]) for i in range(2)]
        self.py = [P.ps(f"py{i}", [128, TB]) for i in range(2)]
        self.xs = [nc.dram_tensor(f"xs{i}", [nb, 128, NCH, TB], F32).ap() for i in range(2)]
        self.hs = nc.dram_tensor("hs", [nb, 128, NCH, TB], BF16).ap()
        self.A = nc.dram_tensor("Asp", [nb, 128, FCH, TB], BF16).ap()
        B = Buf
        BL = lambda n: [B() for _ in range(n)]
        self.b_xn2 = [BL(NCH), BL(NCH)]
        self.b_hbk = BL(nb)
        self.b_abc = BL(FCH)
        self.b_wa, self.b_wb = BL(4), BL(2)
        self.b_mu, self.b_msq, self.b_rstd = B(), B(), B()
        self.b_sg, self.b_zt, self.b_tmp, self.b_xr, self.b_at = BL(2), BL(2), BL(2), BL(4), BL(2)
        self.b_zb, self.b_zq = BL(2), BL(2)
        self.b_s1, self.b_s2 = B(), B()
        self.b_pg, self.b_pu, self.b_py = BL(2), BL(2), BL(2)
        self.b_ones, self.b_vecs = B(), B()
        self.b_xs = [BL(nb), BL(nb)]
        self.b_hs = BL(nb)
        self.b_A = BL(nb)
        self.W_HB = self.b_hbk + self.b_abc
        self.wa_i = self.wb_i = self.zt_i = self.xr_i = self.at_i = self.pp_i = self.zb_i = 0
        self.xs_cur = 0
        ones, vecs = self.ones, self.vecs
        P.op("dve", lambda e: e.memset(ones[:], 1.0), writes=[self.b_ones])
        P.op("sp", lambda e: e.dma_start(out=vecs[:], in_=vecs_ap), writes=[self.b_vecs], dma="vecs")

    def vc(self, v, c):
        return self.vecs[:, v, c:c + 1]

    def whb(self, bi):
        return [self.b_hbk[bi]] + self.b_abc

    def derive(self, v, mul=None, add=None):
        vecs = self.vecs
        if add is not None:
            self.P.op("dve", lambda e: e.tensor_scalar_add(out=vecs[:, v, :], in0=vecs[:, v, :], scalar1=float(add)),
                      reads=[self.b_vecs], writes=[self.b_vecs])
        if mul is not None:
            self.P.op("dve", lambda e: e.tensor_scalar_mul(out=vecs[:, v, :], in0=vecs[:, v, :], scalar1=float(mul)),
                      reads=[self.b_vecs], writes=[self.b_vecs])

    def segparts(self, c0, c1):
        out = []
        for si, (s0, s1) in enumerate(self.segs):
            a, b = max(c0, s0), min(c1, s1)
            if a < b:
                out.append((si, a - c0, b - c0))
        return out

    def first_prologue(self, x_ap, x_buf, vshift, vscale1):
        P = self.P
        hb = self.hb
        for bi, (c0, c1) in enumerate(self.blocks):
            w = c1 - c0
            s = bi % 2
            xn = self.xn2[s]
            P.op("sp", lambda e, xn=xn, w=w, c0=c0, c1=c1: e.dma_start(out=xn[:, :, :w], in_=chunked(x_ap)[:, :, c0:c1]),
                 reads=[x_buf], writes=self.b_xn2[s], dma=f"xin{s}")
            for si, a, b in self.segparts(c0, c1):
                for c in range(NCH):
                    P.op("act", lambda e, xn=xn, c=c, si=si, a=a, b=b, c0=c0: e.activation(
                        out=hb[:, c, c0 + a:c0 + b], in_=xn[:, c, a:b], func=AF.Identity,
                        scale=self.vc(vscale1[si], c), bias=self.vc(vshift[si], c)),
                        reads=[self.b_vecs, self.b_xn2[s][c]], cowrites=self.whb(bi))

    def load_hb(self, only=None):
        hb, hs = self.hb, self.hs
        for bi, (c0, c1) in enumerate(self.blocks):
            if only is not None and bi not in only:
                continue
            w = c1 - c0
            self.P.op("sp", lambda e, bi=bi, c0=c0, c1=c1, w=w: e.dma_start(out=hb[:, :, c0:c1], in_=hs[bi, :, :, :w]),
                      reads=[self.b_hs[bi]], cowrites=self.whb(bi), dma=f"ldh{bi % 2}")
            self.b_hs[bi].w = []

    def load_wa(self, W_ap, g):
        i = self.wa_i
        self.wa_i = (i + 1) % 4
        t = self.wa[i]
        self.P.op("pool", lambda e: e.dma_start(out=t, in_=W_ap[g]), writes=[self.b_wa[i]], dma=f"wa{i}")
        return i

    def load_wa_pair(self, W2_ap, g):
        p = self.wap_i
        self.wap_i ^= 1
        t = self.wa2[p]
        self.P.op("pool", lambda e: e.dma_start(out=t[:], in_=W2_ap[g]), writes=[self.b_wa[2 * p], self.b_wa[2 * p + 1]],
                  dma=f"wp{p}")
        self.wa_i = (2 * p + 2) % 4
        return 2 * p, 2 * p + 1

    def load_wb(self, W_ap, g):
        i = self.wb_i
        self.wb_i = (i + 1) % 2
        t = self.wb[i]
        self.P.op("pool", lambda e: e.dma_start(out=t[:], in_=W_ap[g]), writes=[self.b_wb[i]], dma=f"wb{i}")
        return i

    def mm(self, dst, dst_buf, wt, wbuf, kc, X, lo, hi, xbufs):
        w = hi - lo

        def f(e):
            for k in range(kc):
                ins = e.matmul(dst[:, :w], lhsT=wt[:, k, :], rhs=X[:, k, lo:hi], start=(k == 0), stop=(k == kc - 1))
            return ins
        self.P.op("pe", f, reads=[wbuf] + list(xbufs), writes=[dst_buf])

    def store(self, dst_ap, dst_buf, row0, c0, c1, tile, tbuf):
        w = c1 - c0
        self.P.op("sp", lambda e: e.dma_start(out=dst_ap[row0:row0 + 128, c0:c1], in_=tile[:, :w]),
                  reads=[tbuf], cowrites=[dst_buf], dma="st")

    def phase1(self, wgu):
        P = self.P
        hb, A = self.hb, self.A
        ld = lambda fc: self.load_wa_pair(wgu, fc)
        nxt = ld(0)
        for fc in range(FCH):
            ig, iu = nxt
            if fc + 1 < FCH:
                nxt = ld(fc + 1)
            for bi, (c0, c1) in enumerate(self.blocks):
                w = c1 - c0
                pb = self.pp_i
                self.pp_i ^= 1
                self.mm(self.pg[pb], self.b_pg[pb], self.wa[ig], self.b_wa[ig], NCH, hb, c0, c1, [self.b_hbk[bi]])
                self.mm(self.pu[pb], self.b_pu[pb], self.wa[iu], self.b_wa[iu], NCH, hb, c0, c1, [self.b_hbk[bi]])
                sg, pg, pu = self.sg[pb], self.pg[pb], self.pu[pb]
                ai = self.at_i
                self.at_i ^= 1
                at = self.at[ai]
                P.op("act", lambda e, sg=sg, pg=pg, w=w: e.activation(out=sg[:, :w], in_=pg[:, :w], func=AF.Silu),
                     reads=[self.b_pg[pb]], writes=[self.b_sg[pb]])
                P.op("dve", lambda e, sg=sg, pu=pu, at=at, w=w: e.tensor_tensor(out=at[:, :w], in0=sg[:, :w], in1=pu[:, :w],
                                                                               op=ALU.mult),
                     reads=[self.b_sg[pb], self.b_pu[pb]], writes=[self.b_at[ai]])
                P.op("sp", lambda e, at=at, bi=bi, fc=fc, w=w: e.dma_start(out=A[bi, :, fc, :w], in_=at[:, :w]),
                     reads=[self.b_at[ai]], cowrites=[self.b_A[bi]], dma="sta")

    def phase2(self, kind, W, res, vgate, ln, mod=None, out_ap=None, reload_hb=False, extra=None):
        P = self.P
        hb, ab, A, ones = self.hb, self.ab, self.A, self.ones
        s1, s2, mu, msq, rstd = self.s1, self.s2, self.mu, self.msq, self.rstd
        vg, vb = ln
        xs_in = self.xs[self.xs_cur]
        b_xs_in = self.b_xs[self.xs_cur]
        xs_out = self.xs[self.xs_cur ^ 1]
        b_xs_out = self.b_xs[self.xs_cur ^ 1]
        hs = self.hs
        pending = []

        def ln_apply(bi, c0, c1, s, c):
            w = c1 - c0
            xn = self.xn2[s]
            bx = self.b_xn2[s][c]
            P.op("dve", lambda e: e.tensor_tensor(out=xn[:, c, :w], in0=xn[:, c, :w], in1=mu[:, :w], op=ALU.subtract),
                 reads=[self.b_mu], writes=[bx])
            P.op("dve", lambda e: e.tensor_tensor(out=xn[:, c, :w], in0=xn[:, c, :w], in1=rstd[:, :w], op=ALU.mult),
                 reads=[self.b_rstd], writes=[bx])
            P.op("act", lambda e: e.activation(out=xn[:, c, :w], in_=xn[:, c, :w], func=AF.Identity,
                                               scale=self.vc(vg, c), bias=self.vc(vb, c)),
                 reads=[self.b_vecs], writes=[bx])
            if mod is not None:
                ai = self.at_i
                self.at_i ^= 1
                at = self.at[ai]
                for si, a, b in self.segparts(c0, c1):
                    P.op("act", lambda e, si=si, a=a, b=b: e.activation(
                        out=at[:, a:b], in_=xn[:, c, a:b], func=AF.Identity,
                        scale=self.vc(mod[1][si], c), bias=self.vc(mod[0][si], c)),
                        reads=[self.b_vecs, bx], cowrites=[self.b_at[ai]])
                P.op("sp", lambda e: e.dma_start(out=hs[bi, :, c, :w], in_=at[:, :w]),
                     reads=[self.b_at[ai]], cowrites=[self.b_hs[bi]], dma="sth")
                self.b_at[ai].w = []
            if c == NCH - 1:
                if out_ap is not None:
                    P.op("sp", lambda e: e.dma_start(out=chunked(out_ap)[:, :, c0:c1], in_=xn[:, :, :w]),
                         reads=self.b_xn2[s], writes=[Buf()], dma=f"xo{s}")
                else:
                    P.op("sp", lambda e: e.dma_start(out=xs_out[bi, :, :, :w], in_=xn[:, :, :w]),
                         reads=self.b_xn2[s], writes=[b_xs_out[bi]], dma=f"xo{s}")

        seq = [(bi_, dc_) for bi_ in range(len(self.blocks)) for dc_ in range(NCH)]

        def xr_load(k):
            if k >= len(seq):
                return
            bi_, dc_ = seq[k]
            c0_, c1_ = self.blocks[bi_]
            w_ = c1_ - c0_
            xi_ = k % 4
            xr_ = self.xr[xi_]
            if res[0] == "xs":
                rv, rb = xs_in[bi_, :, dc_, :w_], b_xs_in[bi_]
            else:
                rv, rb = res[1][dc_ * 128:(dc_ + 1) * 128, c0_:c1_], res[2]
            P.op("sp", lambda e: e.dma_start(out=xr_[:, :w_], in_=rv), reads=[rb], writes=[self.b_xr[xi_]], dma=f"xr{xi_}")

        for k0 in range(3):
            xr_load(k0)
        scaled = True if res[0] == "xs" else res[3]
        kpos = 0
        for bi, (c0, c1) in enumerate(self.blocks):
            w = c1 - c0
            s = bi % 2
            xn = self.xn2[s]
            if kind == "ffn":
                for q in range(4):
                    k0, k1 = q * 11, (q + 1) * 11
                    wr = self.b_abc[k0:k1] + (self.b_hbk if q == 0 else [])
                    P.op("sp", lambda e, bi=bi, w=w, k0=k0, k1=k1: e.dma_start(out=ab[:, k0:k1, :w], in_=A[bi, :, k0:k1, :w]),
                         reads=[self.b_A[bi]], writes=wr, dma=f"lda{q}")
                self.b_A[bi].w = []
                ldw = lambda dc: self.load_wb(W, dc)
                nxt = ldw(0)
            else:
                nxtp = self.load_wa_pair(W, 0)
            stats_prev = None
            for dc in range(NCH):
                if kind == "ffn":
                    iw = nxt
                    if dc + 1 < NCH:
                        nxt = ldw(dc + 1)
                else:
                    if dc % 2 == 0:
                        curp = nxtp
                        if dc + 2 < NCH:
                            nxtp = self.load_wa_pair(W, dc // 2 + 1)
                    iw = curp[dc % 2]
                pb = dc % 2
                xi = kpos % 4
                xr = self.xr[xi]
                if kind == "ffn" and dc == 0:
                    for q in range(4):
                        def f(e, q=q, iw=iw, pb=pb, w=w):
                            for k in range(q * 11, (q + 1) * 11):
                                ins = e.matmul(self.py[pb][:, :w], lhsT=self.wb[iw][:, k, :], rhs=ab[:, k, 0:w],
                                               start=(k == 0), stop=(k == FCH - 1))
                            return ins
                        P.op("pe", f, reads=[self.b_wb[iw]] + self.b_abc[q * 11:(q + 1) * 11], writes=[self.b_py[pb]])
                elif kind == "ffn":
                    self.mm(self.py[pb], self.b_py[pb], self.wb[iw], self.b_wb[iw], FCH, ab, 0, w, self.b_abc)
                else:
                    self.mm(self.py[pb], self.b_py[pb], self.wa[iw], self.b_wa[iw], NCH, hb, c0, c1, [self.b_hbk[bi]])
                py = self.py[pb]
                bx = self.b_xn2[s][dc]
                if scaled:
                    first = True
                    for si, a, b in self.segparts(c0, c1):
                        kw = dict(writes=[bx]) if first else dict(cowrites=[bx])
                        first = False
                        P.op("dve", lambda e, si=si, a=a, b=b, xn=xn, xr=xr, py=py, dc=dc: e.scalar_tensor_tensor(
                            out=xn[:, dc, a:b], in0=py[:, a:b], scalar=self.vc(vgate[si], dc), in1=xr[:, a:b],
                            op0=ALU.mult, op1=ALU.add),
                            reads=[self.b_py[pb], self.b_vecs, self.b_xr[xi]], **kw)
                else:
                    ti = self.zt_i
                    self.zt_i ^= 1
                    tmp = self.tmp[ti]
                    for si, a, b in self.segparts(c0, c1):
                        P.op("act", lambda e, si=si, a=a, b=b, tmp=tmp, py=py, dc=dc: e.activation(
                            out=tmp[:, a:b], in_=py[:, a:b], func=AF.Copy, scale=self.vc(vgate[si], dc)),
                            reads=[self.b_py[pb], self.b_vecs], cowrites=[self.b_tmp[ti]])
                    P.op("dve", lambda e, xn=xn, xr=xr, tmp=tmp, dc=dc, w=w: e.scalar_tensor_tensor(
                        out=xn[:, dc, :w], in0=xr[:, :w], scalar=ALPHA, in1=tmp[:, :w], op0=ALU.mult, op1=ALU.add),
                        reads=[self.b_xr[xi], self.b_tmp[ti]], writes=[bx])
                    self.b_tmp[ti].w = []
                xr_load(kpos + 3)
                kpos += 1
                zi = self.zb_i
                self.zb_i ^= 1
                zb, zq = self.zb[zi], self.zq[zi]
                P.op("act", lambda e, xn=xn, zb=zb, dc=dc, w=w: e.activation(out=zb[:, :w], in_=xn[:, dc, :w], func=AF.Copy),
                     reads=[bx], writes=[self.b_zb[zi]])
                P.op("dve", lambda e, xn=xn, zq=zq, dc=dc, w=w: e.tensor_tensor(out=zq[:, :w], in0=xn[:, dc, :w],
                                                                               in1=xn[:, dc, :w], op=ALU.mult),
                     reads=[bx], writes=[self.b_zq[zi]])

                def stats(e, zb=zb, zq=zq, dc=dc, w=w):
                    e.matmul(s1[:, :w], lhsT=ones[:], rhs=zb[:, :w], start=(dc == 0), stop=(dc == NCH - 1))
                    return e.matmul(s2[:, :w], lhsT=ones[:], rhs=zq[:, :w], start=(dc == 0), stop=(dc == NCH - 1))
                this_stats = (stats, [self.b_zb[zi], self.b_zq[zi], self.b_ones])
                if stats_prev is not None:
                    P.op("pe", stats_prev[0], reads=stats_prev[1], cowrites=[self.b_s1, self.b_s2])
                stats_prev = this_stats
                if pending:
                    pending.pop(0)()
                if extra and extra.get(bi):
                    ex = extra[bi]
                    for _ in range(-(-len(ex) // (NCH - dc))):
                        ex.pop(0)()
            P.op("pe", stats_prev[0], reads=stats_prev[1], cowrites=[self.b_s1, self.b_s2])
            while pending:
                pending.pop(0)()
            P.op("dve", lambda e, w=w: e.tensor_scalar_mul(out=mu[:, :w], in0=s1[:, :w], scalar1=1.0 / D),
                 reads=[self.b_s1], writes=[self.b_mu])
            P.op("dve", lambda e, w=w: e.tensor_tensor(out=msq[:, :w], in0=mu[:, :w], in1=mu[:, :w], op=ALU.mult),
                 reads=[self.b_mu], writes=[self.b_msq])
            P.op("dve", lambda e, w=w: e.scalar_tensor_tensor(out=rstd[:, :w], in0=s2[:, :w], scalar=1.0 / D, in1=msq[:, :w],
                                                              op0=ALU.mult, op1=ALU.subtract),
                 reads=[self.b_s2, self.b_msq], writes=[self.b_rstd])
            P.op("dve", lambda e, w=w: e.tensor_scalar_add(out=rstd[:, :w], in0=rstd[:, :w], scalar1=LN_EPS),
                 reads=[self.b_rstd], writes=[self.b_rstd])
            P.op("act", lambda e, w=w: e.activation(out=rstd[:, :w], in_=rstd[:, :w], func=AF.Sqrt),
                 reads=[self.b_rstd], writes=[self.b_rstd])
            P.op("dve", lambda e, w=w: e.reciprocal(out=rstd[:, :w], in_=rstd[:, :w]),
                 reads=[self.b_rstd], writes=[self.b_rstd])
            self.b_s1.w, self.b_s2.w = [], []
            for c in range(NCH):
                pending.append(lambda bi=bi, c0=c0, c1=c1, s=s, c=c: ln_apply(bi, c0, c1, s, c))
        nb = len(self.blocks)
        if reload_hb:
            self.load_hb(only=range(nb - 1))
        while pending:
            pending.pop(0)()
        if reload_hb:
            self.load_hb(only=[nb - 1])
        if out_ap is None:
            self.xs_cur ^= 1


def tile_w(W, gcols):
    K, N = W.shape
    return np.ascontiguousarray(W.reshape(K // 128, 128, N // gcols, gcols).transpose(2, 1, 0, 3))


def dram_in(nc, name, shape, dt=F32):
    return nc.dram_tensor(name, list(shape), dt, kind="ExternalInput").ap()


def dram_out(nc, name, shape, dt=F32):
    return nc.dram_tensor(name, list(shape), dt, kind="ExternalOutput").ap()


def ffn_w_in(nc, sfx):
    return (dram_in(nc, "wgu" + sfx, [FCH, 128, NCH, 256]), dram_in(nc, "wd" + sfx, [NCH, 128, FCH, 128]))


MODC = 2 * 9 * D // NCORE


def build_L0():
    nc = bass.Bass("TRN2", target_bir_lowering=False)
    cv_ap = dram_in(nc, "cv", [128, NCH, 3])
    w_ap = dram_in(nc, "w", [D, MODC])
    b_ap = dram_in(nc, "b", [3, MODC])
    o_ap = dram_out(nc, "mod", [3, MODC])
    P = Prog(nc)
    cv = P.sb("cv", [128, NCH, 3], F32)
    cb = P.sb("cb", [128, NCH, 3], BF16)
    bt = P.sb("bt", [3, MODC], F32)
    ot = P.sb("ot", [3, MODC], F32)
    wt = [P.sb(f"w{i}", [128, NCH, 512], BF16) for i in range(2)]
    ps = [P.ps(f"ps{i}", [128, 512]) for i in range(2)]
    b_cv, b_cb, b_bt, b_ot = Buf(), Buf(), Buf(), Buf()
    b_w, b_ps = [Buf(), Buf()], [Buf(), Buf()]
    P.op("sp", lambda e: e.dma_start(out=cv[:], in_=cv_ap), writes=[b_cv], dma="cv")
    P.op("sp", lambda e: e.dma_start(out=bt[:], in_=b_ap), writes=[b_bt], dma="bt")
    P.op("act", lambda e: e.activation(out=cb[:], in_=cv[:], func=AF.Silu), reads=[b_cv], writes=[b_cb])
    wv = w_ap.rearrange("(k p) f -> p k f", p=128)
    for t in range(MODC // 512):
        i = t % 2
        P.op("pool", lambda e, t=t, i=i: e.dma_start(out=wt[i][:], in_=wv[:, :, t * 512:(t + 1) * 512]),
             writes=[b_w[i]], dma=f"w{i}")

        def f(e, i=i):
            for k in range(NCH):
                ins = e.matmul(ps[i][0:3, :], lhsT=cb[:, k, :], rhs=wt[i][:, k, :], start=(k == 0), stop=(k == NCH - 1))
            return ins
        P.op("pe", f, reads=[b_cb, b_w[i]], writes=[b_ps[i]])
        P.op("dve", lambda e, t=t, i=i: e.tensor_tensor(out=ot[:, t * 512:(t + 1) * 512], in0=ps[i][0:3, :],
                                                        in1=bt[:, t * 512:(t + 1) * 512], op=ALU.add),
             reads=[b_ps[i], b_bt], writes=[b_ot])
    P.op("sp", lambda e: e.dma_start(out=o_ap, in_=ot[:]), reads=[b_ot], writes=[Buf()], dma="st")
    P.finish()
    P.emit()
    return nc


def simple_proj(R, W, ng, dst, b_o):
    P = R.P
    nxt = R.load_wa(W, 0)
    for g in range(ng):
        iw = nxt
        if g + 1 < ng:
            nxt = R.load_wa(W, g + 1)
        for bi, (c0, c1) in enumerate(R.blocks):
            w = c1 - c0
            pb = R.pp_i
            R.pp_i ^= 1
            R.mm(R.pg[pb], R.b_pg[pb], R.wa[iw], R.b_wa[iw], NCH, R.hb, c0, c1, [R.b_hbk[bi]])
            i = R.zt_i
            R.zt_i ^= 1
            zt, pg = R.zt[i], R.pg[pb]
            P.op("act", lambda e, pg=pg, zt=zt, w=w: e.activation(out=zt[:, :w], in_=pg[:, :w], func=AF.Copy),
                 reads=[R.b_pg[pb]], writes=[R.b_zt[i]])
            R.store(dst, b_o, g * 128, c0, c1, zt, R.b_zt[i])


def build_LA(T, segs):
    nc = bass.Bass("TRN2", target_bir_lowering=False)
    nseg = len(segs)
    nvec = 5 * nseg + 2
    x_ap = dram_in(nc, "xT", [D, T])
    vecs_ap = dram_in(nc, "vecs", [128, nvec, NCH])
    wgu, wd = ffn_w_in(nc, "")
    w_gb = dram_in(nc, "w_gb", [8, 128, NCH, 128])
    w_gc = dram_in(nc, "w_gc", [8, 128, NCH, 128])
    w_xi = dram_in(nc, "w_xi", [8, 128, NCH, 128])
    w_uf = dram_in(nc, "w_uf", [8, 128, NCH, 128])
    xn1 = dram_out(nc, "xn1", [D, T])
    gb = dram_out(nc, "gb", [1024, T])
    vv = dram_out(nc, "vv", [1024, T])
    uf = dram_out(nc, "uf", [1024, T])
    P = Prog(nc)
    R = RL(P, T, segs, vecs_ap, nvec)
    V = lambda s, k: 5 * s + k
    SL = lambda k: [V(s, k) for s in range(nseg)]
    VG, VB = 5 * nseg, 5 * nseg + 1
    for s in range(nseg):
        R.derive(V(s, 1), add=1.0)
        R.derive(V(s, 2), mul=0.5)
        R.derive(V(s, 4), add=1.0)
        R.derive(V(s, 4), mul=1.0 / ALPHA)
    R.derive(VG, mul=ALPHA)
    R.derive(VB, mul=ALPHA)
    b_x = Buf()
    b_o = Buf()
    R.first_prologue(x_ap, b_x, SL(0), SL(1))
    R.phase1(wgu)
    R.phase2("ffn", wd, ("ap", x_ap, b_x, False), SL(2), ln=(VG, VB), mod=(SL(3), SL(4)), out_ap=xn1, reload_hb=True)
    simple_proj(R, w_gb, 8, gb, b_o)
    ld = lambda jj: (R.load_wa(w_gc, jj), R.load_wa(w_xi, jj))
    nxt = ld(0)
    for jj in range(8):
        ic, ix = nxt
        if jj + 1 < 8:
            nxt = ld(jj + 1)
        for bi, (c0, c1) in enumerate(R.blocks):
            w = c1 - c0
            pb = R.pp_i
            R.pp_i ^= 1
            R.mm(R.pg[pb], R.b_pg[pb], R.wa[ic], R.b_wa[ic], NCH, R.hb, c0, c1, [R.b_hbk[bi]])
            R.mm(R.pu[pb], R.b_pu[pb], R.wa[ix], R.b_wa[ix], NCH, R.hb, c0, c1, [R.b_hbk[bi]])
            i = R.zt_i
            R.zt_i ^= 1
            zt, sg, pg, pu = R.zt[i], R.sg[pb], R.pg[pb], R.pu[pb]
            P.op("act", lambda e, pg=pg, sg=sg, w=w: e.activation(out=sg[:, :w], in_=pg[:, :w], func=AF.Copy),
                 reads=[R.b_pg[pb]], writes=[R.b_sg[pb]])
            P.op("dve", lambda e, pu=pu, sg=sg, zt=zt, w=w: e.tensor_tensor(out=zt[:, :w], in0=pu[:, :w], in1=sg[:, :w],
                                                                           op=ALU.mult),
                 reads=[R.b_pu[pb], R.b_sg[pb]], writes=[R.b_zt[i]])
            R.store(vv, b_o, jj * 128, c0, c1, zt, R.b_zt[i])
    simple_proj(R, w_uf, 8, uf, b_o)
    P.finish()
    P.emit()
    return nc


def build_LF():
    nc = bass.Bass("TRN2", target_bir_lowering=False)
    xl = dram_in(nc, "xl", [2, SEQ, 128])
    xc = dram_in(nc, "xc", [2, 128, CTX])
    ftw_ap = dram_in(nc, "ftw", [128, 64, 256])
    fc_ap = dram_in(nc, "fc", [128, 2, 256])
    c64_ap = dram_in(nc, "c64", [64, 2, 64])
    c256_ap = dram_in(nc, "c256", [128, 4, 256])
    yl = dram_out(nc, "yl", [2, 128, SEQ])
    yc = dram_out(nc, "yc", [2, 128, CTX])
    P = Prog(nc)
    ftw = P.sb("ftw", [128, 64, 256], BF16)
    fc = P.sb("fc", [128, 2, 256], BF16)
    c64 = P.sb("c64", [64, 2, 64], BF16)
    c256 = P.sb("c256", [128, 4, 256], BF16)
    XA = P.sb("XA", [128, 64, 128], BF16)
    D1 = P.sb("D1", [128, 64, 256], BF16)
    D2 = P.sb("D2", [64, 128, 256], BF16)
    Y = P.sb("Y", [128, SEQ], F32)
    XC = P.sb("XC", [128, CTX], BF16)
    DA = P.sb("DA", [128, 2, 256], BF16)
    YC = P.sb("YC", [128, CTX], F32)
    pp = [P.ps(f"pp{i}", [128, 512]) for i in range(4)]
    b_pp = [Buf() for _ in range(4)]
    b_t, b_XA, b_D1, b_D2, b_Y, b_XC, b_DA, b_YC = (Buf() for _ in range(8))
    for t, ap, k in ((ftw, ftw_ap, "t0"), (fc, fc_ap, "t1"), (c64, c64_ap, "t2"), (c256, c256_ap, "t3")):
        P.op("pool", lambda e, t=t, ap=ap: e.dma_start(out=t[:], in_=ap), cowrites=[b_t], dma=k)
    pi = [0]

    def nextp():
        pi[0] = (pi[0] + 1) % 4
        return pi[0]
    ev = [0]

    def evac(out_ap, in_ap, rd, wr, co=False):
        ev[0] ^= 1
        kw = dict(cowrites=[wr]) if co else dict(writes=[wr])
        if ev[0]:
            P.op("act", lambda e: e.activation(out=out_ap, in_=in_ap, func=AF.Copy), reads=[rd], **kw)
        else:
            P.op("dve", lambda e: e.tensor_copy(out=out_ap, in_=in_ap), reads=[rd], **kw)

    for u in range(2):
        src = xl[u].rearrange("(a r) c -> a r c", r=64)
        P.op("pool", lambda e, src=src: e.dma_start(out=XA[:], in_=src), writes=[b_XA], dma="xa")
        P.op("pool", lambda e, u=u: e.dma_start(out=XC[:], in_=xc[u]), writes=[b_XC], dma="xc")
        for b0 in range(0, 64, 2):
            i = nextp()

            def f(e, b0=b0, i=i):
                for q in range(2):
                    ins = e.matmul(pp[i][:, q * 256:(q + 1) * 256], lhsT=XA[:, b0 + q, :], rhs=ftw[:, b0 + q, :],
                                   start=True, stop=True)
                return ins
            P.op("pe", f, reads=[b_XA, b_t], writes=[b_pp[i]])
            evac(D1[:, b0:b0 + 2, :], pp[i][:].rearrange("p (q n) -> p q n", q=2), b_pp[i], b_D1, co=True)
        for a0 in range(0, 128, 2):
            i = nextp()

            def f(e, a0=a0, i=i):
                for q in range(2):
                    e.matmul(pp[i][0:64, q * 256:(q + 1) * 256], lhsT=D1[:, :, a0 + q], rhs=fc[:, 0, :], start=True, stop=False)
                    ins = e.matmul(pp[i][0:64, q * 256:(q + 1) * 256], lhsT=D1[:, :, 128 + a0 + q], rhs=fc[:, 1, :],
                                   start=False, stop=True)
                return ins
            P.op("pe", f, reads=[b_D1, b_t], writes=[b_pp[i]])
            evac(D2[:, a0:a0 + 2, :], pp[i][0:64, :].rearrange("p (q n) -> p q n", q=2), b_pp[i], b_D2, co=True)
        Yv = Y[:].rearrange("p (b a) -> p a b", a=128)
        for a0 in range(0, 128, 8):
            i = nextp()

            def f(e, a0=a0, i=i):
                for q in range(8):
                    e.matmul(pp[i][:, q * 64:(q + 1) * 64], lhsT=D2[:, a0 + q, 0:128], rhs=c64[:, 0, :], start=True, stop=False)
                    ins = e.matmul(pp[i][:, q * 64:(q + 1) * 64], lhsT=D2[:, a0 + q, 128:256], rhs=c64[:, 1, :],
                                   start=False, stop=True)
                return ins
            P.op("pe", f, reads=[b_D2, b_t], writes=[b_pp[i]])
            evac(Yv[:, a0:a0 + 8, :], pp[i][:].rearrange("p (a b) -> p a b", a=8), b_pp[i], b_Y, co=True)
        P.op("sp", lambda e, u=u: e.dma_start(out=yl[u], in_=Y[:]), reads=[b_Y], writes=[Buf()], dma="sty")
        i = nextp()

        def f(e, i=i):
            for j in range(2):
                ins = e.matmul(pp[i][:, j * 256:(j + 1) * 256], lhsT=XC[:, j * 128:(j + 1) * 128], rhs=fc[:, 0, :],
                               start=True, stop=True)
            return ins
        P.op("pe", f, reads=[b_XC, b_t], writes=[b_pp[i]])
        evac(DA[:], pp[i][:].rearrange("p (j n) -> p j n", j=2), b_pp[i], b_DA)
        i = nextp()

        def f(e, i=i):
            n = 0
            for j in range(2):
                for ri in range(2):
                    ins = e.matmul(pp[i][:, 0:256], lhsT=DA[:, j, ri * 128:(ri + 1) * 128], rhs=c256[:, 2 * ri + j, :],
                                   start=(n == 0), stop=(n == 3))
                    n += 1
            return ins
        P.op("pe", f, reads=[b_DA, b_t], writes=[b_pp[i]])
        evac(YC[:], pp[i][:, 0:256], b_pp[i], b_YC)
        P.op("sp", lambda e, u=u: e.dma_start(out=yc[u], in_=YC[:]), reads=[b_YC], writes=[Buf()], dma="styc")
    P.finish()
    P.emit()
    return nc


def fft_tables():
    a = np.arange(128)[:, None, None]
    b = np.arange(64)[None, :, None]
    ap = np.arange(128)[None, None, :]
    th = 2 * np.pi * (a * ap / 128.0 + b * ap / 8192.0)
    ftw = np.concatenate([np.cos(th), -np.sin(th)], axis=-1) / np.sqrt(128.0)
    c = np.arange(128)[:, None]
    cp = np.arange(128)[None, :]
    th = 2 * np.pi * c * cp / 128.0
    cr, ci = np.cos(th) / np.sqrt(128.0), -np.sin(th) / np.sqrt(128.0)
    fc = np.stack([np.concatenate([cr, ci], 1), np.concatenate([-ci, cr], 1)], axis=1)
    bb = np.arange(64)[:, None]
    bp = np.arange(64)[None, :]
    th = 2 * np.pi * bb * bp / 64.0
    c64 = np.stack([np.cos(th), np.sin(th)], axis=1) / 8.0
    l = np.arange(256)[:, None]
    lp = np.arange(256)[None, :]
    th = 2 * np.pi * l * lp / 256.0
    C, S = np.cos(th) / 16.0, np.sin(th) / 16.0
    c256 = np.stack([C[0:128], C[128:256], S[0:128], S[128:256]], axis=1)
    f = lambda x: np.ascontiguousarray(x, dtype=np.float32)
    return f(ftw), f(fc), f(c64), f(c256)


NBLK = SEQ // 128


def build_LC(nblk=NBLK):
    nc = bass.Bass("TRN2", target_bir_lowering=False)
    S = nblk * 128
    NS = 6
    qt_ap = dram_in(nc, "qt", [128, 4, S])
    kt_ap = dram_in(nc, "kt", [128, S + CTX])
    v_ap = dram_in(nc, "v", [128, nblk + 2, 64])
    mask_ap = dram_in(nc, "mask", [128, 384])
    sink_ap = dram_in(nc, "sink", [128, 8])
    id_ap = dram_in(nc, "ident", [128, 128])
    o_ap = dram_out(nc, "o", [S, 512])
    P = Prog(nc)
    QT = P.sb("QT", [128, 4, S], BF16)
    KT = P.sb("KT", [128, S + CTX], BF16)
    V = P.sb("V", [128, nblk + 2, 64], BF16)
    mask = P.sb("mask", [128, 384], F32)
    sink = P.sb("sink", [128, 8], F32)
    sink8 = P.sb("sink8", [128, 8], F32)
    ident = P.sb("ident", [128, 128], BF16)
    sc = [P.sb(f"sc{i}", [128, 648], F32) for i in range(NS)]
    pb = [P.sb(f"pb{i}", [128, 648], BF16) for i in range(NS)]
    pTs = [P.sb(f"pTs{i}", [128, 5, 128], BF16) for i in range(NS)]
    sm = [P.sb(f"sm{i}", [128, 4], F32) for i in range(NS)]
    ot = [P.sb(f"ot{i}", [128, 512], F32) for i in range(2)]
    scA = [P.ps(f"scA{i}", [128, 512]) for i in range(2)]
    scB = [P.ps(f"scB{i}", [128, 512]) for i in range(2)]
    pTt = [P.ps(f"pTt{i}", [128, 8, 128], BF16) for i in range(2)]
    ops = [P.ps(f"ops{i}", [128, 512]) for i in range(2)]
    BL = lambda n: [Buf() for _ in range(n)]
    b_c, b_s8 = Buf(), Buf()
    b_sc, b_pb, b_pTs, b_sm = BL(NS), BL(NS), BL(NS), BL(NS)
    b_ot, b_scA, b_scB, b_pTt, b_ops = BL(2), BL(2), BL(2), BL(2), BL(2)
    for t, ap, k in ((QT, qt_ap, "c0"), (KT, kt_ap, "c1"), (V, v_ap, "c2"), (ident, id_ap, "c3")):
        P.op("pool", lambda e, t=t, ap=ap: e.dma_start(out=t[:], in_=ap), cowrites=[b_c], dma=k)
    for t, ap, k in ((mask, mask_ap, "c4"), (sink, sink_ap, "c5")):
        P.op("sp", lambda e, t=t, ap=ap: e.dma_start(out=t[:], in_=ap), cowrites=[b_c], dma=k)
    P.op("dve", lambda e: e.tensor_scalar_mul(out=sink8[:], in0=sink[:], scalar1=8.0), reads=[b_c], writes=[b_s8])
    units = [(i, r) for i in range(nblk) for r in range(8)]
    N = len(units)

    def geo(n):
        i, r = units[n]
        kb0, kb1 = max(i - 1, 0), min(i + 1, nblk - 1)
        nloc = kb1 - kb0 + 1
        nk = nloc * 128
        mo = (kb0 - (i - 1)) * 128
        return i, r, kb0, nloc, nk, mo, nk + 256

    def S1(n):
        i, r, kb0, nloc, nk, mo, L = geo(n)
        s, p = n % NS, n % 2
        hh, j = r // 4, r % 4
        p0, p1 = hh * 64, hh * 64 + 64
        q = QT[p0:p1, j, i * 128:(i + 1) * 128]

        def f(e):
            e.matmul(scA[p][:, 0:nk], lhsT=q, rhs=KT[p0:p1, kb0 * 128:kb0 * 128 + nk], start=True, stop=True)
            return e.matmul(scB[p][:, 0:256], lhsT=q, rhs=KT[p0:p1, S:S + CTX], start=True, stop=True)
        P.op("pe", f, reads=[b_c], writes=[b_scA[p], b_scB[p]])
        P.op("dve", lambda e: e.tensor_tensor(out=sc[s][:, 0:nk], in0=scA[p][:, 0:nk], in1=mask[:, mo:mo + nk], op=ALU.add),
             reads=[b_scA[p], b_c], writes=[b_sc[s]])
        P.op("act", lambda e: e.activation(out=sc[s][:, nk:L], in_=scB[p][:, 0:256], func=AF.Copy),
             reads=[b_scB[p]], cowrites=[b_sc[s]])
        P.op("pool", lambda e: e.tensor_copy(out=sc[s][:, L:L + 1], in_=sink8[:, r:r + 1]),
             reads=[b_s8], cowrites=[b_sc[s]])
        m = sm[s]
        P.op("dve", lambda e: e.reduce_max(out=m[:, 0:1], in_=sc[s][:, 0:L + 1], axis=AX.X),
             reads=[b_sc[s]], writes=[b_sm[s]])
        P.op("dve", lambda e: e.tensor_scalar_mul(out=m[:, 1:2], in0=m[:, 0:1], scalar1=-0.125),
             reads=[b_sm[s]], writes=[b_sm[s]])

    def S2(n):
        i, r, kb0, nloc, nk, mo, L = geo(n)
        s, p = n % NS, n % 2
        nt = nloc + 2
        m = sm[s]
        P.op("act", lambda e: e.activation(out=pb[s][:, 0:L + 1], in_=sc[s][:, 0:L + 1], func=AF.Exp,
                                           scale=0.125, bias=m[:, 1:2], accum_out=m[:, 2:3]),
             reads=[b_sc[s], b_sm[s]], writes=[b_pb[s], b_sm[s]])
        P.op("dve", lambda e: e.reciprocal(out=m[:, 3:4], in_=m[:, 2:3]), reads=[b_sm[s]], writes=[b_sm[s]])

    def S2b(n):
        i, r, kb0, nloc, nk, mo, L = geo(n)
        s, p = n % NS, n % 2
        nt = nloc + 2

        def f(e):
            for t in range(nt):
                ins = e.transpose(out=pTt[p][:, t, :], in_=pb[s][:, t * 128:(t + 1) * 128], identity=ident[:])
            return ins
        P.op("pe", f, reads=[b_pb[s], b_c], writes=[b_pTt[p]])
        if n % 2:
            P.op("act", lambda e: e.activation(out=pTs[s][:, 0:nt, :], in_=pTt[p][:, 0:nt, :], func=AF.Copy),
                 reads=[b_pTt[p]], writes=[b_pTs[s]])
        else:
            P.op("dve", lambda e: e.tensor_copy(out=pTs[s][:, 0:nt, :], in_=pTt[p][:, 0:nt, :]),
                 reads=[b_pTt[p]], writes=[b_pTs[s]])

    def S3(n):
        i, r, kb0, nloc, nk, mo, L = geo(n)
        s, p = n % NS, n % 2
        nt = nloc + 2
        oi = i % 2
        m = sm[s]

        def f(e):
            for t in range(nt):
                vb = kb0 + t if t < nloc else nblk + (t - nloc)
                ins = e.matmul(ops[p][:, 0:64], lhsT=pTs[s][:, t, :], rhs=V[:, vb, :], start=(t == 0), stop=(t == nt - 1))
            return ins
        P.op("pe", f, reads=[b_pTs[s], b_c], writes=[b_ops[p]])
        P.op("act", lambda e: e.activation(out=ot[oi][:, r * 64:(r + 1) * 64], in_=ops[p][:, 0:64], func=AF.Copy,
                                           scale=m[:, 3:4]),
             reads=[b_ops[p], b_sm[s]], cowrites=[b_ot[oi]])
        if r == 7:
            P.op("sp", lambda e: e.dma_start(out=o_ap[i * 128:(i + 1) * 128, :], in_=ot[oi][:]),
                 reads=[b_ot[oi]], writes=[Buf()], dma=f"so{oi}")
            b_ot[oi].w = []

    for k in range(N + 3):
        if k < N:
            S1(k)
        if 0 <= k - 1 < N:
            S2(k - 1)
        if 0 <= k - 2 < N:
            S2b(k - 2)
        if 0 <= k - 3 < N:
            S3(k - 3)
    P.finish()
    P.emit()
    return nc


def attn_mask():
    a = np.arange(128)[:, None]
    j = np.arange(384)[None, :]
    return np.where(np.abs(j - 128 - a) <= 128, 0.0, -1e30).astype(np.float32)


def build_LB(T, segs):
    nc = bass.Bass("TRN2", target_bir_lowering=False)
    nseg = len(segs)
    nvec = 9 * nseg + 11
    xn1 = dram_in(nc, "xn1", [D, T])
    gb = dram_in(nc, "gb", [1024, T])
    vp = dram_in(nc, "vp", [1024, T])
    vv = dram_in(nc, "vv", [1024, T])
    vn = dram_in(nc, "vn", [1024, T])
    yb = dram_in(nc, "yb", [1024, T])
    vecs_ap = dram_in(nc, "vecs", [128, nvec, NCH])
    wout = dram_in(nc, "wout", [8, 128, NCH, 256])
    wgu2, wd2 = ffn_w_in(nc, "2")
    wgu3, wd3 = ffn_w_in(nc, "3")
    wqk = dram_in(nc, "wqk", [9, 128, NCH, 256])
    wv = dram_in(nc, "wv", [2, 128, NCH, 128])
    cos_ap = dram_in(nc, "cosT", [128, T])
    sin_ap = dram_in(nc, "sinT", [128, T])
    xn4 = dram_out(nc, "xn4", [D, T])
    qr = dram_out(nc, "qr", [D, T])
    kr = dram_out(nc, "kr", [256, T])
    vo = dram_out(nc, "vo", [256, T])
    P = Prog(nc)
    R = RL(P, T, segs, vecs_ap, nvec)
    V = lambda s, k: 9 * s + k
    L0 = 9 * nseg
    for s in range(nseg):
        R.derive(V(s, 3), mul=0.5)
        R.derive(V(s, 6), mul=0.5)
        for k in (2, 5, 8):
            R.derive(V(s, k), add=1.0)
            R.derive(V(s, k), mul=1.0 / ALPHA)
    for k in range(2, 8):
        R.derive(L0 + k, mul=ALPHA)
    SL = lambda k: [V(s, k) for s in range(nseg)]
    b_in = Buf()
    b_o = Buf()

    def mix_fill(bi, c0, c1, w, j):
        tl = [R.sg[0], R.sg[1], R.zt[0], R.zt[1], R.tmp[0]]
        tb = [R.b_sg[0], R.b_sg[1], R.b_zt[0], R.b_zt[1], R.b_tmp[0]]
        if True:
            for k, src in enumerate((gb, vp, vv, vn, yb)):
                P.op("sp", lambda e, k=k, src=src, j=j: e.dma_start(out=tl[k][:, :w], in_=src[j * 128:(j + 1) * 128, c0:c1]),
                     writes=[tb[k]], dma=f"m{k}")
            P.op("dve", lambda e, j=j: e.tensor_scalar_mul(out=tl[1][:, :w], in0=tl[1][:, :w], scalar1=R.vc(L0 + 8, j)),
                 reads=[R.b_vecs], writes=[tb[1]])
            P.op("dve", lambda e, j=j: e.scalar_tensor_tensor(out=tl[1][:, :w], in0=tl[2][:, :w], scalar=R.vc(L0 + 9, j),
                                                              in1=tl[1][:, :w], op0=ALU.mult, op1=ALU.add),
                 reads=[R.b_vecs, tb[2]], writes=[tb[1]])
            P.op("dve", lambda e, j=j: e.scalar_tensor_tensor(out=tl[1][:, :w], in0=tl[3][:, :w], scalar=R.vc(L0 + 10, j),
                                                              in1=tl[1][:, :w], op0=ALU.mult, op1=ALU.add),
                 reads=[R.b_vecs, tb[3]], writes=[tb[1]])
            P.op("dve", lambda e, j=j: e.tensor_tensor(out=R.hb[:, j, c0:c1], in0=tl[1][:, :w], in1=tl[0][:, :w], op=ALU.mult),
                 reads=[tb[0], tb[1]], cowrites=R.whb(bi))
            P.op("act", lambda e, j=j: e.activation(out=R.hb[:, 8 + j, c0:c1], in_=tl[4][:, :w], func=AF.Copy),
                 reads=[tb[4]], cowrites=R.whb(bi))

    extra = {}
    for bi, (c0, c1) in enumerate(R.blocks):
        fl = [lambda bi=bi, c0=c0, c1=c1, j=j: mix_fill(bi, c0, c1, c1 - c0, j) for j in range(8)]
        if bi == 0:
            for f in fl:
                f()
        else:
            extra[bi - 1] = fl
    R.phase2("proj", wout, ("ap", xn1, b_in, True), SL(0), ln=(L0 + 2, L0 + 3), mod=(SL(1), SL(2)), reload_hb=True,
             extra=extra)
    R.phase1(wgu2)
    R.phase2("ffn", wd2, ("xs",), SL(3), ln=(L0 + 4, L0 + 5), mod=(SL(4), SL(5)), reload_hb=True)
    R.phase1(wgu3)
    R.phase2("ffn", wd3, ("xs",), SL(6), ln=(L0 + 6, L0 + 7), mod=(SL(7), SL(8)), out_ap=xn4, reload_hb=True)
    rt = R.xn2[0][:, :, :].rearrange("p c t -> p (c t)")
    cosT, sinT = rt[:, 0:T], rt[:, T:2 * T]
    P.op("sp", lambda e: e.dma_start(out=cosT, in_=cos_ap), writes=R.b_xn2[0], dma="r0")
    P.op("sp", lambda e: e.dma_start(out=sinT, in_=sin_ap), cowrites=R.b_xn2[0], dma="r1")
    b_rope = R.b_xn2[0][0]
    nxt = R.load_wa_pair(wqk, 0)
    for g in range(9):
        iA, iB = nxt
        if g + 1 < 9:
            nxt = R.load_wa_pair(wqk, g + 1)
        for bi, (c0, c1) in enumerate(R.blocks):
            w = c1 - c0
            pb = R.pp_i
            R.pp_i ^= 1
            pg, pu = R.pg[pb], R.pu[pb]
            R.mm(pg, R.b_pg[pb], R.wa[iA], R.b_wa[iA], NCH, R.hb, c0, c1, [R.b_hbk[bi]])
            R.mm(pu, R.b_pu[pb], R.wa[iB], R.b_wa[iB], NCH, R.hb, c0, c1, [R.b_hbk[bi]])
            dst, base = (qr, g * 256) if g < 8 else (kr, 0)
            for half, (pa, ba, pbb, bb, op) in enumerate(((pg, R.b_pg[pb], pu, R.b_pu[pb], ALU.subtract),
                                                          (pu, R.b_pu[pb], pg, R.b_pg[pb], ALU.add))):
                i = R.zt_i
                R.zt_i ^= 1
                zt, tmp, sg = R.zt[i], R.tmp[i], R.sg[i]
                P.op("dve", lambda e, sg=sg, pa=pa, w=w, c0=c0, c1=c1: e.tensor_tensor(
                    out=sg[:, :w], in0=pa[:, :w], in1=cosT[:, c0:c1], op=ALU.mult),
                    reads=[ba, b_rope], writes=[R.b_sg[i]])
                P.op("dve", lambda e, tmp=tmp, pbb=pbb, w=w, c0=c0, c1=c1: e.tensor_tensor(
                    out=tmp[:, :w], in0=pbb[:, :w], in1=sinT[:, c0:c1], op=ALU.mult),
                    reads=[bb, b_rope], writes=[R.b_tmp[i]])
                P.op("dve", lambda e, zt=zt, sg=sg, tmp=tmp, w=w, op=op: e.tensor_tensor(
                    out=zt[:, :w], in0=sg[:, :w], in1=tmp[:, :w], op=op),
                    reads=[R.b_sg[i], R.b_tmp[i]], writes=[R.b_zt[i]])
                R.store(dst, b_o, base + half * 128, c0, c1, zt, R.b_zt[i])
    nxt = R.load_wa(wv, 0)
    for g in range(2):
        iw = nxt
        if g + 1 < 2:
            nxt = R.load_wa(wv, g + 1)
        for bi, (c0, c1) in enumerate(R.blocks):
            w = c1 - c0
            pb = R.pp_i
            R.pp_i ^= 1
            pg = R.pg[pb]
            R.mm(pg, R.b_pg[pb], R.wa[iw], R.b_wa[iw], NCH, R.hb, c0, c1, [R.b_hbk[bi]])
            i = R.zt_i
            R.zt_i ^= 1
            zt = R.zt[i]
            P.op("act", lambda e, zt=zt, pg=pg, w=w: e.activation(out=zt[:, :w], in_=pg[:, :w], func=AF.Copy),
                 reads=[R.b_pg[pb]], writes=[R.b_zt[i]])
            R.store(vo, b_o, g * 128, c0, c1, zt, R.b_zt[i])
    P.finish()
    P.emit()
    return nc


def build_LD(T):
    nc = bass.Bass("TRN2", target_bir_lowering=False)
    segs = [(0, T)]
    nvec = 10
    xn4 = dram_in(nc, "xn4", [D, T])
    oT = dram_in(nc, "oT", [D, T])
    vecs_ap = dram_in(nc, "vecs", [128, nvec, NCH])
    wout = dram_in(nc, "wout", [8, 128, NCH, 256])
    wgu, wd = ffn_w_in(nc, "")
    out = dram_out(nc, "out", [D, T])
    P = Prog(nc)
    R = RL(P, T, segs, vecs_ap, nvec)
    R.derive(2, add=1.0)
    R.derive(2, mul=1.0 / ALPHA)
    R.derive(3, mul=0.5)
    R.derive(6, mul=ALPHA)
    R.derive(7, mul=ALPHA)
    b_in = Buf()
    for bi, (c0, c1) in enumerate(R.blocks):
        P.op("pool", lambda e, c0=c0, c1=c1: e.dma_start(out=R.hb[:, :, c0:c1], in_=chunked(oT)[:, :, c0:c1]),
             cowrites=R.whb(bi), dma="oin")
    R.phase2("proj", wout, ("ap", xn4, b_in, True), [0], ln=(6, 7), mod=([1], [2]), reload_hb=True)
    R.phase1(wgu)
    R.phase2("ffn", wd, ("xs",), [3], ln=(8, 9), mod=None, out_ap=out)
    P.finish()
    P.emit()
    return nc


def lay(v):
    v = np.asarray(v, dtype=np.float32)
    if v.shape[0] < D:
        v = np.concatenate([v, np.zeros(D - v.shape[0], np.float32)])
    return v.reshape(NCH, 128).T


X1 = np.concatenate([np.arange(0, 16), np.arange(32, 48)])
X2 = X1 + 16


def rope_tables(pos):
    nf = 16
    inv = np.power(10000.0, -np.arange(nf, dtype=np.float64) / nf)
    row = (pos // 64).astype(np.float64)
    col = (pos % 64).astype(np.float64)
    i = np.arange(32)
    axis, f = i // 16, i % 16
    ang = np.where(axis[:, None] == 0, row[None, :], col[None, :]) * inv[f][:, None]
    c, s = np.cos(ang), np.sin(ang)
    return np.tile(c, (4, 1)).astype(np.float32), np.tile(s, (4, 1)).astype(np.float32)


def qk_pair_cols(g):
    heads = range(4 * g, 4 * g + 4)
    return np.concatenate([h * 64 + X1 for h in heads] + [h * 64 + X2 for h in heads])


def unpair_rows(a):
    n = a.shape[0] // 256
    return a.reshape(n, 2, 4, 32, -1).transpose(0, 2, 1, 3, 4).reshape(n * 256, -1)


_CACHE = {}


def _prog(key, fn):
    if key not in _CACHE:
        _CACHE[key] = fn()
    return _CACHE[key]


def _run(nc, ins):
    res = run_bass_kernel_spmd(nc, ins, core_ids=list(range(NCORE)))
    return res.results


def kernel(x, c, ctx, c_ctx, w_mod, b_mod, ln_g, ln_b, ffn_w_gate, ffn_w_up, ffn_w_down,
           ab_w_in, ab_conv, ab_w_out, attn_w_in, attn_sink, attn_w_out):
    f32 = lambda a: np.ascontiguousarray(np.asarray(a), dtype=np.float32)
    x, c, ctx, c_ctx = f32(x), f32(c), f32(ctx), f32(c_ctx)
    w_mod, b_mod, ln_g, ln_b = f32(w_mod), f32(b_mod), f32(ln_g), f32(ln_b)
    ffn_w_gate, ffn_w_up, ffn_w_down = f32(ffn_w_gate), f32(ffn_w_up), f32(ffn_w_down)
    ab_w_in, ab_conv, ab_w_out = f32(ab_w_in), f32(ab_conv), f32(ab_w_out)
    attn_w_in, attn_sink, attn_w_out = f32(attn_w_in), f32(attn_sink), f32(attn_w_out)
    LT, CT = SEQ // 4, CTX // 4
    T = LT + CT
    segs = [(0, LT), (LT, T)]
    cores = [(r // 4, r % 4) for r in range(NCORE)]

    cv = np.ascontiguousarray(np.stack([lay(c[0]), lay(c[1]), lay(c_ctx)], axis=-1))
    wm = np.concatenate([w_mod[0], w_mod[1]], axis=1)
    bm = b_mod.reshape(-1)
    ins = []
    for r in range(NCORE):
        sl = slice(r * MODC, (r + 1) * MODC)
        ins.append({"cv": cv, "w": np.ascontiguousarray(wm[:, sl]),
                    "b": np.ascontiguousarray(np.broadcast_to(bm[sl], (3, MODC)))})
    res = _run(_prog("L0", build_L0), ins)
    del wm
    mod = np.concatenate([res[r]["mod"] for r in range(NCORE)], axis=1).reshape(3, 2, 9, D)

    def mv(b, s, layer, k):
        return lay(mod[b if s == 0 else 2, layer, k])

    _ffw = {}

    def ffw(l, i, sfx):
        if (l, i) not in _ffw:
            _ffw[(l, i)] = (np.concatenate([tile_w(ffn_w_gate[l, i], 128), tile_w(ffn_w_up[l, i], 128)], axis=-1),
                            tile_w(ffn_w_down[l, i], 128))
        a, c_ = _ffw[(l, i)]
        return {"wgu" + sfx: a, "wd" + sfx: c_}

    w_gb, w_gc = tile_w(ab_w_in[0][:, 0:1024], 128), tile_w(ab_w_in[0][:, 1024:2048], 128)
    w_xi, w_uf = tile_w(ab_w_in[0][:, 2048:3072], 128), tile_w(ab_w_in[0][:, 3072:4096], 128)

    ins = []
    for b, q in cores:
        xT = np.concatenate([x[b, q * LT:(q + 1) * LT].T, ctx[b, q * CT:(q + 1) * CT].T], axis=1)
        vecs = [mv(b, s, 0, k) for s in range(2) for k in range(5)] + [lay(ln_g[0, 0]), lay(ln_b[0, 0])]
        ins.append({"xT": np.ascontiguousarray(xT), "vecs": np.ascontiguousarray(np.stack(vecs, axis=1)),
                    **ffw(0, 0, ""), "w_gb": w_gb, "w_gc": w_gc, "w_xi": w_xi, "w_uf": w_uf})
    resA = _run(_prog("LA", lambda: build_LA(T, segs)), ins)

    def gather(res, name, nrow):
        lat = np.empty((2, nrow, SEQ), np.float32)
        cx = np.empty((2, nrow, CTX), np.float32)
        for r, (b, q) in enumerate(cores):
            a = res[r][name]
            lat[b, :, q * LT:(q + 1) * LT] = a[:, :LT]
            cx[b, :, q * CT:(q + 1) * CT] = a[:, LT:]
        return lat, cx

    UFl, UFc = gather(resA, "uf", 1024)
    VVl, VVc = gather(resA, "vv", 1024)

    ftw, fc, c64, c256 = fft_tables()
    ins = []
    for b, q in cores:
        gs = [2 * q, 2 * q + 1]
        xl = np.stack([UFl[b, g * 128:(g + 1) * 128].T for g in gs])
        xc = np.stack([UFc[b, g * 128:(g + 1) * 128] for g in gs])
        ins.append({"xl": np.ascontiguousarray(xl), "xc": np.ascontiguousarray(xc),
                    "ftw": ftw, "fc": fc, "c64": c64, "c256": c256})
    resF = _run(_prog("LF", build_LF), ins)
    YBl = np.empty((2, 1024, SEQ), np.float32)
    YBc = np.empty((2, 1024, CTX), np.float32)
    for r, (b, q) in enumerate(cores):
        for u in range(2):
            g = 2 * q + u
            YBl[b, g * 128:(g + 1) * 128] = resF[r]["yl"][u]
            YBc[b, g * 128:(g + 1) * 128] = resF[r]["yc"][u]
    del UFl, UFc

    def shift(a, k):
        o = np.zeros_like(a)
        if k > 0:
            o[..., k:] = a[..., :-k]
        else:
            o[..., :k] = a[..., -k:]
        return o

    VPl, VPc, VNl, VNc = shift(VVl, 1), shift(VVc, 1), shift(VVl, -1), shift(VVc, -1)

    def cols(lat, cx, b, q):
        return np.ascontiguousarray(np.concatenate([lat[b][:, q * LT:(q + 1) * LT], cx[b][:, q * CT:(q + 1) * CT]], axis=1))

    wqk_t = np.concatenate([tile_w(np.ascontiguousarray(attn_w_in[0][:, qk_pair_cols(g)]), 256) for g in range(9)], axis=0)
    wv_t = tile_w(np.ascontiguousarray(attn_w_in[0][:, 2304:2560]), 128)
    wout_t = tile_w(ab_w_out[0], 256)
    _ffw.pop((0, 0), None)
    ins = []
    for r, (b, q) in enumerate(cores):
        vecs = []
        for s in range(2):
            vecs += [mv(b, s, 0, 5), mv(b, s, 0, 6), mv(b, s, 0, 7), mv(b, s, 0, 8),
                     mv(b, s, 1, 0), mv(b, s, 1, 1), mv(b, s, 1, 2), mv(b, s, 1, 3), mv(b, s, 1, 4)]
        vecs += [lay(ln_g[0, 0]), lay(ln_b[0, 0]), lay(ln_g[0, 1]), lay(ln_b[0, 1]), lay(ln_g[0, 2]), lay(ln_b[0, 2]),
                 lay(ln_g[1, 0]), lay(ln_b[1, 0]), lay(ab_conv[0, 0]), lay(ab_conv[0, 1]), lay(ab_conv[0, 2])]
        cl, sl_ = rope_tables(np.arange(q * LT, (q + 1) * LT))
        cosT = np.concatenate([cl, np.ones((128, CT), np.float32)], axis=1)
        sinT = np.concatenate([sl_, np.zeros((128, CT), np.float32)], axis=1)
        ins.append({"xn1": resA[r]["xn1"], "gb": resA[r]["gb"], "vp": cols(VPl, VPc, b, q), "vv": resA[r]["vv"],
                    "vn": cols(VNl, VNc, b, q), "yb": cols(YBl, YBc, b, q),
                    "vecs": np.ascontiguousarray(np.stack(vecs, axis=1)), "wout": wout_t,
                    **ffw(0, 1, "2"), **ffw(1, 0, "3"), "wqk": wqk_t, "wv": wv_t,
                    "cosT": np.ascontiguousarray(cosT), "sinT": np.ascontiguousarray(sinT)})
    resB = _run(_prog("LB", lambda: build_LB(T, segs)), ins)
    del resA, VPl, VNl, YBl, VVl
    Ql, _ = gather(resB, "qr", D)
    Kl, Kc = gather(resB, "kr", 256)
    Ql = np.stack([unpair_rows(Ql[b]) for b in range(2)])
    Kl = np.stack([unpair_rows(Kl[b]) for b in range(2)])
    Kc = np.stack([unpair_rows(Kc[b]) for b in range(2)])
    Vl, Vc = gather(resB, "vo", 256)

    mask = attn_mask()
    ident = np.eye(128, dtype=np.float32)
    ins = []
    for r in range(NCORE):
        b, g = r // 4, r % 4
        qt = Ql[b, g * 512:(g + 1) * 512].reshape(2, 4, 64, SEQ).transpose(0, 2, 1, 3).reshape(128, 4, SEQ)
        k1 = np.concatenate([Kl[b, g * 64:(g + 1) * 64], Kc[b, g * 64:(g + 1) * 64]], axis=1)
        vl = Vl[b, g * 64:(g + 1) * 64].T.reshape(NBLK, 128, 64)
        vc_ = Vc[b, g * 64:(g + 1) * 64].T.reshape(2, 128, 64)
        vall = np.concatenate([vl, vc_], axis=0).transpose(1, 0, 2)
        ins.append({"qt": np.ascontiguousarray(qt), "kt": np.ascontiguousarray(np.concatenate([k1, k1], axis=0)),
                    "v": np.ascontiguousarray(vall), "mask": mask,
                    "sink": np.ascontiguousarray(np.broadcast_to(attn_sink[0, g * 8:(g + 1) * 8], (128, 8))),
                    "ident": ident})
    resC = _run(_prog("LC", build_LC), ins)
    del Ql
    O = np.empty((2, D, SEQ), np.float32)
    for r in range(NCORE):
        b, g = r // 4, r % 4
        O[b, g * 512:(g + 1) * 512] = resC[r]["o"].T
    del resC

    _ffw.clear()
    awout_t = tile_w(attn_w_out[0], 256)
    ins = []
    for r, (b, q) in enumerate(cores):
        vecs = [mv(b, 0, 1, 5), mv(b, 0, 1, 6), mv(b, 0, 1, 7), mv(b, 0, 1, 8),
                lay(ln_g[1, 0]), lay(ln_b[1, 0]), lay(ln_g[1, 1]), lay(ln_b[1, 1]), lay(ln_g[1, 2]), lay(ln_b[1, 2])]
        ins.append({"xn4": np.ascontiguousarray(resB[r]["xn4"][:, :LT]), "oT": np.ascontiguousarray(O[b][:, q * LT:(q + 1) * LT]),
                    "vecs": np.ascontiguousarray(np.stack(vecs, axis=1)), "wout": awout_t, **ffw(1, 1, "")})
    resD = _run(_prog("LD", lambda: build_LD(LT)), ins)
    out = np.empty((2, SEQ, D), np.float32)
    for r, (b, q) in enumerate(cores):
        out[b, q * LT:(q + 1) * LT] = resD[r]["out"].T
    return out
```
